# Optimizing a Trainium2 kernel written in Bass

```python
import math
import jax, jax.numpy as jnp
from jax import lax
import numpy as np

D_MODEL = 1024
BATCH = 8
SEQ = 4096
DEPTH = 4

ATTN_HEADS = 8
ATTN_HEAD_DIM = 64
ATTN_V_DIM = 2 * ATTN_HEAD_DIM
ATTN_WIDTH = ATTN_HEADS * ATTN_V_DIM
Q_BLOCK = 128
SSD_EXPAND = 2
SSD_WIDTH = SSD_EXPAND * D_MODEL
SSD_HEAD_DIM = 64
SSD_HEADS = SSD_WIDTH // SSD_HEAD_DIM
SSD_GROUPS = 4
SSD_HEADS_PER_GROUP = SSD_HEADS // SSD_GROUPS
SSD_STATE = 128
SSD_CONV = 4
SSD_CHUNK = 128
CONV_DIM = SSD_WIDTH + 2 * SSD_GROUPS * SSD_STATE
DT_MIN = 0.001
DT_MAX = 0.1
N_BRANCHES = 2
IN_SPLITS = (ATTN_WIDTH, ATTN_WIDTH, ATTN_WIDTH, ATTN_WIDTH,
             CONV_DIM, SSD_WIDTH, SSD_HEADS,
             N_BRANCHES * D_MODEL)
D_IN_PROJ = sum(IN_SPLITS)
SPLIT_POINTS = [int(i) for i in np.cumsum(IN_SPLITS)[:-1]]
EPS = 1e-6

kernel_name = "hybrid_diffattn_ssd_gated_merge"


def rms_norm(x, w):
    xf = x.astype(jnp.float32)
    y = xf * lax.rsqrt(jnp.mean(xf * xf, axis=-1, keepdims=True) + EPS)
    return (y * w.astype(jnp.float32)).astype(x.dtype)


def lambda_init_fn(layer_idx):
    return 0.8 - 0.6 * math.exp(-0.3 * layer_idx)


def diff_attention(q, k, v, q_norm_w, k_norm_w, lam, subln_w, lambda_init):
    b, s = q.shape[0], q.shape[1]
    q = rms_norm(q, q_norm_w)
    k = rms_norm(k, k_norm_w)
    scale = ATTN_HEAD_DIM ** -0.5
    n_blk = s // Q_BLOCK
    q_blocks = q.reshape(b, n_blk, Q_BLOCK, ATTN_HEADS, 2, ATTN_HEAD_DIM).transpose(1, 0, 2, 3, 4, 5)
    key_pos = jnp.arange(s)

    def one_block(args):
        qb, i = args
        q_pos = i * Q_BLOCK + jnp.arange(Q_BLOCK)
        scores = jnp.einsum('bqhmd,bkhmd->bhmqk', qb, k).astype(jnp.float32) * scale
        causal = key_pos[None, :] <= q_pos[:, None]
        scores = jnp.where(causal, scores, -jnp.inf)
        p = jax.nn.softmax(scores, axis=-1)
        a = p[:, :, 0] - lam * p[:, :, 1]
        return jnp.einsum('bhqk,bkhe->bqhe', a.astype(v.dtype), v)

    o = lax.map(one_block, (q_blocks, jnp.arange(n_blk)))
    o = o.transpose(1, 0, 2, 3, 4).reshape(b, s, ATTN_HEADS, ATTN_V_DIM)
    o = rms_norm(o, subln_w) * (1.0 - lambda_init)
    return o.reshape(b, s, ATTN_WIDTH)


def causal_depthwise_conv(x, w, bias):
    y = lax.conv_general_dilated(
        x, w[:, None, :], window_strides=(1,), padding=[(SSD_CONV - 1, 0)],
        dimension_numbers=('NWC', 'WIO', 'NWC'), feature_group_count=x.shape[-1])
    return y + bias


def ssd_chunked(x, dt, A, B, C, D):
    b, s = x.shape[0], x.shape[1]
    nc, L, G, J = s // SSD_CHUNK, SSD_CHUNK, SSD_GROUPS, SSD_HEADS_PER_GROUP
    xf = x.astype(jnp.float32).reshape(b, nc, L, G, J, SSD_HEAD_DIM)
    dtc = dt.astype(jnp.float32).reshape(b, nc, L, G, J)
    Bc = B.astype(jnp.float32).reshape(b, nc, L, G, SSD_STATE)
    Cc = C.astype(jnp.float32).reshape(b, nc, L, G, SSD_STATE)
    a_cum = jnp.cumsum(dtc * A.astype(jnp.float32).reshape(G, J), axis=2)
    xdt = xf * dtc[..., None]
    a_t = a_cum.transpose(0, 1, 3, 4, 2)
    seg = a_t[..., :, None] - a_t[..., None, :]
    causal = jnp.tril(jnp.ones((L, L), dtype=bool))
    decay = jnp.exp(jnp.where(causal, seg, -jnp.inf))
    cb = jnp.einsum('bclgn,bcsgn->bcgls', Cc, Bc)
    y_diag = jnp.einsum('bcgjls,bcsgjp->bclgjp', cb[:, :, :, None] * decay, xdt)
    decay_states = jnp.exp(a_cum[:, :, -1:] - a_cum)
    states = jnp.einsum('bclgn,bclgjp->bcgjpn', Bc, xdt * decay_states[..., None])
    chunk_decay = jnp.exp(a_cum[:, :, -1])

    def step(h, inp):
        st, dec = inp
        return h * dec[..., None, None] + st, h

    h0 = jnp.zeros((b, G, J, SSD_HEAD_DIM, SSD_STATE), jnp.float32)
    _, prev = lax.scan(step, h0, (states.transpose(1, 0, 2, 3, 4, 5), chunk_decay.transpose(1, 0, 2, 3)))
    prev = prev.transpose(1, 0, 2, 3, 4, 5)
    y_off = jnp.einsum('bclgn,bcgjpn->bclgjp', Cc, prev) * jnp.exp(a_cum)[..., None]
    y = (y_diag + y_off).reshape(b, s, SSD_HEADS, SSD_HEAD_DIM)
    y = y + xf.reshape(b, s, SSD_HEADS, SSD_HEAD_DIM) * D.astype(jnp.float32)[:, None]
    return y.reshape(b, s, SSD_WIDTH).astype(x.dtype)


def gated_group_rms_norm(y, z, w):
    b, s = y.shape[0], y.shape[1]
    g = (y * jax.nn.silu(z)).reshape(b, s, SSD_GROUPS, SSD_WIDTH // SSD_GROUPS)
    gf = g.astype(jnp.float32)
    gf = gf * lax.rsqrt(jnp.mean(gf * gf, axis=-1, keepdims=True) + EPS)
    return (gf.reshape(b, s, SSD_WIDTH) * w.astype(jnp.float32)).astype(y.dtype)


def hybrid_layer(x, layer_idx, norm_w, w_in, q_norm_w, k_norm_w, diff_lambda, subln_w,
                 conv_w, conv_b, dt_bias, a_log, d_skip, ssd_norm_w,
                 w_proj_attn, w_proj_ssd, w_out):
    b, s = x.shape[0], x.shape[1]
    h = rms_norm(x, norm_w)
    proj = h @ w_in
    q, k, v, z_a, xbc, z_s, dt_raw, gate_logits = jnp.split(proj, SPLIT_POINTS, axis=-1)

    lambda_init = lambda_init_fn(layer_idx)
    lf = diff_lambda.astype(jnp.float32)
    lam = jnp.exp(jnp.sum(lf[0] * lf[1])) - jnp.exp(jnp.sum(lf[2] * lf[3])) + lambda_init
    y_a = diff_attention(q.reshape(b, s, ATTN_HEADS, 2, ATTN_HEAD_DIM),
                         k.reshape(b, s, ATTN_HEADS, 2, ATTN_HEAD_DIM),
                         v.reshape(b, s, ATTN_HEADS, ATTN_V_DIM),
                         q_norm_w, k_norm_w, lam, subln_w, lambda_init)
    y_a = y_a * jax.nn.silu(z_a)

    xbc = jax.nn.silu(causal_depthwise_conv(xbc, conv_w, conv_b))
    xs, Bm, Cm = jnp.split(xbc, [SSD_WIDTH, SSD_WIDTH + SSD_GROUPS * SSD_STATE], axis=-1)
    dt = jax.nn.softplus(dt_raw.astype(jnp.float32) + dt_bias.astype(jnp.float32))
    A = -jnp.exp(a_log.astype(jnp.float32))
    y_s = ssd_chunked(xs.reshape(b, s, SSD_HEADS, SSD_HEAD_DIM), dt, A,
                      Bm.reshape(b, s, SSD_GROUPS, SSD_STATE), Cm.reshape(b, s, SSD_GROUPS, SSD_STATE), d_skip)
    y_s = gated_group_rms_norm(y_s, z_s, ssd_norm_w)

    g_a, g_s = jnp.split(gate_logits, N_BRANCHES, axis=-1)
    merged = jax.nn.sigmoid(g_a) * (y_a @ w_proj_attn) + jax.nn.sigmoid(g_s) * (y_s @ w_proj_ssd)
    return x + (merged @ w_out).astype(x.dtype)


def setup_inputs(seed: int = 0) -> dict:
    key = jax.random.key(seed)
    ks = jax.random.split(key, 20)
    f32 = jnp.float32
    nrm = lambda k, shape, sc: jax.random.normal(k, shape, f32) * sc
    dt0 = jnp.exp(jax.random.uniform(ks[9], (DEPTH, SSD_HEADS), f32) * (math.log(DT_MAX) - math.log(DT_MIN)) + math.log(DT_MIN))
    dt0 = jnp.maximum(dt0, 1e-4)
    return {
        "x": nrm(ks[0], (BATCH, SEQ, D_MODEL), 1.0),
        "norm_w": 1.0 + nrm(ks[1], (DEPTH, D_MODEL), 0.02),
        "w_in": nrm(ks[2], (DEPTH, D_MODEL, D_IN_PROJ), D_MODEL ** -0.5),
        "q_norm_w": 1.0 + nrm(ks[3], (DEPTH, ATTN_HEAD_DIM), 0.02),
        "k_norm_w": 1.0 + nrm(ks[4], (DEPTH, ATTN_HEAD_DIM), 0.02),
        "diff_lambda": nrm(ks[5], (DEPTH, 4, ATTN_HEAD_DIM), 0.1),
        "subln_w": 1.0 + nrm(ks[6], (DEPTH, ATTN_V_DIM), 0.02),
        "conv_w": nrm(ks[7], (DEPTH, SSD_CONV, CONV_DIM), SSD_CONV ** -0.5),
        "conv_b": nrm(ks[8], (DEPTH, CONV_DIM), 0.02),
        "dt_bias": dt0 + jnp.log(-jnp.expm1(-dt0)),
        "a_log": jnp.log(jax.random.uniform(ks[10], (DEPTH, SSD_HEADS), f32, 1.0, 16.0)),
        "d_skip": 1.0 + nrm(ks[11], (DEPTH, SSD_HEADS), 0.02),
        "ssd_norm_w": 1.0 + nrm(ks[12], (DEPTH, SSD_WIDTH), 0.02),
        "w_proj_attn": nrm(ks[13], (DEPTH, ATTN_WIDTH, D_MODEL), ATTN_WIDTH ** -0.5),
        "w_proj_ssd": nrm(ks[14], (DEPTH, SSD_WIDTH, D_MODEL), SSD_WIDTH ** -0.5),
        "w_out": nrm(ks[15], (DEPTH, D_MODEL, D_MODEL), D_MODEL ** -0.5),
    }


def reference(x, norm_w, w_in, q_norm_w, k_norm_w, diff_lambda, subln_w, conv_w, conv_b,
              dt_bias, a_log, d_skip, ssd_norm_w, w_proj_attn, w_proj_ssd, w_out):
    for l in range(DEPTH):
        x = hybrid_layer(x, l, norm_w[l], w_in[l], q_norm_w[l], k_norm_w[l], diff_lambda[l], subln_w[l],
                         conv_w[l], conv_b[l], dt_bias[l], a_log[l], d_skip[l], ssd_norm_w[l],
                         w_proj_attn[l], w_proj_ssd[l], w_out[l])
    return x
```

```python
import contextlib
import math
import numpy as np
import concourse.bass as bass
import concourse.mybir as mybir
from concourse.bass_utils import run_bass_kernel_spmd
from concourse.alu_op_type import AluOpType as ALU

AF = mybir.ActivationFunctionType
F32 = mybir.dt.float32
BF16 = mybir.dt.bfloat16
AX = mybir.AxisListType

S = 4096
D = 1024
DIN = 11296
OFF_Q, OFF_K, OFF_V, OFF_ZA, OFF_XBC, OFF_ZS, OFF_DT, OFF_G = 0, 1024, 2048, 3072, 4096, 7168, 9216, 9248
EPS = 1e-6
import os as _os
DEPTH = int(_os.environ.get('KDEPTH', '4'))


class Op:
    __slots__ = ("eng", "fn", "args", "kw", "reads", "writes", "dma", "idx",
                 "deps", "signal", "token", "waits")

    def __init__(self, eng, fn, args, kw, reads, writes, dma):
        self.eng = eng
        self.fn = fn
        self.args = args
        self.kw = kw
        self.reads = reads
        self.writes = writes
        self.dma = dma
        self.deps = set()
        self.signal = False
        self.token = None
        self.waits = []


class Prog:
    def __init__(self, nc):
        self.nc = nc
        self.ops = []
        self.q = {"pe": nc.tensor, "act": nc.scalar, "dve": nc.vector,
                  "pool": nc.gpsimd, "sp": nc.sync}

    def add(self, eng, fn, *args, reads=(), writes=(), dma=None, **kw):
        op = Op(eng, fn, args, kw, tuple(reads), tuple(writes), dma)
        op.idx = len(self.ops)
        self.ops.append(op)
        return op

    def finish(self, sems, final_eng="sp"):
        ops = self.ops
        last_w = {}
        readers = {}
        sem_waiters = {}
        dma_cum = {}
        for op in ops:
            deps = set()
            raw = set()
            for r in op.reads:
                w = last_w.get(r)
                if w is not None:
                    deps.add(w)
                    raw.add(w)
            for wkey in op.writes:
                w = last_w.get(wkey)
                if w is not None:
                    deps.add(w)
                for ridx in readers.get(wkey, {}).values():
                    deps.add(ridx)
            deps.discard(op.idx)
            keep = set()
            for d in deps:
                dop = ops[d]
                if dop.dma is None and op.dma is None and dop.eng == op.eng:
                    if op.eng == "pe" or d not in raw:
                        continue
                keep.add(d)
            if op.dma is not None:
                for e, widx in sem_waiters.get(op.dma, {}).items():
                    if e != op.eng and widx != op.idx:
                        keep.add(widx)
            for d in keep:
                if ops[d].dma is not None:
                    sem_waiters.setdefault(ops[d].dma, {})[op.eng] = op.idx
            op.deps = keep
            for r in op.reads:
                rd = readers.setdefault(r, {})
                if op.dma is not None:
                    rd[("dma", op.dma)] = op.idx
                else:
                    rd[op.eng] = op.idx
            for wkey in op.writes:
                last_w[wkey] = op.idx
                readers[wkey] = {}
        eng_cnt = {}
        for op in ops:
            for d in op.deps:
                if ops[d].dma is None:
                    ops[d].signal = True
        known = {}
        for op in ops:
            kn = known.setdefault(op.eng, {})
            waits = {}
            for d in sorted(op.deps):
                dop = ops[d]
                if dop.dma is not None:
                    s = ("dma", dop.dma)
                    v = dma_cum[dop.dma]
                else:
                    s = ("eng", dop.eng)
                    v = dop.token[1]
                if kn.get(s, 0) >= v:
                    continue
                waits[s] = max(waits.get(s, 0), v)
            for s, v in waits.items():
                kn[s] = v
            op.waits = list(waits.items())
            if op.dma is not None:
                dma_cum[op.dma] = dma_cum.get(op.dma, 0) + 16
                op.token = (("dma", op.dma), dma_cum[op.dma])
            else:
                if op.signal:
                    eng_cnt[op.eng] = eng_cnt.get(op.eng, 0) + 1
                    op.token = (("eng", op.eng), eng_cnt[op.eng])
                else:
                    op.token = (("eng", op.eng), eng_cnt.get(op.eng, 0) + 1)
        n_wait = 0
        for op in ops:
            q = self.q[op.eng]
            for s, v in op.waits:
                q.wait_ge(sems(s), v)
                n_wait += 1
            ins = op.fn(*op.args, **op.kw)
            if op.dma is not None:
                ins.then_inc(sems(("dma", op.dma)), 16)
            elif op.signal:
                ins.then_inc(sems(("eng", op.eng)), 1)
        q = self.q[final_eng]
        for k, v in dma_cum.items():
            q.wait_ge(sems(("dma", k)), v)
        for e, v in eng_cnt.items():
            if e != final_eng:
                q.wait_ge(sems(("eng", e)), v)
        self.stats = dict(n_ops=len(ops), n_wait=n_wait, eng_cnt=dict(eng_cnt),
                          n_dma_sems=len(dma_cum))


def lambda_init_fn(layer_idx):
    return 0.8 - 0.6 * math.exp(-0.3 * layer_idx)


def build(depth=DEPTH, dbg=False, phases="ASGTD"):
    nc = bass.Bass("TRN2", target_bir_lowering=False)
    P = Prog(nc)
    es = contextlib.ExitStack()
    uid = [0]

    def din(name, shape, dt=F32):
        return nc.dram_tensor(name, list(shape), dt, kind="ExternalInput").ap()

    def dscr(name, shape, dt):
        return nc.dram_tensor(name, list(shape), dt, kind="ExternalOutput" if dbg else "Internal").ap()

    x_d = din("x", [S, D])
    w_in_d = din("w_in", [DEPTH, D, DIN])
    w_pa_d = din("w_pa", [DEPTH, D, D])
    w_ps_d = din("w_ps", [DEPTH, 2 * D, D])
    w_out_d = din("w_out", [DEPTH, D, D])
    nw_d = din("nw_rep", [DEPTH, 128, D])
    qkw_d = din("qkw", [DEPTH, 128, 2])
    dl_d = din("dl_rep", [DEPTH, 128, 256])
    sw_d = din("sw_rep", [DEPTH, 128, 128])
    cw_d = din("convw", [DEPTH, 128, 24 * 4])
    cb_d = din("convb", [DEPTH, 128, 24])
    hp_d = din("hp_rep", [DEPTH, 128, 96])
    snw_d = din("snw_rep", [DEPTH, 128, 2048])
    cst_d = din("consts", [128, 5 * 128])
    out_d = nc.dram_tensor("out", [S, D], F32, kind="ExternalOutput").ap()
    yaT_d = dscr("yaT", [8, 128, S], BF16)
    ysT_d = dscr("ysT", [16, 128, S], BF16)
    sgT_d = dscr("sgT", [16, 128, S], BF16)
    xtm_d = dscr("xtm", [S, 2560], BF16)
    bct_d = dscr("bct", [8, 128, S], BF16)

    sem_cache = {}

    def sems(key):
        if key not in sem_cache:
            sem_cache[key] = es.enter_context(nc.semaphore("s%d" % len(sem_cache)))
        return sem_cache[key]

    class Scope:
        def __init__(self):
            self.es = contextlib.ExitStack()

        def __enter__(self):
            self.es.__enter__()
            return self

        def __exit__(self, *a):
            return self.es.__exit__(*a)

        def sb(self, name, shape, dt):
            uid[0] += 1
            return self.es.enter_context(nc.sbuf_tensor("%s_%d" % (name, uid[0]), list(shape), dt))

        def pool(self, name, n, shape, dt):
            return RPool(self, name, n, shape, dt)

    class RPool:
        def __init__(self, sc, name, n, shape, dt):
            self.name = name
            self.tiles = [sc.sb("%s%d" % (name, i), shape, dt) for i in range(n)]
            self.i = 0

        def next(self):
            j = self.i % len(self.tiles)
            self.i += 1
            return self.tiles[j], (self.name, j)

    def op(eng, fn, *a, r=(), w=(), dma=None, **kw):
        lk = tuple(("lock", k[1]) for k in tuple(r) + tuple(w) if isinstance(k, tuple) and k[0] == "bank")
        return P.add(eng, fn, *a, reads=tuple(r) + ("PH",), writes=tuple(w) + lk, dma=dma, **kw)

    mm = nc.tensor.matmul
    tr = nc.tensor.transpose
    act = nc.scalar.activation
    tt = nc.vector.tensor_tensor
    ts = nc.vector.tensor_scalar
    stt = nc.vector.scalar_tensor_tensor
    MUL, ADD, SUB = ALU.mult, ALU.add, ALU.subtract

    with es:
        g = Scope()
        es.enter_context(g)
        banks = [es.enter_context(nc.psum_tensor("bank%d" % i, [128, 512], F32)) for i in range(8)]

        def bk(i):
            return ("bank", i)

        def bf(i):
            return banks[i][:].bitcast(BF16)

        cst_f = g.sb("cst_f", [128, 5 * 128], F32)
        cst_b = g.sb("cst_b", [128, 5 * 128], BF16)
        op("sp", nc.sync.dma_start, out=cst_f[:], in_=cst_d, w=["cst_f"], dma="cst_f")
        op("pool", nc.gpsimd.dma_start, out=cst_b[:], in_=cst_d, w=["cst_b"], dma="cst_b")
        ident_f = cst_f[:, 0:128]
        U_f = cst_f[:, 256:384]
        Ls_f = cst_f[:, 384:512]
        ident_b = cst_b[:, 0:128]
        maskU_b = cst_b[:, 128:256]
        BD_b = cst_b[:, 512:640]
        CST = ["cst_f", "cst_b"]
        epst = g.sb("eps", [128, 1], F32)
        op("dve", nc.vector.memset, epst[:], EPS, w=["eps"])
        ones_f = g.sb("ones_f", [128, 128], F32)
        op("dve", nc.vector.memset, ones_f[:], 1.0, w=["ones_f"])
        bar_t = g.sb("bar_t", [128, 1], F32)
        nw = g.sb("nw", [128, D], F32)
        qkw = g.sb("qkw", [128, 2], F32)
        dl = g.sb("dl", [128, 256], F32)
        sw = g.sb("sw", [128, 128], F32)
        cw = g.sb("cw", [128, 96], F32)
        cb = g.sb("cb", [128, 24], F32)
        hp = g.sb("hp", [128, 96], F32)
        snw = g.sb("snw", [128, 2048], F32)
        neglam = g.sb("neglam", [128, 1], F32)
        Aneg = g.sb("Aneg", [128, 32], F32)
        lamt = g.sb("lamt", [128, 4], F32)
        lamp = g.sb("lamp", [128, 128], F32)

        def barrier():
            op("dve", nc.vector.memset, bar_t[:], 0.0, w=["PH"])

        def load_params(l):
            li = lambda_init_fn(l)
            for t, src, key in ((nw, nw_d, "nw"), (qkw, qkw_d, "qkw"), (dl, dl_d, "dl"), (sw, sw_d, "sw"),
                                (cw, cw_d, "cw"), (cb, cb_d, "cb"), (hp, hp_d, "hp"), (snw, snw_d, "snw")):
                op("sp", nc.sync.dma_start, out=t[:], in_=src[l], w=[key], dma=key)
            op("dve", ts, out=qkw[:, 0:1], in0=qkw[:, 0:1], scalar1=0.125, scalar2=None, op0=MUL, r=["qkw"], w=["qkw"])
            op("dve", ts, out=sw[:], in0=sw[:], scalar1=float(1.0 - li), scalar2=None, op0=MUL, r=["sw"], w=["sw"])
            op("dve", tt, out=lamp[:, 0:64], in0=dl[:, 0:64], in1=dl[:, 64:128], op=MUL, r=["dl"], w=["lamp"])
            op("dve", tt, out=lamp[:, 64:128], in0=dl[:, 128:192], in1=dl[:, 192:256], op=MUL, r=["dl"], w=["lamp"])
            op("dve", nc.vector.reduce_sum, out=lamt[:, 0:2], in_=lamp[:].rearrange("p (a b) -> p a b", b=64), axis=AX.X,
               r=["lamp"], w=["lamt"])
            op("act", act, out=lamt[:, 2:4], in_=lamt[:, 0:2], func=AF.Exp, r=["lamt"], w=["lamt2"])
            op("dve", tt, out=neglam[:], in0=lamt[:, 3:4], in1=lamt[:, 2:3], op=SUB, r=["lamt2"], w=["neglam"])
            op("dve", ts, out=neglam[:], in0=neglam[:], scalar1=float(-li), scalar2=None, op0=ADD, r=["neglam"], w=["neglam"])
            op("act", act, out=Aneg[:], in_=hp[:, 32:64], func=AF.Exp, r=["hp"], w=["Aneg"])
            op("dve", ts, out=Aneg[:], in0=Aneg[:], scalar1=-1.0, scalar2=None, op0=MUL, r=["Aneg"], w=["Aneg"])

        PARAMS = ["nw", "qkw", "dl", "sw", "cw", "cb", "hp", "snw", "neglam", "Aneg"]

        def phase_A(l, hT, xsrc):
            with Scope() as sc:
                xt = sc.pool("xt", 2, [128, D], F32)
                hb = sc.pool("hb", 2, [128, D], BF16)
                junk = sc.sb("junk", [128, D], BF16)
                ssp = sc.pool("ss", 2, [128, 1], F32)
                rsp = sc.pool("rs", 2, [128, 1], F32)
                for t in range(32):
                    x_t, kx = xt.next()
                    op("sp", nc.sync.dma_start, out=x_t[:], in_=xsrc[t * 128:(t + 1) * 128, :],
                       r=[("xres", t // 4)], w=[kx], dma=kx)
                    s_t, ks = ssp.next()
                    op("act", act, out=junk[:], in_=x_t[:], func=AF.Square, accum_out=s_t[:], r=[kx], w=["junkA", ks])
                    r_t, kr = rsp.next()
                    op("act", act, out=r_t[:], in_=s_t[:], func=AF.Sqrt, bias=epst[:], scale=1.0 / D,
                       r=[ks, "eps"], w=[kr])
                    op("dve", nc.vector.reciprocal, out=r_t[:], in_=r_t[:], r=[kr], w=[kr])
                    h_t, kh = hb.next()
                    op("dve", stt, out=h_t[:], in0=x_t[:], scalar=r_t[:], in1=nw[:], op0=MUL, op1=MUL,
                       r=[kx, kr, "nw"], w=[kh])
                    b = t % 2
                    ptv = bf(b).rearrange("p (a b) -> p a b", b=128)
                    for kc in range(8):
                        op("pe", tr, out=ptv[:, kc, :], in_=h_t[:, kc * 128:(kc + 1) * 128], identity=ident_b,
                           r=[kh] + CST, w=[bk(b)])
                    if t % 2 == 0:
                        op("act", nc.scalar.copy, out=hT[:, :, t * 128:(t + 1) * 128], in_=ptv, r=[bk(b)], w=[("hT", t)])
                    else:
                        op("dve", nc.vector.tensor_copy, out=hT[:, :, t * 128:(t + 1) * 128], in_=ptv, r=[bk(b)],
                           w=[("hT", t)])

        def hTk(tb):
            return [("hT", 4 * tb + i) for i in range(4)]

        def phase_S1(l, hT):
            w_l = w_in_d[l].rearrange("(kc p) n -> p kc n", p=128)
            xtm_v = xtm_d.rearrange("(t p) c -> p t c", p=128)
            with Scope() as sc:
                wp = sc.pool("wcc", 2, [128, 8, 128], BF16)
                xcp = sc.pool("xc", 2, [128, S + 3], F32)
                accp = sc.pool("acc", 2, [128, 2048], F32)
                xop = sc.pool("xo", 2, [128, S], BF16)
                tmp_ = sc.pool("tmt", 3, [128, 8, 128], BF16)
                for t_, k_ in ((xcp.tiles[0], (xcp.name, 0)), (xcp.tiles[1], (xcp.name, 1))):
                    op("dve", nc.vector.memset, t_[:, 0:3], 0.0, w=[k_])
                nb = 0
                for cc in range(24):
                    w_t, kw_ = wp.next()
                    c0 = OFF_XBC + cc * 128
                    op("pool", nc.gpsimd.dma_start, out=w_t[:], in_=w_l[:, :, c0:c0 + 128], w=[kw_], dma=kw_)
                    xc, kxc = xcp.next()
                    for tb in range(8):
                        b = nb % 4
                        nb += 1
                        for kc in range(8):
                            op("pe", mm, banks[b][:], lhsT=w_t[:, kc, :], rhs=hT[:, kc, tb * 512:(tb + 1) * 512],
                               start=(kc == 0), stop=(kc == 7), r=[kw_] + hTk(tb), w=[bk(b)])
                        op("act", nc.scalar.copy, out=xc[:, 3 + tb * 512: 3 + (tb + 1) * 512], in_=banks[b][:],
                           r=[bk(b)], w=[kxc])
                    xo, kxo = xop.next()
                    for half in range(2):
                        a_t, ka = accp.next()
                        o0 = half * 2048
                        op("dve", ts, out=a_t[:], in0=xc[:, o0:o0 + 2048], scalar1=cw[:, cc * 4:cc * 4 + 1],
                           scalar2=cb[:, cc:cc + 1], op0=MUL, op1=ADD, r=[kxc, "cw", "cb"], w=[ka])
                        for k in range(1, 4):
                            op("dve", stt, out=a_t[:], in0=xc[:, o0 + k:o0 + k + 2048],
                               scalar=cw[:, cc * 4 + k:cc * 4 + k + 1], in1=a_t[:], op0=MUL, op1=ADD,
                               r=[kxc, "cw", ka], w=[ka])
                        op("act", act, out=xo[:, o0:o0 + 2048], in_=a_t[:], func=AF.Silu, r=[ka], w=[kxo])
                    if cc >= 16:
                        op("sp", nc.sync.dma_start, out=bct_d[cc - 16], in_=xo[:], r=[kxo], w=[("bct", cc - 16)], dma=kxo)
                    if cc < 20:
                        for q4 in range(4):
                            b = 4 + (q4 % 2)
                            ptv = bf(b).rearrange("p (a b) -> p a b", b=128)
                            for i in range(8):
                                t0 = (q4 * 8 + i) * 128
                                op("pe", tr, out=ptv[:, i, :], in_=xo[:, t0:t0 + 128], identity=ident_b,
                                   r=[kxo] + CST, w=[bk(b)])
                            tm, ktm = tmp_.next()
                            if q4 % 2 == 0:
                                op("act", nc.scalar.copy, out=tm[:], in_=ptv, r=[bk(b)], w=[ktm])
                            else:
                                op("dve", nc.vector.tensor_copy, out=tm[:], in_=ptv, r=[bk(b)], w=[ktm])
                            op("sp", nc.sync.dma_start, out=xtm_v[:, q4 * 8:(q4 + 1) * 8, cc * 128:(cc + 1) * 128],
                               in_=tm[:], r=[ktm], w=[("xtm", cc, q4)], dma=ktm)

        def phase_S2(l, hT):
            w_l = w_in_d[l].rearrange("(kc p) n -> p kc n", p=128)
            bct_v = bct_d.rearrange("g p t -> p g t")
            ysT_v = ysT_d.rearrange("k p t -> p k t")
            with Scope() as sc:
                wzs = sc.sb("wzs", [128, 8, 2048], BF16)
                wdt = sc.sb("wdt", [128, 8, 32], BF16)
                op("pool", nc.gpsimd.dma_start, out=wzs[:], in_=w_l[:, :, OFF_ZS:OFF_ZS + 2048], w=["wzs"], dma="wzs")
                op("pool", nc.gpsimd.dma_start, out=wdt[:], in_=w_l[:, :, OFF_DT:OFF_DT + 32], w=["wdt"], dma="wdt")
                Dd = sc.sb("Dd", [128, 32, 128], BF16)
                op("dve", tt, out=Dd[:], in0=ident_f.unsqueeze(1).broadcast_to([128, 32, 128]),
                   in1=hp[:, 64:96].unsqueeze(2).broadcast_to([128, 32, 128]), op=MUL, r=["hp"] + CST, w=["Dd"])
                st = sc.sb("st", [128, 2048], F32)
                stb = sc.sb("stb", [128, 2048], BF16)
                op("dve", nc.vector.memset, st[:], 0.0, w=["st"])
                op("dve", nc.vector.memset, stb[:], 0.0, w=[("stb", i) for i in range(4)])
                xtp = sc.pool("xtc", 2, [128, 2560], BF16)
                bcp = sc.pool("bcc", 2, [128, 8, 128], BF16)
                smp = sc.pool("sm", 2, [128, 8, 32], F32)
                xdtp = sc.pool("xdt", 2, [128, 32, 64], BF16)
                xdsp = sc.pool("xds", 2, [128, 32, 64], BF16)
                cbmp = sc.pool("cbm", 2, [128, 4, 128], BF16)
                ltp = sc.pool("lt", 2, [128, 4, 128], F32)
                dcp = sc.pool("dcT", 2, [128, 4, 128], BF16)
                mtp = sc.pool("MT", 3, [128, 4, 128], BF16)
                t1p = sc.pool("t1", 2, [128, 512], F32)
                szp = sc.pool("sz", 2, [128, 512], F32)
                gnp = sc.pool("gn", 2, [128, 512], BF16)
                junk = sc.sb("junkS", [128, 512], BF16)
                sqp = sc.pool("ssq", 2, [128, 1], F32)
                rqp = sc.pool("rsq", 2, [128, 1], F32)
                ysp = sc.pool("ysc", 2, [128, 16, 128], BF16)
                nrb = 0
                for c in range(32):
                    tok = slice(c * 128, (c + 1) * 128)
                    hk = [("hT", c)]
                    xt_c, kxt = xtp.next()
                    op("sp", nc.sync.dma_start, out=xt_c[:], in_=xtm_d[tok, :],
                       r=[("xtm", cc, c // 8) for cc in range(20)], w=[kxt], dma=kxt)
                    bc_c, kbc = bcp.next()
                    op("sp", nc.sync.dma_start, out=bc_c[:], in_=bct_v[:, :, tok],
                       r=[("bct", i) for i in range(8)], w=[kbc], dma=kbc)
                    sm, ksm = smp.next()
                    for kc in range(8):
                        op("pe", mm, banks[0][:, 0:32], lhsT=hT[:, kc, tok], rhs=wdt[:, kc, :], start=(kc == 0),
                           stop=(kc == 7), r=hk + ["wdt"], w=[bk(0)])
                    op("dve", tt, out=sm[:, 0, :], in0=banks[0][:, 0:32], in1=hp[:, 0:32], op=ADD, r=[bk(0), "hp"], w=[ksm])
                    op("act", act, out=sm[:, 1, :], in_=sm[:, 0, :], func=AF.Exp, r=[ksm], w=[ksm])
                    op("act", act, out=sm[:, 2, :], in_=sm[:, 1, :], func=AF.Ln, bias=1.0, r=[ksm], w=[ksm])
                    op("dve", tt, out=sm[:, 3, :], in0=sm[:, 2, :], in1=Aneg[:], op=MUL, r=[ksm, "Aneg"], w=[ksm])
                    op("pe", mm, banks[0][:, 32:64], lhsT=U_f, rhs=sm[:, 3, :], start=True, stop=True,
                       r=[ksm] + CST, w=[bk(0)])
                    op("pe", mm, banks[0][:, 64:96], lhsT=ones_f[:], rhs=sm[:, 3, :], start=True, stop=True,
                       r=[ksm, "ones_f"], w=[bk(0)])
                    op("dve", nc.vector.tensor_copy, out=sm[:, 4, :], in_=banks[0][:, 32:64], r=[bk(0)], w=[ksm])
                    op("act", act, out=sm[:, 5, :], in_=banks[0][:, 32:64], func=AF.Exp, r=[bk(0)], w=[ksm])
                    op("dve", tt, out=sm[:, 6, :], in0=banks[0][:, 64:96], in1=sm[:, 4, :], op=SUB, r=[bk(0), ksm], w=[ksm])
                    op("act", act, out=sm[:, 6, :], in_=sm[:, 6, :], func=AF.Exp, r=[ksm], w=[ksm])
                    op("act", act, out=sm[:, 7, :], in_=banks[0][:, 64:96], func=AF.Exp, r=[bk(0)], w=[ksm])
                    xdt, kxdt = xdtp.next()
                    op("dve", tt, out=xdt[:], in0=xt_c[:, 0:2048].rearrange("p (a b) -> p a b", b=64),
                       in1=sm[:, 2, :].unsqueeze(2).broadcast_to([128, 32, 64]), op=MUL, r=[kxt, ksm], w=[kxdt])
                    xds, kxds = xdsp.next()
                    op("pool", nc.gpsimd.tensor_tensor, out=xds[:], in0=xdt[:],
                       in1=sm[:, 6, :].unsqueeze(2).broadcast_to([128, 32, 64]), op=MUL, r=[kxdt, ksm], w=[kxds])
                    cbv = banks[1][:].rearrange("p (a b) -> p a b", b=128)
                    for gi in range(4):
                        op("pe", mm, cbv[:, gi, :], lhsT=bc_c[:, gi, :], rhs=bc_c[:, 4 + gi, :], start=True, stop=True,
                           r=[kbc], w=[bk(1)])
                    cbm, kcbm = cbmp.next()
                    op("dve", tt, out=cbm[:], in0=cbv, in1=maskU_b.unsqueeze(1).broadcast_to([128, 4, 128]), op=MUL,
                       r=[bk(1)] + CST, w=[kcbm])
                    ys_c, kys = ysp.next()
                    for gi in range(4):
                        for hq in range(2):
                            h0 = gi * 8 + hq * 4
                            lt, klt = ltp.next()
                            op("pool", nc.gpsimd.tensor_tensor, out=lt[:],
                               in0=Ls_f.unsqueeze(1).broadcast_to([128, 4, 128]),
                               in1=sm[:, 3, h0:h0 + 4].unsqueeze(2).broadcast_to([128, 4, 128]), op=MUL,
                               r=[ksm] + CST, w=[klt])
                            rb = 2 + (nrb % 2)
                            nrb += 1
                            rbv = banks[rb][:].rearrange("p (a b) -> p a b", b=128)
                            for i in range(4):
                                op("pe", mm, rbv[:, i, :], lhsT=lt[:, i, :], rhs=U_f, start=True, stop=True,
                                   r=[klt] + CST, w=[bk(rb)])
                            dc, kdc = dcp.next()
                            op("act", act, out=dc[:], in_=rbv, func=AF.Exp, r=[bk(rb)], w=[kdc])
                            mt, kmt = mtp.next()
                            op("dve", tt, out=mt[:], in0=dc[:], in1=cbm[:, gi:gi + 1, :].broadcast_to([128, 4, 128]),
                               op=MUL, r=[kdc, kcbm], w=[kmt])
                            for i in range(4):
                                hh = h0 + i
                                j = hq * 4 + i
                                op("pe", mm, banks[4][:, j * 64:(j + 1) * 64], lhsT=mt[:, i, :], rhs=xdt[:, hh, :],
                                   start=True, stop=False, r=[kmt, kxdt], w=[bk(4)])
                                op("pe", mm, banks[4][:, j * 64:(j + 1) * 64], lhsT=Dd[:, hh, :],
                                   rhs=xt_c[:, hh * 64:(hh + 1) * 64], start=False, stop=True, r=["Dd", kxt], w=[bk(4)])
                        op("pe", mm, banks[5][:], lhsT=bc_c[:, 4 + gi, :], rhs=stb[:, gi * 512:(gi + 1) * 512],
                           start=True, stop=True, r=[kbc, ("stb", gi)], w=[bk(5)])
                        for kc in range(8):
                            op("pe", mm, banks[6][:], lhsT=hT[:, kc, tok], rhs=wzs[:, kc, gi * 512:(gi + 1) * 512],
                               start=(kc == 0), stop=(kc == 7), r=hk + ["wzs"], w=[bk(6)])
                        op("pe", mm, banks[7][:], lhsT=xt_c[:, 2048 + gi * 128:2048 + (gi + 1) * 128],
                           rhs=xds[:, gi * 8:(gi + 1) * 8, :], start=True, stop=True, r=[kxt, kxds], w=[bk(7)])
                        t1, kt1 = t1p.next()
                        op("dve", tt, out=t1[:].rearrange("p (a b) -> p a b", b=64),
                           in0=banks[5][:].rearrange("p (a b) -> p a b", b=64),
                           in1=sm[:, 5, gi * 8:(gi + 1) * 8].unsqueeze(2).broadcast_to([128, 8, 64]), op=MUL,
                           r=[bk(5), ksm], w=[kt1])
                        y, ky = t1, kt1
                        op("dve", tt, out=y[:], in0=banks[4][:], in1=t1[:], op=ADD, r=[bk(4), kt1], w=[ky])
                        sz, ksz = szp.next()
                        op("act", act, out=sz[:], in_=banks[6][:], func=AF.Silu, r=[bk(6)], w=[ksz])
                        gt, kgt = sz, ksz
                        op("pool", nc.gpsimd.tensor_tensor, out=gt[:], in0=y[:], in1=sz[:], op=MUL, r=[ky, ksz], w=[kgt])
                        sq, ksq = sqp.next()
                        op("act", act, out=junk[:], in_=gt[:], func=AF.Square, accum_out=sq[:], r=[kgt], w=["junkS", ksq])
                        rq, krq = rqp.next()
                        op("act", act, out=rq[:], in_=sq[:], func=AF.Sqrt, bias=epst[:], scale=1.0 / 512.0,
                           r=[ksq, "eps"], w=[krq])
                        op("dve", nc.vector.reciprocal, out=rq[:], in_=rq[:], r=[krq], w=[krq])
                        gn, kgn = gnp.next()
                        op("dve", stt, out=gn[:], in0=gt[:], scalar=rq[:], in1=snw[:, gi * 512:(gi + 1) * 512],
                           op0=MUL, op1=MUL, r=[kgt, krq, "snw"], w=[kgn])
                        ptv = bf(5).rearrange("p (a b) -> p a b", b=128)
                        for i in range(4):
                            op("pe", tr, out=ptv[:, i, :], in_=gn[:, i * 128:(i + 1) * 128], identity=ident_b,
                               r=[kgn] + CST, w=[bk(5)])
                        op("act", nc.scalar.copy, out=ys_c[:, gi * 4:(gi + 1) * 4, :], in_=ptv[:, 0:4, :], r=[bk(5)], w=[kys])
                        stv = st[:, gi * 512:(gi + 1) * 512]
                        op("pool", nc.gpsimd.tensor_tensor, out=stv.rearrange("p (a b) -> p a b", b=64),
                           in0=stv.rearrange("p (a b) -> p a b", b=64),
                           in1=sm[:, 7, gi * 8:(gi + 1) * 8].unsqueeze(2).broadcast_to([128, 8, 64]), op=MUL,
                           r=[("st", gi), ksm], w=[("st", gi)])
                        op("dve", tt, out=stv, in0=banks[7][:], in1=stv, op=ADD, r=[bk(7), ("st", gi)], w=[("st", gi)])
                        op("act", nc.scalar.copy, out=stb[:, gi * 512:(gi + 1) * 512], in_=stv, r=[("st", gi)],
                           w=[("stb", gi)])
                    op("sp", nc.sync.dma_start, out=ysT_v[:, :, tok], in_=ys_c[:], r=[kys], w=[("ysT", c // 4)], dma=kys)

        def phase_G(l, hT):
            w_l = w_in_d[l].rearrange("(kc p) n -> p kc n", p=128)
            with Scope() as sc:
                wp = sc.pool("wg", 2, [128, 8, 128], BF16)
                sgp = sc.pool("sg", 2, [128, S], BF16)
                nb = 0
                for gc in range(16):
                    w_t, kw_ = wp.next()
                    c0 = OFF_G + gc * 128
                    op("pool", nc.gpsimd.dma_start, out=w_t[:], in_=w_l[:, :, c0:c0 + 128], w=[kw_], dma=kw_)
                    sg, ksg = sgp.next()
                    for tb in range(8):
                        b = nb % 4
                        nb += 1
                        for kc in range(8):
                            op("pe", mm, banks[b][:], lhsT=w_t[:, kc, :], rhs=hT[:, kc, tb * 512:(tb + 1) * 512],
                               start=(kc == 0), stop=(kc == 7), r=[kw_] + hTk(tb), w=[bk(b)])
                        op("act", act, out=sg[:, tb * 512:(tb + 1) * 512], in_=banks[b][:], func=AF.Sigmoid,
                           r=[bk(b)], w=[ksg])
                    op("sp", nc.sync.dma_start, out=sgT_d[gc], in_=sg[:], r=[ksg], w=[("sgT", gc)], dma=ksg)

        def phase_T(l, hT):
            w_l = w_in_d[l].rearrange("(kc p) n -> p kc n", p=128)
            LOOK = 2
            with Scope() as sc:
                whp = sc.pool("wh", 2, [128, 8, 4, 128], BF16)
                qTp = sc.pool("qT", 2, [128, S], BF16)
                kTp = sc.pool("kT", 2, [128, S], BF16)
                vxp = sc.pool("vx", 2, [128, 32, 130], BF16)
                szp = sc.pool("sza", 2, [128, 32, 128], BF16)
                sqp = sc.pool("sq", 2, [128, 512], BF16)
                lnp = sc.pool("lnq", 2, [128, 512], F32)
                rsp = sc.pool("rst", 2, [128, 512], F32)
                ezp = sc.pool("ez", 2, [128, 2, 128], F32)
                ptp = sc.pool("PT", 6, [128, 512], BF16)
                accp = sc.pool("accs", 2, [128, 8, 130], F32)
                rlp = sc.pool("rl", 2, [128, 8], F32)
                o1p = sc.pool("o1", 2, [128, 4, 128], F32)
                op_ = sc.pool("o", 2, [128, 4, 128], F32)
                sqo = sc.sb("sqo", [128, 4, 128], F32)
                ssp = sc.pool("sso", 2, [128, 4], F32)
                rop = sc.pool("ro", 2, [128, 4], F32)
                ybp = sc.pool("yb", 2, [128, 4, 128], BF16)
                yTp = sc.pool("yaT", 2, [128, 512], BF16)
                for t_, k_ in ((vxp.tiles[0], (vxp.name, 0)), (vxp.tiles[1], (vxp.name, 1))):
                    op("dve", nc.vector.memset, t_[:, :, 128:129], 1.0, w=[k_])
                nS = [0]
                nI = [0]
                for h in range(8):
                    wh, kwh = whp.next()
                    kws = []
                    for i, off in enumerate((OFF_Q, OFF_K, OFF_V, OFF_ZA)):
                        c0 = off + h * 128
                        kwi = kwh + (i,)
                        kws.append(kwi)
                        op("pool", nc.gpsimd.dma_start, out=wh[:, :, i, :], in_=w_l[:, :, c0:c0 + 128], w=[kwi], dma=kwi)
                    qT, kq = qTp.next()
                    kT, kk = kTp.next()
                    vx, kv = vxp.next()
                    sza, kz = szp.next()
                    for which, dst, kd in ((0, qT, kq), (1, kT, kk)):
                        for tb in range(8):
                            ba = 3 + (nI[0] % 2)
                            bs = 5 + (nI[0] % 2)
                            nI[0] += 1
                            for kc in range(8):
                                op("pe", mm, banks[ba][:], lhsT=wh[:, kc, which, :], rhs=hT[:, kc, tb * 512:(tb + 1) * 512],
                                   start=(kc == 0), stop=(kc == 7), r=[kws[which]] + hTk(tb), w=[bk(ba)])
                            sq, ksq = sqp.next()
                            op("act", act, out=sq[:], in_=banks[ba][:], func=AF.Square, r=[bk(ba)], w=[ksq])
                            op("pe", mm, banks[bs][:], lhsT=BD_b, rhs=sq[:], start=True, stop=True, r=[ksq] + CST, w=[bk(bs)])
                            ln, kln = lnp.next()
                            op("act", act, out=ln[:], in_=banks[bs][:], func=AF.Ln, bias=epst[:], scale=1.0 / 64.0,
                               r=[bk(bs), "eps"], w=[kln])
                            rs, krs = rsp.next()
                            op("act", act, out=rs[:], in_=ln[:], func=AF.Exp, scale=-0.5, r=[kln], w=[krs])
                            op("dve", stt, out=dst[:, tb * 512:(tb + 1) * 512], in0=banks[ba][:],
                               scalar=qkw[:, which:which + 1], in1=rs[:], op0=MUL, op1=MUL, r=[bk(ba), krs, "qkw"], w=[kd])
                    for t2 in range(16):
                        b = 6 + (t2 % 2)
                        pv = banks[b][:].rearrange("p (a b) -> p a b", b=256)
                        for i in range(2):
                            t = t2 * 2 + i
                            for kc in range(8):
                                op("pe", mm, pv[:, i, :], lhsT=hT[:, kc, t * 128:(t + 1) * 128], rhs=wh[:, kc, 2:4, :],
                                   start=(kc == 0), stop=(kc == 7), r=[kws[2], kws[3], ("hT", t)], w=[bk(b)])
                        op("dve", nc.vector.tensor_copy, out=vx[:, t2 * 2:t2 * 2 + 2, 0:128], in_=pv[:, :, 0:128],
                           r=[bk(b)], w=[kv])
                        ez, kez = ezp.next()
                        op("act", act, out=ez[:], in_=pv[:, :, 128:256], func=AF.Exp, scale=-1.0, r=[bk(b)], w=[kez])
                        op("dve", ts, out=ez[:], in0=ez[:], scalar1=1.0, scalar2=None, op0=ADD, r=[kez], w=[kez])
                        op("dve", nc.vector.reciprocal, out=ez[:], in_=ez[:], r=[kez], w=[kez])
                        op("dve", tt, out=sza[:, t2 * 2:t2 * 2 + 2, :], in0=pv[:, :, 128:256], in1=ez[:], op=MUL,
                           r=[bk(b), kez], w=[kz])
                    op("pool", nc.gpsimd.tensor_tensor, out=sza[:], in0=sza[:],
                       in1=sw[:].unsqueeze(1).broadcast_to([128, 32, 128]), op=MUL, r=[kz, "sw"], w=[kz])
                    accv = [banks[i][:, 0:387].rearrange("p (a b) -> p a b", b=129) for i in range(3)]
                    steps = [(qb, t, m) for qb in range(8) for t in range(4 * qb + 4) for m in range(2)]
                    pts = {}
                    started = {}

                    def emit_qk(j):
                        qb, t, m = steps[j]
                        r0 = max(0, t - 4 * qb)
                        off = r0 * 128
                        sb_ = 3 + (nS[0] % 3)
                        nS[0] += 1
                        pr = slice(m * 64, (m + 1) * 64)
                        op("pe", mm, banks[sb_][:, 0:512 - off], lhsT=kT[pr, t * 128:(t + 1) * 128],
                           rhs=qT[pr, qb * 512 + off:(qb + 1) * 512], start=True, stop=True,
                           r=[kk, kq], w=[bk(sb_)])
                        pt, kpt = ptp.next()
                        op("act", act, out=pt[:, 0:512 - off], in_=banks[sb_][:, 0:512 - off], func=AF.Exp,
                           r=[bk(sb_)], w=[kpt])
                        if t >= 4 * qb:
                            op("pool", nc.gpsimd.tensor_tensor, out=pt[:, 0:128], in0=pt[:, 0:128], in1=maskU_b,
                               op=MUL, r=[kpt] + CST, w=[kpt])
                        pts[j] = (pt, kpt)

                    def emit_pv(i):
                        qb, t, m = steps[i]
                        r0 = max(0, t - 4 * qb)
                        pt, kpt = pts.pop(i)
                        for jq in range(r0, 4):
                            a = m * 4 + jq
                            ab, asl = a // 3, a % 3
                            st_ = not started.get((qb, ab), False)
                            started[(qb, ab)] = True
                            op("pe", mm, accv[ab][:, asl, :], lhsT=pt[:, (jq - r0) * 128:(jq - r0 + 1) * 128],
                               rhs=vx[:, t, 0:129], start=st_, stop=(t == 4 * qb + jq), skip_group_check=True,
                               r=[kpt, kv], w=[bk(ab)])

                    def emit_fin(qb):
                        acs, kac = accp.next()
                        op("dve", nc.vector.tensor_copy, out=acs[:, 0:3, 0:129], in_=accv[0], r=[bk(0)], w=[kac])
                        op("dve", nc.vector.tensor_copy, out=acs[:, 3:6, 0:129], in_=accv[1], r=[bk(1)], w=[kac])
                        op("dve", nc.vector.tensor_copy, out=acs[:, 6:8, 0:129], in_=accv[2][:, 0:2, :], r=[bk(2)], w=[kac])
                        rl, krl = rlp.next()
                        op("dve", nc.vector.reciprocal, out=rl[:], in_=acs[:, :, 128], r=[kac], w=[krl])
                        op("dve", ts, out=rl[:, 4:8], in0=rl[:, 4:8], scalar1=neglam[:, 0:1], scalar2=None, op0=MUL,
                           r=[krl, "neglam"], w=[krl])
                        o1, ko1 = o1p.next()
                        op("pool", nc.gpsimd.tensor_tensor, out=o1[:], in0=acs[:, 4:8, 0:128],
                           in1=rl[:, 4:8].unsqueeze(2).broadcast_to([128, 4, 128]), op=MUL, r=[kac, krl], w=[ko1])
                        o, ko = op_.next()
                        op("dve", tt, out=o[:], in0=acs[:, 0:4, 0:128],
                           in1=rl[:, 0:4].unsqueeze(2).broadcast_to([128, 4, 128]), op=MUL, r=[kac, krl], w=[ko])
                        op("dve", tt, out=o[:], in0=o[:], in1=o1[:], op=ADD, r=[ko, ko1], w=[ko])
                        op("pool", nc.gpsimd.tensor_tensor, out=sqo[:], in0=o[:], in1=o[:], op=MUL, r=[ko], w=["sqo"])
                        ss, kss = ssp.next()
                        op("dve", nc.vector.reduce_sum, out=ss[:], in_=sqo[:], axis=AX.X, r=["sqo"], w=[kss])
                        ro, kro = rop.next()
                        op("act", act, out=ro[:], in_=ss[:], func=AF.Ln, bias=epst[:], scale=1.0 / 128.0,
                           r=[kss, "eps"], w=[kro])
                        op("act", act, out=ro[:], in_=ro[:], func=AF.Exp, scale=-0.5, r=[kro], w=[kro])
                        op("dve", tt, out=o[:], in0=o[:], in1=ro[:].unsqueeze(2).broadcast_to([128, 4, 128]), op=MUL,
                           r=[ko, kro], w=[ko])
                        yb, kyb = ybp.next()
                        op("pool", nc.gpsimd.tensor_tensor, out=yb[:], in0=o[:], in1=sza[:, qb * 4:(qb + 1) * 4, :], op=MUL,
                           r=[ko, kz], w=[kyb])
                        ptv = bf(7)[:, 0:512].rearrange("p (a b) -> p a b", b=128)
                        for jq in range(4):
                            op("pe", tr, out=ptv[:, jq, :], in_=yb[:, jq, :], identity=ident_b, r=[kyb] + CST, w=[bk(7)])
                        yT, kyT = yTp.next()
                        op("dve", nc.vector.tensor_copy, out=yT[:], in_=bf(7)[:, 0:512], r=[bk(7)], w=[kyT])
                        op("sp", nc.sync.dma_start, out=yaT_d[h][:, qb * 512:(qb + 1) * 512], in_=yT[:], r=[kyT],
                           w=[("yaT", h, qb)], dma=kyT)

                    n = len(steps)
                    for i in range(-LOOK, n):
                        j = i + LOOK
                        if j < n:
                            emit_qk(j)
                        if i >= 0:
                            emit_pv(i)
                            qb, t, m = steps[i]
                            if t == 4 * qb + 3 and m == 1:
                                emit_fin(qb)

        def phase_D(l, xsrc):
            wpa_v = w_pa_d[l].rearrange("(kc p) n -> p kc n", p=128)
            wps_v = w_ps_d[l].rearrange("(kc p) n -> p kc n", p=128)
            wo_v = w_out_d[l].rearrange("(kc p) n -> p kc n", p=128)
            yaT_v = yaT_d.rearrange("h p t -> p h t")
            ysT_v = ysT_d.rearrange("k p t -> p k t")
            sgT_v = sgT_d.rearrange("k p t -> p k t")
            with Scope() as sc:
                wpa = sc.sb("wpa", [128, 8, D], BF16)
                wps = sc.sb("wps", [128, 16, D], BF16)
                wo = sc.sb("wo", [128, 8, D], BF16)
                op("pool", nc.gpsimd.dma_start, out=wpa[:], in_=wpa_v, w=["wpa"], dma="wpa")
                op("pool", nc.gpsimd.dma_start, out=wps[:, 0:8], in_=wps_v[:, 0:8], w=["wps"], dma="wps")
                op("pool", nc.gpsimd.dma_start, out=wps[:, 8:16], in_=wps_v[:, 8:16], w=["wps"], dma="wps")
                op("pool", nc.gpsimd.dma_start, out=wo[:], in_=wo_v, w=["wo"], dma="wo")
                yap = sc.pool("yab", 2, [128, 8, 512], BF16)
                ysp = sc.pool("ysb", 2, [128, 16, 512], BF16)
                sgp = sc.pool("sgb", 1, [128, 16, 512], BF16)
                xrp = sc.pool("xr", 1, [128, 4, D], F32)
                m1p = sc.pool("m1", 2, [128, 512], F32)
                m2p = sc.pool("m2", 2, [128, 512], F32)
                mTp = sc.pool("mT", 2, [128, 8, 512], BF16)
                xop = sc.pool("xo", 1, [128, 4, D], F32)
                nb = 0
                for tb in range(8):
                    tok = slice(tb * 512, (tb + 1) * 512)
                    ya, kya = yap.next()
                    ys, kys = ysp.next()
                    sg, ksg = sgp.next()
                    xr, kxr = xrp.next()
                    op("sp", nc.sync.dma_start, out=ya[:], in_=yaT_v[:, :, tok], r=[("yaT", h, tb) for h in range(8)],
                       w=[kya], dma=kya)
                    op("sp", nc.sync.dma_start, out=ys[:], in_=ysT_v[:, :, tok], r=[("ysT", tb)], w=[kys], dma=kys)
                    op("sp", nc.sync.dma_start, out=sg[:], in_=sgT_v[:, :, tok], r=[("sgT", i) for i in range(16)],
                       w=[ksg], dma=ksg)
                    op("sp", nc.sync.dma_start, out=xr[:], in_=xsrc[tok, :].rearrange("(t p) c -> p t c", p=128),
                       r=[("xres", tb)], w=[kxr], dma=kxr)
                    mT, kmT = mTp.next()
                    for cc in range(8):
                        b1 = nb % 6
                        b2 = (nb + 1) % 6
                        nb += 2
                        for kc in range(8):
                            op("pe", mm, banks[b1][:], lhsT=wpa[:, kc, cc * 128:(cc + 1) * 128], rhs=ya[:, kc, :],
                               start=(kc == 0), stop=(kc == 7), r=["wpa", kya], w=[bk(b1)])
                        for kc in range(16):
                            op("pe", mm, banks[b2][:], lhsT=wps[:, kc, cc * 128:(cc + 1) * 128], rhs=ys[:, kc, :],
                               start=(kc == 0), stop=(kc == 15), r=["wps", kys], w=[bk(b2)])
                        m1, km1 = m1p.next()
                        op("dve", tt, out=m1[:], in0=banks[b1][:], in1=sg[:, cc, :], op=MUL, r=[bk(b1), ksg], w=[km1])
                        m2, km2 = m2p.next()
                        op("dve", tt, out=m2[:], in0=banks[b2][:], in1=sg[:, 8 + cc, :], op=MUL, r=[bk(b2), ksg], w=[km2])
                        op("pool", nc.gpsimd.tensor_tensor, out=mT[:, cc, :], in0=m1[:], in1=m2[:], op=ADD,
                           r=[km1, km2], w=[kmT])
                    xo, kxo = xop.next()
                    for t4 in range(4):
                        for half in range(2):
                            b = 6 + (t4 * 2 + half) % 2
                            for kc in range(8):
                                op("pe", mm, banks[b][:], lhsT=mT[:, kc, t4 * 128:(t4 + 1) * 128],
                                   rhs=wo[:, kc, half * 512:(half + 1) * 512], start=(kc == 0), stop=(kc == 7),
                                   r=[kmT, "wo"], w=[bk(b)])
                            op("dve", tt, out=xo[:, t4, half * 512:(half + 1) * 512], in0=banks[b][:],
                               in1=xr[:, t4, half * 512:(half + 1) * 512], op=ADD, r=[bk(b), kxr], w=[kxo])
                    op("sp", nc.sync.dma_start, out=out_d[tok, :].rearrange("(t p) c -> p t c", p=128), in_=xo[:],
                       r=[kxo], w=[("xres", tb)], dma=kxo)

        for l in range(depth):
            xsrc = x_d if l == 0 else out_d
            barrier()
            load_params(l)
            with Scope() as lsc:
                hT = lsc.sb("hT", [128, 8, S], BF16)
                if "A" in phases:
                    phase_A(l, hT, xsrc)
                    barrier()
                if "S" in phases:
                    phase_S1(l, hT)
                    barrier()
                    phase_S2(l, hT)
                    barrier()
                if "G" in phases:
                    phase_G(l, hT)
                    barrier()
                if "T" in phases:
                    phase_T(l, hT)
                    barrier()
            if "D" in phases:
                phase_D(l, xsrc)
        P.finish(sems)
    return nc, P.stats


def make_consts():
    i = np.arange(128)
    ident = np.eye(128, dtype=np.float32)
    maskU = (i[None, :] >= i[:, None]).astype(np.float32)
    Lstrict = (i[:, None] > i[None, :]).astype(np.float32)
    bd = ((i[:, None] // 64) == (i[None, :] // 64)).astype(np.float32)
    return np.concatenate([ident, maskU, maskU, Lstrict, bd], axis=1).astype(np.float32)


def host_layout(inputs):
    inputs = {k: (np.asarray(v)[:DEPTH] if k != "x" else v) for k, v in inputs.items()}
    f = lambda a: np.ascontiguousarray(np.asarray(a, dtype=np.float32))
    rep = lambda a: np.ascontiguousarray(np.broadcast_to(np.asarray(a, np.float32)[:, None, :], (a.shape[0], 128, a.shape[1])))
    qn, kn = np.asarray(inputs["q_norm_w"], np.float32), np.asarray(inputs["k_norm_w"], np.float32)
    qkw = np.stack([np.tile(qn, (1, 2)), np.tile(kn, (1, 2))], axis=-1)
    cw = np.asarray(inputs["conv_w"], np.float32)
    convw = cw.transpose(0, 2, 1).reshape(DEPTH, 24, 128, 4).transpose(0, 2, 1, 3).reshape(DEPTH, 128, 96)
    convb = np.asarray(inputs["conv_b"], np.float32).reshape(DEPTH, 24, 128).transpose(0, 2, 1)
    hp = np.concatenate([inputs["dt_bias"], inputs["a_log"], inputs["d_skip"]], axis=-1).astype(np.float32)
    common = {
        "w_in": f(inputs["w_in"]), "w_pa": f(inputs["w_proj_attn"]), "w_ps": f(inputs["w_proj_ssd"]),
        "w_out": f(inputs["w_out"]),
        "nw_rep": rep(inputs["norm_w"]), "qkw": f(qkw),
        "dl_rep": rep(np.asarray(inputs["diff_lambda"], np.float32).reshape(DEPTH, 256)),
        "sw_rep": rep(inputs["subln_w"]), "convw": f(convw), "convb": f(convb),
        "hp_rep": rep(hp), "snw_rep": rep(inputs["ssd_norm_w"]), "consts": make_consts(),
    }
    return common


_NC_CACHE = {}


def kernel(**inputs):
    common = host_layout(inputs)
    x = np.asarray(inputs["x"], np.float32)
    n = x.shape[0]
    if "nc" not in _NC_CACHE:
        _NC_CACHE["nc"] = build()[0]
    nc = _NC_CACHE["nc"]
    in_maps = [dict(common, x=np.ascontiguousarray(x[b])) for b in range(n)]
    res = run_bass_kernel_spmd(nc, in_maps, core_ids=list(range(n)))
    return np.stack([np.asarray(r["out"], np.float32) for r in res.results], axis=0)
```

```python
import contextlib
import math
import numpy as np
import concourse.bass as bass
import concourse.mybir as mybir
from concourse.bass_utils import run_bass_kernel_spmd
from concourse.alu_op_type import AluOpType as ALU

AF = mybir.ActivationFunctionType
F32 = mybir.dt.float32
BF16 = mybir.dt.bfloat16
AX = mybir.AxisListType

S = 4096
D = 1024
DIN = 11296
OFF_Q, OFF_K, OFF_V, OFF_ZA, OFF_XBC, OFF_ZS, OFF_DT, OFF_G = 0, 1024, 2048, 3072, 4096, 7168, 9216, 9248
EPS = 1e-6
import os as _os
DEPTH = int(_os.environ.get('KDEPTH', '4'))


class Op:
    __slots__ = ("eng", "fn", "args", "kw", "reads", "writes", "dma", "idx",
                 "deps", "signal", "token", "waits")

    def __init__(self, eng, fn, args, kw, reads, writes, dma):
        self.eng = eng
        self.fn = fn
        self.args = args
        self.kw = kw
        self.reads = reads
        self.writes = writes
        self.dma = dma
        self.deps = set()
        self.signal = False
        self.token = None
        self.waits = []


class Prog:
    def __init__(self, nc):
        self.nc = nc
        self.ops = []
        self.q = {"pe": nc.tensor, "act": nc.scalar, "dve": nc.vector,
                  "pool": nc.gpsimd, "sp": nc.sync}

    def add(self, eng, fn, *args, reads=(), writes=(), dma=None, **kw):
        op = Op(eng, fn, args, kw, tuple(reads), tuple(writes), dma)
        op.idx = len(self.ops)
        self.ops.append(op)
        return op

    def finish(self, sems, final_eng="sp"):
        ops = self.ops
        last_w = {}
        readers = {}
        sem_waiters = {}
        dma_cum = {}
        for op in ops:
            deps = set()
            raw = set()
            for r in op.reads:
                w = last_w.get(r)
                if w is not None:
                    deps.add(w)
                    raw.add(w)
            for wkey in op.writes:
                w = last_w.get(wkey)
                if w is not None:
                    deps.add(w)
                for ridx in readers.get(wkey, {}).values():
                    deps.add(ridx)
            deps.discard(op.idx)
            keep = set()
            for d in deps:
                dop = ops[d]
                if dop.dma is None and op.dma is None and dop.eng == op.eng:
                    if op.eng == "pe" or d not in raw:
                        continue
                keep.add(d)
            if op.dma is not None:
                for e, widx in sem_waiters.get(op.dma, {}).items():
                    if e != op.eng and widx != op.idx:
                        keep.add(widx)
            for d in keep:
                if ops[d].dma is not None:
                    sem_waiters.setdefault(ops[d].dma, {})[op.eng] = op.idx
            op.deps = keep
            for r in op.reads:
                rd = readers.setdefault(r, {})
                if op.dma is not None:
                    rd[("dma", op.dma)] = op.idx
                else:
                    rd[op.eng] = op.idx
            for wkey in op.writes:
                last_w[wkey] = op.idx
                readers[wkey] = {}
        eng_cnt = {}
        for op in ops:
            for d in op.deps:
                if ops[d].dma is None:
                    ops[d].signal = True
        known = {}
        for op in ops:
            kn = known.setdefault(op.eng, {})
            waits = {}
            for d in sorted(op.deps):
                dop = ops[d]
                if dop.dma is not None:
                    s = ("dma", dop.dma)
                    v = dma_cum[dop.dma]
                else:
                    s = ("eng", dop.eng)
                    v = dop.token[1]
                if kn.get(s, 0) >= v:
                    continue
                waits[s] = max(waits.get(s, 0), v)
            for s, v in waits.items():
                kn[s] = v
            op.waits = list(waits.items())
            if op.dma is not None:
                dma_cum[op.dma] = dma_cum.get(op.dma, 0) + 16
                op.token = (("dma", op.dma), dma_cum[op.dma])
            else:
                if op.signal:
                    eng_cnt[op.eng] = eng_cnt.get(op.eng, 0) + 1
                    op.token = (("eng", op.eng), eng_cnt[op.eng])
                else:
                    op.token = (("eng", op.eng), eng_cnt.get(op.eng, 0) + 1)
        n_wait = 0
        for op in ops:
            q = self.q[op.eng]
            for s, v in op.waits:
                q.wait_ge(sems(s), v)
                n_wait += 1
            ins = op.fn(*op.args, **op.kw)
            if op.dma is not None:
                ins.then_inc(sems(("dma", op.dma)), 16)
            elif op.signal:
                ins.then_inc(sems(("eng", op.eng)), 1)
        q = self.q[final_eng]
        for k, v in dma_cum.items():
            q.wait_ge(sems(("dma", k)), v)
        for e, v in eng_cnt.items():
            if e != final_eng:
                q.wait_ge(sems(("eng", e)), v)
        self.stats = dict(n_ops=len(ops), n_wait=n_wait, eng_cnt=dict(eng_cnt),
                          n_dma_sems=len(dma_cum))


def lambda_init_fn(layer_idx):
    return 0.8 - 0.6 * math.exp(-0.3 * layer_idx)


def build(depth=DEPTH, dbg=False, phases="ASGTD"):
    nc = bass.Bass("TRN2", target_bir_lowering=False)
    P = Prog(nc)
    es = contextlib.ExitStack()
    uid = [0]

    def din(name, shape, dt=F32):
        return nc.dram_tensor(name, list(shape), dt, kind="ExternalInput").ap()

    def dscr(name, shape, dt):
        return nc.dram_tensor(name, list(shape), dt, kind="ExternalOutput" if dbg else "Internal").ap()

    x_d = din("x", [S, D])
    w_in_d = din("w_in", [DEPTH, D, DIN])
    w_pa_d = din("w_pa", [DEPTH, D, D])
    w_ps_d = din("w_ps", [DEPTH, 2 * D, D])
    w_out_d = din("w_out", [DEPTH, D, D])
    nw_d = din("nw_rep", [DEPTH, 128, D])
    qkw_d = din("qkw", [DEPTH, 128, 2])
    dl_d = din("dl_rep", [DEPTH, 128, 256])
    sw_d = din("sw_rep", [DEPTH, 128, 128])
    swc_d = din("swc", [DEPTH, 128, 1])
    cw_d = din("convw", [DEPTH, 128, 24 * 4])
    cb_d = din("convb", [DEPTH, 128, 24])
    hp_d = din("hp_rep", [DEPTH, 128, 96])
    snw_d = din("snw_rep", [DEPTH, 128, 2048])
    cst_d = din("consts", [128, 6 * 128])
    out_d = nc.dram_tensor("out", [S, D], F32, kind="ExternalOutput").ap()
    yaT_d = dscr("yaT", [8, 128, S], BF16)
    ysT_d = dscr("ysT", [16, 128, S], BF16)
    sgT_d = dscr("sgT", [16, 128, S], BF16)
    xtm_d = dscr("xtm", [S, 2560], BF16)
    bct_d = dscr("bct", [8, 128, S], BF16)

    sem_cache = {}

    def sems(key):
        if key not in sem_cache:
            sem_cache[key] = es.enter_context(nc.semaphore("s%d" % len(sem_cache)))
        return sem_cache[key]

    class Scope:
        def __init__(self):
            self.es = contextlib.ExitStack()

        def __enter__(self):
            self.es.__enter__()
            return self

        def __exit__(self, *a):
            return self.es.__exit__(*a)

        def sb(self, name, shape, dt):
            uid[0] += 1
            return self.es.enter_context(nc.sbuf_tensor("%s_%d" % (name, uid[0]), list(shape), dt))

        def pool(self, name, n, shape, dt):
            return RPool(self, name, n, shape, dt)

    class RPool:
        def __init__(self, sc, name, n, shape, dt):
            self.name = name
            self.tiles = [sc.sb("%s%d" % (name, i), shape, dt) for i in range(n)]
            self.i = 0

        def next(self):
            j = self.i % len(self.tiles)
            self.i += 1
            return self.tiles[j], (self.name, j)

    def op(eng, fn, *a, r=(), w=(), dma=None, **kw):
        lk = tuple(("lock", k[1]) for k in tuple(r) + tuple(w) if isinstance(k, tuple) and k[0] == "bank")
        return P.add(eng, fn, *a, reads=tuple(r) + ("PH",), writes=tuple(w) + lk, dma=dma, **kw)

    mm = nc.tensor.matmul
    tr = nc.tensor.transpose
    act = nc.scalar.activation
    tt = nc.vector.tensor_tensor
    ts = nc.vector.tensor_scalar
    stt = nc.vector.scalar_tensor_tensor
    MUL, ADD, SUB = ALU.mult, ALU.add, ALU.subtract

    with es:
        g = Scope()
        es.enter_context(g)
        pbig = [es.enter_context(nc.psum_tensor("pbig%d" % i, [128, 1024], F32)) for i in range(4)]
        banks = [pbig[i // 2][:, (i % 2) * 512:(i % 2 + 1) * 512] for i in range(8)]

        def bk(i):
            return ("bank", i)

        def bf(i):
            return banks[i][:].bitcast(BF16)

        cst_f = g.sb("cst_f", [128, 6 * 128], F32)
        cst_b = g.sb("cst_b", [128, 6 * 128], BF16)
        op("sp", nc.sync.dma_start, out=cst_f[:], in_=cst_d, w=["cst_f"], dma="cst_f")
        op("pool", nc.gpsimd.dma_start, out=cst_b[:], in_=cst_d, w=["cst_b"], dma="cst_b")
        ident_f = cst_f[:, 0:128]
        U_f = cst_f[:, 256:384]
        Ls_f = cst_f[:, 384:512]
        ident_b = cst_b[:, 0:128]
        maskU_b = cst_b[:, 128:256]
        BD_b = cst_b[:, 512:640]
        Mneg_b = cst_b[:, 640:768]
        CST = ["cst_f", "cst_b"]
        epst = g.sb("eps", [128, 1], F32)
        op("dve", nc.vector.memset, epst[:], EPS, w=["eps"])
        ones_f = g.sb("ones_f", [128, 128], F32)
        op("dve", nc.vector.memset, ones_f[:], 1.0, w=["ones_f"])
        ones_b = g.sb("ones_b", [128, 128], BF16)
        op("dve", nc.vector.memset, ones_b[:], 1.0, w=["ones_b"])
        swc = g.sb("swc", [128, 1], F32)
        bar_t = g.sb("bar_t", [128, 1], F32)
        nw = g.sb("nw", [128, D], F32)
        qkw = g.sb("qkw", [128, 2], F32)
        dl = g.sb("dl", [128, 256], F32)
        sw = g.sb("sw", [128, 128], F32)
        cw = g.sb("cw", [128, 96], F32)
        cb = g.sb("cb", [128, 24], F32)
        hp = g.sb("hp", [128, 96], F32)
        snw = g.sb("snw", [128, 2048], F32)
        neglam = g.sb("neglam", [128, 1], F32)
        Aneg = g.sb("Aneg", [128, 32], F32)
        lamt = g.sb("lamt", [128, 4], F32)
        lamp = g.sb("lamp", [128, 128], F32)

        def barrier():
            op("dve", nc.vector.memset, bar_t[:], 0.0, w=["PH"])

        def load_params(l):
            li = lambda_init_fn(l)
            for t, src, key in ((nw, nw_d, "nw"), (qkw, qkw_d, "qkw"), (dl, dl_d, "dl"), (sw, sw_d, "sw"),
                                (cw, cw_d, "cw"), (cb, cb_d, "cb"), (hp, hp_d, "hp"), (snw, snw_d, "snw")):
                op("sp", nc.sync.dma_start, out=t[:], in_=src[l], w=[key], dma=key)
            op("dve", ts, out=qkw[:, 0:1], in0=qkw[:, 0:1], scalar1=0.125, scalar2=None, op0=MUL, r=["qkw"], w=["qkw"])
            op("dve", ts, out=sw[:], in0=sw[:], scalar1=float(1.0 - li), scalar2=None, op0=MUL, r=["sw"], w=["sw"])
            op("sp", nc.sync.dma_start, out=swc[:], in_=swc_d[l], w=["swc"], dma="swc")
            op("dve", ts, out=swc[:], in0=swc[:], scalar1=float(1.0 - li), scalar2=None, op0=MUL, r=["swc"], w=["swc"])
            op("dve", tt, out=lamp[:, 0:64], in0=dl[:, 0:64], in1=dl[:, 64:128], op=MUL, r=["dl"], w=["lamp"])
            op("dve", tt, out=lamp[:, 64:128], in0=dl[:, 128:192], in1=dl[:, 192:256], op=MUL, r=["dl"], w=["lamp"])
            op("dve", nc.vector.reduce_sum, out=lamt[:, 0:2], in_=lamp[:].rearrange("p (a b) -> p a b", b=64), axis=AX.X,
               r=["lamp"], w=["lamt"])
            op("act", act, out=lamt[:, 2:4], in_=lamt[:, 0:2], func=AF.Exp, r=["lamt"], w=["lamt2"])
            op("dve", tt, out=neglam[:], in0=lamt[:, 3:4], in1=lamt[:, 2:3], op=SUB, r=["lamt2"], w=["neglam"])
            op("dve", ts, out=neglam[:], in0=neglam[:], scalar1=float(-li), scalar2=None, op0=ADD, r=["neglam"], w=["neglam"])
            op("act", act, out=Aneg[:], in_=hp[:, 32:64], func=AF.Exp, r=["hp"], w=["Aneg"])
            op("dve", ts, out=Aneg[:], in0=Aneg[:], scalar1=-1.0, scalar2=None, op0=MUL, r=["Aneg"], w=["Aneg"])

        PARAMS = ["nw", "qkw", "dl", "sw", "cw", "cb", "hp", "snw", "neglam", "Aneg"]

        def phase_A(l, hT, xsrc):
            with Scope() as sc:
                xt = sc.pool("xt", 2, [128, D], F32)
                hb = sc.pool("hb", 2, [128, D], BF16)
                junk = sc.sb("junk", [128, D], BF16)
                ssp = sc.pool("ss", 2, [128, 1], F32)
                rsp = sc.pool("rs", 2, [128, 1], F32)
                for t in range(32):
                    x_t, kx = xt.next()
                    op("sp", nc.sync.dma_start, out=x_t[:], in_=xsrc[t * 128:(t + 1) * 128, :],
                       r=[("xres", t // 4)], w=[kx], dma=kx)
                    s_t, ks = ssp.next()
                    op("act", act, out=junk[:], in_=x_t[:], func=AF.Square, accum_out=s_t[:], r=[kx], w=["junkA", ks])
                    r_t, kr = rsp.next()
                    op("act", act, out=r_t[:], in_=s_t[:], func=AF.Sqrt, bias=epst[:], scale=1.0 / D,
                       r=[ks, "eps"], w=[kr])
                    op("dve", nc.vector.reciprocal, out=r_t[:], in_=r_t[:], r=[kr], w=[kr])
                    h_t, kh = hb.next()
                    op("dve", stt, out=h_t[:], in0=x_t[:], scalar=r_t[:], in1=nw[:], op0=MUL, op1=MUL,
                       r=[kx, kr, "nw"], w=[kh])
                    b = t % 2
                    ptv = bf(b).rearrange("p (a b) -> p a b", b=128)
                    for kc in range(8):
                        op("pe", tr, out=ptv[:, kc, :], in_=h_t[:, kc * 128:(kc + 1) * 128], identity=ident_b,
                           r=[kh] + CST, w=[bk(b)])
                    if t % 2 == 0:
                        op("act", nc.scalar.copy, out=hT[:, :, t * 128:(t + 1) * 128], in_=ptv, r=[bk(b)], w=[("hT", t)])
                    else:
                        op("dve", nc.vector.tensor_copy, out=hT[:, :, t * 128:(t + 1) * 128], in_=ptv, r=[bk(b)],
                           w=[("hT", t)])

        def hTk(tb):
            return [("hT", 4 * tb + i) for i in range(4)]

        def phase_S1(l, hT):
            w_l = w_in_d[l].rearrange("(kc p) n -> p kc n", p=128)
            xtm_v = xtm_d.rearrange("(t p) c -> p t c", p=128)
            with Scope() as sc:
                wp = sc.pool("wcc", 2, [128, 8, 128], BF16)
                xcp = sc.pool("xc", 2, [128, S + 3], F32)
                accp = sc.pool("acc", 2, [128, 2048], F32)
                xop = sc.pool("xo", 2, [128, S], BF16)
                tmp_ = sc.pool("tmt", 3, [128, 8, 128], BF16)
                for t_, k_ in ((xcp.tiles[0], (xcp.name, 0)), (xcp.tiles[1], (xcp.name, 1))):
                    op("dve", nc.vector.memset, t_[:, 0:3], 0.0, w=[k_])
                nb = 0
                for cc in range(24):
                    w_t, kw_ = wp.next()
                    c0 = OFF_XBC + cc * 128
                    op("pool", nc.gpsimd.dma_start, out=w_t[:], in_=w_l[:, :, c0:c0 + 128], w=[kw_], dma=kw_)
                    xc, kxc = xcp.next()
                    for tb in range(8):
                        b = nb % 4
                        nb += 1
                        for kc in range(8):
                            op("pe", mm, banks[b][:], lhsT=w_t[:, kc, :], rhs=hT[:, kc, tb * 512:(tb + 1) * 512],
                               start=(kc == 0), stop=(kc == 7), r=[kw_] + hTk(tb), w=[bk(b)])
                        op("act", nc.scalar.copy, out=xc[:, 3 + tb * 512: 3 + (tb + 1) * 512], in_=banks[b][:],
                           r=[bk(b)], w=[kxc])
                    xo, kxo = xop.next()
                    for half in range(2):
                        a_t, ka = accp.next()
                        o0 = half * 2048
                        op("dve", ts, out=a_t[:], in0=xc[:, o0:o0 + 2048], scalar1=cw[:, cc * 4:cc * 4 + 1],
                           scalar2=cb[:, cc:cc + 1], op0=MUL, op1=ADD, r=[kxc, "cw", "cb"], w=[ka])
                        for k in range(1, 4):
                            op("dve", stt, out=a_t[:], in0=xc[:, o0 + k:o0 + k + 2048],
                               scalar=cw[:, cc * 4 + k:cc * 4 + k + 1], in1=a_t[:], op0=MUL, op1=ADD,
                               r=[kxc, "cw", ka], w=[ka])
                        op("act", act, out=xo[:, o0:o0 + 2048], in_=a_t[:], func=AF.Silu, r=[ka], w=[kxo])
                    if cc >= 16:
                        op("sp", nc.sync.dma_start, out=bct_d[cc - 16], in_=xo[:], r=[kxo], w=[("bct", cc - 16)], dma=kxo)
                    if cc < 20:
                        for q4 in range(4):
                            b = 4 + (q4 % 2)
                            ptv = bf(b).rearrange("p (a b) -> p a b", b=128)
                            for i in range(8):
                                t0 = (q4 * 8 + i) * 128
                                op("pe", tr, out=ptv[:, i, :], in_=xo[:, t0:t0 + 128], identity=ident_b,
                                   r=[kxo] + CST, w=[bk(b)])
                            tm, ktm = tmp_.next()
                            if q4 % 2 == 0:
                                op("act", nc.scalar.copy, out=tm[:], in_=ptv, r=[bk(b)], w=[ktm])
                            else:
                                op("dve", nc.vector.tensor_copy, out=tm[:], in_=ptv, r=[bk(b)], w=[ktm])
                            op("sp", nc.sync.dma_start, out=xtm_v[:, q4 * 8:(q4 + 1) * 8, cc * 128:(cc + 1) * 128],
                               in_=tm[:], r=[ktm], w=[("xtm", cc, q4)], dma=ktm)

        def phase_S2(l, hT):
            w_l = w_in_d[l].rearrange("(kc p) n -> p kc n", p=128)
            bct_v = bct_d.rearrange("g p t -> p g t")
            ysT_v = ysT_d.rearrange("k p t -> p k t")
            with Scope() as sc:
                wzs = sc.sb("wzs", [128, 8, 2048], BF16)
                wdt = sc.sb("wdt", [128, 8, 32], BF16)
                op("pool", nc.gpsimd.dma_start, out=wzs[:], in_=w_l[:, :, OFF_ZS:OFF_ZS + 2048], w=["wzs"], dma="wzs")
                op("pool", nc.gpsimd.dma_start, out=wdt[:], in_=w_l[:, :, OFF_DT:OFF_DT + 32], w=["wdt"], dma="wdt")
                Dd = sc.sb("Dd", [128, 32, 128], BF16)
                op("dve", tt, out=Dd[:], in0=ident_f.unsqueeze(1).broadcast_to([128, 32, 128]),
                   in1=hp[:, 64:96].unsqueeze(2).broadcast_to([128, 32, 128]), op=MUL, r=["hp"] + CST, w=["Dd"])
                st = sc.sb("st", [128, 2048], F32)
                stb = sc.sb("stb", [128, 2048], BF16)
                op("dve", nc.vector.memset, st[:], 0.0, w=["st"])
                op("dve", nc.vector.memset, stb[:], 0.0, w=[("stb", i) for i in range(4)])
                xtp = sc.pool("xtc", 2, [128, 2560], BF16)
                bcp = sc.pool("bcc", 2, [128, 8, 128], BF16)
                smp = sc.pool("sm", 2, [128, 8, 32], F32)
                xdtp = sc.pool("xdt", 2, [128, 32, 64], BF16)
                xdsp = sc.pool("xds", 2, [128, 32, 64], BF16)
                cbmp = sc.pool("cbm", 2, [128, 4, 128], BF16)
                ltp = sc.pool("lt", 2, [128, 4, 128], F32)
                dcp = sc.pool("dcT", 2, [128, 4, 128], BF16)
                mtp = sc.pool("MT", 3, [128, 4, 128], BF16)
                t1p = sc.pool("t1", 2, [128, 512], F32)
                szp = sc.pool("sz", 2, [128, 512], F32)
                gnp = sc.pool("gn", 2, [128, 512], BF16)
                junk = sc.sb("junkS", [128, 512], BF16)
                sqp = sc.pool("ssq", 2, [128, 1], F32)
                rqp = sc.pool("rsq", 2, [128, 1], F32)
                ysp = sc.pool("ysc", 2, [128, 16, 128], BF16)
                nrb = 0
                for c in range(32):
                    tok = slice(c * 128, (c + 1) * 128)
                    hk = [("hT", c)]
                    xt_c, kxt = xtp.next()
                    op("sp", nc.sync.dma_start, out=xt_c[:], in_=xtm_d[tok, :],
                       r=[("xtm", cc, c // 8) for cc in range(20)], w=[kxt], dma=kxt)
                    bc_c, kbc = bcp.next()
                    op("sp", nc.sync.dma_start, out=bc_c[:], in_=bct_v[:, :, tok],
                       r=[("bct", i) for i in range(8)], w=[kbc], dma=kbc)
                    sm, ksm = smp.next()
                    for kc in range(8):
                        op("pe", mm, banks[0][:, 0:32], lhsT=hT[:, kc, tok], rhs=wdt[:, kc, :], start=(kc == 0),
                           stop=(kc == 7), r=hk + ["wdt"], w=[bk(0)])
                    op("dve", tt, out=sm[:, 0, :], in0=banks[0][:, 0:32], in1=hp[:, 0:32], op=ADD, r=[bk(0), "hp"], w=[ksm])
                    op("act", act, out=sm[:, 1, :], in_=sm[:, 0, :], func=AF.Exp, r=[ksm], w=[ksm])
                    op("act", act, out=sm[:, 2, :], in_=sm[:, 1, :], func=AF.Ln, bias=1.0, r=[ksm], w=[ksm])
                    op("dve", tt, out=sm[:, 3, :], in0=sm[:, 2, :], in1=Aneg[:], op=MUL, r=[ksm, "Aneg"], w=[ksm])
                    op("pe", mm, banks[0][:, 32:64], lhsT=U_f, rhs=sm[:, 3, :], start=True, stop=True,
                       r=[ksm] + CST, w=[bk(0)])
                    op("pe", mm, banks[0][:, 64:96], lhsT=ones_f[:], rhs=sm[:, 3, :], start=True, stop=True,
                       r=[ksm, "ones_f"], w=[bk(0)])
                    op("dve", nc.vector.tensor_copy, out=sm[:, 4, :], in_=banks[0][:, 32:64], r=[bk(0)], w=[ksm])
                    op("act", act, out=sm[:, 5, :], in_=banks[0][:, 32:64], func=AF.Exp, r=[bk(0)], w=[ksm])
                    op("dve", tt, out=sm[:, 6, :], in0=banks[0][:, 64:96], in1=sm[:, 4, :], op=SUB, r=[bk(0), ksm], w=[ksm])
                    op("act", act, out=sm[:, 6, :], in_=sm[:, 6, :], func=AF.Exp, r=[ksm], w=[ksm])
                    op("act", act, out=sm[:, 7, :], in_=banks[0][:, 64:96], func=AF.Exp, r=[bk(0)], w=[ksm])
                    xdt, kxdt = xdtp.next()
                    op("dve", tt, out=xdt[:], in0=xt_c[:, 0:2048].rearrange("p (a b) -> p a b", b=64),
                       in1=sm[:, 2, :].unsqueeze(2).broadcast_to([128, 32, 64]), op=MUL, r=[kxt, ksm], w=[kxdt])
                    xds, kxds = xdsp.next()
                    op("pool", nc.gpsimd.tensor_tensor, out=xds[:], in0=xdt[:],
                       in1=sm[:, 6, :].unsqueeze(2).broadcast_to([128, 32, 64]), op=MUL, r=[kxdt, ksm], w=[kxds])
                    cbv = banks[1][:].rearrange("p (a b) -> p a b", b=128)
                    for gi in range(4):
                        op("pe", mm, cbv[:, gi, :], lhsT=bc_c[:, gi, :], rhs=bc_c[:, 4 + gi, :], start=True, stop=True,
                           r=[kbc], w=[bk(1)])
                    cbm, kcbm = cbmp.next()
                    op("dve", tt, out=cbm[:], in0=cbv, in1=maskU_b.unsqueeze(1).broadcast_to([128, 4, 128]), op=MUL,
                       r=[bk(1)] + CST, w=[kcbm])
                    ys_c, kys = ysp.next()
                    for gi in range(4):
                        for hq in range(2):
                            h0 = gi * 8 + hq * 4
                            lt, klt = ltp.next()
                            op("pool", nc.gpsimd.tensor_tensor, out=lt[:],
                               in0=Ls_f.unsqueeze(1).broadcast_to([128, 4, 128]),
                               in1=sm[:, 3, h0:h0 + 4].unsqueeze(2).broadcast_to([128, 4, 128]), op=MUL,
                               r=[ksm] + CST, w=[klt])
                            rb = 2 + (nrb % 2)
                            nrb += 1
                            rbv = banks[rb][:].rearrange("p (a b) -> p a b", b=128)
                            for i in range(4):
                                op("pe", mm, rbv[:, i, :], lhsT=lt[:, i, :], rhs=U_f, start=True, stop=True,
                                   r=[klt] + CST, w=[bk(rb)])
                            dc, kdc = dcp.next()
                            op("act", act, out=dc[:], in_=rbv, func=AF.Exp, r=[bk(rb)], w=[kdc])
                            mt, kmt = mtp.next()
                            op("dve", tt, out=mt[:], in0=dc[:], in1=cbm[:, gi:gi + 1, :].broadcast_to([128, 4, 128]),
                               op=MUL, r=[kdc, kcbm], w=[kmt])
                            for i in range(4):
                                hh = h0 + i
                                j = hq * 4 + i
                                op("pe", mm, banks[4][:, j * 64:(j + 1) * 64], lhsT=mt[:, i, :], rhs=xdt[:, hh, :],
                                   start=True, stop=False, r=[kmt, kxdt], w=[bk(4)])
                                op("pe", mm, banks[4][:, j * 64:(j + 1) * 64], lhsT=Dd[:, hh, :],
                                   rhs=xt_c[:, hh * 64:(hh + 1) * 64], start=False, stop=True, r=["Dd", kxt], w=[bk(4)])
                        op("pe", mm, banks[5][:], lhsT=bc_c[:, 4 + gi, :], rhs=stb[:, gi * 512:(gi + 1) * 512],
                           start=True, stop=True, r=[kbc, ("stb", gi)], w=[bk(5)])
                        for kc in range(8):
                            op("pe", mm, banks[6][:], lhsT=hT[:, kc, tok], rhs=wzs[:, kc, gi * 512:(gi + 1) * 512],
                               start=(kc == 0), stop=(kc == 7), r=hk + ["wzs"], w=[bk(6)])
                        op("pe", mm, banks[7][:], lhsT=xt_c[:, 2048 + gi * 128:2048 + (gi + 1) * 128],
                           rhs=xds[:, gi * 8:(gi + 1) * 8, :], start=True, stop=True, r=[kxt, kxds], w=[bk(7)])
                        t1, kt1 = t1p.next()
                        op("dve", tt, out=t1[:].rearrange("p (a b) -> p a b", b=64),
                           in0=banks[5][:].rearrange("p (a b) -> p a b", b=64),
                           in1=sm[:, 5, gi * 8:(gi + 1) * 8].unsqueeze(2).broadcast_to([128, 8, 64]), op=MUL,
                           r=[bk(5), ksm], w=[kt1])
                        y, ky = t1, kt1
                        op("dve", tt, out=y[:], in0=banks[4][:], in1=t1[:], op=ADD, r=[bk(4), kt1], w=[ky])
                        sz, ksz = szp.next()
                        op("act", act, out=sz[:], in_=banks[6][:], func=AF.Exp, scale=-1.0, r=[bk(6)], w=[ksz])
                        op("act", act, out=sz[:], in_=sz[:], func=AF.Ln, bias=1.0, r=[ksz], w=[ksz])
                        op("act", act, out=sz[:], in_=sz[:], func=AF.Exp, scale=-1.0, r=[ksz], w=[ksz])
                        op("dve", tt, out=sz[:], in0=banks[6][:], in1=sz[:], op=MUL, r=[bk(6), ksz], w=[ksz])
                        gt, kgt = sz, ksz
                        op("pool", nc.gpsimd.tensor_tensor, out=gt[:], in0=y[:], in1=sz[:], op=MUL, r=[ky, ksz], w=[kgt])
                        sq, ksq = sqp.next()
                        op("act", act, out=junk[:], in_=gt[:], func=AF.Square, accum_out=sq[:], r=[kgt], w=["junkS", ksq])
                        rq, krq = rqp.next()
                        op("act", act, out=rq[:], in_=sq[:], func=AF.Ln, bias=epst[:], scale=1.0 / 512.0,
                           r=[ksq, "eps"], w=[krq])
                        op("act", act, out=rq[:], in_=rq[:], func=AF.Exp, scale=-0.5, r=[krq], w=[krq])
                        gn, kgn = gnp.next()
                        op("dve", stt, out=gn[:], in0=gt[:], scalar=rq[:], in1=snw[:, gi * 512:(gi + 1) * 512],
                           op0=MUL, op1=MUL, r=[kgt, krq, "snw"], w=[kgn])
                        ptv = bf(5).rearrange("p (a b) -> p a b", b=128)
                        for i in range(4):
                            op("pe", tr, out=ptv[:, i, :], in_=gn[:, i * 128:(i + 1) * 128], identity=ident_b,
                               r=[kgn] + CST, w=[bk(5)])
                        op("act", nc.scalar.copy, out=ys_c[:, gi * 4:(gi + 1) * 4, :], in_=ptv[:, 0:4, :], r=[bk(5)], w=[kys])
                        stv = st[:, gi * 512:(gi + 1) * 512]
                        op("pool", nc.gpsimd.tensor_tensor, out=stv.rearrange("p (a b) -> p a b", b=64),
                           in0=stv.rearrange("p (a b) -> p a b", b=64),
                           in1=sm[:, 7, gi * 8:(gi + 1) * 8].unsqueeze(2).broadcast_to([128, 8, 64]), op=MUL,
                           r=[("st", gi), ksm], w=[("st", gi)])
                        op("dve", tt, out=stv, in0=banks[7][:], in1=stv, op=ADD, r=[bk(7), ("st", gi)], w=[("st", gi)])
                        op("act", nc.scalar.copy, out=stb[:, gi * 512:(gi + 1) * 512], in_=stv, r=[("st", gi)],
                           w=[("stb", gi)])
                    op("sp", nc.sync.dma_start, out=ysT_v[:, :, tok], in_=ys_c[:], r=[kys], w=[("ysT", c // 4)], dma=kys)

        def phase_G(l, hT):
            w_l = w_in_d[l].rearrange("(kc p) n -> p kc n", p=128)
            with Scope() as sc:
                wp = sc.pool("wg", 2, [128, 8, 128], BF16)
                sgp = sc.pool("sg", 2, [128, S], BF16)
                nb = 0
                for gc in range(16):
                    w_t, kw_ = wp.next()
                    c0 = OFF_G + gc * 128
                    op("pool", nc.gpsimd.dma_start, out=w_t[:], in_=w_l[:, :, c0:c0 + 128], w=[kw_], dma=kw_)
                    sg, ksg = sgp.next()
                    for tb in range(8):
                        b = nb % 4
                        nb += 1
                        for kc in range(8):
                            op("pe", mm, banks[b][:], lhsT=w_t[:, kc, :], rhs=hT[:, kc, tb * 512:(tb + 1) * 512],
                               start=(kc == 0), stop=(kc == 7), r=[kw_] + hTk(tb), w=[bk(b)])
                        op("act", act, out=sg[:, tb * 512:(tb + 1) * 512], in_=banks[b][:], func=AF.Sigmoid,
                           r=[bk(b)], w=[ksg])
                    op("sp", nc.sync.dma_start, out=sgT_d[gc], in_=sg[:], r=[ksg], w=[("sgT", gc)], dma=ksg)

        def phase_T(l, hT):
            w_l = w_in_d[l].rearrange("(kc p) n -> p kc n", p=128)
            LOOK = 2
            with Scope() as sc:
                whp = sc.pool("wh", 2, [128, 8, 4, 128], BF16)
                qz = [sc.sb("qz%d" % m, [128, S], BF16) for m in range(2)]
                kT = sc.sb("kT", [128, S], BF16)
                vt = sc.sb("vt", [128, 32, 128], BF16)
                szT = sc.sb("szT", [128, S], BF16)
                sqp = sc.pool("sq", 2, [128, 512], BF16)
                lnp = sc.pool("lnq", 2, [128, 512], F32)
                rsp = sc.pool("rst", 2, [128, 512], F32)
                ezp = sc.pool("ez", 2, [128, 512], F32)
                ptp = sc.pool("PT", 4, [128, 2, 512], BF16)
                s1pp = sc.pool("s1p", 2, [128, 512], F32)
                s0cp = sc.pool("s0c", 2, [128, 512], F32)
                s1cp = sc.pool("s1c", 2, [128, 512], F32)
                lb0p = sc.pool("lb0", 1, [128, 512], F32)
                lb1p = sc.pool("lb1", 1, [128, 512], F32)
                u0p = sc.pool("u0", 1, [128, 512], F32)
                u1p = sc.pool("u1", 1, [128, 512], F32)
                tqp = sc.pool("tq", 1, [128, 512], F32)
                sqo = sc.pool("sqo", 1, [128, 512], BF16)
                agp = sc.pool("arg", 1, [128, 512], F32)
                ybp = sc.pool("yb", 2, [128, 512], BF16)
                op("dve", nc.vector.memset, qz[0][64:128, :], 0.0, w=["qz0"])
                op("dve", nc.vector.memset, qz[1][0:64, :], 0.0, w=["qz1"])
                nS = [0]
                nI = [0]
                pairs = ((2, 3), (4, 5))
                for h in range(8):
                    wh, kwh = whp.next()
                    kws = []
                    for i, off in enumerate((OFF_Q, OFF_K, OFF_V, OFF_ZA)):
                        c0 = off + h * 128
                        kwi = kwh + (i,)
                        kws.append(kwi)
                        op("pool", nc.gpsimd.dma_start, out=wh[:, :, i, :], in_=w_l[:, :, c0:c0 + 128], w=[kwi], dma=kwi)
                    for which in (0, 1):
                        for tb in range(8):
                            ba = 2 + 2 * (nI[0] % 2)
                            bs = ba + 1
                            nI[0] += 1
                            cs = slice(tb * 512, (tb + 1) * 512)
                            for kc in range(8):
                                op("pe", mm, banks[ba][:], lhsT=wh[:, kc, which, :], rhs=hT[:, kc, cs],
                                   start=(kc == 0), stop=(kc == 7), r=[kws[which]] + hTk(tb), w=[bk(ba)])
                            sq, ksq = sqp.next()
                            op("act", act, out=sq[:], in_=banks[ba][:], func=AF.Square, r=[bk(ba)], w=[ksq])
                            op("pe", mm, banks[bs][:], lhsT=BD_b, rhs=sq[:], start=True, stop=True, r=[ksq] + CST, w=[bk(bs)])
                            ln, kln = lnp.next()
                            op("act", act, out=ln[:], in_=banks[bs][:], func=AF.Ln, bias=epst[:], scale=1.0 / 64.0,
                               r=[bk(bs), "eps"], w=[kln])
                            rs, krs = rsp.next()
                            op("act", act, out=rs[:], in_=ln[:], func=AF.Exp, scale=-0.5, r=[kln], w=[krs])
                            if which == 0:
                                for m in range(2):
                                    pr = slice(m * 64, (m + 1) * 64)
                                    op("dve", stt, out=qz[m][pr, cs], in0=banks[ba][pr, :], scalar=qkw[pr, 0:1], in1=rs[pr, :],
                                       op0=MUL, op1=MUL, r=[bk(ba), krs, "qkw"], w=["qz%d" % m])
                            else:
                                op("dve", stt, out=kT[:, cs], in0=banks[ba][:], scalar=qkw[:, 1:2], in1=rs[:],
                                   op0=MUL, op1=MUL, r=[bk(ba), krs, "qkw"], w=["kT"])
                    for tb in range(8):
                        ba = 2 + (nI[0] % 4)
                        nI[0] += 1
                        cs = slice(tb * 512, (tb + 1) * 512)
                        for kc in range(8):
                            op("pe", mm, banks[ba][:], lhsT=wh[:, kc, 3, :], rhs=hT[:, kc, cs],
                               start=(kc == 0), stop=(kc == 7), r=[kws[3]] + hTk(tb), w=[bk(ba)])
                        ez, kez = ezp.next()
                        op("act", act, out=ez[:], in_=banks[ba][:], func=AF.Exp, scale=-1.0, r=[bk(ba)], w=[kez])
                        op("act", act, out=ez[:], in_=ez[:], func=AF.Ln, bias=1.0, r=[kez], w=[kez])
                        op("act", act, out=ez[:], in_=ez[:], func=AF.Exp, scale=-1.0, r=[kez], w=[kez])
                        op("dve", stt, out=szT[:, cs], in0=banks[ba][:], scalar=swc[:, 0:1], in1=ez[:], op0=MUL, op1=MUL,
                           r=[bk(ba), kez, "swc"], w=["szT"])
                    for t4 in range(8):
                        b = 2 + (nI[0] % 4)
                        nI[0] += 1
                        pv = banks[b][:].rearrange("p (a b) -> p a b", b=128)
                        for i in range(4):
                            t = t4 * 4 + i
                            for kc in range(8):
                                op("pe", mm, pv[:, i, :], lhsT=hT[:, kc, t * 128:(t + 1) * 128], rhs=wh[:, kc, 2, :],
                                   start=(kc == 0), stop=(kc == 7), r=[kws[2], ("hT", t)], w=[bk(b)])
                        op("dve", nc.vector.tensor_copy, out=vt[:, t4 * 4:(t4 + 1) * 4, :], in_=pv, r=[bk(b)], w=["vt"])
                    steps = [(qb, t) for qb in range(8) for t in range(4 * qb + 4)]
                    pts = {}
                    s1ps = {}

                    def emit_qk(j):
                        qb, t = steps[j]
                        off = max(0, t - 4 * qb) * 128
                        pi = nS[0] % 2
                        nS[0] += 1
                        pb = pairs[pi]
                        diag = t >= 4 * qb
                        for m in range(2):
                            op("pe", mm, banks[pb[m]][:, 0:512 - off], lhsT=kT[:, t * 128:(t + 1) * 128],
                               rhs=qz[m][:, qb * 512 + off:(qb + 1) * 512], start=True, stop=not diag,
                               r=["kT", "qz%d" % m], w=[bk(pb[m])])
                            if diag:
                                op("pe", mm, banks[pb[m]][:, 0:128], lhsT=ident_b, rhs=Mneg_b, start=False, stop=True,
                                   r=CST, w=[bk(pb[m])])
                        pt, kpt = ptp.next()
                        pview = pbig[1 + pi][:].rearrange("p (a b) -> p a b", b=512)
                        op("act", act, out=pt[:, :, 0:512 - off], in_=pview[:, :, 0:512 - off], func=AF.Exp,
                           r=[bk(pb[0]), bk(pb[1])], w=[kpt])
                        pts[j] = (pt, kpt)

                    def emit_pv(i):
                        qb, t = steps[i]
                        off = max(0, t - 4 * qb) * 128
                        pt, kpt = pts.pop(i)
                        for m in range(2):
                            op("pe", mm, banks[m][:, off:512], lhsT=vt[:, t, :], rhs=pt[:, m, 0:512 - off],
                               start=(t == 0), stop=(t == 4 * qb + 3), r=[kpt, "vt"], w=[bk(m)])
                        if t == 0:
                            op("dve", nc.vector.tensor_copy, out=banks[6][:], in_=pt[:, 0, :], r=[kpt], w=[bk(6)])
                            op("dve", nc.vector.tensor_copy, out=banks[7][:], in_=pt[:, 1, :], r=[kpt], w=[bk(7)])
                            s1ps[qb] = s1pp.next()
                            op("pool", nc.gpsimd.memset, s1ps[qb][0][:], 0.0, w=[s1ps[qb][1]])
                        else:
                            op("dve", tt, out=banks[6][:, off:512], in0=banks[6][:, off:512], in1=pt[:, 0, 0:512 - off],
                               op=ADD, r=[kpt, bk(6)], w=[bk(6)])
                            if t % 3 == 0:
                                op("dve", tt, out=banks[7][:, off:512], in0=banks[7][:, off:512], in1=pt[:, 1, 0:512 - off],
                                   op=ADD, r=[kpt, bk(7)], w=[bk(7)])
                            else:
                                sp_, ksp = s1ps[qb]
                                op("pool", nc.gpsimd.tensor_tensor, out=sp_[:, off:512], in0=sp_[:, off:512],
                                   in1=pt[:, 1, 0:512 - off], op=ADD, r=[kpt, ksp], w=[ksp])

                    def emit_fin(qb):
                        cs = slice(qb * 512, (qb + 1) * 512)
                        s1p_, ks1p = s1ps.pop(qb)
                        s0c, ks0c = s0cp.next()
                        s1c, ks1c = s1cp.next()
                        op("dve", nc.vector.tensor_copy, out=s0c[:], in_=banks[6][:], r=[bk(6)], w=[ks0c])
                        op("dve", nc.vector.tensor_copy, out=s1c[:], in_=banks[7][:], r=[bk(7)], w=[ks1c])
                        pi = nS[0] % 2
                        nS[0] += 1
                        bx, by = pairs[pi]
                        op("pe", mm, banks[bx][:], lhsT=ones_f[:], rhs=s0c[:], start=True, stop=True, r=[ks0c, "ones_f"], w=[bk(bx)])
                        op("pe", mm, banks[by][:], lhsT=ones_f[:], rhs=s1c[:], start=True, stop=False, r=[ks1c, "ones_f"], w=[bk(by)])
                        op("pe", mm, banks[by][:], lhsT=ones_f[:], rhs=s1p_[:], start=False, stop=True, r=[ks1p, "ones_f"], w=[bk(by)])
                        lb0, kl0 = lb0p.next()
                        lb1, kl1 = lb1p.next()
                        op("act", nc.scalar.copy, out=lb0[:], in_=banks[bx][:], r=[bk(bx)], w=[kl0])
                        op("act", nc.scalar.copy, out=lb1[:], in_=banks[by][:], r=[bk(by)], w=[kl1])
                        u0, ku0 = u0p.next()
                        u1, ku1 = u1p.next()
                        op("dve", tt, out=u1[:], in0=banks[1][:], in1=lb0[:], op=MUL, r=[bk(1), kl0], w=[ku1])
                        op("dve", tt, out=u0[:], in0=banks[0][:], in1=lb1[:], op=MUL, r=[bk(0), kl1], w=[ku0])
                        op("dve", stt, out=u0[:], in0=u1[:], scalar=neglam[:, 0:1], in1=u0[:], op0=MUL, op1=ADD,
                           r=[ku0, ku1, "neglam"], w=[ku0])
                        tq, ktq = tqp.next()
                        op("pool", nc.gpsimd.tensor_tensor, out=tq[:], in0=lb0[:], in1=lb1[:], op=MUL, r=[kl0, kl1], w=[ktq])
                        op("pool", nc.gpsimd.tensor_tensor, out=tq[:], in0=tq[:], in1=tq[:], op=MUL, r=[ktq], w=[ktq])
                        sq, ksq = sqo.next()
                        op("pool", nc.gpsimd.tensor_tensor, out=sq[:], in0=u0[:], in1=u0[:], op=MUL, r=[ku0], w=[ksq])
                        op("pe", mm, banks[bx][:], lhsT=ones_b[:], rhs=sq[:], start=True, stop=True, r=[ksq, "ones_b"], w=[bk(bx)])
                        ag, kag = agp.next()
                        op("dve", stt, out=ag[:], in0=banks[bx][:], scalar=float(1.0 / (128.0 * EPS)), in1=tq[:],
                           op0=MUL, op1=ADD, r=[bk(bx), ktq], w=[kag])
                        op("act", act, out=ag[:], in_=ag[:], func=AF.Ln, r=[kag], w=[kag])
                        op("act", act, out=ag[:], in_=ag[:], func=AF.Exp, scale=-0.5, r=[kag], w=[kag])
                        op("dve", stt, out=u0[:], in0=u0[:], scalar=float(EPS ** -0.5), in1=ag[:], op0=MUL, op1=MUL,
                           r=[ku0, kag], w=[ku0])
                        yb, kyb = ybp.next()
                        op("pool", nc.gpsimd.tensor_tensor, out=yb[:], in0=u0[:], in1=szT[:, cs], op=MUL,
                           r=[ku0, "szT"], w=[kyb])
                        op("sp", nc.sync.dma_start, out=yaT_d[h][:, cs], in_=yb[:], r=[kyb], w=[("yaT", h, qb)], dma=kyb)

                    n = len(steps)
                    for i in range(-LOOK, n):
                        j = i + LOOK
                        if j < n:
                            emit_qk(j)
                        if i >= 0:
                            emit_pv(i)
                            qb, t = steps[i]
                            if t == 4 * qb + 3:
                                emit_fin(qb)

        def phase_D(l, xsrc):
            wpa_v = w_pa_d[l].rearrange("(kc p) n -> p kc n", p=128)
            wps_v = w_ps_d[l].rearrange("(kc p) n -> p kc n", p=128)
            wo_v = w_out_d[l].rearrange("(kc p) n -> p kc n", p=128)
            yaT_v = yaT_d.rearrange("h p t -> p h t")
            ysT_v = ysT_d.rearrange("k p t -> p k t")
            sgT_v = sgT_d.rearrange("k p t -> p k t")
            with Scope() as sc:
                wpa = sc.sb("wpa", [128, 8, D], BF16)
                wps = sc.sb("wps", [128, 16, D], BF16)
                wo = sc.sb("wo", [128, 8, D], BF16)
                op("pool", nc.gpsimd.dma_start, out=wpa[:], in_=wpa_v, w=["wpa"], dma="wpa")
                op("pool", nc.gpsimd.dma_start, out=wps[:, 0:8], in_=wps_v[:, 0:8], w=["wps"], dma="wps")
                op("pool", nc.gpsimd.dma_start, out=wps[:, 8:16], in_=wps_v[:, 8:16], w=["wps"], dma="wps")
                op("pool", nc.gpsimd.dma_start, out=wo[:], in_=wo_v, w=["wo"], dma="wo")
                yap = sc.pool("yab", 2, [128, 8, 512], BF16)
                ysp = sc.pool("ysb", 2, [128, 16, 512], BF16)
                sgp = sc.pool("sgb", 1, [128, 16, 512], BF16)
                xrp = sc.pool("xr", 1, [128, 4, D], F32)
                m1p = sc.pool("m1", 2, [128, 512], F32)
                m2p = sc.pool("m2", 2, [128, 512], F32)
                mTp = sc.pool("mT", 2, [128, 8, 512], BF16)
                xop = sc.pool("xo", 1, [128, 4, D], F32)
                nb = 0
                for tb in range(8):
                    tok = slice(tb * 512, (tb + 1) * 512)
                    ya, kya = yap.next()
                    ys, kys = ysp.next()
                    sg, ksg = sgp.next()
                    xr, kxr = xrp.next()
                    op("sp", nc.sync.dma_start, out=ya[:], in_=yaT_v[:, :, tok], r=[("yaT", h, tb) for h in range(8)],
                       w=[kya], dma=kya)
                    op("sp", nc.sync.dma_start, out=ys[:], in_=ysT_v[:, :, tok], r=[("ysT", tb)], w=[kys], dma=kys)
                    op("sp", nc.sync.dma_start, out=sg[:], in_=sgT_v[:, :, tok], r=[("sgT", i) for i in range(16)],
                       w=[ksg], dma=ksg)
                    op("sp", nc.sync.dma_start, out=xr[:], in_=xsrc[tok, :].rearrange("(t p) c -> p t c", p=128),
                       r=[("xres", tb)], w=[kxr], dma=kxr)
                    mT, kmT = mTp.next()
                    for cc in range(8):
                        b1 = nb % 6
                        b2 = (nb + 1) % 6
                        nb += 2
                        for kc in range(8):
                            op("pe", mm, banks[b1][:], lhsT=wpa[:, kc, cc * 128:(cc + 1) * 128], rhs=ya[:, kc, :],
                               start=(kc == 0), stop=(kc == 7), r=["wpa", kya], w=[bk(b1)])
                        for kc in range(16):
                            op("pe", mm, banks[b2][:], lhsT=wps[:, kc, cc * 128:(cc + 1) * 128], rhs=ys[:, kc, :],
                               start=(kc == 0), stop=(kc == 15), r=["wps", kys], w=[bk(b2)])
                        m1, km1 = m1p.next()
                        op("dve", tt, out=m1[:], in0=banks[b1][:], in1=sg[:, cc, :], op=MUL, r=[bk(b1), ksg], w=[km1])
                        m2, km2 = m2p.next()
                        op("dve", tt, out=m2[:], in0=banks[b2][:], in1=sg[:, 8 + cc, :], op=MUL, r=[bk(b2), ksg], w=[km2])
                        op("pool", nc.gpsimd.tensor_tensor, out=mT[:, cc, :], in0=m1[:], in1=m2[:], op=ADD,
                           r=[km1, km2], w=[kmT])
                    xo, kxo = xop.next()
                    for t4 in range(4):
                        for half in range(2):
                            b = 6 + (t4 * 2 + half) % 2
                            for kc in range(8):
                                op("pe", mm, banks[b][:], lhsT=mT[:, kc, t4 * 128:(t4 + 1) * 128],
                                   rhs=wo[:, kc, half * 512:(half + 1) * 512], start=(kc == 0), stop=(kc == 7),
                                   r=[kmT, "wo"], w=[bk(b)])
                            op("dve", tt, out=xo[:, t4, half * 512:(half + 1) * 512], in0=banks[b][:],
                               in1=xr[:, t4, half * 512:(half + 1) * 512], op=ADD, r=[bk(b), kxr], w=[kxo])
                    op("sp", nc.sync.dma_start, out=out_d[tok, :].rearrange("(t p) c -> p t c", p=128), in_=xo[:],
                       r=[kxo], w=[("xres", tb)], dma=kxo)

        for l in range(depth):
            xsrc = x_d if l == 0 else out_d
            barrier()
            load_params(l)
            with Scope() as lsc:
                hT = lsc.sb("hT", [128, 8, S], BF16)
                if "A" in phases:
                    phase_A(l, hT, xsrc)
                    barrier()
                if "S" in phases:
                    phase_S1(l, hT)
                    barrier()
                    phase_S2(l, hT)
                    barrier()
                if "G" in phases:
                    phase_G(l, hT)
                    barrier()
                if "T" in phases:
                    phase_T(l, hT)
                    barrier()
            if "D" in phases:
                phase_D(l, xsrc)
        P.finish(sems)
    return nc, P.stats


def make_consts():
    i = np.arange(128)
    ident = np.eye(128, dtype=np.float32)
    maskU = (i[None, :] >= i[:, None]).astype(np.float32)
    Lstrict = (i[:, None] > i[None, :]).astype(np.float32)
    bd = ((i[:, None] // 64) == (i[None, :] // 64)).astype(np.float32)
    mneg = np.where(i[None, :] >= i[:, None], 0.0, -30000.0).astype(np.float32)
    return np.concatenate([ident, maskU, maskU, Lstrict, bd, mneg], axis=1).astype(np.float32)


def host_layout(inputs):
    inputs = {k: (np.asarray(v)[:DEPTH] if k != "x" else v) for k, v in inputs.items()}
    f = lambda a: np.ascontiguousarray(np.asarray(a, dtype=np.float32))
    rep = lambda a: np.ascontiguousarray(np.broadcast_to(np.asarray(a, np.float32)[:, None, :], (a.shape[0], 128, a.shape[1])))
    qn, kn = np.asarray(inputs["q_norm_w"], np.float32), np.asarray(inputs["k_norm_w"], np.float32)
    qkw = np.stack([np.tile(qn, (1, 2)), np.tile(kn, (1, 2))], axis=-1)
    cw = np.asarray(inputs["conv_w"], np.float32)
    convw = cw.transpose(0, 2, 1).reshape(DEPTH, 24, 128, 4).transpose(0, 2, 1, 3).reshape(DEPTH, 128, 96)
    convb = np.asarray(inputs["conv_b"], np.float32).reshape(DEPTH, 24, 128).transpose(0, 2, 1)
    hp = np.concatenate([inputs["dt_bias"], inputs["a_log"], inputs["d_skip"]], axis=-1).astype(np.float32)
    common = {
        "w_in": f(inputs["w_in"]), "w_pa": f(inputs["w_proj_attn"]), "w_ps": f(inputs["w_proj_ssd"]),
        "w_out": f(inputs["w_out"]),
        "nw_rep": rep(inputs["norm_w"]), "qkw": f(qkw),
        "dl_rep": rep(np.asarray(inputs["diff_lambda"], np.float32).reshape(DEPTH, 256)),
        "sw_rep": rep(inputs["subln_w"]), "swc": f(np.asarray(inputs["subln_w"], np.float32)[:, :, None]), "convw": f(convw), "convb": f(convb),
        "hp_rep": rep(hp), "snw_rep": rep(inputs["ssd_norm_w"]), "consts": make_consts(),
    }
    return common


_NC_CACHE = {}


def kernel(**inputs):
    common = host_layout(inputs)
    x = np.asarray(inputs["x"], np.float32)
    n = x.shape[0]
    if "nc" not in _NC_CACHE:
        _NC_CACHE["nc"] = build()[0]
    nc = _NC_CACHE["nc"]
    in_maps = [dict(common, x=np.ascontiguousarray(x[b])) for b in range(n)]
    res = run_bass_kernel_spmd(nc, in_maps, core_ids=list(range(n)))
    return np.stack([np.asarray(r["out"], np.float32) for r in res.results], axis=0)
```

```python
import contextlib
import math
import numpy as np
import concourse.bass as bass
import concourse.mybir as mybir
from concourse.bass_utils import run_bass_kernel_spmd
from concourse.alu_op_type import AluOpType as ALU

AF = mybir.ActivationFunctionType
F32 = mybir.dt.float32
BF16 = mybir.dt.bfloat16
AX = mybir.AxisListType

S = 4096
D = 1024
DIN = 11296
OFF_Q, OFF_K, OFF_V, OFF_ZA, OFF_XBC, OFF_ZS, OFF_DT, OFF_G = 0, 1024, 2048, 3072, 4096, 7168, 9216, 9248
EPS = 1e-6
import os as _os
DEPTH = int(_os.environ.get('KDEPTH', '4'))


class Op:
    __slots__ = ("eng", "fn", "args", "kw", "reads", "writes", "dma", "idx",
                 "deps", "signal", "token", "waits")

    def __init__(self, eng, fn, args, kw, reads, writes, dma):
        self.eng = eng
        self.fn = fn
        self.args = args
        self.kw = kw
        self.reads = reads
        self.writes = writes
        self.dma = dma
        self.deps = set()
        self.signal = False
        self.token = None
        self.waits = []


class Prog:
    def __init__(self, nc):
        self.nc = nc
        self.ops = []
        self.q = {"pe": nc.tensor, "act": nc.scalar, "dve": nc.vector,
                  "pool": nc.gpsimd, "sp": nc.sync}

    def add(self, eng, fn, *args, reads=(), writes=(), dma=None, **kw):
        op = Op(eng, fn, args, kw, tuple(reads), tuple(writes), dma)
        op.idx = len(self.ops)
        self.ops.append(op)
        return op

    def finish(self, sems, final_eng="sp"):
        ops = self.ops
        last_w = {}
        readers = {}
        sem_waiters = {}
        dma_cum = {}
        for op in ops:
            deps = set()
            raw = set()
            for r in op.reads:
                w = last_w.get(r)
                if w is not None:
                    deps.add(w)
                    raw.add(w)
            for wkey in op.writes:
                w = last_w.get(wkey)
                if w is not None:
                    deps.add(w)
                for ridx in readers.get(wkey, {}).values():
                    deps.add(ridx)
            deps.discard(op.idx)
            keep = set()
            for d in deps:
                dop = ops[d]
                if dop.dma is None and op.dma is None and dop.eng == op.eng:
                    if op.eng == "pe" or d not in raw:
                        continue
                keep.add(d)
            if op.dma is not None:
                for e, widx in sem_waiters.get(op.dma, {}).items():
                    if e != op.eng and widx != op.idx:
                        keep.add(widx)
            for d in keep:
                if ops[d].dma is not None:
                    sem_waiters.setdefault(ops[d].dma, {})[op.eng] = op.idx
            op.deps = keep
            for r in op.reads:
                rd = readers.setdefault(r, {})
                if op.dma is not None:
                    rd[("dma", op.dma)] = op.idx
                else:
                    rd[op.eng] = op.idx
            for wkey in op.writes:
                last_w[wkey] = op.idx
                readers[wkey] = {}
        eng_cnt = {}
        for op in ops:
            for d in op.deps:
                if ops[d].dma is None:
                    ops[d].signal = True
        known = {}
        for op in ops:
            kn = known.setdefault(op.eng, {})
            waits = {}
            for d in sorted(op.deps):
                dop = ops[d]
                if dop.dma is not None:
                    s = ("dma", dop.dma)
                    v = dma_cum[dop.dma]
                else:
                    s = ("eng", dop.eng)
                    v = dop.token[1]
                if kn.get(s, 0) >= v:
                    continue
                waits[s] = max(waits.get(s, 0), v)
            for s, v in waits.items():
                kn[s] = v
            op.waits = list(waits.items())
            if op.dma is not None:
                dma_cum[op.dma] = dma_cum.get(op.dma, 0) + 16
                op.token = (("dma", op.dma), dma_cum[op.dma])
            else:
                if op.signal:
                    eng_cnt[op.eng] = eng_cnt.get(op.eng, 0) + 1
                    op.token = (("eng", op.eng), eng_cnt[op.eng])
                else:
                    op.token = (("eng", op.eng), eng_cnt.get(op.eng, 0) + 1)
        n_wait = 0
        for op in ops:
            q = self.q[op.eng]
            for s, v in op.waits:
                q.wait_ge(sems(s), v)
                n_wait += 1
            ins = op.fn(*op.args, **op.kw)
            if op.dma is not None:
                ins.then_inc(sems(("dma", op.dma)), 16)
            elif op.signal:
                ins.then_inc(sems(("eng", op.eng)), 1)
        q = self.q[final_eng]
        for k, v in dma_cum.items():
            q.wait_ge(sems(("dma", k)), v)
        for e, v in eng_cnt.items():
            if e != final_eng:
                q.wait_ge(sems(("eng", e)), v)
        self.stats = dict(n_ops=len(ops), n_wait=n_wait, eng_cnt=dict(eng_cnt),
                          n_dma_sems=len(dma_cum))


def lambda_init_fn(layer_idx):
    return 0.8 - 0.6 * math.exp(-0.3 * layer_idx)


def build(depth=DEPTH, dbg=False, phases="ASGTD"):
    nc = bass.Bass("TRN2", target_bir_lowering=False)
    P = Prog(nc)
    es = contextlib.ExitStack()
    uid = [0]

    def din(name, shape, dt=F32):
        return nc.dram_tensor(name, list(shape), dt, kind="ExternalInput").ap()

    def dscr(name, shape, dt):
        return nc.dram_tensor(name, list(shape), dt, kind="ExternalOutput" if dbg else "Internal").ap()

    x_d = din("x", [S, D])
    w_in_d = din("w_in", [DEPTH, D, DIN])
    w_pa_d = din("w_pa", [DEPTH, D, D])
    w_ps_d = din("w_ps", [DEPTH, 2 * D, D])
    w_out_d = din("w_out", [DEPTH, D, D])
    nw_d = din("nw_rep", [DEPTH, 128, D])
    qkw_d = din("qkw", [DEPTH, 128, 2])
    dl_d = din("dl_rep", [DEPTH, 128, 256])
    sw_d = din("sw_rep", [DEPTH, 128, 128])
    swc_d = din("swc", [DEPTH, 128, 1])
    cw_d = din("convw", [DEPTH, 128, 24 * 4])
    cb_d = din("convb", [DEPTH, 128, 24])
    hp_d = din("hp_rep", [DEPTH, 128, 96])
    snw_d = din("snw_rep", [DEPTH, 128, 2048])
    cst_d = din("consts", [128, 6 * 128])
    out_d = nc.dram_tensor("out", [S, D], F32, kind="ExternalOutput").ap()
    yaT_d = dscr("yaT", [8, 128, S], BF16)
    ysT_d = dscr("ysT", [16, 128, S], BF16)
    sgT_d = dscr("sgT", [16, 128, S], BF16)
    xtm_d = dscr("xtm", [S, 2560], BF16)
    bct_d = dscr("bct", [8, 128, S], BF16)

    sem_cache = {}

    def sems(key):
        if key not in sem_cache:
            sem_cache[key] = es.enter_context(nc.semaphore("s%d" % len(sem_cache)))
        return sem_cache[key]

    class Scope:
        def __init__(self):
            self.es = contextlib.ExitStack()

        def __enter__(self):
            self.es.__enter__()
            return self

        def __exit__(self, *a):
            return self.es.__exit__(*a)

        def sb(self, name, shape, dt):
            uid[0] += 1
            return self.es.enter_context(nc.sbuf_tensor("%s_%d" % (name, uid[0]), list(shape), dt))

        def pool(self, name, n, shape, dt):
            return RPool(self, name, n, shape, dt)

    class RPool:
        def __init__(self, sc, name, n, shape, dt):
            self.name = name
            self.tiles = [sc.sb("%s%d" % (name, i), shape, dt) for i in range(n)]
            self.i = 0

        def next(self):
            j = self.i % len(self.tiles)
            self.i += 1
            return self.tiles[j], (self.name, j)

    def op(eng, fn, *a, r=(), w=(), dma=None, **kw):
        lk = tuple(("lock", k[1]) for k in tuple(r) + tuple(w) if isinstance(k, tuple) and k[0] == "bank")
        return P.add(eng, fn, *a, reads=tuple(r) + ("PH",), writes=tuple(w) + lk, dma=dma, **kw)

    mm = nc.tensor.matmul
    tr = nc.tensor.transpose
    act = nc.scalar.activation
    tt = nc.vector.tensor_tensor
    ts = nc.vector.tensor_scalar
    stt = nc.vector.scalar_tensor_tensor
    MUL, ADD, SUB = ALU.mult, ALU.add, ALU.subtract

    with es:
        g = Scope()
        es.enter_context(g)
        pbig = [es.enter_context(nc.psum_tensor("pbig%d" % i, [128, 1024], F32)) for i in range(4)]
        banks = [pbig[i // 2][:, (i % 2) * 512:(i % 2 + 1) * 512] for i in range(8)]

        def bk(i):
            return ("bank", i)

        def bf(i):
            return banks[i][:].bitcast(BF16)

        cst_f = g.sb("cst_f", [128, 6 * 128], F32)
        cst_b = g.sb("cst_b", [128, 6 * 128], BF16)
        op("sp", nc.sync.dma_start, out=cst_f[:], in_=cst_d, w=["cst_f"], dma="cst_f")
        op("pool", nc.gpsimd.dma_start, out=cst_b[:], in_=cst_d, w=["cst_b"], dma="cst_b")
        ident_f = cst_f[:, 0:128]
        U_f = cst_f[:, 256:384]
        Ls_f = cst_f[:, 384:512]
        ident_b = cst_b[:, 0:128]
        maskU_b = cst_b[:, 128:256]
        BD_b = cst_b[:, 512:640]
        Mneg_b = cst_b[:, 640:768]
        CST = ["cst_f", "cst_b"]
        epst = g.sb("eps", [128, 1], F32)
        op("dve", nc.vector.memset, epst[:], EPS, w=["eps"])
        ones_f = g.sb("ones_f", [128, 128], F32)
        op("dve", nc.vector.memset, ones_f[:], 1.0, w=["ones_f"])
        ones_b = g.sb("ones_b", [128, 128], BF16)
        op("dve", nc.vector.memset, ones_b[:], 1.0, w=["ones_b"])
        swc = g.sb("swc", [128, 1], F32)
        bar_t = g.sb("bar_t", [128, 1], F32)
        nw = g.sb("nw", [128, D], F32)
        qkw = g.sb("qkw", [128, 2], F32)
        dl = g.sb("dl", [128, 256], F32)
        sw = g.sb("sw", [128, 128], F32)
        cw = g.sb("cw", [128, 96], F32)
        cb = g.sb("cb", [128, 24], F32)
        hp = g.sb("hp", [128, 96], F32)
        snw = g.sb("snw", [128, 2048], F32)
        neglam = g.sb("neglam", [128, 1], F32)
        Aneg = g.sb("Aneg", [128, 32], F32)
        lamt = g.sb("lamt", [128, 4], F32)
        lamp = g.sb("lamp", [128, 128], F32)

        def barrier():
            op("dve", nc.vector.memset, bar_t[:], 0.0, w=["PH"])

        def load_params(l):
            li = lambda_init_fn(l)
            for t, src, key in ((nw, nw_d, "nw"), (qkw, qkw_d, "qkw"), (dl, dl_d, "dl"), (sw, sw_d, "sw"),
                                (cw, cw_d, "cw"), (cb, cb_d, "cb"), (hp, hp_d, "hp"), (snw, snw_d, "snw")):
                op("sp", nc.sync.dma_start, out=t[:], in_=src[l], w=[key], dma=key)
            op("dve", ts, out=qkw[:, 0:1], in0=qkw[:, 0:1], scalar1=0.125, scalar2=None, op0=MUL, r=["qkw"], w=["qkw"])
            op("dve", ts, out=sw[:], in0=sw[:], scalar1=float(1.0 - li), scalar2=None, op0=MUL, r=["sw"], w=["sw"])
            op("sp", nc.sync.dma_start, out=swc[:], in_=swc_d[l], w=["swc"], dma="swc")
            op("dve", ts, out=swc[:], in0=swc[:], scalar1=float(1.0 - li), scalar2=None, op0=MUL, r=["swc"], w=["swc"])
            op("dve", tt, out=lamp[:, 0:64], in0=dl[:, 0:64], in1=dl[:, 64:128], op=MUL, r=["dl"], w=["lamp"])
            op("dve", tt, out=lamp[:, 64:128], in0=dl[:, 128:192], in1=dl[:, 192:256], op=MUL, r=["dl"], w=["lamp"])
            op("dve", nc.vector.reduce_sum, out=lamt[:, 0:2], in_=lamp[:].rearrange("p (a b) -> p a b", b=64), axis=AX.X,
               r=["lamp"], w=["lamt"])
            op("act", act, out=lamt[:, 2:4], in_=lamt[:, 0:2], func=AF.Exp, r=["lamt"], w=["lamt2"])
            op("dve", tt, out=neglam[:], in0=lamt[:, 3:4], in1=lamt[:, 2:3], op=SUB, r=["lamt2"], w=["neglam"])
            op("dve", ts, out=neglam[:], in0=neglam[:], scalar1=float(-li), scalar2=None, op0=ADD, r=["neglam"], w=["neglam"])
            op("act", act, out=Aneg[:], in_=hp[:, 32:64], func=AF.Exp, r=["hp"], w=["Aneg"])
            op("dve", ts, out=Aneg[:], in0=Aneg[:], scalar1=-1.0, scalar2=None, op0=MUL, r=["Aneg"], w=["Aneg"])

        PARAMS = ["nw", "qkw", "dl", "sw", "cw", "cb", "hp", "snw", "neglam", "Aneg"]

        def phase_A(l, hT, xsrc):
            with Scope() as sc:
                xt = sc.pool("xt", 2, [128, D], F32)
                hb = sc.pool("hb", 2, [128, D], BF16)
                junk = sc.sb("junk", [128, D], BF16)
                ssp = sc.pool("ss", 2, [128, 1], F32)
                rsp = sc.pool("rs", 2, [128, 1], F32)
                for t in range(32):
                    x_t, kx = xt.next()
                    op("sp", nc.sync.dma_start, out=x_t[:], in_=xsrc[t * 128:(t + 1) * 128, :],
                       r=[("xres", t // 4)], w=[kx], dma=kx)
                    s_t, ks = ssp.next()
                    op("act", act, out=junk[:], in_=x_t[:], func=AF.Square, accum_out=s_t[:], r=[kx], w=["junkA", ks])
                    r_t, kr = rsp.next()
                    op("act", act, out=r_t[:], in_=s_t[:], func=AF.Sqrt, bias=epst[:], scale=1.0 / D,
                       r=[ks, "eps"], w=[kr])
                    op("dve", nc.vector.reciprocal, out=r_t[:], in_=r_t[:], r=[kr], w=[kr])
                    h_t, kh = hb.next()
                    op("dve", stt, out=h_t[:], in0=x_t[:], scalar=r_t[:], in1=nw[:], op0=MUL, op1=MUL,
                       r=[kx, kr, "nw"], w=[kh])
                    b = t % 2
                    ptv = bf(b).rearrange("p (a b) -> p a b", b=128)
                    for kc in range(8):
                        op("pe", tr, out=ptv[:, kc, :], in_=h_t[:, kc * 128:(kc + 1) * 128], identity=ident_b,
                           r=[kh] + CST, w=[bk(b)])
                    if t % 2 == 0:
                        op("act", nc.scalar.copy, out=hT[:, :, t * 128:(t + 1) * 128], in_=ptv, r=[bk(b)], w=[("hT", t)])
                    else:
                        op("dve", nc.vector.tensor_copy, out=hT[:, :, t * 128:(t + 1) * 128], in_=ptv, r=[bk(b)],
                           w=[("hT", t)])

        def hTk(tb):
            return [("hT", 4 * tb + i) for i in range(4)]

        def phase_S1(l, hT):
            w_l = w_in_d[l].rearrange("(kc p) n -> p kc n", p=128)
            xtm_v = xtm_d.rearrange("(t p) c -> p t c", p=128)
            with Scope() as sc:
                wp = sc.pool("wcc", 2, [128, 8, 128], BF16)
                xcp = sc.pool("xc", 2, [128, S + 3], F32)
                accp = sc.pool("acc", 2, [128, 2048], F32)
                xop = sc.pool("xo", 2, [128, S], BF16)
                tmp_ = sc.pool("tmt", 3, [128, 8, 128], BF16)
                for t_, k_ in ((xcp.tiles[0], (xcp.name, 0)), (xcp.tiles[1], (xcp.name, 1))):
                    op("dve", nc.vector.memset, t_[:, 0:3], 0.0, w=[k_])
                nb = 0
                for cc in range(24):
                    w_t, kw_ = wp.next()
                    c0 = OFF_XBC + cc * 128
                    op("pool", nc.gpsimd.dma_start, out=w_t[:], in_=w_l[:, :, c0:c0 + 128], w=[kw_], dma=kw_)
                    xc, kxc = xcp.next()
                    for tb in range(8):
                        b = nb % 4
                        nb += 1
                        for kc in range(8):
                            op("pe", mm, banks[b][:], lhsT=w_t[:, kc, :], rhs=hT[:, kc, tb * 512:(tb + 1) * 512],
                               start=(kc == 0), stop=(kc == 7), r=[kw_] + hTk(tb), w=[bk(b)])
                        op("act", nc.scalar.copy, out=xc[:, 3 + tb * 512: 3 + (tb + 1) * 512], in_=banks[b][:],
                           r=[bk(b)], w=[kxc])
                    xo, kxo = xop.next()
                    for half in range(2):
                        a_t, ka = accp.next()
                        o0 = half * 2048
                        op("dve", ts, out=a_t[:], in0=xc[:, o0:o0 + 2048], scalar1=cw[:, cc * 4:cc * 4 + 1],
                           scalar2=cb[:, cc:cc + 1], op0=MUL, op1=ADD, r=[kxc, "cw", "cb"], w=[ka])
                        for k in range(1, 4):
                            op("dve", stt, out=a_t[:], in0=xc[:, o0 + k:o0 + k + 2048],
                               scalar=cw[:, cc * 4 + k:cc * 4 + k + 1], in1=a_t[:], op0=MUL, op1=ADD,
                               r=[kxc, "cw", ka], w=[ka])
                        op("act", act, out=xo[:, o0:o0 + 2048], in_=a_t[:], func=AF.Silu, r=[ka], w=[kxo])
                    if cc >= 16:
                        op("sp", nc.sync.dma_start, out=bct_d[cc - 16], in_=xo[:], r=[kxo], w=[("bct", cc - 16)], dma=kxo)
                    if cc < 20:
                        for q4 in range(4):
                            b = 4 + (q4 % 2)
                            ptv = bf(b).rearrange("p (a b) -> p a b", b=128)
                            for i in range(8):
                                t0 = (q4 * 8 + i) * 128
                                op("pe", tr, out=ptv[:, i, :], in_=xo[:, t0:t0 + 128], identity=ident_b,
                                   r=[kxo] + CST, w=[bk(b)])
                            tm, ktm = tmp_.next()
                            if q4 % 2 == 0:
                                op("act", nc.scalar.copy, out=tm[:], in_=ptv, r=[bk(b)], w=[ktm])
                            else:
                                op("dve", nc.vector.tensor_copy, out=tm[:], in_=ptv, r=[bk(b)], w=[ktm])
                            op("sp", nc.sync.dma_start, out=xtm_v[:, q4 * 8:(q4 + 1) * 8, cc * 128:(cc + 1) * 128],
                               in_=tm[:], r=[ktm], w=[("xtm", cc, q4)], dma=ktm)

        def phase_S2(l, hT):
            w_l = w_in_d[l].rearrange("(kc p) n -> p kc n", p=128)
            bct_v = bct_d.rearrange("g p t -> p g t")
            ysT_v = ysT_d.rearrange("k p t -> p k t")
            with Scope() as sc:
                wzs = sc.sb("wzs", [128, 8, 2048], BF16)
                wdt = sc.sb("wdt", [128, 8, 32], BF16)
                op("pool", nc.gpsimd.dma_start, out=wzs[:], in_=w_l[:, :, OFF_ZS:OFF_ZS + 2048], w=["wzs"], dma="wzs")
                op("pool", nc.gpsimd.dma_start, out=wdt[:], in_=w_l[:, :, OFF_DT:OFF_DT + 32], w=["wdt"], dma="wdt")
                Dd = sc.sb("Dd", [128, 32, 128], BF16)
                op("dve", tt, out=Dd[:], in0=ident_f.unsqueeze(1).broadcast_to([128, 32, 128]),
                   in1=hp[:, 64:96].unsqueeze(2).broadcast_to([128, 32, 128]), op=MUL, r=["hp"] + CST, w=["Dd"])
                st = sc.sb("st", [128, 2048], F32)
                stb = sc.sb("stb", [128, 2048], BF16)
                op("dve", nc.vector.memset, st[:], 0.0, w=["st"])
                op("dve", nc.vector.memset, stb[:], 0.0, w=[("stb", i) for i in range(4)])
                xtp = sc.pool("xtc", 2, [128, 2560], BF16)
                bcp = sc.pool("bcc", 2, [128, 8, 128], BF16)
                smp = sc.pool("sm", 2, [128, 8, 32], F32)
                xdtp = sc.pool("xdt", 2, [128, 32, 64], BF16)
                xdsp = sc.pool("xds", 2, [128, 32, 64], BF16)
                cbmp = sc.pool("cbm", 2, [128, 4, 128], BF16)
                ltp = sc.pool("lt", 2, [128, 4, 128], F32)
                dcp = sc.pool("dcT", 2, [128, 4, 128], BF16)
                mtp = sc.pool("MT", 3, [128, 4, 128], BF16)
                t1p = sc.pool("t1", 2, [128, 512], F32)
                szp = sc.pool("sz", 2, [128, 512], F32)
                gnp = sc.pool("gn", 2, [128, 512], BF16)
                junk = sc.sb("junkS", [128, 512], BF16)
                sqp = sc.pool("ssq", 2, [128, 1], F32)
                rqp = sc.pool("rsq", 2, [128, 1], F32)
                ysp = sc.pool("ysc", 2, [128, 16, 128], BF16)
                nrb = 0
                for c in range(32):
                    tok = slice(c * 128, (c + 1) * 128)
                    hk = [("hT", c)]
                    xt_c, kxt = xtp.next()
                    op("sp", nc.sync.dma_start, out=xt_c[:], in_=xtm_d[tok, :],
                       r=[("xtm", cc, c // 8) for cc in range(20)], w=[kxt], dma=kxt)
                    bc_c, kbc = bcp.next()
                    op("sp", nc.sync.dma_start, out=bc_c[:], in_=bct_v[:, :, tok],
                       r=[("bct", i) for i in range(8)], w=[kbc], dma=kbc)
                    sm, ksm = smp.next()
                    for kc in range(8):
                        op("pe", mm, banks[0][:, 0:32], lhsT=hT[:, kc, tok], rhs=wdt[:, kc, :], start=(kc == 0),
                           stop=(kc == 7), r=hk + ["wdt"], w=[bk(0)])
                    op("dve", tt, out=sm[:, 0, :], in0=banks[0][:, 0:32], in1=hp[:, 0:32], op=ADD, r=[bk(0), "hp"], w=[ksm])
                    op("act", act, out=sm[:, 1, :], in_=sm[:, 0, :], func=AF.Exp, r=[ksm], w=[ksm])
                    op("act", act, out=sm[:, 2, :], in_=sm[:, 1, :], func=AF.Ln, bias=1.0, r=[ksm], w=[ksm])
                    op("dve", tt, out=sm[:, 3, :], in0=sm[:, 2, :], in1=Aneg[:], op=MUL, r=[ksm, "Aneg"], w=[ksm])
                    op("pe", mm, banks[0][:, 32:64], lhsT=U_f, rhs=sm[:, 3, :], start=True, stop=True,
                       r=[ksm] + CST, w=[bk(0)])
                    op("pe", mm, banks[0][:, 64:96], lhsT=ones_f[:], rhs=sm[:, 3, :], start=True, stop=True,
                       r=[ksm, "ones_f"], w=[bk(0)])
                    op("dve", nc.vector.tensor_copy, out=sm[:, 4, :], in_=banks[0][:, 32:64], r=[bk(0)], w=[ksm])
                    op("act", act, out=sm[:, 5, :], in_=banks[0][:, 32:64], func=AF.Exp, r=[bk(0)], w=[ksm])
                    op("dve", tt, out=sm[:, 6, :], in0=banks[0][:, 64:96], in1=sm[:, 4, :], op=SUB, r=[bk(0), ksm], w=[ksm])
                    op("act", act, out=sm[:, 6, :], in_=sm[:, 6, :], func=AF.Exp, r=[ksm], w=[ksm])
                    op("act", act, out=sm[:, 7, :], in_=banks[0][:, 64:96], func=AF.Exp, r=[bk(0)], w=[ksm])
                    xdt, kxdt = xdtp.next()
                    op("dve", tt, out=xdt[:], in0=xt_c[:, 0:2048].rearrange("p (a b) -> p a b", b=64),
                       in1=sm[:, 2, :].unsqueeze(2).broadcast_to([128, 32, 64]), op=MUL, r=[kxt, ksm], w=[kxdt])
                    xds, kxds = xdsp.next()
                    op("pool", nc.gpsimd.tensor_tensor, out=xds[:], in0=xdt[:],
                       in1=sm[:, 6, :].unsqueeze(2).broadcast_to([128, 32, 64]), op=MUL, r=[kxdt, ksm], w=[kxds])
                    cbv = banks[1][:].rearrange("p (a b) -> p a b", b=128)
                    for gi in range(4):
                        op("pe", mm, cbv[:, gi, :], lhsT=bc_c[:, gi, :], rhs=bc_c[:, 4 + gi, :], start=True, stop=True,
                           r=[kbc], w=[bk(1)])
                    cbm, kcbm = cbmp.next()
                    op("dve", tt, out=cbm[:], in0=cbv, in1=maskU_b.unsqueeze(1).broadcast_to([128, 4, 128]), op=MUL,
                       r=[bk(1)] + CST, w=[kcbm])
                    ys_c, kys = ysp.next()
                    for gi in range(4):
                        for hq in range(2):
                            h0 = gi * 8 + hq * 4
                            lt, klt = ltp.next()
                            op("pool", nc.gpsimd.tensor_tensor, out=lt[:],
                               in0=Ls_f.unsqueeze(1).broadcast_to([128, 4, 128]),
                               in1=sm[:, 3, h0:h0 + 4].unsqueeze(2).broadcast_to([128, 4, 128]), op=MUL,
                               r=[ksm] + CST, w=[klt])
                            rb = 2 + (nrb % 2)
                            nrb += 1
                            rbv = banks[rb][:].rearrange("p (a b) -> p a b", b=128)
                            for i in range(4):
                                op("pe", mm, rbv[:, i, :], lhsT=lt[:, i, :], rhs=U_f, start=True, stop=True,
                                   r=[klt] + CST, w=[bk(rb)])
                            dc, kdc = dcp.next()
                            op("act", act, out=dc[:], in_=rbv, func=AF.Exp, r=[bk(rb)], w=[kdc])
                            mt, kmt = mtp.next()
                            op("dve", tt, out=mt[:], in0=dc[:], in1=cbm[:, gi:gi + 1, :].broadcast_to([128, 4, 128]),
                               op=MUL, r=[kdc, kcbm], w=[kmt])
                            for i in range(4):
                                hh = h0 + i
                                j = hq * 4 + i
                                op("pe", mm, banks[4][:, j * 64:(j + 1) * 64], lhsT=mt[:, i, :], rhs=xdt[:, hh, :],
                                   start=True, stop=False, r=[kmt, kxdt], w=[bk(4)])
                                op("pe", mm, banks[4][:, j * 64:(j + 1) * 64], lhsT=Dd[:, hh, :],
                                   rhs=xt_c[:, hh * 64:(hh + 1) * 64], start=False, stop=True, r=["Dd", kxt], w=[bk(4)])
                        op("pe", mm, banks[5][:], lhsT=bc_c[:, 4 + gi, :], rhs=stb[:, gi * 512:(gi + 1) * 512],
                           start=True, stop=True, r=[kbc, ("stb", gi)], w=[bk(5)])
                        for kc in range(8):
                            op("pe", mm, banks[6][:], lhsT=hT[:, kc, tok], rhs=wzs[:, kc, gi * 512:(gi + 1) * 512],
                               start=(kc == 0), stop=(kc == 7), r=hk + ["wzs"], w=[bk(6)])
                        op("pe", mm, banks[7][:], lhsT=xt_c[:, 2048 + gi * 128:2048 + (gi + 1) * 128],
                           rhs=xds[:, gi * 8:(gi + 1) * 8, :], start=True, stop=True, r=[kxt, kxds], w=[bk(7)])
                        t1, kt1 = t1p.next()
                        op("dve", tt, out=t1[:].rearrange("p (a b) -> p a b", b=64),
                           in0=banks[5][:].rearrange("p (a b) -> p a b", b=64),
                           in1=sm[:, 5, gi * 8:(gi + 1) * 8].unsqueeze(2).broadcast_to([128, 8, 64]), op=MUL,
                           r=[bk(5), ksm], w=[kt1])
                        y, ky = t1, kt1
                        op("dve", tt, out=y[:], in0=banks[4][:], in1=t1[:], op=ADD, r=[bk(4), kt1], w=[ky])
                        sz, ksz = szp.next()
                        op("act", act, out=sz[:], in_=banks[6][:], func=AF.Exp, scale=-1.0, r=[bk(6)], w=[ksz])
                        op("act", act, out=sz[:], in_=sz[:], func=AF.Ln, bias=1.0, r=[ksz], w=[ksz])
                        op("act", act, out=sz[:], in_=sz[:], func=AF.Exp, scale=-1.0, r=[ksz], w=[ksz])
                        op("dve", tt, out=sz[:], in0=banks[6][:], in1=sz[:], op=MUL, r=[bk(6), ksz], w=[ksz])
                        gt, kgt = sz, ksz
                        op("pool", nc.gpsimd.tensor_tensor, out=gt[:], in0=y[:], in1=sz[:], op=MUL, r=[ky, ksz], w=[kgt])
                        sq, ksq = sqp.next()
                        op("act", act, out=junk[:], in_=gt[:], func=AF.Square, accum_out=sq[:], r=[kgt], w=["junkS", ksq])
                        rq, krq = rqp.next()
                        op("act", act, out=rq[:], in_=sq[:], func=AF.Ln, bias=epst[:], scale=1.0 / 512.0,
                           r=[ksq, "eps"], w=[krq])
                        op("act", act, out=rq[:], in_=rq[:], func=AF.Exp, scale=-0.5, r=[krq], w=[krq])
                        gn, kgn = gnp.next()
                        op("dve", stt, out=gn[:], in0=gt[:], scalar=rq[:], in1=snw[:, gi * 512:(gi + 1) * 512],
                           op0=MUL, op1=MUL, r=[kgt, krq, "snw"], w=[kgn])
                        ptv = bf(5).rearrange("p (a b) -> p a b", b=128)
                        for i in range(4):
                            op("pe", tr, out=ptv[:, i, :], in_=gn[:, i * 128:(i + 1) * 128], identity=ident_b,
                               r=[kgn] + CST, w=[bk(5)])
                        op("act", nc.scalar.copy, out=ys_c[:, gi * 4:(gi + 1) * 4, :], in_=ptv[:, 0:4, :], r=[bk(5)], w=[kys])
                        stv = st[:, gi * 512:(gi + 1) * 512]
                        op("pool", nc.gpsimd.tensor_tensor, out=stv.rearrange("p (a b) -> p a b", b=64),
                           in0=stv.rearrange("p (a b) -> p a b", b=64),
                           in1=sm[:, 7, gi * 8:(gi + 1) * 8].unsqueeze(2).broadcast_to([128, 8, 64]), op=MUL,
                           r=[("st", gi), ksm], w=[("st", gi)])
                        op("dve", tt, out=stv, in0=banks[7][:], in1=stv, op=ADD, r=[bk(7), ("st", gi)], w=[("st", gi)])
                        op("act", nc.scalar.copy, out=stb[:, gi * 512:(gi + 1) * 512], in_=stv, r=[("st", gi)],
                           w=[("stb", gi)])
                    op("sp", nc.sync.dma_start, out=ysT_v[:, :, tok], in_=ys_c[:], r=[kys], w=[("ysT", c // 4)], dma=kys)

        def phase_G(l, hT):
            w_l = w_in_d[l].rearrange("(kc p) n -> p kc n", p=128)
            with Scope() as sc:
                wp = sc.pool("wg", 2, [128, 8, 128], BF16)
                sgp = sc.pool("sg", 2, [128, S], BF16)
                nb = 0
                for gc in range(16):
                    w_t, kw_ = wp.next()
                    c0 = OFF_G + gc * 128
                    op("pool", nc.gpsimd.dma_start, out=w_t[:], in_=w_l[:, :, c0:c0 + 128], w=[kw_], dma=kw_)
                    sg, ksg = sgp.next()
                    for tb in range(8):
                        b = nb % 4
                        nb += 1
                        for kc in range(8):
                            op("pe", mm, banks[b][:], lhsT=w_t[:, kc, :], rhs=hT[:, kc, tb * 512:(tb + 1) * 512],
                               start=(kc == 0), stop=(kc == 7), r=[kw_] + hTk(tb), w=[bk(b)])
                        op("act", act, out=sg[:, tb * 512:(tb + 1) * 512], in_=banks[b][:], func=AF.Sigmoid,
                           r=[bk(b)], w=[ksg])
                    op("sp", nc.sync.dma_start, out=sgT_d[gc], in_=sg[:], r=[ksg], w=[("sgT", gc)], dma=ksg)

        def phase_T(l, hT):
            w_l = w_in_d[l].rearrange("(kc p) n -> p kc n", p=128)
            LOOK = 2
            LSC = 2.0 ** -10
            CSC = EPS / (LSC * LSC)
            with Scope() as sc:
                whp = sc.pool("wh", 2, [128, 8, 4, 128], BF16)
                qz = [sc.sb("qz%d" % m, [128, S], BF16) for m in range(2)]
                kT = sc.sb("kT", [128, S], BF16)
                vt = sc.sb("vt", [128, 32, 128], BF16)
                szT = sc.sb("szT", [128, S], BF16)
                sqp = sc.pool("sq", 2, [128, 512], BF16)
                lnp = sc.pool("lnq", 2, [128, 512], F32)
                rsp = sc.pool("rst", 2, [128, 512], F32)
                ezp = sc.pool("ez", 2, [128, 512], F32)
                ptp = sc.pool("PT", 4, [128, 2, 512], BF16)
                s1pp = sc.pool("s1p", 2, [128, 512], F32)
                s0cp = sc.pool("s0c", 2, [128, 512], F32)
                s1cp = sc.pool("s1c", 2, [128, 512], F32)
                lb0p = sc.pool("lb0", 1, [128, 512], F32)
                lb1p = sc.pool("lb1", 1, [128, 512], F32)
                u0p = sc.pool("u0", 1, [128, 512], F32)
                u1p = sc.pool("u1", 1, [128, 512], F32)
                tqp = sc.pool("tq", 1, [128, 512], F32)
                sqo = sc.pool("sqo", 1, [128, 512], BF16)
                agp = sc.pool("arg", 1, [128, 512], F32)
                ybp = sc.pool("yb", 2, [128, 512], BF16)
                op("dve", nc.vector.memset, qz[0][64:128, :], 0.0, w=["qz0"])
                op("dve", nc.vector.memset, qz[1][0:64, :], 0.0, w=["qz1"])
                nS = [0]
                nI = [0]
                pairs = ((2, 3), (4, 5))
                for h in range(8):
                    wh, kwh = whp.next()
                    kws = []
                    for i, off in enumerate((OFF_Q, OFF_K, OFF_V, OFF_ZA)):
                        c0 = off + h * 128
                        kwi = kwh + (i,)
                        kws.append(kwi)
                        op("pool", nc.gpsimd.dma_start, out=wh[:, :, i, :], in_=w_l[:, :, c0:c0 + 128], w=[kwi], dma=kwi)
                    for which in (0, 1):
                        for tb in range(8):
                            ba = 2 + 2 * (nI[0] % 2)
                            bs = ba + 1
                            nI[0] += 1
                            cs = slice(tb * 512, (tb + 1) * 512)
                            for kc in range(8):
                                op("pe", mm, banks[ba][:], lhsT=wh[:, kc, which, :], rhs=hT[:, kc, cs],
                                   start=(kc == 0), stop=(kc == 7), r=[kws[which]] + hTk(tb), w=[bk(ba)])
                            sq, ksq = sqp.next()
                            op("act", act, out=sq[:], in_=banks[ba][:], func=AF.Square, r=[bk(ba)], w=[ksq])
                            op("pe", mm, banks[bs][:], lhsT=BD_b, rhs=sq[:], start=True, stop=True, r=[ksq] + CST, w=[bk(bs)])
                            ln, kln = lnp.next()
                            op("act", act, out=ln[:], in_=banks[bs][:], func=AF.Ln, bias=epst[:], scale=1.0 / 64.0,
                               r=[bk(bs), "eps"], w=[kln])
                            rs, krs = rsp.next()
                            op("act", act, out=rs[:], in_=ln[:], func=AF.Exp, scale=-0.5, r=[kln], w=[krs])
                            if which == 0:
                                for m in range(2):
                                    pr = slice(m * 64, (m + 1) * 64)
                                    op("dve", stt, out=qz[m][pr, cs], in0=banks[ba][pr, :], scalar=qkw[pr, 0:1], in1=rs[pr, :],
                                       op0=MUL, op1=MUL, r=[bk(ba), krs, "qkw"], w=["qz%d" % m])
                            else:
                                op("dve", stt, out=kT[:, cs], in0=banks[ba][:], scalar=qkw[:, 1:2], in1=rs[:],
                                   op0=MUL, op1=MUL, r=[bk(ba), krs, "qkw"], w=["kT"])
                    for tb in range(8):
                        ba = 2 + (nI[0] % 4)
                        nI[0] += 1
                        cs = slice(tb * 512, (tb + 1) * 512)
                        for kc in range(8):
                            op("pe", mm, banks[ba][:], lhsT=wh[:, kc, 3, :], rhs=hT[:, kc, cs],
                               start=(kc == 0), stop=(kc == 7), r=[kws[3]] + hTk(tb), w=[bk(ba)])
                        ez, kez = ezp.next()
                        op("act", act, out=ez[:], in_=banks[ba][:], func=AF.Exp, scale=-1.0, r=[bk(ba)], w=[kez])
                        op("act", act, out=ez[:], in_=ez[:], func=AF.Ln, bias=1.0, r=[kez], w=[kez])
                        op("act", act, out=ez[:], in_=ez[:], func=AF.Exp, scale=-1.0, r=[kez], w=[kez])
                        op("dve", stt, out=szT[:, cs], in0=banks[ba][:], scalar=swc[:, 0:1], in1=ez[:], op0=MUL, op1=MUL,
                           r=[bk(ba), kez, "swc"], w=["szT"])
                    for t4 in range(8):
                        b = 2 + (nI[0] % 4)
                        nI[0] += 1
                        pv = banks[b][:].rearrange("p (a b) -> p a b", b=128)
                        for i in range(4):
                            t = t4 * 4 + i
                            for kc in range(8):
                                op("pe", mm, pv[:, i, :], lhsT=hT[:, kc, t * 128:(t + 1) * 128], rhs=wh[:, kc, 2, :],
                                   start=(kc == 0), stop=(kc == 7), r=[kws[2], ("hT", t)], w=[bk(b)])
                        op("dve", nc.vector.tensor_copy, out=vt[:, t4 * 4:(t4 + 1) * 4, :], in_=pv, r=[bk(b)], w=["vt"])
                    steps = [(qb, t) for qb in range(8) for t in range(4 * qb + 4)]
                    pts = {}
                    s1ps = {}

                    def emit_qk(j):
                        qb, t = steps[j]
                        off = max(0, t - 4 * qb) * 128
                        pi = nS[0] % 2
                        nS[0] += 1
                        pb = pairs[pi]
                        diag = t >= 4 * qb
                        for m in range(2):
                            op("pe", mm, banks[pb[m]][:, 0:512 - off], lhsT=kT[:, t * 128:(t + 1) * 128],
                               rhs=qz[m][:, qb * 512 + off:(qb + 1) * 512], start=True, stop=not diag,
                               r=["kT", "qz%d" % m], w=[bk(pb[m])])
                            if diag:
                                op("pe", mm, banks[pb[m]][:, 0:128], lhsT=ident_b, rhs=Mneg_b, start=False, stop=True,
                                   r=CST, w=[bk(pb[m])])
                        pt, kpt = ptp.next()
                        pview = pbig[1 + pi][:].rearrange("p (a b) -> p a b", b=512)
                        op("act", act, out=pt[:, :, 0:512 - off], in_=pview[:, :, 0:512 - off], func=AF.Exp,
                           r=[bk(pb[0]), bk(pb[1])], w=[kpt])
                        pts[j] = (pt, kpt)

                    def emit_pv(i):
                        qb, t = steps[i]
                        off = max(0, t - 4 * qb) * 128
                        pt, kpt = pts.pop(i)
                        for m in range(2):
                            op("pe", mm, banks[m][:, off:512], lhsT=vt[:, t, :], rhs=pt[:, m, 0:512 - off],
                               start=(t == 0), stop=(t == 4 * qb + 3), r=[kpt, "vt"], w=[bk(m)])
                        if t == 0:
                            op("dve", nc.vector.tensor_copy, out=banks[6][:], in_=pt[:, 0, :], r=[kpt], w=[bk(6)])
                            op("dve", nc.vector.tensor_copy, out=banks[7][:], in_=pt[:, 1, :], r=[kpt], w=[bk(7)])
                            s1ps[qb] = s1pp.next()
                            op("pool", nc.gpsimd.memset, s1ps[qb][0][:], 0.0, w=[s1ps[qb][1]])
                        else:
                            op("dve", tt, out=banks[6][:, off:512], in0=banks[6][:, off:512], in1=pt[:, 0, 0:512 - off],
                               op=ADD, r=[kpt, bk(6)], w=[bk(6)])
                            if t % 3 == 0:
                                op("dve", tt, out=banks[7][:, off:512], in0=banks[7][:, off:512], in1=pt[:, 1, 0:512 - off],
                                   op=ADD, r=[kpt, bk(7)], w=[bk(7)])
                            else:
                                sp_, ksp = s1ps[qb]
                                op("pool", nc.gpsimd.tensor_tensor, out=sp_[:, off:512], in0=sp_[:, off:512],
                                   in1=pt[:, 1, 0:512 - off], op=ADD, r=[kpt, ksp], w=[ksp])

                    def emit_fin(qb):
                        cs = slice(qb * 512, (qb + 1) * 512)
                        s1p_, ks1p = s1ps.pop(qb)
                        s0c, ks0c = s0cp.next()
                        s1c, ks1c = s1cp.next()
                        op("dve", nc.vector.tensor_copy, out=s0c[:], in_=banks[6][:], r=[bk(6)], w=[ks0c])
                        op("dve", nc.vector.tensor_copy, out=s1c[:], in_=banks[7][:], r=[bk(7)], w=[ks1c])
                        pi = nS[0] % 2
                        nS[0] += 1
                        bx, by = pairs[pi]
                        op("pe", mm, banks[bx][:], lhsT=ones_f[:], rhs=s0c[:], start=True, stop=True, r=[ks0c, "ones_f"], w=[bk(bx)])
                        op("pe", mm, banks[by][:], lhsT=ones_f[:], rhs=s1c[:], start=True, stop=False, r=[ks1c, "ones_f"], w=[bk(by)])
                        op("pe", mm, banks[by][:], lhsT=ones_f[:], rhs=s1p_[:], start=False, stop=True, r=[ks1p, "ones_f"], w=[bk(by)])
                        lb0, kl0 = lb0p.next()
                        lb1, kl1 = lb1p.next()
                        op("act", act, out=lb0[:], in_=banks[bx][:], func=AF.Copy, scale=LSC, r=[bk(bx)], w=[kl0])
                        op("act", act, out=lb1[:], in_=banks[by][:], func=AF.Copy, scale=LSC, r=[bk(by)], w=[kl1])
                        u0, ku0 = u0p.next()
                        u1, ku1 = u1p.next()
                        op("dve", tt, out=u1[:], in0=banks[1][:], in1=lb0[:], op=MUL, r=[bk(1), kl0], w=[ku1])
                        op("dve", tt, out=u0[:], in0=banks[0][:], in1=lb1[:], op=MUL, r=[bk(0), kl1], w=[ku0])
                        op("dve", stt, out=u0[:], in0=u1[:], scalar=neglam[:, 0:1], in1=u0[:], op0=MUL, op1=ADD,
                           r=[ku0, ku1, "neglam"], w=[ku0])
                        tq, ktq = tqp.next()
                        op("pool", nc.gpsimd.tensor_tensor, out=tq[:], in0=lb0[:], in1=lb1[:], op=MUL, r=[kl0, kl1], w=[ktq])
                        op("pool", nc.gpsimd.tensor_tensor, out=tq[:], in0=tq[:], in1=tq[:], op=MUL, r=[ktq], w=[ktq])
                        sq, ksq = sqo.next()
                        op("pool", nc.gpsimd.tensor_tensor, out=sq[:], in0=u0[:], in1=u0[:], op=MUL, r=[ku0], w=[ksq])
                        op("pe", mm, banks[bx][:], lhsT=ones_b[:], rhs=sq[:], start=True, stop=True, r=[ksq, "ones_b"], w=[bk(bx)])
                        ag, kag = agp.next()
                        op("dve", stt, out=ag[:], in0=banks[bx][:], scalar=float(1.0 / (128.0 * CSC)), in1=tq[:],
                           op0=MUL, op1=ADD, r=[bk(bx), ktq], w=[kag])
                        op("act", act, out=ag[:], in_=ag[:], func=AF.Ln, r=[kag], w=[kag])
                        op("act", act, out=ag[:], in_=ag[:], func=AF.Exp, scale=-0.5, r=[kag], w=[kag])
                        op("dve", stt, out=u0[:], in0=u0[:], scalar=float(CSC ** -0.5), in1=ag[:], op0=MUL, op1=MUL,
                           r=[ku0, kag], w=[ku0])
                        yb, kyb = ybp.next()
                        op("pool", nc.gpsimd.tensor_tensor, out=yb[:], in0=u0[:], in1=szT[:, cs], op=MUL,
                           r=[ku0, "szT"], w=[kyb])
                        op("sp", nc.sync.dma_start, out=yaT_d[h][:, cs], in_=yb[:], r=[kyb], w=[("yaT", h, qb)], dma=kyb)

                    n = len(steps)
                    for i in range(-LOOK, n):
                        j = i + LOOK
                        if j < n:
                            emit_qk(j)
                        if i >= 0:
                            emit_pv(i)
                            qb, t = steps[i]
                            if t == 4 * qb + 3:
                                emit_fin(qb)

        def phase_D(l, xsrc):
            wpa_v = w_pa_d[l].rearrange("(kc p) n -> p kc n", p=128)
            wps_v = w_ps_d[l].rearrange("(kc p) n -> p kc n", p=128)
            wo_v = w_out_d[l].rearrange("(kc p) n -> p kc n", p=128)
            yaT_v = yaT_d.rearrange("h p t -> p h t")
            ysT_v = ysT_d.rearrange("k p t -> p k t")
            sgT_v = sgT_d.rearrange("k p t -> p k t")
            with Scope() as sc:
                wpa = sc.sb("wpa", [128, 8, D], BF16)
                wps = sc.sb("wps", [128, 16, D], BF16)
                wo = sc.sb("wo", [128, 8, D], BF16)
                op("pool", nc.gpsimd.dma_start, out=wpa[:], in_=wpa_v, w=["wpa"], dma="wpa")
                op("pool", nc.gpsimd.dma_start, out=wps[:, 0:8], in_=wps_v[:, 0:8], w=["wps"], dma="wps")
                op("pool", nc.gpsimd.dma_start, out=wps[:, 8:16], in_=wps_v[:, 8:16], w=["wps"], dma="wps")
                op("pool", nc.gpsimd.dma_start, out=wo[:], in_=wo_v, w=["wo"], dma="wo")
                yap = sc.pool("yab", 2, [128, 8, 512], BF16)
                ysp = sc.pool("ysb", 2, [128, 16, 512], BF16)
                sgp = sc.pool("sgb", 1, [128, 16, 512], BF16)
                xrp = sc.pool("xr", 1, [128, 4, D], F32)
                m1p = sc.pool("m1", 2, [128, 512], F32)
                m2p = sc.pool("m2", 2, [128, 512], F32)
                mTp = sc.pool("mT", 2, [128, 8, 512], BF16)
                xop = sc.pool("xo", 1, [128, 4, D], F32)
                nb = 0
                for tb in range(8):
                    tok = slice(tb * 512, (tb + 1) * 512)
                    ya, kya = yap.next()
                    ys, kys = ysp.next()
                    sg, ksg = sgp.next()
                    xr, kxr = xrp.next()
                    op("sp", nc.sync.dma_start, out=ya[:], in_=yaT_v[:, :, tok], r=[("yaT", h, tb) for h in range(8)],
                       w=[kya], dma=kya)
                    op("sp", nc.sync.dma_start, out=ys[:], in_=ysT_v[:, :, tok], r=[("ysT", tb)], w=[kys], dma=kys)
                    op("sp", nc.sync.dma_start, out=sg[:], in_=sgT_v[:, :, tok], r=[("sgT", i) for i in range(16)],
                       w=[ksg], dma=ksg)
                    op("sp", nc.sync.dma_start, out=xr[:], in_=xsrc[tok, :].rearrange("(t p) c -> p t c", p=128),
                       r=[("xres", tb)], w=[kxr], dma=kxr)
                    mT, kmT = mTp.next()
                    for cc in range(8):
                        b1 = nb % 6
                        b2 = (nb + 1) % 6
                        nb += 2
                        for kc in range(8):
                            op("pe", mm, banks[b1][:], lhsT=wpa[:, kc, cc * 128:(cc + 1) * 128], rhs=ya[:, kc, :],
                               start=(kc == 0), stop=(kc == 7), r=["wpa", kya], w=[bk(b1)])
                        for kc in range(16):
                            op("pe", mm, banks[b2][:], lhsT=wps[:, kc, cc * 128:(cc + 1) * 128], rhs=ys[:, kc, :],
                               start=(kc == 0), stop=(kc == 15), r=["wps", kys], w=[bk(b2)])
                        m1, km1 = m1p.next()
                        op("dve", tt, out=m1[:], in0=banks[b1][:], in1=sg[:, cc, :], op=MUL, r=[bk(b1), ksg], w=[km1])
                        m2, km2 = m2p.next()
                        op("dve", tt, out=m2[:], in0=banks[b2][:], in1=sg[:, 8 + cc, :], op=MUL, r=[bk(b2), ksg], w=[km2])
                        op("pool", nc.gpsimd.tensor_tensor, out=mT[:, cc, :], in0=m1[:], in1=m2[:], op=ADD,
                           r=[km1, km2], w=[kmT])
                    xo, kxo = xop.next()
                    for t4 in range(4):
                        for half in range(2):
                            b = 6 + (t4 * 2 + half) % 2
                            for kc in range(8):
                                op("pe", mm, banks[b][:], lhsT=mT[:, kc, t4 * 128:(t4 + 1) * 128],
                                   rhs=wo[:, kc, half * 512:(half + 1) * 512], start=(kc == 0), stop=(kc == 7),
                                   r=[kmT, "wo"], w=[bk(b)])
                            op("dve", tt, out=xo[:, t4, half * 512:(half + 1) * 512], in0=banks[b][:],
                               in1=xr[:, t4, half * 512:(half + 1) * 512], op=ADD, r=[bk(b), kxr], w=[kxo])
                    op("sp", nc.sync.dma_start, out=out_d[tok, :].rearrange("(t p) c -> p t c", p=128), in_=xo[:],
                       r=[kxo], w=[("xres", tb)], dma=kxo)

        for l in range(depth):
            xsrc = x_d if l == 0 else out_d
            barrier()
            load_params(l)
            with Scope() as lsc:
                hT = lsc.sb("hT", [128, 8, S], BF16)
                if "A" in phases:
                    phase_A(l, hT, xsrc)
                    barrier()
                if "S" in phases:
                    phase_S1(l, hT)
                    barrier()
                    phase_S2(l, hT)
                    barrier()
                if "G" in phases:
                    phase_G(l, hT)
                    barrier()
                if "T" in phases:
                    phase_T(l, hT)
                    barrier()
            if "D" in phases:
                phase_D(l, xsrc)
        P.finish(sems)
    return nc, P.stats


def make_consts():
    i = np.arange(128)
    ident = np.eye(128, dtype=np.float32)
    maskU = (i[None, :] >= i[:, None]).astype(np.float32)
    Lstrict = (i[:, None] > i[None, :]).astype(np.float32)
    bd = ((i[:, None] // 64) == (i[None, :] // 64)).astype(np.float32)
    mneg = np.where(i[None, :] >= i[:, None], 0.0, -30000.0).astype(np.float32)
    return np.concatenate([ident, maskU, maskU, Lstrict, bd, mneg], axis=1).astype(np.float32)


def host_layout(inputs):
    inputs = {k: (np.asarray(v)[:DEPTH] if k != "x" else v) for k, v in inputs.items()}
    f = lambda a: np.ascontiguousarray(np.asarray(a, dtype=np.float32))
    rep = lambda a: np.ascontiguousarray(np.broadcast_to(np.asarray(a, np.float32)[:, None, :], (a.shape[0], 128, a.shape[1])))
    qn, kn = np.asarray(inputs["q_norm_w"], np.float32), np.asarray(inputs["k_norm_w"], np.float32)
    qkw = np.stack([np.tile(qn, (1, 2)), np.tile(kn, (1, 2))], axis=-1)
    cw = np.asarray(inputs["conv_w"], np.float32)
    convw = cw.transpose(0, 2, 1).reshape(DEPTH, 24, 128, 4).transpose(0, 2, 1, 3).reshape(DEPTH, 128, 96)
    convb = np.asarray(inputs["conv_b"], np.float32).reshape(DEPTH, 24, 128).transpose(0, 2, 1)
    hp = np.concatenate([inputs["dt_bias"], inputs["a_log"], inputs["d_skip"]], axis=-1).astype(np.float32)
    common = {
        "w_in": f(inputs["w_in"]), "w_pa": f(inputs["w_proj_attn"]), "w_ps": f(inputs["w_proj_ssd"]),
        "w_out": f(inputs["w_out"]),
        "nw_rep": rep(inputs["norm_w"]), "qkw": f(qkw),
        "dl_rep": rep(np.asarray(inputs["diff_lambda"], np.float32).reshape(DEPTH, 256)),
        "sw_rep": rep(inputs["subln_w"]), "swc": f(np.asarray(inputs["subln_w"], np.float32)[:, :, None]), "convw": f(convw), "convb": f(convb),
        "hp_rep": rep(hp), "snw_rep": rep(inputs["ssd_norm_w"]), "consts": make_consts(),
    }
    return common


_NC_CACHE = {}


def kernel(**inputs):
    common = host_layout(inputs)
    x = np.asarray(inputs["x"], np.float32)
    n = x.shape[0]
    if "nc" not in _NC_CACHE:
        _NC_CACHE["nc"] = build()[0]
    nc = _NC_CACHE["nc"]
    in_maps = [dict(common, x=np.ascontiguousarray(x[b])) for b in range(n)]
    res = run_bass_kernel_spmd(nc, in_maps, core_ids=list(range(n)))
    return np.stack([np.asarray(r["out"], np.float32) for r in res.results], axis=0)
```

```python
import contextlib
import math
import numpy as np
import concourse.bass as bass
import concourse.mybir as mybir
from concourse.bass_utils import run_bass_kernel_spmd
from concourse.alu_op_type import AluOpType as ALU

AF = mybir.ActivationFunctionType
F32 = mybir.dt.float32
BF16 = mybir.dt.bfloat16
AX = mybir.AxisListType

S = 4096
D = 1024
DIN = 11296
OFF_Q, OFF_K, OFF_V, OFF_ZA, OFF_XBC, OFF_ZS, OFF_DT, OFF_G = 0, 1024, 2048, 3072, 4096, 7168, 9216, 9248
EPS = 1e-6
import os as _os
DEPTH = int(_os.environ.get('KDEPTH', '4'))


class Op:
    __slots__ = ("eng", "fn", "args", "kw", "reads", "writes", "dma", "idx",
                 "deps", "signal", "token", "waits")

    def __init__(self, eng, fn, args, kw, reads, writes, dma):
        self.eng = eng
        self.fn = fn
        self.args = args
        self.kw = kw
        self.reads = reads
        self.writes = writes
        self.dma = dma
        self.deps = set()
        self.signal = False
        self.token = None
        self.waits = []


class Prog:
    def __init__(self, nc):
        self.nc = nc
        self.ops = []
        self.q = {"pe": nc.tensor, "act": nc.scalar, "dve": nc.vector,
                  "pool": nc.gpsimd, "sp": nc.sync}

    def add(self, eng, fn, *args, reads=(), writes=(), dma=None, **kw):
        op = Op(eng, fn, args, kw, tuple(reads), tuple(writes), dma)
        op.idx = len(self.ops)
        self.ops.append(op)
        return op

    def finish(self, sems, final_eng="sp"):
        ops = self.ops
        last_w = {}
        readers = {}
        sem_waiters = {}
        dma_cum = {}
        for op in ops:
            deps = set()
            raw = set()
            for r in op.reads:
                w = last_w.get(r)
                if w is not None:
                    deps.add(w)
                    raw.add(w)
            for wkey in op.writes:
                w = last_w.get(wkey)
                if w is not None:
                    deps.add(w)
                for ridx in readers.get(wkey, {}).values():
                    deps.add(ridx)
            deps.discard(op.idx)
            keep = set()
            for d in deps:
                dop = ops[d]
                if dop.dma is None and op.dma is None and dop.eng == op.eng:
                    if op.eng == "pe" or d not in raw:
                        continue
                keep.add(d)
            if op.dma is not None:
                for e, widx in sem_waiters.get(op.dma, {}).items():
                    if e != op.eng and widx != op.idx:
                        keep.add(widx)
            for d in keep:
                if ops[d].dma is not None:
                    sem_waiters.setdefault(ops[d].dma, {})[op.eng] = op.idx
            op.deps = keep
            for r in op.reads:
                rd = readers.setdefault(r, {})
                if op.dma is not None:
                    rd[("dma", op.dma)] = op.idx
                else:
                    rd[op.eng] = op.idx
            for wkey in op.writes:
                last_w[wkey] = op.idx
                readers[wkey] = {}
        eng_cnt = {}
        for op in ops:
            for d in op.deps:
                if ops[d].dma is None:
                    ops[d].signal = True
        known = {}
        for op in ops:
            kn = known.setdefault(op.eng, {})
            waits = {}
            for d in sorted(op.deps):
                dop = ops[d]
                if dop.dma is not None:
                    s = ("dma", dop.dma)
                    v = dma_cum[dop.dma]
                else:
                    s = ("eng", dop.eng)
                    v = dop.token[1]
                if kn.get(s, 0) >= v:
                    continue
                waits[s] = max(waits.get(s, 0), v)
            for s, v in waits.items():
                kn[s] = v
            op.waits = list(waits.items())
            if op.dma is not None:
                dma_cum[op.dma] = dma_cum.get(op.dma, 0) + 16
                op.token = (("dma", op.dma), dma_cum[op.dma])
            else:
                if op.signal:
                    eng_cnt[op.eng] = eng_cnt.get(op.eng, 0) + 1
                    op.token = (("eng", op.eng), eng_cnt[op.eng])
                else:
                    op.token = (("eng", op.eng), eng_cnt.get(op.eng, 0) + 1)
        n_wait = 0
        for op in ops:
            q = self.q[op.eng]
            for s, v in op.waits:
                q.wait_ge(sems(s), v)
                n_wait += 1
            ins = op.fn(*op.args, **op.kw)
            if op.dma is not None:
                ins.then_inc(sems(("dma", op.dma)), 16)
            elif op.signal:
                ins.then_inc(sems(("eng", op.eng)), 1)
        q = self.q[final_eng]
        for k, v in dma_cum.items():
            q.wait_ge(sems(("dma", k)), v)
        for e, v in eng_cnt.items():
            if e != final_eng:
                q.wait_ge(sems(("eng", e)), v)
        self.stats = dict(n_ops=len(ops), n_wait=n_wait, eng_cnt=dict(eng_cnt),
                          n_dma_sems=len(dma_cum))


def lambda_init_fn(layer_idx):
    return 0.8 - 0.6 * math.exp(-0.3 * layer_idx)


def build(depth=DEPTH, dbg=False, phases="ASGTD"):
    nc = bass.Bass("TRN2", target_bir_lowering=False)
    P = Prog(nc)
    es = contextlib.ExitStack()
    uid = [0]

    def din(name, shape, dt=F32):
        return nc.dram_tensor(name, list(shape), dt, kind="ExternalInput").ap()

    def dscr(name, shape, dt):
        return nc.dram_tensor(name, list(shape), dt, kind="ExternalOutput" if dbg else "Internal").ap()

    x_d = din("x", [S, D])
    w_in_d = din("w_in", [DEPTH, D, DIN])
    w_pa_d = din("w_pa", [DEPTH, D, D])
    w_ps_d = din("w_ps", [DEPTH, 2 * D, D])
    w_out_d = din("w_out", [DEPTH, D, D])
    nw_d = din("nw_rep", [DEPTH, 128, D])
    qkw_d = din("qkw", [DEPTH, 128, 2])
    dl_d = din("dl_rep", [DEPTH, 128, 256])
    sw_d = din("sw_rep", [DEPTH, 128, 128])
    swc_d = din("swc", [DEPTH, 128, 1])
    cw_d = din("convw", [DEPTH, 128, 24 * 4])
    cb_d = din("convb", [DEPTH, 128, 24])
    hp_d = din("hp_rep", [DEPTH, 128, 96])
    snw_d = din("snw_rep", [DEPTH, 128, 2048])
    cst_d = din("consts", [128, 6 * 128])
    out_d = nc.dram_tensor("out", [S, D], F32, kind="ExternalOutput").ap()
    yaT_d = dscr("yaT", [8, 128, S], BF16)
    ysT_d = dscr("ysT", [16, 128, S], BF16)
    sgT_d = dscr("sgT", [16, 128, S], BF16)
    xtm_d = dscr("xtm", [S, 2560], BF16)
    bct_d = dscr("bct", [8, 128, S], BF16)

    sem_cache = {}

    def sems(key):
        if key not in sem_cache:
            sem_cache[key] = es.enter_context(nc.semaphore("s%d" % len(sem_cache)))
        return sem_cache[key]

    class Scope:
        def __init__(self):
            self.es = contextlib.ExitStack()

        def __enter__(self):
            self.es.__enter__()
            return self

        def __exit__(self, *a):
            return self.es.__exit__(*a)

        def sb(self, name, shape, dt):
            uid[0] += 1
            return self.es.enter_context(nc.sbuf_tensor("%s_%d" % (name, uid[0]), list(shape), dt))

        def pool(self, name, n, shape, dt):
            return RPool(self, name, n, shape, dt)

    class RPool:
        def __init__(self, sc, name, n, shape, dt):
            self.name = name
            self.tiles = [sc.sb("%s%d" % (name, i), shape, dt) for i in range(n)]
            self.i = 0

        def next(self):
            j = self.i % len(self.tiles)
            self.i += 1
            return self.tiles[j], (self.name, j)

    def op(eng, fn, *a, r=(), w=(), dma=None, **kw):
        lk = tuple(("lock", k[1]) for k in tuple(r) + tuple(w) if isinstance(k, tuple) and k[0] == "bank")
        return P.add(eng, fn, *a, reads=tuple(r) + ("PH",), writes=tuple(w) + lk, dma=dma, **kw)

    mm = nc.tensor.matmul
    tr = nc.tensor.transpose
    act = nc.scalar.activation
    tt = nc.vector.tensor_tensor
    ts = nc.vector.tensor_scalar
    stt = nc.vector.scalar_tensor_tensor
    MUL, ADD, SUB = ALU.mult, ALU.add, ALU.subtract

    with es:
        g = Scope()
        es.enter_context(g)
        pbig = [es.enter_context(nc.psum_tensor("pbig%d" % i, [128, 1024], F32)) for i in range(4)]
        banks = [pbig[i // 2][:, (i % 2) * 512:(i % 2 + 1) * 512] for i in range(8)]

        def bk(i):
            return ("bank", i)

        def bf(i):
            return banks[i][:].bitcast(BF16)

        cst_f = g.sb("cst_f", [128, 6 * 128], F32)
        cst_b = g.sb("cst_b", [128, 6 * 128], BF16)
        op("sp", nc.sync.dma_start, out=cst_f[:], in_=cst_d, w=["cst_f"], dma="cst_f")
        op("pool", nc.gpsimd.dma_start, out=cst_b[:], in_=cst_d, w=["cst_b"], dma="cst_b")
        ident_f = cst_f[:, 0:128]
        U_f = cst_f[:, 256:384]
        Ls_f = cst_f[:, 384:512]
        ident_b = cst_b[:, 0:128]
        maskU_b = cst_b[:, 128:256]
        BD_b = cst_b[:, 512:640]
        Mneg_b = cst_b[:, 640:768]
        CST = ["cst_f", "cst_b"]
        epst = g.sb("eps", [128, 1], F32)
        op("dve", nc.vector.memset, epst[:], EPS, w=["eps"])
        ones_f = g.sb("ones_f", [128, 128], F32)
        op("dve", nc.vector.memset, ones_f[:], 1.0, w=["ones_f"])
        ones_b = g.sb("ones_b", [128, 128], BF16)
        op("dve", nc.vector.memset, ones_b[:], 1.0, w=["ones_b"])
        swc = g.sb("swc", [128, 1], F32)
        bar_t = g.sb("bar_t", [128, 1], F32)
        nw = g.sb("nw", [128, D], F32)
        qkw = g.sb("qkw", [128, 2], F32)
        dl = g.sb("dl", [128, 256], F32)
        sw = g.sb("sw", [128, 128], F32)
        cw = g.sb("cw", [128, 96], F32)
        cb = g.sb("cb", [128, 24], F32)
        hp = g.sb("hp", [128, 96], F32)
        snw = g.sb("snw", [128, 2048], F32)
        neglam = g.sb("neglam", [128, 1], F32)
        Aneg = g.sb("Aneg", [128, 32], F32)
        lamt = g.sb("lamt", [128, 4], F32)
        lamp = g.sb("lamp", [128, 128], F32)

        def barrier():
            op("dve", nc.vector.memset, bar_t[:], 0.0, w=["PH"])

        def load_params(l):
            li = lambda_init_fn(l)
            for t, src, key in ((nw, nw_d, "nw"), (qkw, qkw_d, "qkw"), (dl, dl_d, "dl"), (sw, sw_d, "sw"),
                                (cw, cw_d, "cw"), (cb, cb_d, "cb"), (hp, hp_d, "hp"), (snw, snw_d, "snw")):
                op("sp", nc.sync.dma_start, out=t[:], in_=src[l], w=[key], dma=key)
            op("dve", ts, out=qkw[:, 0:1], in0=qkw[:, 0:1], scalar1=0.125, scalar2=None, op0=MUL, r=["qkw"], w=["qkw"])
            op("dve", ts, out=sw[:], in0=sw[:], scalar1=float(1.0 - li), scalar2=None, op0=MUL, r=["sw"], w=["sw"])
            op("sp", nc.sync.dma_start, out=swc[:], in_=swc_d[l], w=["swc"], dma="swc")
            op("dve", ts, out=swc[:], in0=swc[:], scalar1=float(1.0 - li), scalar2=None, op0=MUL, r=["swc"], w=["swc"])
            op("dve", tt, out=lamp[:, 0:64], in0=dl[:, 0:64], in1=dl[:, 64:128], op=MUL, r=["dl"], w=["lamp"])
            op("dve", tt, out=lamp[:, 64:128], in0=dl[:, 128:192], in1=dl[:, 192:256], op=MUL, r=["dl"], w=["lamp"])
            op("dve", nc.vector.reduce_sum, out=lamt[:, 0:2], in_=lamp[:].rearrange("p (a b) -> p a b", b=64), axis=AX.X,
               r=["lamp"], w=["lamt"])
            op("act", act, out=lamt[:, 2:4], in_=lamt[:, 0:2], func=AF.Exp, r=["lamt"], w=["lamt2"])
            op("dve", tt, out=neglam[:], in0=lamt[:, 3:4], in1=lamt[:, 2:3], op=SUB, r=["lamt2"], w=["neglam"])
            op("dve", ts, out=neglam[:], in0=neglam[:], scalar1=float(-li), scalar2=None, op0=ADD, r=["neglam"], w=["neglam"])
            op("act", act, out=Aneg[:], in_=hp[:, 32:64], func=AF.Exp, r=["hp"], w=["Aneg"])
            op("dve", ts, out=Aneg[:], in0=Aneg[:], scalar1=-1.0, scalar2=None, op0=MUL, r=["Aneg"], w=["Aneg"])

        PARAMS = ["nw", "qkw", "dl", "sw", "cw", "cb", "hp", "snw", "neglam", "Aneg"]

        def phase_A(l, hT, xsrc):
            with Scope() as sc:
                xt = sc.pool("xt", 2, [128, D], F32)
                hb = sc.pool("hb", 2, [128, D], BF16)
                junk = sc.sb("junk", [128, D], BF16)
                ssp = sc.pool("ss", 2, [128, 1], F32)
                rsp = sc.pool("rs", 2, [128, 1], F32)
                for t in range(32):
                    x_t, kx = xt.next()
                    op("sp", nc.sync.dma_start, out=x_t[:], in_=xsrc[t * 128:(t + 1) * 128, :],
                       r=[("xres", t // 4)], w=[kx], dma=kx)
                    s_t, ks = ssp.next()
                    op("act", act, out=junk[:], in_=x_t[:], func=AF.Square, accum_out=s_t[:], r=[kx], w=["junkA", ks])
                    r_t, kr = rsp.next()
                    op("act", act, out=r_t[:], in_=s_t[:], func=AF.Sqrt, bias=epst[:], scale=1.0 / D,
                       r=[ks, "eps"], w=[kr])
                    op("dve", nc.vector.reciprocal, out=r_t[:], in_=r_t[:], r=[kr], w=[kr])
                    h_t, kh = hb.next()
                    op("dve", stt, out=h_t[:], in0=x_t[:], scalar=r_t[:], in1=nw[:], op0=MUL, op1=MUL,
                       r=[kx, kr, "nw"], w=[kh])
                    b = t % 2
                    ptv = bf(b).rearrange("p (a b) -> p a b", b=128)
                    for kc in range(8):
                        op("pe", tr, out=ptv[:, kc, :], in_=h_t[:, kc * 128:(kc + 1) * 128], identity=ident_b,
                           r=[kh] + CST, w=[bk(b)])
                    if t % 2 == 0:
                        op("act", nc.scalar.copy, out=hT[:, :, t * 128:(t + 1) * 128], in_=ptv, r=[bk(b)], w=[("hT", t)])
                    else:
                        op("dve", nc.vector.tensor_copy, out=hT[:, :, t * 128:(t + 1) * 128], in_=ptv, r=[bk(b)],
                           w=[("hT", t)])

        def hTk(tb):
            return [("hT", 4 * tb + i) for i in range(4)]

        def phase_S1(l, hT):
            w_l = w_in_d[l].rearrange("(kc p) n -> p kc n", p=128)
            xtm_v = xtm_d.rearrange("(t p) c -> p t c", p=128)
            with Scope() as sc:
                wp = sc.pool("wcc", 2, [128, 8, 128], BF16)
                xcp = sc.pool("xc", 2, [128, S + 3], F32)
                accp = sc.pool("acc", 2, [128, 2048], F32)
                xop = sc.pool("xo", 2, [128, S], BF16)
                tmp_ = sc.pool("tmt", 3, [128, 8, 128], BF16)
                for t_, k_ in ((xcp.tiles[0], (xcp.name, 0)), (xcp.tiles[1], (xcp.name, 1))):
                    op("dve", nc.vector.memset, t_[:, 0:3], 0.0, w=[k_])
                nb = 0
                for cc in range(24):
                    w_t, kw_ = wp.next()
                    c0 = OFF_XBC + cc * 128
                    op("pool", nc.gpsimd.dma_start, out=w_t[:], in_=w_l[:, :, c0:c0 + 128], w=[kw_], dma=kw_)
                    xc, kxc = xcp.next()
                    for tb in range(8):
                        b = nb % 4
                        nb += 1
                        for kc in range(8):
                            op("pe", mm, banks[b][:], lhsT=w_t[:, kc, :], rhs=hT[:, kc, tb * 512:(tb + 1) * 512],
                               start=(kc == 0), stop=(kc == 7), r=[kw_] + hTk(tb), w=[bk(b)])
                        op("act", nc.scalar.copy, out=xc[:, 3 + tb * 512: 3 + (tb + 1) * 512], in_=banks[b][:],
                           r=[bk(b)], w=[kxc])
                    xo, kxo = xop.next()
                    for half in range(2):
                        a_t, ka = accp.next()
                        o0 = half * 2048
                        op("dve", ts, out=a_t[:], in0=xc[:, o0:o0 + 2048], scalar1=cw[:, cc * 4:cc * 4 + 1],
                           scalar2=cb[:, cc:cc + 1], op0=MUL, op1=ADD, r=[kxc, "cw", "cb"], w=[ka])
                        for k in range(1, 4):
                            op("dve", stt, out=a_t[:], in0=xc[:, o0 + k:o0 + k + 2048],
                               scalar=cw[:, cc * 4 + k:cc * 4 + k + 1], in1=a_t[:], op0=MUL, op1=ADD,
                               r=[kxc, "cw", ka], w=[ka])
                        op("act", act, out=xo[:, o0:o0 + 2048], in_=a_t[:], func=AF.Silu, r=[ka], w=[kxo])
                    if cc >= 16:
                        op("sp", nc.sync.dma_start, out=bct_d[cc - 16], in_=xo[:], r=[kxo], w=[("bct", cc - 16)], dma=kxo)
                    if cc < 20:
                        for q4 in range(4):
                            b = 4 + (q4 % 2)
                            ptv = bf(b).rearrange("p (a b) -> p a b", b=128)
                            for i in range(8):
                                t0 = (q4 * 8 + i) * 128
                                op("pe", tr, out=ptv[:, i, :], in_=xo[:, t0:t0 + 128], identity=ident_b,
                                   r=[kxo] + CST, w=[bk(b)])
                            tm, ktm = tmp_.next()
                            if q4 % 2 == 0:
                                op("act", nc.scalar.copy, out=tm[:], in_=ptv, r=[bk(b)], w=[ktm])
                            else:
                                op("dve", nc.vector.tensor_copy, out=tm[:], in_=ptv, r=[bk(b)], w=[ktm])
                            op("sp", nc.sync.dma_start, out=xtm_v[:, q4 * 8:(q4 + 1) * 8, cc * 128:(cc + 1) * 128],
                               in_=tm[:], r=[ktm], w=[("xtm", cc, q4)], dma=ktm)

        def phase_S2(l, hT):
            w_l = w_in_d[l].rearrange("(kc p) n -> p kc n", p=128)
            bct_v = bct_d.rearrange("g p t -> p g t")
            ysT_v = ysT_d.rearrange("k p t -> p k t")
            with Scope() as sc:
                wzs = sc.sb("wzs", [128, 8, 2048], BF16)
                wdt = sc.sb("wdt", [128, 8, 32], BF16)
                op("pool", nc.gpsimd.dma_start, out=wzs[:], in_=w_l[:, :, OFF_ZS:OFF_ZS + 2048], w=["wzs"], dma="wzs")
                op("pool", nc.gpsimd.dma_start, out=wdt[:], in_=w_l[:, :, OFF_DT:OFF_DT + 32], w=["wdt"], dma="wdt")
                Dd = sc.sb("Dd", [128, 32, 128], BF16)
                op("dve", tt, out=Dd[:], in0=ident_f.unsqueeze(1).broadcast_to([128, 32, 128]),
                   in1=hp[:, 64:96].unsqueeze(2).broadcast_to([128, 32, 128]), op=MUL, r=["hp"] + CST, w=["Dd"])
                st = sc.sb("st", [128, 2048], F32)
                stb = sc.sb("stb", [128, 2048], BF16)
                op("dve", nc.vector.memset, st[:], 0.0, w=["st"])
                op("dve", nc.vector.memset, stb[:], 0.0, w=[("stb", i) for i in range(4)])
                xtp = sc.pool("xtc", 2, [128, 2560], BF16)
                bcp = sc.pool("bcc", 2, [128, 8, 128], BF16)
                smp = sc.pool("sm", 2, [128, 8, 32], F32)
                xdtp = sc.pool("xdt", 2, [128, 32, 64], BF16)
                xdsp = sc.pool("xds", 2, [128, 32, 64], BF16)
                cbmp = sc.pool("cbm", 2, [128, 4, 128], BF16)
                ltp = sc.pool("lt", 2, [128, 4, 128], F32)
                dcp = sc.pool("dcT", 2, [128, 4, 128], BF16)
                t1p = sc.pool("t1", 1, [128, 512], F32)
                szp = sc.pool("sz", 1, [128, 512], F32)
                gnp = sc.pool("gn", 1, [128, 512], BF16)
                junk = sc.sb("junkS", [128, 512], BF16)
                sqp = sc.pool("ssq", 2, [128, 1], F32)
                rqp = sc.pool("rsq", 2, [128, 1], F32)
                ysp = sc.pool("ysc", 2, [128, 16, 128], BF16)
                mtap = sc.pool("MTall", 2, [128, 32, 128], BF16)
                nrb = [0]
                ctx = {}

                def stageA(c):
                    tok = slice(c * 128, (c + 1) * 128)
                    hk = [("hT", c)]
                    xt_c, kxt = xtp.next()
                    op("sp", nc.sync.dma_start, out=xt_c[:], in_=xtm_d[tok, :],
                       r=[("xtm", cc, c // 8) for cc in range(20)], w=[kxt], dma=kxt)
                    bc_c, kbc = bcp.next()
                    op("sp", nc.sync.dma_start, out=bc_c[:], in_=bct_v[:, :, tok],
                       r=[("bct", i) for i in range(8)], w=[kbc], dma=kbc)
                    sm, ksm = smp.next()
                    for kc in range(8):
                        op("pe", mm, banks[0][:, 0:32], lhsT=hT[:, kc, tok], rhs=wdt[:, kc, :], start=(kc == 0),
                           stop=(kc == 7), r=hk + ["wdt"], w=[bk(0)])
                    op("dve", tt, out=sm[:, 0, :], in0=banks[0][:, 0:32], in1=hp[:, 0:32], op=ADD, r=[bk(0), "hp"], w=[ksm])
                    op("act", act, out=sm[:, 1, :], in_=sm[:, 0, :], func=AF.Exp, r=[ksm], w=[ksm])
                    op("act", act, out=sm[:, 2, :], in_=sm[:, 1, :], func=AF.Ln, bias=1.0, r=[ksm], w=[ksm])
                    op("dve", tt, out=sm[:, 3, :], in0=sm[:, 2, :], in1=Aneg[:], op=MUL, r=[ksm, "Aneg"], w=[ksm])
                    op("pe", mm, banks[0][:, 32:64], lhsT=U_f, rhs=sm[:, 3, :], start=True, stop=True,
                       r=[ksm] + CST, w=[bk(0)])
                    op("pe", mm, banks[0][:, 64:96], lhsT=ones_f[:], rhs=sm[:, 3, :], start=True, stop=True,
                       r=[ksm, "ones_f"], w=[bk(0)])
                    op("dve", nc.vector.tensor_copy, out=sm[:, 4, :], in_=banks[0][:, 32:64], r=[bk(0)], w=[ksm])
                    op("act", act, out=sm[:, 5, :], in_=banks[0][:, 32:64], func=AF.Exp, r=[bk(0)], w=[ksm])
                    op("dve", tt, out=sm[:, 6, :], in0=banks[0][:, 64:96], in1=sm[:, 4, :], op=SUB, r=[bk(0), ksm], w=[ksm])
                    op("act", act, out=sm[:, 6, :], in_=sm[:, 6, :], func=AF.Exp, r=[ksm], w=[ksm])
                    op("act", act, out=sm[:, 7, :], in_=banks[0][:, 64:96], func=AF.Exp, r=[bk(0)], w=[ksm])
                    xdt, kxdt = xdtp.next()
                    op("dve", tt, out=xdt[:], in0=xt_c[:, 0:2048].rearrange("p (a b) -> p a b", b=64),
                       in1=sm[:, 2, :].unsqueeze(2).broadcast_to([128, 32, 64]), op=MUL, r=[kxt, ksm], w=[kxdt])
                    xds, kxds = xdsp.next()
                    op("pool", nc.gpsimd.tensor_tensor, out=xds[:], in0=xdt[:],
                       in1=sm[:, 6, :].unsqueeze(2).broadcast_to([128, 32, 64]), op=MUL, r=[kxdt, ksm], w=[kxds])
                    cbv = banks[1][:].rearrange("p (a b) -> p a b", b=128)
                    for gi in range(4):
                        op("pe", mm, cbv[:, gi, :], lhsT=bc_c[:, gi, :], rhs=bc_c[:, 4 + gi, :], start=True, stop=True,
                           r=[kbc], w=[bk(1)])
                    cbm, kcbm = cbmp.next()
                    op("dve", tt, out=cbm[:], in0=cbv, in1=maskU_b.unsqueeze(1).broadcast_to([128, 4, 128]), op=MUL,
                       r=[bk(1)] + CST, w=[kcbm])
                    mta, kmta = mtap.next()
                    for q8 in range(8):
                        gi = q8 // 2
                        h0 = q8 * 4
                        lt, klt = ltp.next()
                        op("pool", nc.gpsimd.tensor_tensor, out=lt[:],
                           in0=Ls_f.unsqueeze(1).broadcast_to([128, 4, 128]),
                           in1=sm[:, 3, h0:h0 + 4].unsqueeze(2).broadcast_to([128, 4, 128]), op=MUL,
                           r=[ksm] + CST, w=[klt])
                        rb = 2 + (nrb[0] % 2)
                        nrb[0] += 1
                        rbv = banks[rb][:].rearrange("p (a b) -> p a b", b=128)
                        for i in range(4):
                            op("pe", mm, rbv[:, i, :], lhsT=lt[:, i, :], rhs=U_f, start=True, stop=True,
                               r=[klt] + CST, w=[bk(rb)])
                        dc, kdc = dcp.next()
                        op("act", act, out=dc[:], in_=rbv, func=AF.Exp, r=[bk(rb)], w=[kdc])
                        op("dve", tt, out=mta[:, h0:h0 + 4, :], in0=dc[:], in1=cbm[:, gi:gi + 1, :].broadcast_to([128, 4, 128]),
                           op=MUL, r=[kdc, kcbm], w=[kmta + (q8,)])
                    ctx[c] = dict(tok=tok, hk=hk, xt_c=xt_c, kxt=kxt, bc_c=bc_c, kbc=kbc, sm=sm, ksm=ksm, xdt=xdt, kxdt=kxdt,
                                  xds=xds, kxds=kxds, mta=mta, kmta=kmta)

                def stageB(c):
                    d_ = ctx.pop(c)
                    tok, hk, xt_c, kxt, bc_c, kbc = d_["tok"], d_["hk"], d_["xt_c"], d_["kxt"], d_["bc_c"], d_["kbc"]
                    sm, ksm, xdt, kxdt, xds, kxds, mta, kmta = (d_["sm"], d_["ksm"], d_["xdt"], d_["kxdt"], d_["xds"],
                                                               d_["kxds"], d_["mta"], d_["kmta"])
                    ys_c, kys = ysp.next()
                    for gi in range(4):
                        for j in range(8):
                            hh = gi * 8 + j
                            op("pe", mm, banks[4][:, j * 64:(j + 1) * 64], lhsT=mta[:, hh, :], rhs=xdt[:, hh, :],
                               start=True, stop=False, r=[kmta + (hh // 4,), kxdt], w=[bk(4)])
                            op("pe", mm, banks[4][:, j * 64:(j + 1) * 64], lhsT=Dd[:, hh, :],
                               rhs=xt_c[:, hh * 64:(hh + 1) * 64], start=False, stop=True, r=["Dd", kxt], w=[bk(4)])
                        op("pe", mm, banks[5][:], lhsT=bc_c[:, 4 + gi, :], rhs=stb[:, gi * 512:(gi + 1) * 512],
                           start=True, stop=True, r=[kbc, ("stb", gi)], w=[bk(5)])
                        for kc in range(8):
                            op("pe", mm, banks[6][:], lhsT=hT[:, kc, tok], rhs=wzs[:, kc, gi * 512:(gi + 1) * 512],
                               start=(kc == 0), stop=(kc == 7), r=hk + ["wzs"], w=[bk(6)])
                        op("pe", mm, banks[7][:], lhsT=xt_c[:, 2048 + gi * 128:2048 + (gi + 1) * 128],
                           rhs=xds[:, gi * 8:(gi + 1) * 8, :], start=True, stop=True, r=[kxt, kxds], w=[bk(7)])
                        t1, kt1 = t1p.next()
                        op("dve", tt, out=t1[:].rearrange("p (a b) -> p a b", b=64),
                           in0=banks[5][:].rearrange("p (a b) -> p a b", b=64),
                           in1=sm[:, 5, gi * 8:(gi + 1) * 8].unsqueeze(2).broadcast_to([128, 8, 64]), op=MUL,
                           r=[bk(5), ksm], w=[kt1])
                        y, ky = t1, kt1
                        op("dve", tt, out=y[:], in0=banks[4][:], in1=t1[:], op=ADD, r=[bk(4), kt1], w=[ky])
                        sz, ksz = szp.next()
                        op("act", act, out=sz[:], in_=banks[6][:], func=AF.Exp, scale=-1.0, r=[bk(6)], w=[ksz])
                        op("act", act, out=sz[:], in_=sz[:], func=AF.Ln, bias=1.0, r=[ksz], w=[ksz])
                        op("act", act, out=sz[:], in_=sz[:], func=AF.Exp, scale=-1.0, r=[ksz], w=[ksz])
                        op("dve", tt, out=sz[:], in0=banks[6][:], in1=sz[:], op=MUL, r=[bk(6), ksz], w=[ksz])
                        gt, kgt = sz, ksz
                        op("pool", nc.gpsimd.tensor_tensor, out=gt[:], in0=y[:], in1=sz[:], op=MUL, r=[ky, ksz], w=[kgt])
                        sq, ksq = sqp.next()
                        op("act", act, out=junk[:], in_=gt[:], func=AF.Square, accum_out=sq[:], r=[kgt], w=["junkS", ksq])
                        rq, krq = rqp.next()
                        op("act", act, out=rq[:], in_=sq[:], func=AF.Ln, bias=epst[:], scale=1.0 / 512.0,
                           r=[ksq, "eps"], w=[krq])
                        op("act", act, out=rq[:], in_=rq[:], func=AF.Exp, scale=-0.5, r=[krq], w=[krq])
                        gn, kgn = gnp.next()
                        op("dve", stt, out=gn[:], in0=gt[:], scalar=rq[:], in1=snw[:, gi * 512:(gi + 1) * 512],
                           op0=MUL, op1=MUL, r=[kgt, krq, "snw"], w=[kgn])
                        ptv = bf(5).rearrange("p (a b) -> p a b", b=128)
                        for i in range(4):
                            op("pe", tr, out=ptv[:, i, :], in_=gn[:, i * 128:(i + 1) * 128], identity=ident_b,
                               r=[kgn] + CST, w=[bk(5)])
                        op("act", nc.scalar.copy, out=ys_c[:, gi * 4:(gi + 1) * 4, :], in_=ptv[:, 0:4, :], r=[bk(5)], w=[kys])
                        stv = st[:, gi * 512:(gi + 1) * 512]
                        op("pool", nc.gpsimd.tensor_tensor, out=stv.rearrange("p (a b) -> p a b", b=64),
                           in0=stv.rearrange("p (a b) -> p a b", b=64),
                           in1=sm[:, 7, gi * 8:(gi + 1) * 8].unsqueeze(2).broadcast_to([128, 8, 64]), op=MUL,
                           r=[("st", gi), ksm], w=[("st", gi)])
                        op("dve", tt, out=stv, in0=banks[7][:], in1=stv, op=ADD, r=[bk(7), ("st", gi)], w=[("st", gi)])
                        op("act", nc.scalar.copy, out=stb[:, gi * 512:(gi + 1) * 512], in_=stv, r=[("st", gi)],
                           w=[("stb", gi)])
                    op("sp", nc.sync.dma_start, out=ysT_v[:, :, tok], in_=ys_c[:], r=[kys], w=[("ysT", c // 4)], dma=kys)

                stageA(0)
                for c in range(32):
                    if c + 1 < 32:
                        stageA(c + 1)
                    stageB(c)

        def phase_G(l, hT):
            w_l = w_in_d[l].rearrange("(kc p) n -> p kc n", p=128)
            with Scope() as sc:
                wp = sc.pool("wg", 2, [128, 8, 128], BF16)
                sgp = sc.pool("sg", 2, [128, S], BF16)
                nb = 0
                for gc in range(16):
                    w_t, kw_ = wp.next()
                    c0 = OFF_G + gc * 128
                    op("pool", nc.gpsimd.dma_start, out=w_t[:], in_=w_l[:, :, c0:c0 + 128], w=[kw_], dma=kw_)
                    sg, ksg = sgp.next()
                    for tb in range(8):
                        b = nb % 4
                        nb += 1
                        for kc in range(8):
                            op("pe", mm, banks[b][:], lhsT=w_t[:, kc, :], rhs=hT[:, kc, tb * 512:(tb + 1) * 512],
                               start=(kc == 0), stop=(kc == 7), r=[kw_] + hTk(tb), w=[bk(b)])
                        op("act", act, out=sg[:, tb * 512:(tb + 1) * 512], in_=banks[b][:], func=AF.Sigmoid,
                           r=[bk(b)], w=[ksg])
                    op("sp", nc.sync.dma_start, out=sgT_d[gc], in_=sg[:], r=[ksg], w=[("sgT", gc)], dma=ksg)

        def phase_T(l, hT):
            w_l = w_in_d[l].rearrange("(kc p) n -> p kc n", p=128)
            LOOK = 2
            LSC = 2.0 ** -10
            CSC = EPS / (LSC * LSC)
            with Scope() as sc:
                whp = sc.pool("wh", 2, [128, 8, 4, 128], BF16)
                qz = [sc.sb("qz%d" % m, [128, S], BF16) for m in range(2)]
                kT = sc.sb("kT", [128, S], BF16)
                vt = sc.sb("vt", [128, 32, 128], BF16)
                szT = sc.sb("szT", [128, S], BF16)
                sqp = sc.pool("sq", 2, [128, 512], BF16)
                lnp = sc.pool("lnq", 2, [128, 512], F32)
                rsp = sc.pool("rst", 2, [128, 512], F32)
                ezp = sc.pool("ez", 2, [128, 512], F32)
                ptp = sc.pool("PT", 4, [128, 2, 512], BF16)
                s1pp = sc.pool("s1p", 2, [128, 512], F32)
                s0cp = sc.pool("s0c", 2, [128, 512], F32)
                s1cp = sc.pool("s1c", 2, [128, 512], F32)
                lb0p = sc.pool("lb0", 1, [128, 512], F32)
                lb1p = sc.pool("lb1", 1, [128, 512], F32)
                u0p = sc.pool("u0", 1, [128, 512], F32)
                u1p = sc.pool("u1", 1, [128, 512], F32)
                tqp = sc.pool("tq", 1, [128, 512], F32)
                sqo = sc.pool("sqo", 1, [128, 512], BF16)
                agp = sc.pool("arg", 1, [128, 512], F32)
                ybp = sc.pool("yb", 2, [128, 512], BF16)
                op("dve", nc.vector.memset, qz[0][64:128, :], 0.0, w=["qz0"])
                op("dve", nc.vector.memset, qz[1][0:64, :], 0.0, w=["qz1"])
                nS = [0]
                nI = [0]
                pairs = ((2, 3), (4, 5))
                for h in range(8):
                    wh, kwh = whp.next()
                    kws = []
                    for i, off in enumerate((OFF_Q, OFF_K, OFF_V, OFF_ZA)):
                        c0 = off + h * 128
                        kwi = kwh + (i,)
                        kws.append(kwi)
                        op("pool", nc.gpsimd.dma_start, out=wh[:, :, i, :], in_=w_l[:, :, c0:c0 + 128], w=[kwi], dma=kwi)
                    for which in (0, 1):
                        for tb in range(8):
                            ba = 2 + 2 * (nI[0] % 2)
                            bs = ba + 1
                            nI[0] += 1
                            cs = slice(tb * 512, (tb + 1) * 512)
                            for kc in range(8):
                                op("pe", mm, banks[ba][:], lhsT=wh[:, kc, which, :], rhs=hT[:, kc, cs],
                                   start=(kc == 0), stop=(kc == 7), r=[kws[which]] + hTk(tb), w=[bk(ba)])
                            sq, ksq = sqp.next()
                            op("act", act, out=sq[:], in_=banks[ba][:], func=AF.Square, r=[bk(ba)], w=[ksq])
                            op("pe", mm, banks[bs][:], lhsT=BD_b, rhs=sq[:], start=True, stop=True, r=[ksq] + CST, w=[bk(bs)])
                            ln, kln = lnp.next()
                            op("act", act, out=ln[:], in_=banks[bs][:], func=AF.Ln, bias=epst[:], scale=1.0 / 64.0,
                               r=[bk(bs), "eps"], w=[kln])
                            rs, krs = rsp.next()
                            op("act", act, out=rs[:], in_=ln[:], func=AF.Exp, scale=-0.5, r=[kln], w=[krs])
                            if which == 0:
                                for m in range(2):
                                    pr = slice(m * 64, (m + 1) * 64)
                                    op("dve", stt, out=qz[m][pr, cs], in0=banks[ba][pr, :], scalar=qkw[pr, 0:1], in1=rs[pr, :],
                                       op0=MUL, op1=MUL, r=[bk(ba), krs, "qkw"], w=["qz%d" % m])
                            else:
                                op("dve", stt, out=kT[:, cs], in0=banks[ba][:], scalar=qkw[:, 1:2], in1=rs[:],
                                   op0=MUL, op1=MUL, r=[bk(ba), krs, "qkw"], w=["kT"])
                    for tb in range(8):
                        ba = 2 + (nI[0] % 4)
                        nI[0] += 1
                        cs = slice(tb * 512, (tb + 1) * 512)
                        for kc in range(8):
                            op("pe", mm, banks[ba][:], lhsT=wh[:, kc, 3, :], rhs=hT[:, kc, cs],
                               start=(kc == 0), stop=(kc == 7), r=[kws[3]] + hTk(tb), w=[bk(ba)])
                        ez, kez = ezp.next()
                        op("act", act, out=ez[:], in_=banks[ba][:], func=AF.Exp, scale=-1.0, r=[bk(ba)], w=[kez])
                        op("act", act, out=ez[:], in_=ez[:], func=AF.Ln, bias=1.0, r=[kez], w=[kez])
                        op("act", act, out=ez[:], in_=ez[:], func=AF.Exp, scale=-1.0, r=[kez], w=[kez])
                        op("dve", stt, out=szT[:, cs], in0=banks[ba][:], scalar=swc[:, 0:1], in1=ez[:], op0=MUL, op1=MUL,
                           r=[bk(ba), kez, "swc"], w=["szT"])
                    for t4 in range(8):
                        b = 2 + (nI[0] % 4)
                        nI[0] += 1
                        pv = banks[b][:].rearrange("p (a b) -> p a b", b=128)
                        for i in range(4):
                            t = t4 * 4 + i
                            for kc in range(8):
                                op("pe", mm, pv[:, i, :], lhsT=hT[:, kc, t * 128:(t + 1) * 128], rhs=wh[:, kc, 2, :],
                                   start=(kc == 0), stop=(kc == 7), r=[kws[2], ("hT", t)], w=[bk(b)])
                        op("dve", nc.vector.tensor_copy, out=vt[:, t4 * 4:(t4 + 1) * 4, :], in_=pv, r=[bk(b)], w=["vt"])
                    steps = [(qb, t) for qb in range(8) for t in range(4 * qb + 4)]
                    pts = {}
                    s1ps = {}

                    def emit_qk(j):
                        qb, t = steps[j]
                        off = max(0, t - 4 * qb) * 128
                        pi = nS[0] % 2
                        nS[0] += 1
                        pb = pairs[pi]
                        diag = t >= 4 * qb
                        for m in range(2):
                            op("pe", mm, banks[pb[m]][:, 0:512 - off], lhsT=kT[:, t * 128:(t + 1) * 128],
                               rhs=qz[m][:, qb * 512 + off:(qb + 1) * 512], start=True, stop=not diag,
                               r=["kT", "qz%d" % m], w=[bk(pb[m])])
                            if diag:
                                op("pe", mm, banks[pb[m]][:, 0:128], lhsT=ident_b, rhs=Mneg_b, start=False, stop=True,
                                   r=CST, w=[bk(pb[m])])
                        pt, kpt = ptp.next()
                        pview = pbig[1 + pi][:].rearrange("p (a b) -> p a b", b=512)
                        op("act", act, out=pt[:, :, 0:512 - off], in_=pview[:, :, 0:512 - off], func=AF.Exp,
                           r=[bk(pb[0]), bk(pb[1])], w=[kpt])
                        pts[j] = (pt, kpt)

                    def emit_pv(i):
                        qb, t = steps[i]
                        off = max(0, t - 4 * qb) * 128
                        pt, kpt = pts.pop(i)
                        for m in range(2):
                            op("pe", mm, banks[m][:, off:512], lhsT=vt[:, t, :], rhs=pt[:, m, 0:512 - off],
                               start=(t == 0), stop=(t == 4 * qb + 3), r=[kpt, "vt"], w=[bk(m)])
                        if t == 0:
                            op("dve", nc.vector.tensor_copy, out=banks[6][:], in_=pt[:, 0, :], r=[kpt], w=[bk(6)])
                            op("dve", nc.vector.tensor_copy, out=banks[7][:], in_=pt[:, 1, :], r=[kpt], w=[bk(7)])
                            s1ps[qb] = s1pp.next()
                            op("pool", nc.gpsimd.memset, s1ps[qb][0][:], 0.0, w=[s1ps[qb][1]])
                        else:
                            op("dve", tt, out=banks[6][:, off:512], in0=banks[6][:, off:512], in1=pt[:, 0, 0:512 - off],
                               op=ADD, r=[kpt, bk(6)], w=[bk(6)])
                            if t % 3 == 0:
                                op("dve", tt, out=banks[7][:, off:512], in0=banks[7][:, off:512], in1=pt[:, 1, 0:512 - off],
                                   op=ADD, r=[kpt, bk(7)], w=[bk(7)])
                            else:
                                sp_, ksp = s1ps[qb]
                                op("pool", nc.gpsimd.tensor_tensor, out=sp_[:, off:512], in0=sp_[:, off:512],
                                   in1=pt[:, 1, 0:512 - off], op=ADD, r=[kpt, ksp], w=[ksp])

                    def emit_fin(qb):
                        cs = slice(qb * 512, (qb + 1) * 512)
                        s1p_, ks1p = s1ps.pop(qb)
                        s0c, ks0c = s0cp.next()
                        s1c, ks1c = s1cp.next()
                        op("dve", nc.vector.tensor_copy, out=s0c[:], in_=banks[6][:], r=[bk(6)], w=[ks0c])
                        op("dve", nc.vector.tensor_copy, out=s1c[:], in_=banks[7][:], r=[bk(7)], w=[ks1c])
                        pi = nS[0] % 2
                        nS[0] += 1
                        bx, by = pairs[pi]
                        op("pe", mm, banks[bx][:], lhsT=ones_f[:], rhs=s0c[:], start=True, stop=True, r=[ks0c, "ones_f"], w=[bk(bx)])
                        op("pe", mm, banks[by][:], lhsT=ones_f[:], rhs=s1c[:], start=True, stop=False, r=[ks1c, "ones_f"], w=[bk(by)])
                        op("pe", mm, banks[by][:], lhsT=ones_f[:], rhs=s1p_[:], start=False, stop=True, r=[ks1p, "ones_f"], w=[bk(by)])
                        lb0, kl0 = lb0p.next()
                        lb1, kl1 = lb1p.next()
                        op("act", act, out=lb0[:], in_=banks[bx][:], func=AF.Copy, scale=LSC, r=[bk(bx)], w=[kl0])
                        op("act", act, out=lb1[:], in_=banks[by][:], func=AF.Copy, scale=LSC, r=[bk(by)], w=[kl1])
                        u0, ku0 = u0p.next()
                        u1, ku1 = u1p.next()
                        op("dve", tt, out=u1[:], in0=banks[1][:], in1=lb0[:], op=MUL, r=[bk(1), kl0], w=[ku1])
                        op("dve", tt, out=u0[:], in0=banks[0][:], in1=lb1[:], op=MUL, r=[bk(0), kl1], w=[ku0])
                        op("dve", stt, out=u0[:], in0=u1[:], scalar=neglam[:, 0:1], in1=u0[:], op0=MUL, op1=ADD,
                           r=[ku0, ku1, "neglam"], w=[ku0])
                        tq, ktq = tqp.next()
                        op("pool", nc.gpsimd.tensor_tensor, out=tq[:], in0=lb0[:], in1=lb1[:], op=MUL, r=[kl0, kl1], w=[ktq])
                        op("pool", nc.gpsimd.tensor_tensor, out=tq[:], in0=tq[:], in1=tq[:], op=MUL, r=[ktq], w=[ktq])
                        sq, ksq = sqo.next()
                        op("pool", nc.gpsimd.tensor_tensor, out=sq[:], in0=u0[:], in1=u0[:], op=MUL, r=[ku0], w=[ksq])
                        op("pe", mm, banks[bx][:], lhsT=ones_b[:], rhs=sq[:], start=True, stop=True, r=[ksq, "ones_b"], w=[bk(bx)])
                        ag, kag = agp.next()
                        op("dve", stt, out=ag[:], in0=banks[bx][:], scalar=float(1.0 / (128.0 * CSC)), in1=tq[:],
                           op0=MUL, op1=ADD, r=[bk(bx), ktq], w=[kag])
                        op("act", act, out=ag[:], in_=ag[:], func=AF.Ln, r=[kag], w=[kag])
                        op("act", act, out=ag[:], in_=ag[:], func=AF.Exp, scale=-0.5, r=[kag], w=[kag])
                        op("dve", stt, out=u0[:], in0=u0[:], scalar=float(CSC ** -0.5), in1=ag[:], op0=MUL, op1=MUL,
                           r=[ku0, kag], w=[ku0])
                        yb, kyb = ybp.next()
                        op("pool", nc.gpsimd.tensor_tensor, out=yb[:], in0=u0[:], in1=szT[:, cs], op=MUL,
                           r=[ku0, "szT"], w=[kyb])
                        op("sp", nc.sync.dma_start, out=yaT_d[h][:, cs], in_=yb[:], r=[kyb], w=[("yaT", h, qb)], dma=kyb)

                    n = len(steps)
                    for i in range(-LOOK, n):
                        j = i + LOOK
                        if j < n:
                            emit_qk(j)
                        if i >= 0:
                            emit_pv(i)
                            qb, t = steps[i]
                            if t == 4 * qb + 3:
                                emit_fin(qb)

        def phase_D(l, xsrc):
            wpa_v = w_pa_d[l].rearrange("(kc p) n -> p kc n", p=128)
            wps_v = w_ps_d[l].rearrange("(kc p) n -> p kc n", p=128)
            wo_v = w_out_d[l].rearrange("(kc p) n -> p kc n", p=128)
            yaT_v = yaT_d.rearrange("h p t -> p h t")
            ysT_v = ysT_d.rearrange("k p t -> p k t")
            sgT_v = sgT_d.rearrange("k p t -> p k t")
            with Scope() as sc:
                wpa = sc.sb("wpa", [128, 8, D], BF16)
                wps = sc.sb("wps", [128, 16, D], BF16)
                wo = sc.sb("wo", [128, 8, D], BF16)
                op("pool", nc.gpsimd.dma_start, out=wpa[:], in_=wpa_v, w=["wpa"], dma="wpa")
                op("pool", nc.gpsimd.dma_start, out=wps[:, 0:8], in_=wps_v[:, 0:8], w=["wps"], dma="wps")
                op("pool", nc.gpsimd.dma_start, out=wps[:, 8:16], in_=wps_v[:, 8:16], w=["wps"], dma="wps")
                op("pool", nc.gpsimd.dma_start, out=wo[:], in_=wo_v, w=["wo"], dma="wo")
                yap = sc.pool("yab", 2, [128, 8, 512], BF16)
                ysp = sc.pool("ysb", 2, [128, 16, 512], BF16)
                sgp = sc.pool("sgb", 1, [128, 16, 512], BF16)
                xrp = sc.pool("xr", 1, [128, 4, D], F32)
                m1p = sc.pool("m1", 2, [128, 512], F32)
                m2p = sc.pool("m2", 2, [128, 512], F32)
                mTp = sc.pool("mT", 2, [128, 8, 512], BF16)
                xop = sc.pool("xo", 1, [128, 4, D], F32)
                nb = 0
                for tb in range(8):
                    tok = slice(tb * 512, (tb + 1) * 512)
                    ya, kya = yap.next()
                    ys, kys = ysp.next()
                    sg, ksg = sgp.next()
                    xr, kxr = xrp.next()
                    op("sp", nc.sync.dma_start, out=ya[:], in_=yaT_v[:, :, tok], r=[("yaT", h, tb) for h in range(8)],
                       w=[kya], dma=kya)
                    op("sp", nc.sync.dma_start, out=ys[:], in_=ysT_v[:, :, tok], r=[("ysT", tb)], w=[kys], dma=kys)
                    op("sp", nc.sync.dma_start, out=sg[:], in_=sgT_v[:, :, tok], r=[("sgT", i) for i in range(16)],
                       w=[ksg], dma=ksg)
                    op("sp", nc.sync.dma_start, out=xr[:], in_=xsrc[tok, :].rearrange("(t p) c -> p t c", p=128),
                       r=[("xres", tb)], w=[kxr], dma=kxr)
                    mT, kmT = mTp.next()
                    for cc in range(8):
                        b1 = nb % 6
                        b2 = (nb + 1) % 6
                        nb += 2
                        for kc in range(8):
                            op("pe", mm, banks[b1][:], lhsT=wpa[:, kc, cc * 128:(cc + 1) * 128], rhs=ya[:, kc, :],
                               start=(kc == 0), stop=(kc == 7), r=["wpa", kya], w=[bk(b1)])
                        for kc in range(16):
                            op("pe", mm, banks[b2][:], lhsT=wps[:, kc, cc * 128:(cc + 1) * 128], rhs=ys[:, kc, :],
                               start=(kc == 0), stop=(kc == 15), r=["wps", kys], w=[bk(b2)])
                        m1, km1 = m1p.next()
                        op("dve", tt, out=m1[:], in0=banks[b1][:], in1=sg[:, cc, :], op=MUL, r=[bk(b1), ksg], w=[km1])
                        m2, km2 = m2p.next()
                        op("dve", tt, out=m2[:], in0=banks[b2][:], in1=sg[:, 8 + cc, :], op=MUL, r=[bk(b2), ksg], w=[km2])
                        op("pool", nc.gpsimd.tensor_tensor, out=mT[:, cc, :], in0=m1[:], in1=m2[:], op=ADD,
                           r=[km1, km2], w=[kmT])
                    xo, kxo = xop.next()
                    for t4 in range(4):
                        for half in range(2):
                            b = 6 + (t4 * 2 + half) % 2
                            for kc in range(8):
                                op("pe", mm, banks[b][:], lhsT=mT[:, kc, t4 * 128:(t4 + 1) * 128],
                                   rhs=wo[:, kc, half * 512:(half + 1) * 512], start=(kc == 0), stop=(kc == 7),
                                   r=[kmT, "wo"], w=[bk(b)])
                            op("dve", tt, out=xo[:, t4, half * 512:(half + 1) * 512], in0=banks[b][:],
                               in1=xr[:, t4, half * 512:(half + 1) * 512], op=ADD, r=[bk(b), kxr], w=[kxo])
                    op("sp", nc.sync.dma_start, out=out_d[tok, :].rearrange("(t p) c -> p t c", p=128), in_=xo[:],
                       r=[kxo], w=[("xres", tb)], dma=kxo)

        for l in range(depth):
            xsrc = x_d if l == 0 else out_d
            barrier()
            load_params(l)
            with Scope() as lsc:
                hT = lsc.sb("hT", [128, 8, S], BF16)
                if "A" in phases:
                    phase_A(l, hT, xsrc)
                    barrier()
                if "S" in phases:
                    phase_S1(l, hT)
                    barrier()
                    phase_S2(l, hT)
                    barrier()
                if "G" in phases:
                    phase_G(l, hT)
                    barrier()
                if "T" in phases:
                    phase_T(l, hT)
                    barrier()
            if "D" in phases:
                phase_D(l, xsrc)
        P.finish(sems)
    return nc, P.stats


def make_consts():
    i = np.arange(128)
    ident = np.eye(128, dtype=np.float32)
    maskU = (i[None, :] >= i[:, None]).astype(np.float32)
    Lstrict = (i[:, None] > i[None, :]).astype(np.float32)
    bd = ((i[:, None] // 64) == (i[None, :] // 64)).astype(np.float32)
    mneg = np.where(i[None, :] >= i[:, None], 0.0, -30000.0).astype(np.float32)
    return np.concatenate([ident, maskU, maskU, Lstrict, bd, mneg], axis=1).astype(np.float32)


def host_layout(inputs):
    inputs = {k: (np.asarray(v)[:DEPTH] if k != "x" else v) for k, v in inputs.items()}
    f = lambda a: np.ascontiguousarray(np.asarray(a, dtype=np.float32))
    rep = lambda a: np.ascontiguousarray(np.broadcast_to(np.asarray(a, np.float32)[:, None, :], (a.shape[0], 128, a.shape[1])))
    qn, kn = np.asarray(inputs["q_norm_w"], np.float32), np.asarray(inputs["k_norm_w"], np.float32)
    qkw = np.stack([np.tile(qn, (1, 2)), np.tile(kn, (1, 2))], axis=-1)
    cw = np.asarray(inputs["conv_w"], np.float32)
    convw = cw.transpose(0, 2, 1).reshape(DEPTH, 24, 128, 4).transpose(0, 2, 1, 3).reshape(DEPTH, 128, 96)
    convb = np.asarray(inputs["conv_b"], np.float32).reshape(DEPTH, 24, 128).transpose(0, 2, 1)
    hp = np.concatenate([inputs["dt_bias"], inputs["a_log"], inputs["d_skip"]], axis=-1).astype(np.float32)
    common = {
        "w_in": f(inputs["w_in"]), "w_pa": f(inputs["w_proj_attn"]), "w_ps": f(inputs["w_proj_ssd"]),
        "w_out": f(inputs["w_out"]),
        "nw_rep": rep(inputs["norm_w"]), "qkw": f(qkw),
        "dl_rep": rep(np.asarray(inputs["diff_lambda"], np.float32).reshape(DEPTH, 256)),
        "sw_rep": rep(inputs["subln_w"]), "swc": f(np.asarray(inputs["subln_w"], np.float32)[:, :, None]), "convw": f(convw), "convb": f(convb),
        "hp_rep": rep(hp), "snw_rep": rep(inputs["ssd_norm_w"]), "consts": make_consts(),
    }
    return common


_NC_CACHE = {}


def kernel(**inputs):
    common = host_layout(inputs)
    x = np.asarray(inputs["x"], np.float32)
    n = x.shape[0]
    if "nc" not in _NC_CACHE:
        _NC_CACHE["nc"] = build()[0]
    nc = _NC_CACHE["nc"]
    in_maps = [dict(common, x=np.ascontiguousarray(x[b])) for b in range(n)]
    res = run_bass_kernel_spmd(nc, in_maps, core_ids=list(range(n)))
    return np.stack([np.asarray(r["out"], np.float32) for r in res.results], axis=0)
```

```python
import contextlib
import math
import numpy as np
import concourse.bass as bass
import concourse.mybir as mybir
from concourse.bass_utils import run_bass_kernel_spmd
from concourse.alu_op_type import AluOpType as ALU

AF = mybir.ActivationFunctionType
F32 = mybir.dt.float32
BF16 = mybir.dt.bfloat16
AX = mybir.AxisListType

S = 4096
D = 1024
DIN = 11296
OFF_Q, OFF_K, OFF_V, OFF_ZA, OFF_XBC, OFF_ZS, OFF_DT, OFF_G = 0, 1024, 2048, 3072, 4096, 7168, 9216, 9248
EPS = 1e-6
import os as _os
DEPTH = int(_os.environ.get('KDEPTH', '4'))


class Op:
    __slots__ = ("eng", "fn", "args", "kw", "reads", "writes", "dma", "idx",
                 "deps", "signal", "token", "waits")

    def __init__(self, eng, fn, args, kw, reads, writes, dma):
        self.eng = eng
        self.fn = fn
        self.args = args
        self.kw = kw
        self.reads = reads
        self.writes = writes
        self.dma = dma
        self.deps = set()
        self.signal = False
        self.token = None
        self.waits = []


class Prog:
    def __init__(self, nc):
        self.nc = nc
        self.ops = []
        self.q = {"pe": nc.tensor, "act": nc.scalar, "dve": nc.vector,
                  "pool": nc.gpsimd, "sp": nc.sync}

    def add(self, eng, fn, *args, reads=(), writes=(), dma=None, **kw):
        op = Op(eng, fn, args, kw, tuple(reads), tuple(writes), dma)
        op.idx = len(self.ops)
        self.ops.append(op)
        return op

    def finish(self, sems, final_eng="sp"):
        ops = self.ops
        for i_, op_ in enumerate(ops):
            op_.idx = i_
        last_w = {}
        readers = {}
        sem_waiters = {}
        dma_cum = {}
        for op in ops:
            deps = set()
            raw = set()
            for r in op.reads:
                w = last_w.get(r)
                if w is not None:
                    deps.add(w)
                    raw.add(w)
            for wkey in op.writes:
                w = last_w.get(wkey)
                if w is not None:
                    deps.add(w)
                for ridx in readers.get(wkey, {}).values():
                    deps.add(ridx)
            deps.discard(op.idx)
            keep = set()
            for d in deps:
                dop = ops[d]
                if dop.dma is None and op.dma is None and dop.eng == op.eng:
                    if op.eng == "pe" or d not in raw:
                        continue
                keep.add(d)
            if op.dma is not None:
                for e, widx in sem_waiters.get(op.dma, {}).items():
                    if e != op.eng and widx != op.idx:
                        keep.add(widx)
            for d in keep:
                if ops[d].dma is not None:
                    sem_waiters.setdefault(ops[d].dma, {})[op.eng] = op.idx
            op.deps = keep
            for r in op.reads:
                rd = readers.setdefault(r, {})
                if op.dma is not None:
                    rd[("dma", op.dma)] = op.idx
                else:
                    rd[op.eng] = op.idx
            for wkey in op.writes:
                last_w[wkey] = op.idx
                readers[wkey] = {}
        eng_cnt = {}
        for op in ops:
            for d in op.deps:
                if ops[d].dma is None:
                    ops[d].signal = True
        known = {}
        for op in ops:
            kn = known.setdefault(op.eng, {})
            waits = {}
            for d in sorted(op.deps):
                dop = ops[d]
                if dop.dma is not None:
                    s = ("dma", dop.dma)
                    v = dma_cum[dop.dma]
                else:
                    s = ("eng", dop.eng)
                    v = dop.token[1]
                if kn.get(s, 0) >= v:
                    continue
                waits[s] = max(waits.get(s, 0), v)
            for s, v in waits.items():
                kn[s] = v
            op.waits = list(waits.items())
            if op.dma is not None:
                dma_cum[op.dma] = dma_cum.get(op.dma, 0) + 16
                op.token = (("dma", op.dma), dma_cum[op.dma])
            else:
                if op.signal:
                    eng_cnt[op.eng] = eng_cnt.get(op.eng, 0) + 1
                    op.token = (("eng", op.eng), eng_cnt[op.eng])
                else:
                    op.token = (("eng", op.eng), eng_cnt.get(op.eng, 0) + 1)
        n_wait = 0
        for op in ops:
            q = self.q[op.eng]
            for s, v in op.waits:
                q.wait_ge(sems(s), v)
                n_wait += 1
            ins = op.fn(*op.args, **op.kw)
            if op.dma is not None:
                ins.then_inc(sems(("dma", op.dma)), 16)
            elif op.signal:
                ins.then_inc(sems(("eng", op.eng)), 1)
        q = self.q[final_eng]
        for k, v in dma_cum.items():
            q.wait_ge(sems(("dma", k)), v)
        for e, v in eng_cnt.items():
            if e != final_eng:
                q.wait_ge(sems(("eng", e)), v)
        self.stats = dict(n_ops=len(ops), n_wait=n_wait, eng_cnt=dict(eng_cnt),
                          n_dma_sems=len(dma_cum))


def lambda_init_fn(layer_idx):
    return 0.8 - 0.6 * math.exp(-0.3 * layer_idx)


def build(depth=DEPTH, dbg=False, phases="ASGTD"):
    nc = bass.Bass("TRN2", target_bir_lowering=False)
    P = Prog(nc)
    es = contextlib.ExitStack()
    uid = [0]

    def din(name, shape, dt=F32):
        return nc.dram_tensor(name, list(shape), dt, kind="ExternalInput").ap()

    def dscr(name, shape, dt):
        return nc.dram_tensor(name, list(shape), dt, kind="ExternalOutput" if dbg else "Internal").ap()

    x_d = din("x", [S, D])
    w_in_d = din("w_in", [DEPTH, D, DIN])
    w_pa_d = din("w_pa", [DEPTH, D, D])
    w_ps_d = din("w_ps", [DEPTH, 2 * D, D])
    w_out_d = din("w_out", [DEPTH, D, D])
    nw_d = din("nw_rep", [DEPTH, 128, D])
    qkw_d = din("qkw", [DEPTH, 128, 2])
    dl_d = din("dl_rep", [DEPTH, 128, 256])
    sw_d = din("sw_rep", [DEPTH, 128, 128])
    swc_d = din("swc", [DEPTH, 128, 1])
    cw_d = din("convw", [DEPTH, 128, 24 * 4])
    cb_d = din("convb", [DEPTH, 128, 24])
    hp_d = din("hp_rep", [DEPTH, 128, 96])
    snw_d = din("snw_rep", [DEPTH, 128, 2048])
    cst_d = din("consts", [128, 6 * 128])
    out_d = nc.dram_tensor("out", [S, D], F32, kind="ExternalOutput").ap()
    yaT_d = dscr("yaT", [8, 128, S], BF16)
    ysT_d = dscr("ysT", [16, 128, S], BF16)
    sgT_d = dscr("sgT", [16, 128, S], BF16)
    xtm_d = dscr("xtm", [S, 2560], BF16)
    bct_d = dscr("bct", [8, 128, S], BF16)

    sem_cache = {}

    def sems(key):
        if key not in sem_cache:
            sem_cache[key] = es.enter_context(nc.semaphore("s%d" % len(sem_cache)))
        return sem_cache[key]

    class Scope:
        def __init__(self):
            self.es = contextlib.ExitStack()

        def __enter__(self):
            self.es.__enter__()
            return self

        def __exit__(self, *a):
            return self.es.__exit__(*a)

        def sb(self, name, shape, dt):
            uid[0] += 1
            return self.es.enter_context(nc.sbuf_tensor("%s_%d" % (name, uid[0]), list(shape), dt))

        def pool(self, name, n, shape, dt):
            return RPool(self, name, n, shape, dt)

    class RPool:
        def __init__(self, sc, name, n, shape, dt):
            self.name = name
            self.tiles = [sc.sb("%s%d" % (name, i), shape, dt) for i in range(n)]
            self.i = 0

        def next(self):
            j = self.i % len(self.tiles)
            self.i += 1
            return self.tiles[j], (self.name, j)

    def op(eng, fn, *a, r=(), w=(), dma=None, **kw):
        lk = tuple(("lock", k[1]) for k in tuple(r) + tuple(w) if isinstance(k, tuple) and k[0] == "bank")
        return P.add(eng, fn, *a, reads=tuple(r) + ("PH",), writes=tuple(w) + lk, dma=dma, **kw)

    mm = nc.tensor.matmul
    tr = nc.tensor.transpose
    act = nc.scalar.activation
    tt = nc.vector.tensor_tensor
    ts = nc.vector.tensor_scalar
    stt = nc.vector.scalar_tensor_tensor
    MUL, ADD, SUB = ALU.mult, ALU.add, ALU.subtract

    with es:
        g = Scope()
        es.enter_context(g)
        pbig = [es.enter_context(nc.psum_tensor("pbig%d" % i, [128, 1024], F32)) for i in range(4)]
        banks = [pbig[i // 2][:, (i % 2) * 512:(i % 2 + 1) * 512] for i in range(8)]

        def bk(i):
            return ("bank", i)

        def bf(i):
            return banks[i][:].bitcast(BF16)

        cst_f = g.sb("cst_f", [128, 6 * 128], F32)
        cst_b = g.sb("cst_b", [128, 6 * 128], BF16)
        op("sp", nc.sync.dma_start, out=cst_f[:], in_=cst_d, w=["cst_f"], dma="cst_f")
        op("pool", nc.gpsimd.dma_start, out=cst_b[:], in_=cst_d, w=["cst_b"], dma="cst_b")
        ident_f = cst_f[:, 0:128]
        U_f = cst_f[:, 256:384]
        Ls_f = cst_f[:, 384:512]
        ident_b = cst_b[:, 0:128]
        maskU_b = cst_b[:, 128:256]
        BD_b = cst_b[:, 512:640]
        Mneg_b = cst_b[:, 640:768]
        CST = ["cst_f", "cst_b"]
        epst = g.sb("eps", [128, 1], F32)
        op("dve", nc.vector.memset, epst[:], EPS, w=["eps"])
        ones_f = g.sb("ones_f", [128, 128], F32)
        op("dve", nc.vector.memset, ones_f[:], 1.0, w=["ones_f"])
        ones_b = g.sb("ones_b", [128, 128], BF16)
        op("dve", nc.vector.memset, ones_b[:], 1.0, w=["ones_b"])
        swc = g.sb("swc", [128, 1], F32)
        bar_t = g.sb("bar_t", [128, 1], F32)
        qkw = g.sb("qkw", [128, 2], F32)
        dl = g.sb("dl", [128, 256], F32)
        sw = g.sb("sw", [128, 128], F32)
        cw = g.sb("cw", [128, 96], F32)
        cb = g.sb("cb", [128, 24], F32)
        hp = g.sb("hp", [128, 96], F32)
        snw = g.sb("snw", [128, 2048], BF16)
        neglam = g.sb("neglam", [128, 1], F32)
        Aneg = g.sb("Aneg", [128, 32], F32)
        lamt = g.sb("lamt", [128, 4], F32)
        lamp = g.sb("lamp", [128, 128], F32)

        def barrier():
            op("dve", nc.vector.memset, bar_t[:], 0.0, w=["PH"])

        def load_params(l):
            li = lambda_init_fn(l)
            for t, src, key in ((qkw, qkw_d, "qkw"), (dl, dl_d, "dl"), (sw, sw_d, "sw"),
                                (cw, cw_d, "cw"), (cb, cb_d, "cb"), (hp, hp_d, "hp")):
                op("sp", nc.sync.dma_start, out=t[:], in_=src[l], w=[key], dma=key)
            op("pool", nc.gpsimd.dma_start, out=snw[:], in_=snw_d[l], w=["snw"], dma="snw")
            op("dve", ts, out=qkw[:, 0:1], in0=qkw[:, 0:1], scalar1=0.125, scalar2=None, op0=MUL, r=["qkw"], w=["qkw"])
            op("dve", ts, out=sw[:], in0=sw[:], scalar1=float(1.0 - li), scalar2=None, op0=MUL, r=["sw"], w=["sw"])
            op("sp", nc.sync.dma_start, out=swc[:], in_=swc_d[l], w=["swc"], dma="swc")
            op("dve", ts, out=swc[:], in0=swc[:], scalar1=float(1.0 - li), scalar2=None, op0=MUL, r=["swc"], w=["swc"])
            op("dve", tt, out=lamp[:, 0:64], in0=dl[:, 0:64], in1=dl[:, 64:128], op=MUL, r=["dl"], w=["lamp"])
            op("dve", tt, out=lamp[:, 64:128], in0=dl[:, 128:192], in1=dl[:, 192:256], op=MUL, r=["dl"], w=["lamp"])
            op("dve", nc.vector.reduce_sum, out=lamt[:, 0:2], in_=lamp[:].rearrange("p (a b) -> p a b", b=64), axis=AX.X,
               r=["lamp"], w=["lamt"])
            op("act", act, out=lamt[:, 2:4], in_=lamt[:, 0:2], func=AF.Exp, r=["lamt"], w=["lamt2"])
            op("dve", tt, out=neglam[:], in0=lamt[:, 3:4], in1=lamt[:, 2:3], op=SUB, r=["lamt2"], w=["neglam"])
            op("dve", ts, out=neglam[:], in0=neglam[:], scalar1=float(-li), scalar2=None, op0=ADD, r=["neglam"], w=["neglam"])
            op("act", act, out=Aneg[:], in_=hp[:, 32:64], func=AF.Exp, r=["hp"], w=["Aneg"])
            op("dve", ts, out=Aneg[:], in0=Aneg[:], scalar1=-1.0, scalar2=None, op0=MUL, r=["Aneg"], w=["Aneg"])

        PARAMS = ["nw", "qkw", "dl", "sw", "cw", "cb", "hp", "snw", "neglam", "Aneg"]

        def phase_A(l, hT, xsrc):
            with Scope() as sc:
                nw = sc.sb("nw", [128, D], F32)
                op("sp", nc.sync.dma_start, out=nw[:], in_=nw_d[l], w=["nw"], dma="nw")
                xt = sc.pool("xt", 2, [128, D], F32)
                hb = sc.pool("hb", 2, [128, D], BF16)
                junk = sc.sb("junk", [128, D], BF16)
                ssp = sc.pool("ss", 2, [128, 1], F32)
                rsp = sc.pool("rs", 2, [128, 1], F32)
                for t in range(32):
                    x_t, kx = xt.next()
                    op("sp", nc.sync.dma_start, out=x_t[:], in_=xsrc[t * 128:(t + 1) * 128, :],
                       r=[("xres", t // 4)], w=[kx], dma=kx)
                    s_t, ks = ssp.next()
                    op("act", act, out=junk[:], in_=x_t[:], func=AF.Square, accum_out=s_t[:], r=[kx], w=["junkA", ks])
                    r_t, kr = rsp.next()
                    op("act", act, out=r_t[:], in_=s_t[:], func=AF.Sqrt, bias=epst[:], scale=1.0 / D,
                       r=[ks, "eps"], w=[kr])
                    op("dve", nc.vector.reciprocal, out=r_t[:], in_=r_t[:], r=[kr], w=[kr])
                    h_t, kh = hb.next()
                    op("dve", stt, out=h_t[:], in0=x_t[:], scalar=r_t[:], in1=nw[:], op0=MUL, op1=MUL,
                       r=[kx, kr, "nw"], w=[kh])
                    b = t % 2
                    ptv = bf(b).rearrange("p (a b) -> p a b", b=128)
                    for kc in range(8):
                        op("pe", tr, out=ptv[:, kc, :], in_=h_t[:, kc * 128:(kc + 1) * 128], identity=ident_b,
                           r=[kh] + CST, w=[bk(b)])
                    if t % 2 == 0:
                        op("act", nc.scalar.copy, out=hT[:, :, t * 128:(t + 1) * 128], in_=ptv, r=[bk(b)], w=[("hT", t)])
                    else:
                        op("dve", nc.vector.tensor_copy, out=hT[:, :, t * 128:(t + 1) * 128], in_=ptv, r=[bk(b)],
                           w=[("hT", t)])

        def hTk(tb):
            return [("hT", 4 * tb + i) for i in range(4)]

        def phase_S1(l, hT):
            w_l = w_in_d[l].rearrange("(kc p) n -> p kc n", p=128)
            xtm_v = xtm_d.rearrange("(t p) c -> p t c", p=128)
            with Scope() as sc:
                wp = sc.pool("wcc", 2, [128, 8, 128], BF16)
                xcp = sc.pool("xc", 2, [128, S + 3], F32)
                accp = sc.pool("acc", 2, [128, 2048], F32)
                xop = sc.pool("xo", 2, [128, S], BF16)
                tmp_ = sc.pool("tmt", 3, [128, 8, 128], BF16)
                for t_, k_ in ((xcp.tiles[0], (xcp.name, 0)), (xcp.tiles[1], (xcp.name, 1))):
                    op("dve", nc.vector.memset, t_[:, 0:3], 0.0, w=[k_])
                nb = 0
                for cc in range(24):
                    w_t, kw_ = wp.next()
                    c0 = OFF_XBC + cc * 128
                    op("pool", nc.gpsimd.dma_start, out=w_t[:], in_=w_l[:, :, c0:c0 + 128], w=[kw_], dma=kw_)
                    xc, kxc = xcp.next()
                    for tb in range(8):
                        b = nb % 4
                        nb += 1
                        for kc in range(8):
                            op("pe", mm, banks[b][:], lhsT=w_t[:, kc, :], rhs=hT[:, kc, tb * 512:(tb + 1) * 512],
                               start=(kc == 0), stop=(kc == 7), r=[kw_] + hTk(tb), w=[bk(b)])
                        op("act", nc.scalar.copy, out=xc[:, 3 + tb * 512: 3 + (tb + 1) * 512], in_=banks[b][:],
                           r=[bk(b)], w=[kxc])
                    xo, kxo = xop.next()
                    for half in range(2):
                        a_t, ka = accp.next()
                        o0 = half * 2048
                        op("dve", ts, out=a_t[:], in0=xc[:, o0:o0 + 2048], scalar1=cw[:, cc * 4:cc * 4 + 1],
                           scalar2=cb[:, cc:cc + 1], op0=MUL, op1=ADD, r=[kxc, "cw", "cb"], w=[ka])
                        for k in range(1, 4):
                            op("dve", stt, out=a_t[:], in0=xc[:, o0 + k:o0 + k + 2048],
                               scalar=cw[:, cc * 4 + k:cc * 4 + k + 1], in1=a_t[:], op0=MUL, op1=ADD,
                               r=[kxc, "cw", ka], w=[ka])
                        op("act", act, out=xo[:, o0:o0 + 2048], in_=a_t[:], func=AF.Silu, r=[ka], w=[kxo])
                    if cc >= 16:
                        op("sp", nc.sync.dma_start, out=bct_d[cc - 16], in_=xo[:], r=[kxo], w=[("bct", cc - 16)], dma=kxo)
                    if cc < 20:
                        for q4 in range(4):
                            b = 4 + (q4 % 2)
                            ptv = bf(b).rearrange("p (a b) -> p a b", b=128)
                            for i in range(8):
                                t0 = (q4 * 8 + i) * 128
                                op("pe", tr, out=ptv[:, i, :], in_=xo[:, t0:t0 + 128], identity=ident_b,
                                   r=[kxo] + CST, w=[bk(b)])
                            tm, ktm = tmp_.next()
                            if q4 % 2 == 0:
                                op("act", nc.scalar.copy, out=tm[:], in_=ptv, r=[bk(b)], w=[ktm])
                            else:
                                op("dve", nc.vector.tensor_copy, out=tm[:], in_=ptv, r=[bk(b)], w=[ktm])
                            op("sp", nc.sync.dma_start, out=xtm_v[:, q4 * 8:(q4 + 1) * 8, cc * 128:(cc + 1) * 128],
                               in_=tm[:], r=[ktm], w=[("xtm", cc, q4)], dma=ktm)

        def phase_S2(l, hT):
            w_l = w_in_d[l].rearrange("(kc p) n -> p kc n", p=128)
            bct_v = bct_d.rearrange("g p t -> p g t")
            ysT_v = ysT_d.rearrange("k p t -> p k t")
            with Scope() as sc:
                wzs = sc.sb("wzs", [128, 8, 2048], BF16)
                wdt = sc.sb("wdt", [128, 8, 32], BF16)
                op("pool", nc.gpsimd.dma_start, out=wzs[:], in_=w_l[:, :, OFF_ZS:OFF_ZS + 2048], w=["wzs"], dma="wzs")
                op("pool", nc.gpsimd.dma_start, out=wdt[:], in_=w_l[:, :, OFF_DT:OFF_DT + 32], w=["wdt"], dma="wdt")
                st = sc.sb("st", [128, 2048], F32)
                stb = sc.sb("stb", [128, 2048], BF16)
                op("dve", nc.vector.memset, st[:], 0.0, w=[("st", i) for i in range(4)])
                op("dve", nc.vector.memset, stb[:], 0.0, w=[("stb", i) for i in range(4)])
                xtp = sc.pool("xtc", 2, [128, 2560], BF16)
                bcp = sc.pool("bcc", 2, [128, 8, 128], BF16)
                smp = sc.pool("sm", 2, [128, 8, 32], F32)
                xdtp = sc.pool("xdt", 2, [128, 32, 64], BF16)
                xdsp = sc.pool("xds", 2, [128, 32, 64], BF16)
                xDp = sc.pool("xD", 2, [128, 32, 64], BF16)
                szap = sc.pool("szall", 2, [128, 2048], BF16)
                cbmp = sc.pool("cbm", 2, [128, 4, 128], BF16)
                ltp = sc.pool("lt", 2, [128, 4, 128], F32)
                dcp = sc.pool("dcT", 2, [128, 4, 128], BF16)
                mtap = sc.pool("MTall", 2, [128, 32, 128], BF16)
                sztp = sc.pool("szt", 1, [128, 512], F32)
                t1p = sc.pool("t1", 2, [128, 512], F32)
                gnp = sc.pool("gn", 2, [128, 512], BF16)
                junk = sc.sb("junkS", [128, 512], BF16)
                sqp = sc.pool("ssq", 2, [128, 1], F32)
                rqp = sc.pool("rsq", 2, [128, 1], F32)
                ysp = sc.pool("ysg", 3, [128, 4, 128], BF16)
                ctx = {}

                def stageA(c):
                    tok = slice(c * 128, (c + 1) * 128)
                    hk = [("hT", c)]
                    xt_c, kxt = xtp.next()
                    op("sp", nc.sync.dma_start, out=xt_c[:], in_=xtm_d[tok, :],
                       r=[("xtm", cc, c // 8) for cc in range(20)], w=[kxt], dma=kxt)
                    bc_c, kbc = bcp.next()
                    op("sp", nc.sync.dma_start, out=bc_c[:], in_=bct_v[:, :, tok],
                       r=[("bct", i) for i in range(8)], w=[kbc], dma=kbc)
                    sm, ksm = smp.next()
                    for kc in range(8):
                        op("pe", mm, banks[0][:, 0:32], lhsT=hT[:, kc, tok], rhs=wdt[:, kc, :], start=(kc == 0),
                           stop=(kc == 7), r=hk + ["wdt"], w=[bk(0)])
                    op("dve", tt, out=sm[:, 0, :], in0=banks[0][:, 0:32], in1=hp[:, 0:32], op=ADD, r=[bk(0), "hp"], w=[ksm])
                    op("act", act, out=sm[:, 1, :], in_=sm[:, 0, :], func=AF.Exp, r=[ksm], w=[ksm])
                    op("act", act, out=sm[:, 2, :], in_=sm[:, 1, :], func=AF.Ln, bias=1.0, r=[ksm], w=[ksm])
                    op("dve", tt, out=sm[:, 3, :], in0=sm[:, 2, :], in1=Aneg[:], op=MUL, r=[ksm, "Aneg"], w=[ksm])
                    op("pe", mm, banks[0][:, 32:64], lhsT=U_f, rhs=sm[:, 3, :], start=True, stop=True,
                       r=[ksm] + CST, w=[bk(0)])
                    op("pe", mm, banks[0][:, 64:96], lhsT=ones_f[:], rhs=sm[:, 3, :], start=True, stop=True,
                       r=[ksm, "ones_f"], w=[bk(0)])
                    op("dve", nc.vector.tensor_copy, out=sm[:, 4, :], in_=banks[0][:, 32:64], r=[bk(0)], w=[ksm])
                    op("act", act, out=sm[:, 5, :], in_=banks[0][:, 32:64], func=AF.Exp, r=[bk(0)], w=[ksm])
                    op("dve", tt, out=sm[:, 6, :], in0=banks[0][:, 64:96], in1=sm[:, 4, :], op=SUB, r=[bk(0), ksm], w=[ksm])
                    op("act", act, out=sm[:, 6, :], in_=sm[:, 6, :], func=AF.Exp, r=[ksm], w=[ksm])
                    op("act", act, out=sm[:, 7, :], in_=banks[0][:, 64:96], func=AF.Exp, r=[bk(0)], w=[ksm])
                    xv = xt_c[:, 0:2048].rearrange("p (a b) -> p a b", b=64)
                    xdt, kxdt = xdtp.next()
                    op("dve", tt, out=xdt[:], in0=xv, in1=sm[:, 2, :].unsqueeze(2).broadcast_to([128, 32, 64]), op=MUL,
                       r=[kxt, ksm], w=[kxdt])
                    xds, kxds = xdsp.next()
                    op("pool", nc.gpsimd.tensor_tensor, out=xds[:], in0=xdt[:],
                       in1=sm[:, 6, :].unsqueeze(2).broadcast_to([128, 32, 64]), op=MUL, r=[kxdt, ksm], w=[kxds])
                    xD, kxD = xDp.next()
                    op("pool", nc.gpsimd.tensor_tensor, out=xD[:], in0=xv,
                       in1=hp[:, 64:96].unsqueeze(2).broadcast_to([128, 32, 64]), op=MUL, r=[kxt, "hp"], w=[kxD])
                    cbv = banks[1][:].rearrange("p (a b) -> p a b", b=128)
                    for gi in range(4):
                        op("pe", mm, cbv[:, gi, :], lhsT=bc_c[:, gi, :], rhs=bc_c[:, 4 + gi, :], start=True, stop=True,
                           r=[kbc], w=[bk(1)])
                    cbm, kcbm = cbmp.next()
                    op("dve", tt, out=cbm[:], in0=cbv, in1=maskU_b.unsqueeze(1).broadcast_to([128, 4, 128]), op=MUL,
                       r=[bk(1)] + CST, w=[kcbm])
                    mta, kmta = mtap.next()
                    sza, ksza = szap.next()
                    for q8 in range(8):
                        gi = q8 // 2
                        h0 = q8 * 4
                        lt, klt = ltp.next()
                        op("pool", nc.gpsimd.tensor_tensor, out=lt[:],
                           in0=Ls_f.unsqueeze(1).broadcast_to([128, 4, 128]),
                           in1=sm[:, 3, h0:h0 + 4].unsqueeze(2).broadcast_to([128, 4, 128]), op=MUL,
                           r=[ksm] + CST, w=[klt])
                        rbv = banks[2][:].rearrange("p (a b) -> p a b", b=128)
                        for i in range(4):
                            op("pe", mm, rbv[:, i, :], lhsT=lt[:, i, :], rhs=U_f, start=True, stop=True,
                               r=[klt] + CST, w=[bk(2)])
                        dc, kdc = dcp.next()
                        op("act", act, out=dc[:], in_=rbv, func=AF.Exp, r=[bk(2)], w=[kdc])
                        op("dve", tt, out=mta[:, h0:h0 + 4, :], in0=dc[:], in1=cbm[:, gi:gi + 1, :].broadcast_to([128, 4, 128]),
                           op=MUL, r=[kdc, kcbm], w=[kmta + (q8,)])
                        if q8 % 2 == 1:
                            for kc in range(8):
                                op("pe", mm, banks[3][:], lhsT=hT[:, kc, tok], rhs=wzs[:, kc, gi * 512:(gi + 1) * 512],
                                   start=(kc == 0), stop=(kc == 7), r=hk + ["wzs"], w=[bk(3)])
                            szt, kszt = sztp.next()
                            op("act", act, out=szt[:], in_=banks[3][:], func=AF.Exp, scale=-1.0, r=[bk(3)], w=[kszt])
                            op("act", act, out=szt[:], in_=szt[:], func=AF.Ln, bias=1.0, r=[kszt], w=[kszt])
                            op("act", act, out=szt[:], in_=szt[:], func=AF.Exp, scale=-1.0, r=[kszt], w=[kszt])
                            op("dve", tt, out=sza[:, gi * 512:(gi + 1) * 512], in0=banks[3][:], in1=szt[:], op=MUL,
                               r=[bk(3), kszt], w=[ksza + (gi,)])
                    ctx[c] = dict(tok=tok, xt_c=xt_c, kxt=kxt, bc_c=bc_c, kbc=kbc, sm=sm, ksm=ksm, xdt=xdt, kxdt=kxdt,
                                  xds=xds, kxds=kxds, xD=xD, kxD=kxD, mta=mta, kmta=kmta, sza=sza, ksza=ksza)

                def stageB(c):
                    d_ = ctx.pop(c)
                    tok, xt_c, kxt, bc_c, kbc = d_["tok"], d_["xt_c"], d_["kxt"], d_["bc_c"], d_["kbc"]
                    sm, ksm, xdt, kxdt, xds, kxds = d_["sm"], d_["ksm"], d_["xdt"], d_["kxdt"], d_["xds"], d_["kxds"]
                    xD, kxD, mta, kmta, sza, ksza = d_["xD"], d_["kxD"], d_["mta"], d_["kmta"], d_["sza"], d_["ksza"]
                    for pair in ((0, 1), (2, 3)):
                        YB = {pair[0]: 4, pair[1]: 6}
                        XB = {pair[0]: 5, pair[1]: 7}
                        T = {}
                        for gi in pair:
                            yb_, xb_ = YB[gi], XB[gi]
                            op("pe", mm, banks[yb_][:], lhsT=ident_b, rhs=xD[:, gi * 8:(gi + 1) * 8, :], start=True, stop=False,
                               r=[kxD] + CST, w=[bk(yb_)])
                            for j in range(8):
                                hh = gi * 8 + j
                                op("pe", mm, banks[yb_][:, j * 64:(j + 1) * 64], lhsT=mta[:, hh, :], rhs=xdt[:, hh, :],
                                   start=False, stop=(j == 7), skip_group_check=True,
                                   r=[kmta + (hh // 4,), kxdt], w=[bk(yb_)])
                            op("pe", mm, banks[xb_][:], lhsT=bc_c[:, 4 + gi, :], rhs=stb[:, gi * 512:(gi + 1) * 512],
                               start=True, stop=True, r=[kbc, ("stb", gi)], w=[bk(xb_)])
                        for gi in pair:
                            t1, kt1 = t1p.next()
                            T[gi] = (t1, kt1)
                            op("dve", tt, out=t1[:].rearrange("p (a b) -> p a b", b=64),
                               in0=banks[XB[gi]][:].rearrange("p (a b) -> p a b", b=64),
                               in1=sm[:, 5, gi * 8:(gi + 1) * 8].unsqueeze(2).broadcast_to([128, 8, 64]), op=MUL,
                               r=[bk(XB[gi]), ksm], w=[kt1])
                        for gi in pair:
                            t1, kt1 = T[gi]
                            op("dve", tt, out=t1[:], in0=banks[YB[gi]][:], in1=t1[:], op=ADD, r=[bk(YB[gi]), kt1], w=[kt1])
                        for gi in pair:
                            op("pe", mm, banks[XB[gi]][:], lhsT=xt_c[:, 2048 + gi * 128:2048 + (gi + 1) * 128],
                               rhs=xds[:, gi * 8:(gi + 1) * 8, :], start=True, stop=True, r=[kxt, kxds], w=[bk(XB[gi])])
                        for gi in pair:
                            t1, kt1 = T[gi]
                            op("pool", nc.gpsimd.tensor_tensor, out=t1[:], in0=t1[:], in1=sza[:, gi * 512:(gi + 1) * 512],
                               op=MUL, r=[kt1, ksza + (gi,)], w=[kt1])
                        R = {}
                        for gi in pair:
                            t1, kt1 = T[gi]
                            sq, ksq = sqp.next()
                            op("act", act, out=junk[:], in_=t1[:], func=AF.Square, accum_out=sq[:], r=[kt1], w=["junkS", ksq])
                            rq, krq = rqp.next()
                            R[gi] = (sq, ksq, rq, krq)
                        for gi in pair:
                            sq, ksq, rq, krq = R[gi]
                            op("act", act, out=rq[:], in_=sq[:], func=AF.Ln, bias=epst[:], scale=1.0 / 512.0,
                               r=[ksq, "eps"], w=[krq])
                        for gi in pair:
                            sq, ksq, rq, krq = R[gi]
                            op("act", act, out=rq[:], in_=rq[:], func=AF.Exp, scale=-0.5, r=[krq], w=[krq])
                        for gi in pair:
                            stv = st[:, gi * 512:(gi + 1) * 512]
                            op("pool", nc.gpsimd.tensor_tensor, out=stv.rearrange("p (a b) -> p a b", b=64),
                               in0=stv.rearrange("p (a b) -> p a b", b=64),
                               in1=sm[:, 7, gi * 8:(gi + 1) * 8].unsqueeze(2).broadcast_to([128, 8, 64]), op=MUL,
                               r=[("st", gi), ksm], w=[("st", gi)])
                        for gi in pair:
                            stv = st[:, gi * 512:(gi + 1) * 512]
                            op("dve", tt, out=stv, in0=banks[XB[gi]][:], in1=stv, op=ADD, r=[bk(XB[gi]), ("st", gi)],
                               w=[("st", gi)])
                        G = {}
                        for gi in pair:
                            t1, kt1 = T[gi]
                            sq, ksq, rq, krq = R[gi]
                            gn, kgn = gnp.next()
                            G[gi] = (gn, kgn)
                            op("dve", stt, out=gn[:], in0=t1[:], scalar=rq[:], in1=snw[:, gi * 512:(gi + 1) * 512],
                               op0=MUL, op1=MUL, r=[kt1, krq, "snw"], w=[kgn])
                        for gi in pair:
                            stv = st[:, gi * 512:(gi + 1) * 512]
                            op("act", nc.scalar.copy, out=stb[:, gi * 512:(gi + 1) * 512], in_=stv, r=[("st", gi)],
                               w=[("stb", gi)])
                        for gi in pair:
                            gn, kgn = G[gi]
                            ptv = bf(XB[gi]).rearrange("p (a b) -> p a b", b=128)
                            for i in range(4):
                                op("pe", tr, out=ptv[:, i, :], in_=gn[:, i * 128:(i + 1) * 128], identity=ident_b,
                                   r=[kgn] + CST, w=[bk(XB[gi])])
                        for gi in pair:
                            ptv = bf(XB[gi]).rearrange("p (a b) -> p a b", b=128)
                            ys_g, kys = ysp.next()
                            op("act", nc.scalar.copy, out=ys_g[:], in_=ptv[:, 0:4, :], r=[bk(XB[gi])], w=[kys])
                            op("sp", nc.sync.dma_start, out=ysT_v[:, gi * 4:(gi + 1) * 4, tok], in_=ys_g[:], r=[kys],
                               w=[("ysT", c // 4, gi)], dma=kys)

                def record(fn, c):
                    saved = P.ops
                    P.ops = []
                    fn(c)
                    lst = P.ops
                    P.ops = saved
                    return lst

                stageA(0)
                for c in range(32):
                    la = record(stageA, c + 1) if c + 1 < 32 else []
                    lb = record(stageB, c)
                    ia = ib = 0
                    while ia < len(la) or ib < len(lb):
                        if ib >= len(lb) or (ia < len(la) and ia * len(lb) <= ib * len(la)):
                            P.ops.append(la[ia])
                            ia += 1
                        else:
                            P.ops.append(lb[ib])
                            ib += 1

        def phase_G(l, hT):
            w_l = w_in_d[l].rearrange("(kc p) n -> p kc n", p=128)
            with Scope() as sc:
                wp = sc.pool("wg", 2, [128, 8, 128], BF16)
                sgp = sc.pool("sg", 2, [128, S], BF16)
                nb = 0
                for gc in range(16):
                    w_t, kw_ = wp.next()
                    c0 = OFF_G + gc * 128
                    op("pool", nc.gpsimd.dma_start, out=w_t[:], in_=w_l[:, :, c0:c0 + 128], w=[kw_], dma=kw_)
                    sg, ksg = sgp.next()
                    for tb in range(8):
                        b = nb % 4
                        nb += 1
                        for kc in range(8):
                            op("pe", mm, banks[b][:], lhsT=w_t[:, kc, :], rhs=hT[:, kc, tb * 512:(tb + 1) * 512],
                               start=(kc == 0), stop=(kc == 7), r=[kw_] + hTk(tb), w=[bk(b)])
                        op("act", act, out=sg[:, tb * 512:(tb + 1) * 512], in_=banks[b][:], func=AF.Sigmoid,
                           r=[bk(b)], w=[ksg])
                    op("sp", nc.sync.dma_start, out=sgT_d[gc], in_=sg[:], r=[ksg], w=[("sgT", gc)], dma=ksg)

        def phase_T(l, hT):
            w_l = w_in_d[l].rearrange("(kc p) n -> p kc n", p=128)
            LOOK = 2
            LSC = 2.0 ** -10
            CSC = EPS / (LSC * LSC)
            with Scope() as sc:
                whp = sc.pool("wh", 2, [128, 8, 4, 128], BF16)
                qz = [sc.sb("qz%d" % m, [128, S], BF16) for m in range(2)]
                kT = sc.sb("kT", [128, S], BF16)
                vt = sc.sb("vt", [128, 32, 128], BF16)
                szT = sc.sb("szT", [128, S], BF16)
                sqp = sc.pool("sq", 2, [128, 512], BF16)
                lnp = sc.pool("lnq", 2, [128, 512], F32)
                rsp = sc.pool("rst", 2, [128, 512], F32)
                ezp = sc.pool("ez", 2, [128, 512], F32)
                ptp = sc.pool("PT", 4, [128, 2, 512], BF16)
                s1pp = sc.pool("s1p", 2, [128, 512], F32)
                s0cp = sc.pool("s0c", 2, [128, 512], F32)
                s1cp = sc.pool("s1c", 2, [128, 512], F32)
                lb0p = sc.pool("lb0", 1, [128, 512], F32)
                lb1p = sc.pool("lb1", 1, [128, 512], F32)
                u0p = sc.pool("u0", 1, [128, 512], F32)
                u1p = sc.pool("u1", 1, [128, 512], F32)
                tqp = sc.pool("tq", 1, [128, 512], F32)
                sqo = sc.pool("sqo", 1, [128, 512], BF16)
                agp = sc.pool("arg", 1, [128, 512], F32)
                ybp = sc.pool("yb", 2, [128, 512], BF16)
                op("dve", nc.vector.memset, qz[0][64:128, :], 0.0, w=["qz0"])
                op("dve", nc.vector.memset, qz[1][0:64, :], 0.0, w=["qz1"])
                nS = [0]
                nI = [0]
                pairs = ((2, 3), (4, 5))
                for h in range(8):
                    wh, kwh = whp.next()
                    kws = []
                    for i, off in enumerate((OFF_Q, OFF_K, OFF_V, OFF_ZA)):
                        c0 = off + h * 128
                        kwi = kwh + (i,)
                        kws.append(kwi)
                        op("pool", nc.gpsimd.dma_start, out=wh[:, :, i, :], in_=w_l[:, :, c0:c0 + 128], w=[kwi], dma=kwi)
                    for which in (0, 1):
                        for tb in range(8):
                            ba = 2 + 2 * (nI[0] % 2)
                            bs = ba + 1
                            nI[0] += 1
                            cs = slice(tb * 512, (tb + 1) * 512)
                            for kc in range(8):
                                op("pe", mm, banks[ba][:], lhsT=wh[:, kc, which, :], rhs=hT[:, kc, cs],
                                   start=(kc == 0), stop=(kc == 7), r=[kws[which]] + hTk(tb), w=[bk(ba)])
                            sq, ksq = sqp.next()
                            op("act", act, out=sq[:], in_=banks[ba][:], func=AF.Square, r=[bk(ba)], w=[ksq])
                            op("pe", mm, banks[bs][:], lhsT=BD_b, rhs=sq[:], start=True, stop=True, r=[ksq] + CST, w=[bk(bs)])
                            ln, kln = lnp.next()
                            op("act", act, out=ln[:], in_=banks[bs][:], func=AF.Ln, bias=epst[:], scale=1.0 / 64.0,
                               r=[bk(bs), "eps"], w=[kln])
                            rs, krs = rsp.next()
                            op("act", act, out=rs[:], in_=ln[:], func=AF.Exp, scale=-0.5, r=[kln], w=[krs])
                            if which == 0:
                                for m in range(2):
                                    pr = slice(m * 64, (m + 1) * 64)
                                    op("dve", stt, out=qz[m][pr, cs], in0=banks[ba][pr, :], scalar=qkw[pr, 0:1], in1=rs[pr, :],
                                       op0=MUL, op1=MUL, r=[bk(ba), krs, "qkw"], w=["qz%d" % m])
                            else:
                                op("dve", stt, out=kT[:, cs], in0=banks[ba][:], scalar=qkw[:, 1:2], in1=rs[:],
                                   op0=MUL, op1=MUL, r=[bk(ba), krs, "qkw"], w=["kT"])
                    for tb in range(8):
                        ba = 2 + (nI[0] % 4)
                        nI[0] += 1
                        cs = slice(tb * 512, (tb + 1) * 512)
                        for kc in range(8):
                            op("pe", mm, banks[ba][:], lhsT=wh[:, kc, 3, :], rhs=hT[:, kc, cs],
                               start=(kc == 0), stop=(kc == 7), r=[kws[3]] + hTk(tb), w=[bk(ba)])
                        ez, kez = ezp.next()
                        op("act", act, out=ez[:], in_=banks[ba][:], func=AF.Exp, scale=-1.0, r=[bk(ba)], w=[kez])
                        op("act", act, out=ez[:], in_=ez[:], func=AF.Ln, bias=1.0, r=[kez], w=[kez])
                        op("act", act, out=ez[:], in_=ez[:], func=AF.Exp, scale=-1.0, r=[kez], w=[kez])
                        op("dve", stt, out=szT[:, cs], in0=banks[ba][:], scalar=swc[:, 0:1], in1=ez[:], op0=MUL, op1=MUL,
                           r=[bk(ba), kez, "swc"], w=["szT"])
                    for t4 in range(8):
                        b = 2 + (nI[0] % 4)
                        nI[0] += 1
                        pv = banks[b][:].rearrange("p (a b) -> p a b", b=128)
                        for i in range(4):
                            t = t4 * 4 + i
                            for kc in range(8):
                                op("pe", mm, pv[:, i, :], lhsT=hT[:, kc, t * 128:(t + 1) * 128], rhs=wh[:, kc, 2, :],
                                   start=(kc == 0), stop=(kc == 7), r=[kws[2], ("hT", t)], w=[bk(b)])
                        op("dve", nc.vector.tensor_copy, out=vt[:, t4 * 4:(t4 + 1) * 4, :], in_=pv, r=[bk(b)], w=["vt"])
                    steps = [(qb, t) for qb in range(8) for t in range(4 * qb + 4)]
                    pts = {}
                    s1ps = {}

                    def emit_qk(j):
                        qb, t = steps[j]
                        off = max(0, t - 4 * qb) * 128
                        pi = nS[0] % 2
                        nS[0] += 1
                        pb = pairs[pi]
                        diag = t >= 4 * qb
                        for m in range(2):
                            op("pe", mm, banks[pb[m]][:, 0:512 - off], lhsT=kT[:, t * 128:(t + 1) * 128],
                               rhs=qz[m][:, qb * 512 + off:(qb + 1) * 512], start=True, stop=not diag,
                               r=["kT", "qz%d" % m], w=[bk(pb[m])])
                            if diag:
                                op("pe", mm, banks[pb[m]][:, 0:128], lhsT=ident_b, rhs=Mneg_b, start=False, stop=True,
                                   r=CST, w=[bk(pb[m])])
                        pt, kpt = ptp.next()
                        pview = pbig[1 + pi][:].rearrange("p (a b) -> p a b", b=512)
                        op("act", act, out=pt[:, :, 0:512 - off], in_=pview[:, :, 0:512 - off], func=AF.Exp,
                           r=[bk(pb[0]), bk(pb[1])], w=[kpt])
                        pts[j] = (pt, kpt)

                    def emit_pv(i):
                        qb, t = steps[i]
                        off = max(0, t - 4 * qb) * 128
                        pt, kpt = pts.pop(i)
                        for m in range(2):
                            op("pe", mm, banks[m][:, off:512], lhsT=vt[:, t, :], rhs=pt[:, m, 0:512 - off],
                               start=(t == 0), stop=(t == 4 * qb + 3), r=[kpt, "vt"], w=[bk(m)])
                        if t == 0:
                            op("dve", nc.vector.tensor_copy, out=banks[6][:], in_=pt[:, 0, :], r=[kpt], w=[bk(6)])
                            op("dve", nc.vector.tensor_copy, out=banks[7][:], in_=pt[:, 1, :], r=[kpt], w=[bk(7)])
                            s1ps[qb] = s1pp.next()
                            op("pool", nc.gpsimd.memset, s1ps[qb][0][:], 0.0, w=[s1ps[qb][1]])
                        else:
                            op("dve", tt, out=banks[6][:, off:512], in0=banks[6][:, off:512], in1=pt[:, 0, 0:512 - off],
                               op=ADD, r=[kpt, bk(6)], w=[bk(6)])
                            if t % 3 == 0:
                                op("dve", tt, out=banks[7][:, off:512], in0=banks[7][:, off:512], in1=pt[:, 1, 0:512 - off],
                                   op=ADD, r=[kpt, bk(7)], w=[bk(7)])
                            else:
                                sp_, ksp = s1ps[qb]
                                op("pool", nc.gpsimd.tensor_tensor, out=sp_[:, off:512], in0=sp_[:, off:512],
                                   in1=pt[:, 1, 0:512 - off], op=ADD, r=[kpt, ksp], w=[ksp])

                    def emit_fin(qb):
                        cs = slice(qb * 512, (qb + 1) * 512)
                        s1p_, ks1p = s1ps.pop(qb)
                        s0c, ks0c = s0cp.next()
                        s1c, ks1c = s1cp.next()
                        op("dve", nc.vector.tensor_copy, out=s0c[:], in_=banks[6][:], r=[bk(6)], w=[ks0c])
                        op("dve", nc.vector.tensor_copy, out=s1c[:], in_=banks[7][:], r=[bk(7)], w=[ks1c])
                        pi = nS[0] % 2
                        nS[0] += 1
                        bx, by = pairs[pi]
                        op("pe", mm, banks[bx][:], lhsT=ones_f[:], rhs=s0c[:], start=True, stop=True, r=[ks0c, "ones_f"], w=[bk(bx)])
                        op("pe", mm, banks[by][:], lhsT=ones_f[:], rhs=s1c[:], start=True, stop=False, r=[ks1c, "ones_f"], w=[bk(by)])
                        op("pe", mm, banks[by][:], lhsT=ones_f[:], rhs=s1p_[:], start=False, stop=True, r=[ks1p, "ones_f"], w=[bk(by)])
                        lb0, kl0 = lb0p.next()
                        lb1, kl1 = lb1p.next()
                        op("act", act, out=lb0[:], in_=banks[bx][:], func=AF.Copy, scale=LSC, r=[bk(bx)], w=[kl0])
                        op("act", act, out=lb1[:], in_=banks[by][:], func=AF.Copy, scale=LSC, r=[bk(by)], w=[kl1])
                        u0, ku0 = u0p.next()
                        u1, ku1 = u1p.next()
                        op("dve", tt, out=u1[:], in0=banks[1][:], in1=lb0[:], op=MUL, r=[bk(1), kl0], w=[ku1])
                        op("dve", tt, out=u0[:], in0=banks[0][:], in1=lb1[:], op=MUL, r=[bk(0), kl1], w=[ku0])
                        op("dve", stt, out=u0[:], in0=u1[:], scalar=neglam[:, 0:1], in1=u0[:], op0=MUL, op1=ADD,
                           r=[ku0, ku1, "neglam"], w=[ku0])
                        tq, ktq = tqp.next()
                        op("pool", nc.gpsimd.tensor_tensor, out=tq[:], in0=lb0[:], in1=lb1[:], op=MUL, r=[kl0, kl1], w=[ktq])
                        op("pool", nc.gpsimd.tensor_tensor, out=tq[:], in0=tq[:], in1=tq[:], op=MUL, r=[ktq], w=[ktq])
                        sq, ksq = sqo.next()
                        op("pool", nc.gpsimd.tensor_tensor, out=sq[:], in0=u0[:], in1=u0[:], op=MUL, r=[ku0], w=[ksq])
                        op("pe", mm, banks[bx][:], lhsT=ones_b[:], rhs=sq[:], start=True, stop=True, r=[ksq, "ones_b"], w=[bk(bx)])
                        ag, kag = agp.next()
                        op("dve", stt, out=ag[:], in0=banks[bx][:], scalar=float(1.0 / (128.0 * CSC)), in1=tq[:],
                           op0=MUL, op1=ADD, r=[bk(bx), ktq], w=[kag])
                        op("act", act, out=ag[:], in_=ag[:], func=AF.Ln, r=[kag], w=[kag])
                        op("act", act, out=ag[:], in_=ag[:], func=AF.Exp, scale=-0.5, r=[kag], w=[kag])
                        op("dve", stt, out=u0[:], in0=u0[:], scalar=float(CSC ** -0.5), in1=ag[:], op0=MUL, op1=MUL,
                           r=[ku0, kag], w=[ku0])
                        yb, kyb = ybp.next()
                        op("pool", nc.gpsimd.tensor_tensor, out=yb[:], in0=u0[:], in1=szT[:, cs], op=MUL,
                           r=[ku0, "szT"], w=[kyb])
                        op("sp", nc.sync.dma_start, out=yaT_d[h][:, cs], in_=yb[:], r=[kyb], w=[("yaT", h, qb)], dma=kyb)

                    n = len(steps)
                    for i in range(-LOOK, n):
                        j = i + LOOK
                        if j < n:
                            emit_qk(j)
                        if i >= 0:
                            emit_pv(i)
                            qb, t = steps[i]
                            if t == 4 * qb + 3:
                                emit_fin(qb)

        def phase_D(l, xsrc):
            wpa_v = w_pa_d[l].rearrange("(kc p) n -> p kc n", p=128)
            wps_v = w_ps_d[l].rearrange("(kc p) n -> p kc n", p=128)
            wo_v = w_out_d[l].rearrange("(kc p) n -> p kc n", p=128)
            yaT_v = yaT_d.rearrange("h p t -> p h t")
            ysT_v = ysT_d.rearrange("k p t -> p k t")
            sgT_v = sgT_d.rearrange("k p t -> p k t")
            with Scope() as sc:
                wpa = sc.sb("wpa", [128, 8, D], BF16)
                wps = sc.sb("wps", [128, 16, D], BF16)
                wo = sc.sb("wo", [128, 8, D], BF16)
                op("pool", nc.gpsimd.dma_start, out=wpa[:], in_=wpa_v, w=["wpa"], dma="wpa")
                op("pool", nc.gpsimd.dma_start, out=wps[:, 0:8], in_=wps_v[:, 0:8], w=["wps"], dma="wps")
                op("pool", nc.gpsimd.dma_start, out=wps[:, 8:16], in_=wps_v[:, 8:16], w=["wps"], dma="wps")
                op("pool", nc.gpsimd.dma_start, out=wo[:], in_=wo_v, w=["wo"], dma="wo")
                yap = sc.pool("yab", 2, [128, 8, 512], BF16)
                ysp = sc.pool("ysb", 2, [128, 16, 512], BF16)
                sgp = sc.pool("sgb", 1, [128, 16, 512], BF16)
                xrp = sc.pool("xr", 1, [128, 4, D], F32)
                m1p = sc.pool("m1", 2, [128, 512], F32)
                m2p = sc.pool("m2", 2, [128, 512], F32)
                mTp = sc.pool("mT", 2, [128, 8, 512], BF16)
                xop = sc.pool("xo", 1, [128, 4, D], F32)
                nb = 0
                for tb in range(8):
                    tok = slice(tb * 512, (tb + 1) * 512)
                    ya, kya = yap.next()
                    ys, kys = ysp.next()
                    sg, ksg = sgp.next()
                    xr, kxr = xrp.next()
                    op("sp", nc.sync.dma_start, out=ya[:], in_=yaT_v[:, :, tok], r=[("yaT", h, tb) for h in range(8)],
                       w=[kya], dma=kya)
                    op("sp", nc.sync.dma_start, out=ys[:], in_=ysT_v[:, :, tok], r=[("ysT", tb, gi) for gi in range(4)], w=[kys], dma=kys)
                    op("sp", nc.sync.dma_start, out=sg[:], in_=sgT_v[:, :, tok], r=[("sgT", i) for i in range(16)],
                       w=[ksg], dma=ksg)
                    op("sp", nc.sync.dma_start, out=xr[:], in_=xsrc[tok, :].rearrange("(t p) c -> p t c", p=128),
                       r=[("xres", tb)], w=[kxr], dma=kxr)
                    mT, kmT = mTp.next()
                    for cc in range(8):
                        b1 = nb % 6
                        b2 = (nb + 1) % 6
                        nb += 2
                        for kc in range(8):
                            op("pe", mm, banks[b1][:], lhsT=wpa[:, kc, cc * 128:(cc + 1) * 128], rhs=ya[:, kc, :],
                               start=(kc == 0), stop=(kc == 7), r=["wpa", kya], w=[bk(b1)])
                        for kc in range(16):
                            op("pe", mm, banks[b2][:], lhsT=wps[:, kc, cc * 128:(cc + 1) * 128], rhs=ys[:, kc, :],
                               start=(kc == 0), stop=(kc == 15), r=["wps", kys], w=[bk(b2)])
                        m1, km1 = m1p.next()
                        op("dve", tt, out=m1[:], in0=banks[b1][:], in1=sg[:, cc, :], op=MUL, r=[bk(b1), ksg], w=[km1])
                        m2, km2 = m2p.next()
                        op("dve", tt, out=m2[:], in0=banks[b2][:], in1=sg[:, 8 + cc, :], op=MUL, r=[bk(b2), ksg], w=[km2])
                        op("pool", nc.gpsimd.tensor_tensor, out=mT[:, cc, :], in0=m1[:], in1=m2[:], op=ADD,
                           r=[km1, km2], w=[kmT])
                    xo, kxo = xop.next()
                    for t4 in range(4):
                        for half in range(2):
                            b = 6 + (t4 * 2 + half) % 2
                            for kc in range(8):
                                op("pe", mm, banks[b][:], lhsT=mT[:, kc, t4 * 128:(t4 + 1) * 128],
                                   rhs=wo[:, kc, half * 512:(half + 1) * 512], start=(kc == 0), stop=(kc == 7),
                                   r=[kmT, "wo"], w=[bk(b)])
                            op("dve", tt, out=xo[:, t4, half * 512:(half + 1) * 512], in0=banks[b][:],
                               in1=xr[:, t4, half * 512:(half + 1) * 512], op=ADD, r=[bk(b), kxr], w=[kxo])
                    op("sp", nc.sync.dma_start, out=out_d[tok, :].rearrange("(t p) c -> p t c", p=128), in_=xo[:],
                       r=[kxo], w=[("xres", tb)], dma=kxo)

        for l in range(depth):
            xsrc = x_d if l == 0 else out_d
            barrier()
            load_params(l)
            with Scope() as lsc:
                hT = lsc.sb("hT", [128, 8, S], BF16)
                if "A" in phases:
                    phase_A(l, hT, xsrc)
                    barrier()
                if "S" in phases:
                    if _os.environ.get("SKIPS1") != "1":
                        phase_S1(l, hT)
                        barrier()
                    if _os.environ.get("SKIPS2") != "1":
                        phase_S2(l, hT)
                        barrier()
                if "G" in phases:
                    phase_G(l, hT)
                    barrier()
                if "T" in phases:
                    phase_T(l, hT)
                    barrier()
            if "D" in phases:
                phase_D(l, xsrc)
        P.finish(sems)
    return nc, P.stats


def make_consts():
    i = np.arange(128)
    ident = np.eye(128, dtype=np.float32)
    maskU = (i[None, :] >= i[:, None]).astype(np.float32)
    Lstrict = (i[:, None] > i[None, :]).astype(np.float32)
    bd = ((i[:, None] // 64) == (i[None, :] // 64)).astype(np.float32)
    mneg = np.where(i[None, :] >= i[:, None], 0.0, -30000.0).astype(np.float32)
    return np.concatenate([ident, maskU, maskU, Lstrict, bd, mneg], axis=1).astype(np.float32)


def host_layout(inputs):
    inputs = {k: (np.asarray(v)[:DEPTH] if k != "x" else v) for k, v in inputs.items()}
    f = lambda a: np.ascontiguousarray(np.asarray(a, dtype=np.float32))
    rep = lambda a: np.ascontiguousarray(np.broadcast_to(np.asarray(a, np.float32)[:, None, :], (a.shape[0], 128, a.shape[1])))
    qn, kn = np.asarray(inputs["q_norm_w"], np.float32), np.asarray(inputs["k_norm_w"], np.float32)
    qkw = np.stack([np.tile(qn, (1, 2)), np.tile(kn, (1, 2))], axis=-1)
    cw = np.asarray(inputs["conv_w"], np.float32)
    convw = cw.transpose(0, 2, 1).reshape(DEPTH, 24, 128, 4).transpose(0, 2, 1, 3).reshape(DEPTH, 128, 96)
    convb = np.asarray(inputs["conv_b"], np.float32).reshape(DEPTH, 24, 128).transpose(0, 2, 1)
    hp = np.concatenate([inputs["dt_bias"], inputs["a_log"], inputs["d_skip"]], axis=-1).astype(np.float32)
    common = {
        "w_in": f(inputs["w_in"]), "w_pa": f(inputs["w_proj_attn"]), "w_ps": f(inputs["w_proj_ssd"]),
        "w_out": f(inputs["w_out"]),
        "nw_rep": rep(inputs["norm_w"]), "qkw": f(qkw),
        "dl_rep": rep(np.asarray(inputs["diff_lambda"], np.float32).reshape(DEPTH, 256)),
        "sw_rep": rep(inputs["subln_w"]), "swc": f(np.asarray(inputs["subln_w"], np.float32)[:, :, None]), "convw": f(convw), "convb": f(convb),
        "hp_rep": rep(hp), "snw_rep": rep(inputs["ssd_norm_w"]), "consts": make_consts(),
    }
    return common


_NC_CACHE = {}


def kernel(**inputs):
    common = host_layout(inputs)
    x = np.asarray(inputs["x"], np.float32)
    n = x.shape[0]
    if "nc" not in _NC_CACHE:
        _NC_CACHE["nc"] = build()[0]
    nc = _NC_CACHE["nc"]
    in_maps = [dict(common, x=np.ascontiguousarray(x[b])) for b in range(n)]
    res = run_bass_kernel_spmd(nc, in_maps, core_ids=list(range(n)))
    return np.stack([np.asarray(r["out"], np.float32) for r in res.results], axis=0)
```

```python
import contextlib
import math
import numpy as np
import concourse.bass as bass
import concourse.mybir as mybir
from concourse.bass_utils import run_bass_kernel_spmd
from concourse.alu_op_type import AluOpType as ALU

AF = mybir.ActivationFunctionType
F32 = mybir.dt.float32
BF16 = mybir.dt.bfloat16
AX = mybir.AxisListType

S = 4096
D = 1024
DIN = 11296
OFF_Q, OFF_K, OFF_V, OFF_ZA, OFF_XBC, OFF_ZS, OFF_DT, OFF_G = 0, 1024, 2048, 3072, 4096, 7168, 9216, 9248
EPS = 1e-6
import os as _os
DEPTH = int(_os.environ.get('KDEPTH', '4'))


class Op:
    __slots__ = ("eng", "fn", "args", "kw", "reads", "writes", "dma", "idx",
                 "deps", "signal", "token", "waits")

    def __init__(self, eng, fn, args, kw, reads, writes, dma):
        self.eng = eng
        self.fn = fn
        self.args = args
        self.kw = kw
        self.reads = reads
        self.writes = writes
        self.dma = dma
        self.deps = set()
        self.signal = False
        self.token = None
        self.waits = []


class Prog:
    def __init__(self, nc):
        self.nc = nc
        self.ops = []
        self.q = {"pe": nc.tensor, "act": nc.scalar, "dve": nc.vector,
                  "pool": nc.gpsimd, "sp": nc.sync}

    def add(self, eng, fn, *args, reads=(), writes=(), dma=None, **kw):
        op = Op(eng, fn, args, kw, tuple(reads), tuple(writes), dma)
        op.idx = len(self.ops)
        self.ops.append(op)
        return op

    def finish(self, sems, final_eng="sp"):
        ops = self.ops
        for i_, op_ in enumerate(ops):
            op_.idx = i_
        last_w = {}
        readers = {}
        sem_waiters = {}
        dma_cum = {}
        for op in ops:
            deps = set()
            raw = set()
            for r in op.reads:
                w = last_w.get(r)
                if w is not None:
                    deps.add(w)
                    raw.add(w)
            for wkey in op.writes:
                w = last_w.get(wkey)
                if w is not None:
                    deps.add(w)
                for ridx in readers.get(wkey, {}).values():
                    deps.add(ridx)
            deps.discard(op.idx)
            keep = set()
            for d in deps:
                dop = ops[d]
                if dop.dma is None and op.dma is None and dop.eng == op.eng:
                    if op.eng == "pe" or d not in raw:
                        continue
                keep.add(d)
            if op.dma is not None:
                for e, widx in sem_waiters.get(op.dma, {}).items():
                    if e != op.eng and widx != op.idx:
                        keep.add(widx)
            for d in keep:
                if ops[d].dma is not None:
                    sem_waiters.setdefault(ops[d].dma, {})[op.eng] = op.idx
            op.deps = keep
            for r in op.reads:
                rd = readers.setdefault(r, {})
                if op.dma is not None:
                    rd[("dma", op.dma)] = op.idx
                else:
                    rd[op.eng] = op.idx
            for wkey in op.writes:
                last_w[wkey] = op.idx
                readers[wkey] = {}
        eng_cnt = {}
        for op in ops:
            for d in op.deps:
                if ops[d].dma is None:
                    ops[d].signal = True
        known = {}
        for op in ops:
            kn = known.setdefault(op.eng, {})
            waits = {}
            for d in sorted(op.deps):
                dop = ops[d]
                if dop.dma is not None:
                    s = ("dma", dop.dma)
                    v = dma_cum[dop.dma]
                else:
                    s = ("eng", dop.eng)
                    v = dop.token[1]
                if kn.get(s, 0) >= v:
                    continue
                waits[s] = max(waits.get(s, 0), v)
            for s, v in waits.items():
                kn[s] = v
            op.waits = list(waits.items())
            if op.dma is not None:
                dma_cum[op.dma] = dma_cum.get(op.dma, 0) + 16
                op.token = (("dma", op.dma), dma_cum[op.dma])
            else:
                if op.signal:
                    eng_cnt[op.eng] = eng_cnt.get(op.eng, 0) + 1
                    op.token = (("eng", op.eng), eng_cnt[op.eng])
                else:
                    op.token = (("eng", op.eng), eng_cnt.get(op.eng, 0) + 1)
        n_wait = 0
        for op in ops:
            q = self.q[op.eng]
            for s, v in op.waits:
                q.wait_ge(sems(s), v)
                n_wait += 1
            ins = op.fn(*op.args, **op.kw)
            if op.dma is not None:
                ins.then_inc(sems(("dma", op.dma)), 16)
            elif op.signal:
                ins.then_inc(sems(("eng", op.eng)), 1)
        q = self.q[final_eng]
        for k, v in dma_cum.items():
            q.wait_ge(sems(("dma", k)), v)
        for e, v in eng_cnt.items():
            if e != final_eng:
                q.wait_ge(sems(("eng", e)), v)
        self.stats = dict(n_ops=len(ops), n_wait=n_wait, eng_cnt=dict(eng_cnt),
                          n_dma_sems=len(dma_cum))


def lambda_init_fn(layer_idx):
    return 0.8 - 0.6 * math.exp(-0.3 * layer_idx)


def build(depth=DEPTH, dbg=False, phases="ASGTD"):
    nc = bass.Bass("TRN2", target_bir_lowering=False)
    P = Prog(nc)
    es = contextlib.ExitStack()
    uid = [0]

    def din(name, shape, dt=F32):
        return nc.dram_tensor(name, list(shape), dt, kind="ExternalInput").ap()

    def dscr(name, shape, dt):
        return nc.dram_tensor(name, list(shape), dt, kind="ExternalOutput" if dbg else "Internal").ap()

    x_d = din("x", [S, D])
    w_in_d = din("w_in", [DEPTH, D, DIN])
    w_pa_d = din("w_pa", [DEPTH, D, D])
    w_ps_d = din("w_ps", [DEPTH, 2 * D, D])
    w_out_d = din("w_out", [DEPTH, D, D])
    nw_d = din("nw_rep", [DEPTH, 128, D])
    qkw_d = din("qkw", [DEPTH, 128, 2])
    dl_d = din("dl_rep", [DEPTH, 128, 256])
    sw_d = din("sw_rep", [DEPTH, 128, 128])
    swc_d = din("swc", [DEPTH, 128, 1])
    cw_d = din("convw", [DEPTH, 128, 24 * 4])
    cb_d = din("convb", [DEPTH, 128, 24])
    hp_d = din("hp_rep", [DEPTH, 128, 96])
    snw_d = din("snw_rep", [DEPTH, 128, 2048])
    cst_d = din("consts", [128, 6 * 128])
    out_d = nc.dram_tensor("out", [S, D], F32, kind="ExternalOutput").ap()
    yaT_d = dscr("yaT", [8, 128, S], BF16)
    ysT_d = dscr("ysT", [16, 128, S], BF16)
    sgT_d = dscr("sgT", [16, 128, S], BF16)
    xtm_d = dscr("xtm", [S, 2560], BF16)
    bct_d = dscr("bct", [8, 128, S], BF16)

    sem_cache = {}

    def sems(key):
        if key not in sem_cache:
            sem_cache[key] = es.enter_context(nc.semaphore("s%d" % len(sem_cache)))
        return sem_cache[key]

    class Scope:
        def __init__(self):
            self.es = contextlib.ExitStack()

        def __enter__(self):
            self.es.__enter__()
            return self

        def __exit__(self, *a):
            return self.es.__exit__(*a)

        def sb(self, name, shape, dt):
            uid[0] += 1
            return self.es.enter_context(nc.sbuf_tensor("%s_%d" % (name, uid[0]), list(shape), dt))

        def pool(self, name, n, shape, dt):
            return RPool(self, name, n, shape, dt)

    class RPool:
        def __init__(self, sc, name, n, shape, dt):
            self.name = name
            self.tiles = [sc.sb("%s%d" % (name, i), shape, dt) for i in range(n)]
            self.i = 0

        def next(self):
            j = self.i % len(self.tiles)
            self.i += 1
            return self.tiles[j], (self.name, j)

    def op(eng, fn, *a, r=(), w=(), dma=None, **kw):
        lk = tuple(("lock", k[1]) for k in tuple(r) + tuple(w) if isinstance(k, tuple) and k[0] == "bank")
        return P.add(eng, fn, *a, reads=tuple(r) + ("PH",), writes=tuple(w) + lk, dma=dma, **kw)

    mm = nc.tensor.matmul
    tr = nc.tensor.transpose
    act = nc.scalar.activation
    tt = nc.vector.tensor_tensor
    ts = nc.vector.tensor_scalar
    stt = nc.vector.scalar_tensor_tensor
    MUL, ADD, SUB = ALU.mult, ALU.add, ALU.subtract

    with es:
        g = Scope()
        es.enter_context(g)
        pbig = [es.enter_context(nc.psum_tensor("pbig%d" % i, [128, 1024], F32)) for i in range(4)]
        banks = [pbig[i // 2][:, (i % 2) * 512:(i % 2 + 1) * 512] for i in range(8)]

        def bk(i):
            return ("bank", i)

        def bf(i):
            return banks[i][:].bitcast(BF16)

        cst_f = g.sb("cst_f", [128, 6 * 128], F32)
        cst_b = g.sb("cst_b", [128, 6 * 128], BF16)
        op("sp", nc.sync.dma_start, out=cst_f[:], in_=cst_d, w=["cst_f"], dma="cst_f")
        op("pool", nc.gpsimd.dma_start, out=cst_b[:], in_=cst_d, w=["cst_b"], dma="cst_b")
        ident_f = cst_f[:, 0:128]
        U_f = cst_f[:, 256:384]
        Ls_f = cst_f[:, 384:512]
        ident_b = cst_b[:, 0:128]
        maskU_b = cst_b[:, 128:256]
        BD_b = cst_b[:, 512:640]
        Mneg_b = cst_b[:, 640:768]
        CST = ["cst_f", "cst_b"]
        epst = g.sb("eps", [128, 1], F32)
        op("dve", nc.vector.memset, epst[:], EPS, w=["eps"])
        ones_f = g.sb("ones_f", [128, 128], F32)
        op("dve", nc.vector.memset, ones_f[:], 1.0, w=["ones_f"])
        ones_b = g.sb("ones_b", [128, 128], BF16)
        op("dve", nc.vector.memset, ones_b[:], 1.0, w=["ones_b"])
        swc = g.sb("swc", [128, 1], F32)
        bar_t = g.sb("bar_t", [128, 1], F32)
        qkw = g.sb("qkw", [128, 2], F32)
        dl = g.sb("dl", [128, 256], F32)
        sw = g.sb("sw", [128, 128], F32)
        cw = g.sb("cw", [128, 96], F32)
        cb = g.sb("cb", [128, 24], F32)
        hp = g.sb("hp", [128, 96], F32)
        snw = g.sb("snw", [128, 2048], BF16)
        neglam = g.sb("neglam", [128, 1], F32)
        Aneg = g.sb("Aneg", [128, 32], F32)
        lamt = g.sb("lamt", [128, 4], F32)
        lamp = g.sb("lamp", [128, 128], F32)

        def barrier():
            op("dve", nc.vector.memset, bar_t[:], 0.0, w=["PH"])

        def load_params(l):
            li = lambda_init_fn(l)
            for t, src, key in ((qkw, qkw_d, "qkw"), (dl, dl_d, "dl"), (sw, sw_d, "sw"),
                                (cw, cw_d, "cw"), (cb, cb_d, "cb"), (hp, hp_d, "hp")):
                op("sp", nc.sync.dma_start, out=t[:], in_=src[l], w=[key], dma=key)
            op("pool", nc.gpsimd.dma_start, out=snw[:], in_=snw_d[l], w=["snw"], dma="snw")
            op("dve", ts, out=qkw[:, 0:1], in0=qkw[:, 0:1], scalar1=0.125, scalar2=None, op0=MUL, r=["qkw"], w=["qkw"])
            op("dve", ts, out=sw[:], in0=sw[:], scalar1=float(1.0 - li), scalar2=None, op0=MUL, r=["sw"], w=["sw"])
            op("sp", nc.sync.dma_start, out=swc[:], in_=swc_d[l], w=["swc"], dma="swc")
            op("dve", ts, out=swc[:], in0=swc[:], scalar1=float(1.0 - li), scalar2=None, op0=MUL, r=["swc"], w=["swc"])
            op("dve", tt, out=lamp[:, 0:64], in0=dl[:, 0:64], in1=dl[:, 64:128], op=MUL, r=["dl"], w=["lamp"])
            op("dve", tt, out=lamp[:, 64:128], in0=dl[:, 128:192], in1=dl[:, 192:256], op=MUL, r=["dl"], w=["lamp"])
            op("dve", nc.vector.reduce_sum, out=lamt[:, 0:2], in_=lamp[:].rearrange("p (a b) -> p a b", b=64), axis=AX.X,
               r=["lamp"], w=["lamt"])
            op("act", act, out=lamt[:, 2:4], in_=lamt[:, 0:2], func=AF.Exp, r=["lamt"], w=["lamt2"])
            op("dve", tt, out=neglam[:], in0=lamt[:, 3:4], in1=lamt[:, 2:3], op=SUB, r=["lamt2"], w=["neglam"])
            op("dve", ts, out=neglam[:], in0=neglam[:], scalar1=float(-li), scalar2=None, op0=ADD, r=["neglam"], w=["neglam"])
            op("act", act, out=Aneg[:], in_=hp[:, 32:64], func=AF.Exp, r=["hp"], w=["Aneg"])
            op("dve", ts, out=Aneg[:], in0=Aneg[:], scalar1=-1.0, scalar2=None, op0=MUL, r=["Aneg"], w=["Aneg"])

        PARAMS = ["nw", "qkw", "dl", "sw", "cw", "cb", "hp", "snw", "neglam", "Aneg"]

        def phase_A(l, hT, xsrc):
            with Scope() as sc:
                nw = sc.sb("nw", [128, D], F32)
                op("sp", nc.sync.dma_start, out=nw[:], in_=nw_d[l], w=["nw"], dma="nw")
                xt = sc.pool("xt", 2, [128, D], F32)
                hb = sc.pool("hb", 2, [128, D], BF16)
                junk = sc.sb("junk", [128, D], BF16)
                ssp = sc.pool("ss", 2, [128, 1], F32)
                rsp = sc.pool("rs", 2, [128, 1], F32)
                for t in range(32):
                    x_t, kx = xt.next()
                    op("sp", nc.sync.dma_start, out=x_t[:], in_=xsrc[t * 128:(t + 1) * 128, :],
                       r=[("xres", t // 4)], w=[kx], dma=kx)
                    s_t, ks = ssp.next()
                    op("act", act, out=junk[:], in_=x_t[:], func=AF.Square, accum_out=s_t[:], r=[kx], w=["junkA", ks])
                    r_t, kr = rsp.next()
                    op("act", act, out=r_t[:], in_=s_t[:], func=AF.Sqrt, bias=epst[:], scale=1.0 / D,
                       r=[ks, "eps"], w=[kr])
                    op("dve", nc.vector.reciprocal, out=r_t[:], in_=r_t[:], r=[kr], w=[kr])
                    h_t, kh = hb.next()
                    op("dve", stt, out=h_t[:], in0=x_t[:], scalar=r_t[:], in1=nw[:], op0=MUL, op1=MUL,
                       r=[kx, kr, "nw"], w=[kh])
                    b = t % 2
                    ptv = bf(b).rearrange("p (a b) -> p a b", b=128)
                    for kc in range(8):
                        op("pe", tr, out=ptv[:, kc, :], in_=h_t[:, kc * 128:(kc + 1) * 128], identity=ident_b,
                           r=[kh] + CST, w=[bk(b)])
                    if t % 2 == 0:
                        op("act", nc.scalar.copy, out=hT[:, :, t * 128:(t + 1) * 128], in_=ptv, r=[bk(b)], w=[("hT", t)])
                    else:
                        op("dve", nc.vector.tensor_copy, out=hT[:, :, t * 128:(t + 1) * 128], in_=ptv, r=[bk(b)],
                           w=[("hT", t)])

        def hTk(tb):
            return [("hT", 4 * tb + i) for i in range(4)]

        def phase_S1(l, hT):
            w_l = w_in_d[l].rearrange("(kc p) n -> p kc n", p=128)
            xtm_v = xtm_d.rearrange("(t p) c -> p t c", p=128)
            with Scope() as sc:
                wp = sc.pool("wcc", 2, [128, 8, 128], BF16)
                xcp = sc.pool("xc", 2, [128, S + 4], BF16)
                dwp = sc.pool("dw", 2, [128, 4, 128], BF16)
                xop = sc.pool("xo", 2, [128, S], BF16)
                tmp_ = sc.pool("tmt", 3, [128, 8, 128], BF16)
                for t_, k_ in ((xcp.tiles[0], (xcp.name, 0)), (xcp.tiles[1], (xcp.name, 1))):
                    op("dve", nc.vector.memset, t_[:, 0:3], 0.0, w=[k_])
                nb = 0
                nc_ = 0
                for cc in range(24):
                    w_t, kw_ = wp.next()
                    c0 = OFF_XBC + cc * 128
                    op("pool", nc.gpsimd.dma_start, out=w_t[:], in_=w_l[:, :, c0:c0 + 128], w=[kw_], dma=kw_)
                    dw, kdw = dwp.next()
                    for k in range(4):
                        op("pool", nc.gpsimd.tensor_scalar, out=dw[:, k, :], in0=ident_f, scalar1=cw[:, cc * 4 + k:cc * 4 + k + 1],
                           scalar2=None, op0=MUL, r=["cw"] + CST, w=[kdw])
                    xc, kxc = xcp.next()
                    for tb in range(8):
                        b = nb % 4
                        nb += 1
                        for kc in range(8):
                            op("pe", mm, banks[b][:], lhsT=w_t[:, kc, :], rhs=hT[:, kc, tb * 512:(tb + 1) * 512],
                               start=(kc == 0), stop=(kc == 7), r=[kw_] + hTk(tb), w=[bk(b)])
                        op("act", nc.scalar.copy, out=xc[:, 3 + tb * 512: 3 + (tb + 1) * 512], in_=banks[b][:],
                           r=[bk(b)], w=[kxc + (tb,)])
                    xo, kxo = xop.next()
                    for tb in range(8):
                        b = 6 + (nc_ % 2)
                        nc_ += 1
                        rk = [kxc + (tb,)] + ([kxc + (tb - 1,)] if tb > 0 else [kxc])
                        for k in range(4):
                            op("pe", mm, banks[b][:], lhsT=dw[:, k, :], rhs=xc[:, tb * 512 + k: tb * 512 + k + 512],
                               start=(k == 0), stop=(k == 3), r=[kdw] + rk, w=[bk(b)])
                        op("act", act, out=xo[:, tb * 512:(tb + 1) * 512], in_=banks[b][:], func=AF.Silu, bias=cb[:, cc:cc + 1],
                           r=[bk(b), "cb"], w=[kxo])
                    if cc >= 16:
                        op("sp", nc.sync.dma_start, out=bct_d[cc - 16], in_=xo[:], r=[kxo], w=[("bct", cc - 16)], dma=kxo)
                    if cc < 20:
                        for q4 in range(4):
                            b = 4 + (q4 % 2)
                            ptv = bf(b).rearrange("p (a b) -> p a b", b=128)
                            for i in range(8):
                                t0 = (q4 * 8 + i) * 128
                                op("pe", tr, out=ptv[:, i, :], in_=xo[:, t0:t0 + 128], identity=ident_b,
                                   r=[kxo] + CST, w=[bk(b)])
                            tm, ktm = tmp_.next()
                            if q4 % 2 == 0:
                                op("act", nc.scalar.copy, out=tm[:], in_=ptv, r=[bk(b)], w=[ktm])
                            else:
                                op("dve", nc.vector.tensor_copy, out=tm[:], in_=ptv, r=[bk(b)], w=[ktm])
                            op("sp", nc.sync.dma_start, out=xtm_v[:, q4 * 8:(q4 + 1) * 8, cc * 128:(cc + 1) * 128],
                               in_=tm[:], r=[ktm], w=[("xtm", cc, q4)], dma=ktm)

        def phase_S2(l, hT):
            w_l = w_in_d[l].rearrange("(kc p) n -> p kc n", p=128)
            bct_v = bct_d.rearrange("g p t -> p g t")
            ysT_v = ysT_d.rearrange("k p t -> p k t")
            with Scope() as sc:
                wzs = sc.sb("wzs", [128, 8, 2048], BF16)
                wdt = sc.sb("wdt", [128, 8, 32], BF16)
                op("pool", nc.gpsimd.dma_start, out=wzs[:], in_=w_l[:, :, OFF_ZS:OFF_ZS + 2048], w=["wzs"], dma="wzs")
                op("pool", nc.gpsimd.dma_start, out=wdt[:], in_=w_l[:, :, OFF_DT:OFF_DT + 32], w=["wdt"], dma="wdt")
                st = sc.sb("st", [128, 2048], F32)
                stb = sc.sb("stb", [128, 2048], BF16)
                op("dve", nc.vector.memset, st[:], 0.0, w=[("st", i) for i in range(4)])
                op("dve", nc.vector.memset, stb[:], 0.0, w=[("stb", i) for i in range(4)])
                xtp = sc.pool("xtc", 2, [128, 2560], BF16)
                bcp = sc.pool("bcc", 2, [128, 8, 128], BF16)
                smp = sc.pool("sm", 2, [128, 8, 32], F32)
                xdtp = sc.pool("xdt", 2, [128, 32, 64], BF16)
                xdsp = sc.pool("xds", 2, [128, 32, 64], BF16)
                xDp = sc.pool("xD", 2, [128, 32, 64], BF16)
                szap = sc.pool("szall", 2, [128, 2048], BF16)
                cbmp = sc.pool("cbm", 2, [128, 4, 128], BF16)
                ltp = sc.pool("lt", 2, [128, 4, 128], F32)
                dcp = sc.pool("dcT", 2, [128, 4, 128], BF16)
                mtap = sc.pool("MTall", 2, [128, 32, 128], BF16)
                sztp = sc.pool("szt", 1, [128, 512], F32)
                t1p = sc.pool("t1", 2, [128, 512], F32)
                gnp = sc.pool("gn", 2, [128, 512], BF16)
                junk = sc.sb("junkS", [128, 512], BF16)
                sqp = sc.pool("ssq", 2, [128, 1], F32)
                rqp = sc.pool("rsq", 2, [128, 1], F32)
                ysp = sc.pool("ysg", 3, [128, 4, 128], BF16)
                ctx = {}

                def stageA(c):
                    tok = slice(c * 128, (c + 1) * 128)
                    hk = [("hT", c)]
                    xt_c, kxt = xtp.next()
                    op("sp", nc.sync.dma_start, out=xt_c[:], in_=xtm_d[tok, :],
                       r=[("xtm", cc, c // 8) for cc in range(20)], w=[kxt], dma=kxt)
                    bc_c, kbc = bcp.next()
                    op("sp", nc.sync.dma_start, out=bc_c[:], in_=bct_v[:, :, tok],
                       r=[("bct", i) for i in range(8)], w=[kbc], dma=kbc)
                    sm, ksm = smp.next()
                    for kc in range(8):
                        op("pe", mm, banks[0][:, 0:32], lhsT=hT[:, kc, tok], rhs=wdt[:, kc, :], start=(kc == 0),
                           stop=(kc == 7), r=hk + ["wdt"], w=[bk(0)])
                    op("dve", tt, out=sm[:, 0, :], in0=banks[0][:, 0:32], in1=hp[:, 0:32], op=ADD, r=[bk(0), "hp"], w=[ksm])
                    op("act", act, out=sm[:, 1, :], in_=sm[:, 0, :], func=AF.Exp, r=[ksm], w=[ksm])
                    op("act", act, out=sm[:, 2, :], in_=sm[:, 1, :], func=AF.Ln, bias=1.0, r=[ksm], w=[ksm])
                    op("dve", tt, out=sm[:, 3, :], in0=sm[:, 2, :], in1=Aneg[:], op=MUL, r=[ksm, "Aneg"], w=[ksm])
                    op("pe", mm, banks[0][:, 32:64], lhsT=U_f, rhs=sm[:, 3, :], start=True, stop=True,
                       r=[ksm] + CST, w=[bk(0)])
                    op("pe", mm, banks[0][:, 64:96], lhsT=ones_f[:], rhs=sm[:, 3, :], start=True, stop=True,
                       r=[ksm, "ones_f"], w=[bk(0)])
                    op("dve", nc.vector.tensor_copy, out=sm[:, 4, :], in_=banks[0][:, 32:64], r=[bk(0)], w=[ksm])
                    op("act", act, out=sm[:, 5, :], in_=banks[0][:, 32:64], func=AF.Exp, r=[bk(0)], w=[ksm])
                    op("dve", tt, out=sm[:, 6, :], in0=banks[0][:, 64:96], in1=sm[:, 4, :], op=SUB, r=[bk(0), ksm], w=[ksm])
                    op("act", act, out=sm[:, 6, :], in_=sm[:, 6, :], func=AF.Exp, r=[ksm], w=[ksm])
                    op("act", act, out=sm[:, 7, :], in_=banks[0][:, 64:96], func=AF.Exp, r=[bk(0)], w=[ksm])
                    xv = xt_c[:, 0:2048].rearrange("p (a b) -> p a b", b=64)
                    xdt, kxdt = xdtp.next()
                    op("dve", tt, out=xdt[:], in0=xv, in1=sm[:, 2, :].unsqueeze(2).broadcast_to([128, 32, 64]), op=MUL,
                       r=[kxt, ksm], w=[kxdt])
                    xds, kxds = xdsp.next()
                    op("pool", nc.gpsimd.tensor_tensor, out=xds[:], in0=xdt[:],
                       in1=sm[:, 6, :].unsqueeze(2).broadcast_to([128, 32, 64]), op=MUL, r=[kxdt, ksm], w=[kxds])
                    xD, kxD = xDp.next()
                    op("pool", nc.gpsimd.tensor_tensor, out=xD[:], in0=xv,
                       in1=hp[:, 64:96].unsqueeze(2).broadcast_to([128, 32, 64]), op=MUL, r=[kxt, "hp"], w=[kxD])
                    cbv = banks[1][:].rearrange("p (a b) -> p a b", b=128)
                    for gi in range(4):
                        op("pe", mm, cbv[:, gi, :], lhsT=bc_c[:, gi, :], rhs=bc_c[:, 4 + gi, :], start=True, stop=True,
                           r=[kbc], w=[bk(1)])
                    cbm, kcbm = cbmp.next()
                    op("dve", tt, out=cbm[:], in0=cbv, in1=maskU_b.unsqueeze(1).broadcast_to([128, 4, 128]), op=MUL,
                       r=[bk(1)] + CST, w=[kcbm])
                    mta, kmta = mtap.next()
                    sza, ksza = szap.next()
                    for q8 in range(8):
                        gi = q8 // 2
                        h0 = q8 * 4
                        lt, klt = ltp.next()
                        op("pool", nc.gpsimd.tensor_tensor, out=lt[:],
                           in0=Ls_f.unsqueeze(1).broadcast_to([128, 4, 128]),
                           in1=sm[:, 3, h0:h0 + 4].unsqueeze(2).broadcast_to([128, 4, 128]), op=MUL,
                           r=[ksm] + CST, w=[klt])
                        rbv = banks[2][:].rearrange("p (a b) -> p a b", b=128)
                        for i in range(4):
                            op("pe", mm, rbv[:, i, :], lhsT=lt[:, i, :], rhs=U_f, start=True, stop=True,
                               r=[klt] + CST, w=[bk(2)])
                        dc, kdc = dcp.next()
                        op("act", act, out=dc[:], in_=rbv, func=AF.Exp, r=[bk(2)], w=[kdc])
                        op("dve", tt, out=mta[:, h0:h0 + 4, :], in0=dc[:], in1=cbm[:, gi:gi + 1, :].broadcast_to([128, 4, 128]),
                           op=MUL, r=[kdc, kcbm], w=[kmta + (q8,)])
                        if q8 % 2 == 1:
                            for kc in range(8):
                                op("pe", mm, banks[3][:], lhsT=hT[:, kc, tok], rhs=wzs[:, kc, gi * 512:(gi + 1) * 512],
                                   start=(kc == 0), stop=(kc == 7), r=hk + ["wzs"], w=[bk(3)])
                            szt, kszt = sztp.next()
                            op("act", act, out=szt[:], in_=banks[3][:], func=AF.Exp, scale=-1.0, r=[bk(3)], w=[kszt])
                            op("act", act, out=szt[:], in_=szt[:], func=AF.Ln, bias=1.0, r=[kszt], w=[kszt])
                            op("act", act, out=szt[:], in_=szt[:], func=AF.Exp, scale=-1.0, r=[kszt], w=[kszt])
                            op("dve", tt, out=sza[:, gi * 512:(gi + 1) * 512], in0=banks[3][:], in1=szt[:], op=MUL,
                               r=[bk(3), kszt], w=[ksza + (gi,)])
                    ctx[c] = dict(tok=tok, xt_c=xt_c, kxt=kxt, bc_c=bc_c, kbc=kbc, sm=sm, ksm=ksm, xdt=xdt, kxdt=kxdt,
                                  xds=xds, kxds=kxds, xD=xD, kxD=kxD, mta=mta, kmta=kmta, sza=sza, ksza=ksza)

                def stageB(c):
                    d_ = ctx.pop(c)
                    tok, xt_c, kxt, bc_c, kbc = d_["tok"], d_["xt_c"], d_["kxt"], d_["bc_c"], d_["kbc"]
                    sm, ksm, xdt, kxdt, xds, kxds = d_["sm"], d_["ksm"], d_["xdt"], d_["kxdt"], d_["xds"], d_["kxds"]
                    xD, kxD, mta, kmta, sza, ksza = d_["xD"], d_["kxD"], d_["mta"], d_["kmta"], d_["sza"], d_["ksza"]
                    for pair in ((0, 1), (2, 3)):
                        YB = {pair[0]: 4, pair[1]: 6}
                        XB = {pair[0]: 5, pair[1]: 7}
                        T = {}
                        for gi in pair:
                            yb_, xb_ = YB[gi], XB[gi]
                            op("pe", mm, banks[yb_][:], lhsT=ident_b, rhs=xD[:, gi * 8:(gi + 1) * 8, :], start=True, stop=False,
                               r=[kxD] + CST, w=[bk(yb_)])
                            for j in range(8):
                                hh = gi * 8 + j
                                op("pe", mm, banks[yb_][:, j * 64:(j + 1) * 64], lhsT=mta[:, hh, :], rhs=xdt[:, hh, :],
                                   start=False, stop=(j == 7), skip_group_check=True,
                                   r=[kmta + (hh // 4,), kxdt], w=[bk(yb_)])
                            op("pe", mm, banks[xb_][:], lhsT=bc_c[:, 4 + gi, :], rhs=stb[:, gi * 512:(gi + 1) * 512],
                               start=True, stop=True, r=[kbc, ("stb", gi)], w=[bk(xb_)])
                        for gi in pair:
                            t1, kt1 = t1p.next()
                            T[gi] = (t1, kt1)
                            op("dve", tt, out=t1[:].rearrange("p (a b) -> p a b", b=64),
                               in0=banks[XB[gi]][:].rearrange("p (a b) -> p a b", b=64),
                               in1=sm[:, 5, gi * 8:(gi + 1) * 8].unsqueeze(2).broadcast_to([128, 8, 64]), op=MUL,
                               r=[bk(XB[gi]), ksm], w=[kt1])
                        for gi in pair:
                            t1, kt1 = T[gi]
                            op("dve", tt, out=t1[:], in0=banks[YB[gi]][:], in1=t1[:], op=ADD, r=[bk(YB[gi]), kt1], w=[kt1])
                        for gi in pair:
                            op("pe", mm, banks[XB[gi]][:], lhsT=xt_c[:, 2048 + gi * 128:2048 + (gi + 1) * 128],
                               rhs=xds[:, gi * 8:(gi + 1) * 8, :], start=True, stop=True, r=[kxt, kxds], w=[bk(XB[gi])])
                        for gi in pair:
                            t1, kt1 = T[gi]
                            op("pool", nc.gpsimd.tensor_tensor, out=t1[:], in0=t1[:], in1=sza[:, gi * 512:(gi + 1) * 512],
                               op=MUL, r=[kt1, ksza + (gi,)], w=[kt1])
                        R = {}
                        for gi in pair:
                            t1, kt1 = T[gi]
                            sq, ksq = sqp.next()
                            op("act", act, out=junk[:], in_=t1[:], func=AF.Square, accum_out=sq[:], r=[kt1], w=["junkS", ksq])
                            rq, krq = rqp.next()
                            R[gi] = (sq, ksq, rq, krq)
                        for gi in pair:
                            sq, ksq, rq, krq = R[gi]
                            op("act", act, out=rq[:], in_=sq[:], func=AF.Ln, bias=epst[:], scale=1.0 / 512.0,
                               r=[ksq, "eps"], w=[krq])
                        for gi in pair:
                            sq, ksq, rq, krq = R[gi]
                            op("act", act, out=rq[:], in_=rq[:], func=AF.Exp, scale=-0.5, r=[krq], w=[krq])
                        for gi in pair:
                            stv = st[:, gi * 512:(gi + 1) * 512]
                            op("pool", nc.gpsimd.tensor_tensor, out=stv.rearrange("p (a b) -> p a b", b=64),
                               in0=stv.rearrange("p (a b) -> p a b", b=64),
                               in1=sm[:, 7, gi * 8:(gi + 1) * 8].unsqueeze(2).broadcast_to([128, 8, 64]), op=MUL,
                               r=[("st", gi), ksm], w=[("st", gi)])
                        for gi in pair:
                            stv = st[:, gi * 512:(gi + 1) * 512]
                            op("dve", tt, out=stv, in0=banks[XB[gi]][:], in1=stv, op=ADD, r=[bk(XB[gi]), ("st", gi)],
                               w=[("st", gi)])
                        G = {}
                        for gi in pair:
                            t1, kt1 = T[gi]
                            sq, ksq, rq, krq = R[gi]
                            gn, kgn = gnp.next()
                            G[gi] = (gn, kgn)
                            op("dve", stt, out=gn[:], in0=t1[:], scalar=rq[:], in1=snw[:, gi * 512:(gi + 1) * 512],
                               op0=MUL, op1=MUL, r=[kt1, krq, "snw"], w=[kgn])
                        for gi in pair:
                            stv = st[:, gi * 512:(gi + 1) * 512]
                            op("act", nc.scalar.copy, out=stb[:, gi * 512:(gi + 1) * 512], in_=stv, r=[("st", gi)],
                               w=[("stb", gi)])
                        for gi in pair:
                            gn, kgn = G[gi]
                            ptv = bf(XB[gi]).rearrange("p (a b) -> p a b", b=128)
                            for i in range(4):
                                op("pe", tr, out=ptv[:, i, :], in_=gn[:, i * 128:(i + 1) * 128], identity=ident_b,
                                   r=[kgn] + CST, w=[bk(XB[gi])])
                        for gi in pair:
                            ptv = bf(XB[gi]).rearrange("p (a b) -> p a b", b=128)
                            ys_g, kys = ysp.next()
                            op("act", nc.scalar.copy, out=ys_g[:], in_=ptv[:, 0:4, :], r=[bk(XB[gi])], w=[kys])
                            op("sp", nc.sync.dma_start, out=ysT_v[:, gi * 4:(gi + 1) * 4, tok], in_=ys_g[:], r=[kys],
                               w=[("ysT", c // 4, gi)], dma=kys)

                def record(fn, c):
                    saved = P.ops
                    P.ops = []
                    fn(c)
                    lst = P.ops
                    P.ops = saved
                    return lst

                stageA(0)
                for c in range(32):
                    la = record(stageA, c + 1) if c + 1 < 32 else []
                    lb = record(stageB, c)
                    ia = ib = 0
                    while ia < len(la) or ib < len(lb):
                        if ib >= len(lb) or (ia < len(la) and ia * len(lb) <= ib * len(la)):
                            P.ops.append(la[ia])
                            ia += 1
                        else:
                            P.ops.append(lb[ib])
                            ib += 1

        def phase_G(l, hT):
            w_l = w_in_d[l].rearrange("(kc p) n -> p kc n", p=128)
            with Scope() as sc:
                wp = sc.pool("wg", 2, [128, 8, 128], BF16)
                sgp = sc.pool("sg", 2, [128, S], BF16)
                nb = 0
                for gc in range(16):
                    w_t, kw_ = wp.next()
                    c0 = OFF_G + gc * 128
                    op("pool", nc.gpsimd.dma_start, out=w_t[:], in_=w_l[:, :, c0:c0 + 128], w=[kw_], dma=kw_)
                    sg, ksg = sgp.next()
                    for tb in range(8):
                        b = nb % 4
                        nb += 1
                        for kc in range(8):
                            op("pe", mm, banks[b][:], lhsT=w_t[:, kc, :], rhs=hT[:, kc, tb * 512:(tb + 1) * 512],
                               start=(kc == 0), stop=(kc == 7), r=[kw_] + hTk(tb), w=[bk(b)])
                        op("act", act, out=sg[:, tb * 512:(tb + 1) * 512], in_=banks[b][:], func=AF.Sigmoid,
                           r=[bk(b)], w=[ksg])
                    op("sp", nc.sync.dma_start, out=sgT_d[gc], in_=sg[:], r=[ksg], w=[("sgT", gc)], dma=ksg)

        def phase_T(l, hT):
            w_l = w_in_d[l].rearrange("(kc p) n -> p kc n", p=128)
            LOOK = 2
            LSC = 2.0 ** -10
            CSC = EPS / (LSC * LSC)
            with Scope() as sc:
                whp = sc.pool("wh", 2, [128, 8, 4, 128], BF16)
                qz = [sc.sb("qz%d" % m, [128, S], BF16) for m in range(2)]
                kT = sc.sb("kT", [128, S], BF16)
                vt = sc.sb("vt", [128, 32, 128], BF16)
                szT = sc.sb("szT", [128, S], BF16)
                sqp = sc.pool("sq", 2, [128, 512], BF16)
                lnp = sc.pool("lnq", 2, [128, 512], F32)
                rsp = sc.pool("rst", 2, [128, 512], F32)
                ezp = sc.pool("ez", 2, [128, 512], F32)
                ptp = sc.pool("PT", 4, [128, 2, 512], BF16)
                s1pp = sc.pool("s1p", 2, [128, 512], F32)
                s0cp = sc.pool("s0c", 2, [128, 512], F32)
                s1cp = sc.pool("s1c", 2, [128, 512], F32)
                lb0p = sc.pool("lb0", 1, [128, 512], F32)
                lb1p = sc.pool("lb1", 1, [128, 512], F32)
                u0p = sc.pool("u0", 1, [128, 512], F32)
                u1p = sc.pool("u1", 1, [128, 512], F32)
                tqp = sc.pool("tq", 1, [128, 512], F32)
                sqo = sc.pool("sqo", 1, [128, 512], BF16)
                agp = sc.pool("arg", 1, [128, 512], F32)
                ybp = sc.pool("yb", 2, [128, 512], BF16)
                op("dve", nc.vector.memset, qz[0][64:128, :], 0.0, w=["qz0"])
                op("dve", nc.vector.memset, qz[1][0:64, :], 0.0, w=["qz1"])
                nS = [0]
                nI = [0]
                pairs = ((2, 3), (4, 5))
                for h in range(8):
                    wh, kwh = whp.next()
                    kws = []
                    for i, off in enumerate((OFF_Q, OFF_K, OFF_V, OFF_ZA)):
                        c0 = off + h * 128
                        kwi = kwh + (i,)
                        kws.append(kwi)
                        op("pool", nc.gpsimd.dma_start, out=wh[:, :, i, :], in_=w_l[:, :, c0:c0 + 128], w=[kwi], dma=kwi)
                    for which in (0, 1):
                        for tb in range(8):
                            ba = 2 + 2 * (nI[0] % 2)
                            bs = ba + 1
                            nI[0] += 1
                            cs = slice(tb * 512, (tb + 1) * 512)
                            for kc in range(8):
                                op("pe", mm, banks[ba][:], lhsT=wh[:, kc, which, :], rhs=hT[:, kc, cs],
                                   start=(kc == 0), stop=(kc == 7), r=[kws[which]] + hTk(tb), w=[bk(ba)])
                            sq, ksq = sqp.next()
                            op("act", act, out=sq[:], in_=banks[ba][:], func=AF.Square, r=[bk(ba)], w=[ksq])
                            op("pe", mm, banks[bs][:], lhsT=BD_b, rhs=sq[:], start=True, stop=True, r=[ksq] + CST, w=[bk(bs)])
                            ln, kln = lnp.next()
                            op("act", act, out=ln[:], in_=banks[bs][:], func=AF.Ln, bias=epst[:], scale=1.0 / 64.0,
                               r=[bk(bs), "eps"], w=[kln])
                            rs, krs = rsp.next()
                            op("act", act, out=rs[:], in_=ln[:], func=AF.Exp, scale=-0.5, r=[kln], w=[krs])
                            if which == 0:
                                for m in range(2):
                                    pr = slice(m * 64, (m + 1) * 64)
                                    op("dve", stt, out=qz[m][pr, cs], in0=banks[ba][pr, :], scalar=qkw[pr, 0:1], in1=rs[pr, :],
                                       op0=MUL, op1=MUL, r=[bk(ba), krs, "qkw"], w=["qz%d" % m])
                            else:
                                op("dve", stt, out=kT[:, cs], in0=banks[ba][:], scalar=qkw[:, 1:2], in1=rs[:],
                                   op0=MUL, op1=MUL, r=[bk(ba), krs, "qkw"], w=["kT"])
                    for tb in range(8):
                        ba = 2 + (nI[0] % 4)
                        nI[0] += 1
                        cs = slice(tb * 512, (tb + 1) * 512)
                        for kc in range(8):
                            op("pe", mm, banks[ba][:], lhsT=wh[:, kc, 3, :], rhs=hT[:, kc, cs],
                               start=(kc == 0), stop=(kc == 7), r=[kws[3]] + hTk(tb), w=[bk(ba)])
                        ez, kez = ezp.next()
                        op("act", act, out=ez[:], in_=banks[ba][:], func=AF.Exp, scale=-1.0, r=[bk(ba)], w=[kez])
                        op("act", act, out=ez[:], in_=ez[:], func=AF.Ln, bias=1.0, r=[kez], w=[kez])
                        op("act", act, out=ez[:], in_=ez[:], func=AF.Exp, scale=-1.0, r=[kez], w=[kez])
                        op("dve", stt, out=szT[:, cs], in0=banks[ba][:], scalar=swc[:, 0:1], in1=ez[:], op0=MUL, op1=MUL,
                           r=[bk(ba), kez, "swc"], w=["szT"])
                    for t4 in range(8):
                        b = 2 + (nI[0] % 4)
                        nI[0] += 1
                        pv = banks[b][:].rearrange("p (a b) -> p a b", b=128)
                        for i in range(4):
                            t = t4 * 4 + i
                            for kc in range(8):
                                op("pe", mm, pv[:, i, :], lhsT=hT[:, kc, t * 128:(t + 1) * 128], rhs=wh[:, kc, 2, :],
                                   start=(kc == 0), stop=(kc == 7), r=[kws[2], ("hT", t)], w=[bk(b)])
                        op("dve", nc.vector.tensor_copy, out=vt[:, t4 * 4:(t4 + 1) * 4, :], in_=pv, r=[bk(b)], w=["vt"])
                    steps = [(qb, t) for qb in range(8) for t in range(4 * qb + 4)]
                    pts = {}
                    s1ps = {}

                    def emit_qk(j):
                        qb, t = steps[j]
                        off = max(0, t - 4 * qb) * 128
                        pi = nS[0] % 2
                        nS[0] += 1
                        pb = pairs[pi]
                        diag = t >= 4 * qb
                        for m in range(2):
                            op("pe", mm, banks[pb[m]][:, 0:512 - off], lhsT=kT[:, t * 128:(t + 1) * 128],
                               rhs=qz[m][:, qb * 512 + off:(qb + 1) * 512], start=True, stop=not diag,
                               r=["kT", "qz%d" % m], w=[bk(pb[m])])
                            if diag:
                                op("pe", mm, banks[pb[m]][:, 0:128], lhsT=ident_b, rhs=Mneg_b, start=False, stop=True,
                                   r=CST, w=[bk(pb[m])])
                        pt, kpt = ptp.next()
                        pview = pbig[1 + pi][:].rearrange("p (a b) -> p a b", b=512)
                        op("act", act, out=pt[:, :, 0:512 - off], in_=pview[:, :, 0:512 - off], func=AF.Exp,
                           r=[bk(pb[0]), bk(pb[1])], w=[kpt])
                        pts[j] = (pt, kpt)

                    def emit_pv(i):
                        qb, t = steps[i]
                        off = max(0, t - 4 * qb) * 128
                        pt, kpt = pts.pop(i)
                        for m in range(2):
                            op("pe", mm, banks[m][:, off:512], lhsT=vt[:, t, :], rhs=pt[:, m, 0:512 - off],
                               start=(t == 0), stop=(t == 4 * qb + 3), r=[kpt, "vt"], w=[bk(m)])
                        if t == 0:
                            op("dve", nc.vector.tensor_copy, out=banks[6][:], in_=pt[:, 0, :], r=[kpt], w=[bk(6)])
                            op("dve", nc.vector.tensor_copy, out=banks[7][:], in_=pt[:, 1, :], r=[kpt], w=[bk(7)])
                            s1ps[qb] = s1pp.next()
                            op("pool", nc.gpsimd.memset, s1ps[qb][0][:], 0.0, w=[s1ps[qb][1]])
                        else:
                            op("dve", tt, out=banks[6][:, off:512], in0=banks[6][:, off:512], in1=pt[:, 0, 0:512 - off],
                               op=ADD, r=[kpt, bk(6)], w=[bk(6)])
                            if t % 3 == 0:
                                op("dve", tt, out=banks[7][:, off:512], in0=banks[7][:, off:512], in1=pt[:, 1, 0:512 - off],
                                   op=ADD, r=[kpt, bk(7)], w=[bk(7)])
                            else:
                                sp_, ksp = s1ps[qb]
                                op("pool", nc.gpsimd.tensor_tensor, out=sp_[:, off:512], in0=sp_[:, off:512],
                                   in1=pt[:, 1, 0:512 - off], op=ADD, r=[kpt, ksp], w=[ksp])

                    def emit_fin(qb):
                        cs = slice(qb * 512, (qb + 1) * 512)
                        s1p_, ks1p = s1ps.pop(qb)
                        s0c, ks0c = s0cp.next()
                        s1c, ks1c = s1cp.next()
                        op("dve", nc.vector.tensor_copy, out=s0c[:], in_=banks[6][:], r=[bk(6)], w=[ks0c])
                        op("dve", nc.vector.tensor_copy, out=s1c[:], in_=banks[7][:], r=[bk(7)], w=[ks1c])
                        pi = nS[0] % 2
                        nS[0] += 1
                        bx, by = pairs[pi]
                        op("pe", mm, banks[bx][:], lhsT=ones_f[:], rhs=s0c[:], start=True, stop=True, r=[ks0c, "ones_f"], w=[bk(bx)])
                        op("pe", mm, banks[by][:], lhsT=ones_f[:], rhs=s1c[:], start=True, stop=False, r=[ks1c, "ones_f"], w=[bk(by)])
                        op("pe", mm, banks[by][:], lhsT=ones_f[:], rhs=s1p_[:], start=False, stop=True, r=[ks1p, "ones_f"], w=[bk(by)])
                        lb0, kl0 = lb0p.next()
                        lb1, kl1 = lb1p.next()
                        op("act", act, out=lb0[:], in_=banks[bx][:], func=AF.Copy, scale=LSC, r=[bk(bx)], w=[kl0])
                        op("act", act, out=lb1[:], in_=banks[by][:], func=AF.Copy, scale=LSC, r=[bk(by)], w=[kl1])
                        u0, ku0 = u0p.next()
                        u1, ku1 = u1p.next()
                        op("dve", tt, out=u1[:], in0=banks[1][:], in1=lb0[:], op=MUL, r=[bk(1), kl0], w=[ku1])
                        op("dve", tt, out=u0[:], in0=banks[0][:], in1=lb1[:], op=MUL, r=[bk(0), kl1], w=[ku0])
                        op("dve", stt, out=u0[:], in0=u1[:], scalar=neglam[:, 0:1], in1=u0[:], op0=MUL, op1=ADD,
                           r=[ku0, ku1, "neglam"], w=[ku0])
                        tq, ktq = tqp.next()
                        op("pool", nc.gpsimd.tensor_tensor, out=tq[:], in0=lb0[:], in1=lb1[:], op=MUL, r=[kl0, kl1], w=[ktq])
                        op("pool", nc.gpsimd.tensor_tensor, out=tq[:], in0=tq[:], in1=tq[:], op=MUL, r=[ktq], w=[ktq])
                        sq, ksq = sqo.next()
                        op("pool", nc.gpsimd.tensor_tensor, out=sq[:], in0=u0[:], in1=u0[:], op=MUL, r=[ku0], w=[ksq])
                        op("pe", mm, banks[bx][:], lhsT=ones_b[:], rhs=sq[:], start=True, stop=True, r=[ksq, "ones_b"], w=[bk(bx)])
                        ag, kag = agp.next()
                        op("dve", stt, out=ag[:], in0=banks[bx][:], scalar=float(1.0 / (128.0 * CSC)), in1=tq[:],
                           op0=MUL, op1=ADD, r=[bk(bx), ktq], w=[kag])
                        op("act", act, out=ag[:], in_=ag[:], func=AF.Ln, r=[kag], w=[kag])
                        op("act", act, out=ag[:], in_=ag[:], func=AF.Exp, scale=-0.5, r=[kag], w=[kag])
                        op("dve", stt, out=u0[:], in0=u0[:], scalar=float(CSC ** -0.5), in1=ag[:], op0=MUL, op1=MUL,
                           r=[ku0, kag], w=[ku0])
                        yb, kyb = ybp.next()
                        op("pool", nc.gpsimd.tensor_tensor, out=yb[:], in0=u0[:], in1=szT[:, cs], op=MUL,
                           r=[ku0, "szT"], w=[kyb])
                        op("sp", nc.sync.dma_start, out=yaT_d[h][:, cs], in_=yb[:], r=[kyb], w=[("yaT", h, qb)], dma=kyb)

                    n = len(steps)
                    for i in range(-LOOK, n):
                        j = i + LOOK
                        if j < n:
                            emit_qk(j)
                        if i >= 0:
                            emit_pv(i)
                            qb, t = steps[i]
                            if t == 4 * qb + 3:
                                emit_fin(qb)

        def phase_D(l, xsrc):
            wpa_v = w_pa_d[l].rearrange("(kc p) n -> p kc n", p=128)
            wps_v = w_ps_d[l].rearrange("(kc p) n -> p kc n", p=128)
            wo_v = w_out_d[l].rearrange("(kc p) n -> p kc n", p=128)
            yaT_v = yaT_d.rearrange("h p t -> p h t")
            ysT_v = ysT_d.rearrange("k p t -> p k t")
            sgT_v = sgT_d.rearrange("k p t -> p k t")
            with Scope() as sc:
                wpa = sc.sb("wpa", [128, 8, D], BF16)
                wps = sc.sb("wps", [128, 16, D], BF16)
                wo = sc.sb("wo", [128, 8, D], BF16)
                op("pool", nc.gpsimd.dma_start, out=wpa[:], in_=wpa_v, w=["wpa"], dma="wpa")
                op("pool", nc.gpsimd.dma_start, out=wps[:, 0:8], in_=wps_v[:, 0:8], w=["wps"], dma="wps")
                op("pool", nc.gpsimd.dma_start, out=wps[:, 8:16], in_=wps_v[:, 8:16], w=["wps"], dma="wps")
                op("pool", nc.gpsimd.dma_start, out=wo[:], in_=wo_v, w=["wo"], dma="wo")
                yap = sc.pool("yab", 2, [128, 8, 512], BF16)
                ysp = sc.pool("ysb", 2, [128, 16, 512], BF16)
                sgp = sc.pool("sgb", 1, [128, 16, 512], BF16)
                xrp = sc.pool("xr", 1, [128, 4, D], F32)
                m1p = sc.pool("m1", 2, [128, 512], F32)
                m2p = sc.pool("m2", 2, [128, 512], F32)
                mTp = sc.pool("mT", 2, [128, 8, 512], BF16)
                xop = sc.pool("xo", 1, [128, 4, D], F32)
                nb = 0
                for tb in range(8):
                    tok = slice(tb * 512, (tb + 1) * 512)
                    ya, kya = yap.next()
                    ys, kys = ysp.next()
                    sg, ksg = sgp.next()
                    xr, kxr = xrp.next()
                    op("sp", nc.sync.dma_start, out=ya[:], in_=yaT_v[:, :, tok], r=[("yaT", h, tb) for h in range(8)],
                       w=[kya], dma=kya)
                    op("sp", nc.sync.dma_start, out=ys[:], in_=ysT_v[:, :, tok], r=[("ysT", tb, gi) for gi in range(4)], w=[kys], dma=kys)
                    op("sp", nc.sync.dma_start, out=sg[:], in_=sgT_v[:, :, tok], r=[("sgT", i) for i in range(16)],
                       w=[ksg], dma=ksg)
                    op("sp", nc.sync.dma_start, out=xr[:], in_=xsrc[tok, :].rearrange("(t p) c -> p t c", p=128),
                       r=[("xres", tb)], w=[kxr], dma=kxr)
                    mT, kmT = mTp.next()
                    for cc in range(8):
                        b1 = nb % 6
                        b2 = (nb + 1) % 6
                        nb += 2
                        for kc in range(8):
                            op("pe", mm, banks[b1][:], lhsT=wpa[:, kc, cc * 128:(cc + 1) * 128], rhs=ya[:, kc, :],
                               start=(kc == 0), stop=(kc == 7), r=["wpa", kya], w=[bk(b1)])
                        for kc in range(16):
                            op("pe", mm, banks[b2][:], lhsT=wps[:, kc, cc * 128:(cc + 1) * 128], rhs=ys[:, kc, :],
                               start=(kc == 0), stop=(kc == 15), r=["wps", kys], w=[bk(b2)])
                        m1, km1 = m1p.next()
                        op("dve", tt, out=m1[:], in0=banks[b1][:], in1=sg[:, cc, :], op=MUL, r=[bk(b1), ksg], w=[km1])
                        m2, km2 = m2p.next()
                        op("dve", tt, out=m2[:], in0=banks[b2][:], in1=sg[:, 8 + cc, :], op=MUL, r=[bk(b2), ksg], w=[km2])
                        op("pool", nc.gpsimd.tensor_tensor, out=mT[:, cc, :], in0=m1[:], in1=m2[:], op=ADD,
                           r=[km1, km2], w=[kmT])
                    xo, kxo = xop.next()
                    for t4 in range(4):
                        for half in range(2):
                            b = 6 + (t4 * 2 + half) % 2
                            for kc in range(8):
                                op("pe", mm, banks[b][:], lhsT=mT[:, kc, t4 * 128:(t4 + 1) * 128],
                                   rhs=wo[:, kc, half * 512:(half + 1) * 512], start=(kc == 0), stop=(kc == 7),
                                   r=[kmT, "wo"], w=[bk(b)])
                            op("dve", tt, out=xo[:, t4, half * 512:(half + 1) * 512], in0=banks[b][:],
                               in1=xr[:, t4, half * 512:(half + 1) * 512], op=ADD, r=[bk(b), kxr], w=[kxo])
                    op("sp", nc.sync.dma_start, out=out_d[tok, :].rearrange("(t p) c -> p t c", p=128), in_=xo[:],
                       r=[kxo], w=[("xres", tb)], dma=kxo)

        for l in range(depth):
            xsrc = x_d if l == 0 else out_d
            barrier()
            load_params(l)
            with Scope() as lsc:
                hT = lsc.sb("hT", [128, 8, S], BF16)
                if "A" in phases:
                    phase_A(l, hT, xsrc)
                    barrier()
                if "S" in phases:
                    if _os.environ.get("SKIPS1") != "1":
                        phase_S1(l, hT)
                        barrier()
                    if _os.environ.get("SKIPS2") != "1":
                        phase_S2(l, hT)
                        barrier()
                if "G" in phases:
                    phase_G(l, hT)
                    barrier()
                if "T" in phases:
                    phase_T(l, hT)
                    barrier()
            if "D" in phases:
                phase_D(l, xsrc)
        P.finish(sems)
    return nc, P.stats


def make_consts():
    i = np.arange(128)
    ident = np.eye(128, dtype=np.float32)
    maskU = (i[None, :] >= i[:, None]).astype(np.float32)
    Lstrict = (i[:, None] > i[None, :]).astype(np.float32)
    bd = ((i[:, None] // 64) == (i[None, :] // 64)).astype(np.float32)
    mneg = np.where(i[None, :] >= i[:, None], 0.0, -30000.0).astype(np.float32)
    return np.concatenate([ident, maskU, maskU, Lstrict, bd, mneg], axis=1).astype(np.float32)


def host_layout(inputs):
    inputs = {k: (np.asarray(v)[:DEPTH] if k != "x" else v) for k, v in inputs.items()}
    f = lambda a: np.ascontiguousarray(np.asarray(a, dtype=np.float32))
    rep = lambda a: np.ascontiguousarray(np.broadcast_to(np.asarray(a, np.float32)[:, None, :], (a.shape[0], 128, a.shape[1])))
    qn, kn = np.asarray(inputs["q_norm_w"], np.float32), np.asarray(inputs["k_norm_w"], np.float32)
    qkw = np.stack([np.tile(qn, (1, 2)), np.tile(kn, (1, 2))], axis=-1)
    cw = np.asarray(inputs["conv_w"], np.float32)
    convw = cw.transpose(0, 2, 1).reshape(DEPTH, 24, 128, 4).transpose(0, 2, 1, 3).reshape(DEPTH, 128, 96)
    convb = np.asarray(inputs["conv_b"], np.float32).reshape(DEPTH, 24, 128).transpose(0, 2, 1)
    hp = np.concatenate([inputs["dt_bias"], inputs["a_log"], inputs["d_skip"]], axis=-1).astype(np.float32)
    common = {
        "w_in": f(inputs["w_in"]), "w_pa": f(inputs["w_proj_attn"]), "w_ps": f(inputs["w_proj_ssd"]),
        "w_out": f(inputs["w_out"]),
        "nw_rep": rep(inputs["norm_w"]), "qkw": f(qkw),
        "dl_rep": rep(np.asarray(inputs["diff_lambda"], np.float32).reshape(DEPTH, 256)),
        "sw_rep": rep(inputs["subln_w"]), "swc": f(np.asarray(inputs["subln_w"], np.float32)[:, :, None]), "convw": f(convw), "convb": f(convb),
        "hp_rep": rep(hp), "snw_rep": rep(inputs["ssd_norm_w"]), "consts": make_consts(),
    }
    return common


_NC_CACHE = {}


def kernel(**inputs):
    common = host_layout(inputs)
    x = np.asarray(inputs["x"], np.float32)
    n = x.shape[0]
    if "nc" not in _NC_CACHE:
        _NC_CACHE["nc"] = build()[0]
    nc = _NC_CACHE["nc"]
    in_maps = [dict(common, x=np.ascontiguousarray(x[b])) for b in range(n)]
    res = run_bass_kernel_spmd(nc, in_maps, core_ids=list(range(n)))
    return np.stack([np.asarray(r["out"], np.float32) for r in res.results], axis=0)
```

```python
import contextlib
import math
import numpy as np
import concourse.bass as bass
import concourse.mybir as mybir
from concourse.bass_utils import run_bass_kernel_spmd
from concourse.alu_op_type import AluOpType as ALU

AF = mybir.ActivationFunctionType
F32 = mybir.dt.float32
BF16 = mybir.dt.bfloat16
AX = mybir.AxisListType

S = 4096
D = 1024
DIN = 11296
OFF_Q, OFF_K, OFF_V, OFF_ZA, OFF_XBC, OFF_ZS, OFF_DT, OFF_G = 0, 1024, 2048, 3072, 4096, 7168, 9216, 9248
EPS = 1e-6
import os as _os
DEPTH = int(_os.environ.get('KDEPTH', '4'))


class Op:
    __slots__ = ("eng", "fn", "args", "kw", "reads", "writes", "dma", "idx",
                 "deps", "signal", "token", "waits")

    def __init__(self, eng, fn, args, kw, reads, writes, dma):
        self.eng = eng
        self.fn = fn
        self.args = args
        self.kw = kw
        self.reads = reads
        self.writes = writes
        self.dma = dma
        self.deps = set()
        self.signal = False
        self.token = None
        self.waits = []


class Prog:
    def __init__(self, nc):
        self.nc = nc
        self.ops = []
        self.q = {"pe": nc.tensor, "act": nc.scalar, "dve": nc.vector,
                  "pool": nc.gpsimd, "sp": nc.sync}

    def add(self, eng, fn, *args, reads=(), writes=(), dma=None, **kw):
        op = Op(eng, fn, args, kw, tuple(reads), tuple(writes), dma)
        op.idx = len(self.ops)
        self.ops.append(op)
        return op

    def finish(self, sems, final_eng="sp"):
        ops = self.ops
        for i_, op_ in enumerate(ops):
            op_.idx = i_
        last_w = {}
        readers = {}
        sem_waiters = {}
        dma_cum = {}
        for op in ops:
            deps = set()
            raw = set()
            for r in op.reads:
                w = last_w.get(r)
                if w is not None:
                    deps.add(w)
                    raw.add(w)
            for wkey in op.writes:
                w = last_w.get(wkey)
                if w is not None:
                    deps.add(w)
                for ridx in readers.get(wkey, {}).values():
                    deps.add(ridx)
            deps.discard(op.idx)
            keep = set()
            for d in deps:
                dop = ops[d]
                if dop.dma is None and op.dma is None and dop.eng == op.eng:
                    if op.eng == "pe" or d not in raw:
                        continue
                keep.add(d)
            if op.dma is not None:
                for e, widx in sem_waiters.get(op.dma, {}).items():
                    if e != op.eng and widx != op.idx:
                        keep.add(widx)
            for d in keep:
                if ops[d].dma is not None:
                    sem_waiters.setdefault(ops[d].dma, {})[op.eng] = op.idx
            op.deps = keep
            for r in op.reads:
                rd = readers.setdefault(r, {})
                if op.dma is not None:
                    rd[("dma", op.dma)] = op.idx
                else:
                    rd[op.eng] = op.idx
            for wkey in op.writes:
                last_w[wkey] = op.idx
                readers[wkey] = {}
        eng_cnt = {}
        for op in ops:
            for d in op.deps:
                if ops[d].dma is None:
                    ops[d].signal = True
        known = {}
        for op in ops:
            kn = known.setdefault(op.eng, {})
            waits = {}
            for d in sorted(op.deps):
                dop = ops[d]
                if dop.dma is not None:
                    s = ("dma", dop.dma)
                    v = dma_cum[dop.dma]
                else:
                    s = ("eng", dop.eng)
                    v = dop.token[1]
                if kn.get(s, 0) >= v:
                    continue
                waits[s] = max(waits.get(s, 0), v)
            for s, v in waits.items():
                kn[s] = v
            op.waits = list(waits.items())
            if op.dma is not None:
                dma_cum[op.dma] = dma_cum.get(op.dma, 0) + 16
                op.token = (("dma", op.dma), dma_cum[op.dma])
            else:
                if op.signal:
                    eng_cnt[op.eng] = eng_cnt.get(op.eng, 0) + 1
                    op.token = (("eng", op.eng), eng_cnt[op.eng])
                else:
                    op.token = (("eng", op.eng), eng_cnt.get(op.eng, 0) + 1)
        n_wait = 0
        for op in ops:
            q = self.q[op.eng]
            for s, v in op.waits:
                q.wait_ge(sems(s), v)
                n_wait += 1
            ins = op.fn(*op.args, **op.kw)
            if op.dma is not None:
                ins.then_inc(sems(("dma", op.dma)), 16)
            elif op.signal:
                ins.then_inc(sems(("eng", op.eng)), 1)
        q = self.q[final_eng]
        for k, v in dma_cum.items():
            q.wait_ge(sems(("dma", k)), v)
        for e, v in eng_cnt.items():
            if e != final_eng:
                q.wait_ge(sems(("eng", e)), v)
        self.stats = dict(n_ops=len(ops), n_wait=n_wait, eng_cnt=dict(eng_cnt),
                          n_dma_sems=len(dma_cum))


def lambda_init_fn(layer_idx):
    return 0.8 - 0.6 * math.exp(-0.3 * layer_idx)


def build(depth=DEPTH, dbg=False, phases="ASGTD"):
    nc = bass.Bass("TRN2", target_bir_lowering=False)
    P = Prog(nc)
    es = contextlib.ExitStack()
    uid = [0]

    def din(name, shape, dt=F32):
        return nc.dram_tensor(name, list(shape), dt, kind="ExternalInput").ap()

    def dscr(name, shape, dt):
        return nc.dram_tensor(name, list(shape), dt, kind="ExternalOutput" if dbg else "Internal").ap()

    x_d = din("x", [S, D])
    w_in_d = din("w_in", [DEPTH, D, DIN])
    w_pa_d = din("w_pa", [DEPTH, D, D])
    w_ps_d = din("w_ps", [DEPTH, 2 * D, D])
    w_out_d = din("w_out", [DEPTH, D, D])
    nw_d = din("nw_rep", [DEPTH, 128, D])
    qkw_d = din("qkw", [DEPTH, 128, 2])
    dl_d = din("dl_rep", [DEPTH, 128, 256])
    sw_d = din("sw_rep", [DEPTH, 128, 128])
    swc_d = din("swc", [DEPTH, 128, 1])
    cw_d = din("convw", [DEPTH, 128, 24 * 4])
    cb_d = din("convb", [DEPTH, 128, 24])
    hp_d = din("hp_rep", [DEPTH, 128, 96])
    snw_d = din("snw_rep", [DEPTH, 128, 2048])
    cst_d = din("consts", [128, 6 * 128])
    out_d = nc.dram_tensor("out", [S, D], F32, kind="ExternalOutput").ap()
    yaT_d = dscr("yaT", [8, 128, S], BF16)
    ysT_d = dscr("ysT", [16, 128, S], BF16)
    sgT_d = dscr("sgT", [16, 128, S], BF16)
    xtm_d = dscr("xtm", [S, 2560], BF16)
    bct_d = dscr("bct", [8, 128, S], BF16)

    sem_cache = {}

    def sems(key):
        if key not in sem_cache:
            sem_cache[key] = es.enter_context(nc.semaphore("s%d" % len(sem_cache)))
        return sem_cache[key]

    class Scope:
        def __init__(self):
            self.es = contextlib.ExitStack()

        def __enter__(self):
            self.es.__enter__()
            return self

        def __exit__(self, *a):
            return self.es.__exit__(*a)

        def sb(self, name, shape, dt):
            uid[0] += 1
            return self.es.enter_context(nc.sbuf_tensor("%s_%d" % (name, uid[0]), list(shape), dt))

        def pool(self, name, n, shape, dt):
            return RPool(self, name, n, shape, dt)

    class RPool:
        def __init__(self, sc, name, n, shape, dt):
            self.name = name
            self.tiles = [sc.sb("%s%d" % (name, i), shape, dt) for i in range(n)]
            self.i = 0

        def next(self):
            j = self.i % len(self.tiles)
            self.i += 1
            return self.tiles[j], (self.name, j)

    def op(eng, fn, *a, r=(), w=(), dma=None, **kw):
        lk = tuple(("lock", k[1]) for k in tuple(r) + tuple(w) if isinstance(k, tuple) and k[0] == "bank")
        return P.add(eng, fn, *a, reads=tuple(r) + ("PH",), writes=tuple(w) + lk, dma=dma, **kw)

    mm = nc.tensor.matmul
    tr = nc.tensor.transpose
    act = nc.scalar.activation
    tt = nc.vector.tensor_tensor
    ts = nc.vector.tensor_scalar
    stt = nc.vector.scalar_tensor_tensor
    MUL, ADD, SUB = ALU.mult, ALU.add, ALU.subtract

    with es:
        g = Scope()
        es.enter_context(g)
        pbig = [es.enter_context(nc.psum_tensor("pbig%d" % i, [128, 1024], F32)) for i in range(4)]
        banks = [pbig[i // 2][:, (i % 2) * 512:(i % 2 + 1) * 512] for i in range(8)]

        def bk(i):
            return ("bank", i)

        def bf(i):
            return banks[i][:].bitcast(BF16)

        cst_f = g.sb("cst_f", [128, 6 * 128], F32)
        cst_b = g.sb("cst_b", [128, 6 * 128], BF16)
        op("sp", nc.sync.dma_start, out=cst_f[:], in_=cst_d, w=["cst_f"], dma="cst_f")
        op("pool", nc.gpsimd.dma_start, out=cst_b[:], in_=cst_d, w=["cst_b"], dma="cst_b")
        ident_f = cst_f[:, 0:128]
        U_f = cst_f[:, 256:384]
        Ls_f = cst_f[:, 384:512]
        ident_b = cst_b[:, 0:128]
        maskU_b = cst_b[:, 128:256]
        BD_b = cst_b[:, 512:640]
        Mneg_b = cst_b[:, 640:768]
        CST = ["cst_f", "cst_b"]
        epst = g.sb("eps", [128, 1], F32)
        op("dve", nc.vector.memset, epst[:], EPS, w=["eps"])
        ones_f = g.sb("ones_f", [128, 128], F32)
        op("dve", nc.vector.memset, ones_f[:], 1.0, w=["ones_f"])
        ones_b = g.sb("ones_b", [128, 128], BF16)
        op("dve", nc.vector.memset, ones_b[:], 1.0, w=["ones_b"])
        swc = g.sb("swc", [128, 1], F32)
        bar_t = g.sb("bar_t", [128, 1], F32)
        qkw = g.sb("qkw", [128, 2], F32)
        dl = g.sb("dl", [128, 256], F32)
        sw = g.sb("sw", [128, 128], F32)
        cw = g.sb("cw", [128, 96], F32)
        cb = g.sb("cb", [128, 24], F32)
        hp = g.sb("hp", [128, 96], F32)
        snw = g.sb("snw", [128, 2048], BF16)
        neglam = g.sb("neglam", [128, 1], F32)
        Aneg = g.sb("Aneg", [128, 32], F32)
        lamt = g.sb("lamt", [128, 4], F32)
        lamp = g.sb("lamp", [128, 128], F32)

        def barrier():
            op("dve", nc.vector.memset, bar_t[:], 0.0, w=["PH"])

        def load_params(l):
            li = lambda_init_fn(l)
            for t, src, key in ((qkw, qkw_d, "qkw"), (dl, dl_d, "dl"), (sw, sw_d, "sw"),
                                (cw, cw_d, "cw"), (cb, cb_d, "cb"), (hp, hp_d, "hp")):
                op("sp", nc.sync.dma_start, out=t[:], in_=src[l], w=[key], dma=key)
            op("pool", nc.gpsimd.dma_start, out=snw[:], in_=snw_d[l], w=["snw"], dma="snw")
            op("dve", ts, out=qkw[:, 0:1], in0=qkw[:, 0:1], scalar1=0.125, scalar2=None, op0=MUL, r=["qkw"], w=["qkw"])
            op("dve", ts, out=sw[:], in0=sw[:], scalar1=float(1.0 - li), scalar2=None, op0=MUL, r=["sw"], w=["sw"])
            op("sp", nc.sync.dma_start, out=swc[:], in_=swc_d[l], w=["swc"], dma="swc")
            op("dve", ts, out=swc[:], in0=swc[:], scalar1=float(1.0 - li), scalar2=None, op0=MUL, r=["swc"], w=["swc"])
            op("dve", tt, out=lamp[:, 0:64], in0=dl[:, 0:64], in1=dl[:, 64:128], op=MUL, r=["dl"], w=["lamp"])
            op("dve", tt, out=lamp[:, 64:128], in0=dl[:, 128:192], in1=dl[:, 192:256], op=MUL, r=["dl"], w=["lamp"])
            op("dve", nc.vector.reduce_sum, out=lamt[:, 0:2], in_=lamp[:].rearrange("p (a b) -> p a b", b=64), axis=AX.X,
               r=["lamp"], w=["lamt"])
            op("act", act, out=lamt[:, 2:4], in_=lamt[:, 0:2], func=AF.Exp, r=["lamt"], w=["lamt2"])
            op("dve", tt, out=neglam[:], in0=lamt[:, 3:4], in1=lamt[:, 2:3], op=SUB, r=["lamt2"], w=["neglam"])
            op("dve", ts, out=neglam[:], in0=neglam[:], scalar1=float(-li), scalar2=None, op0=ADD, r=["neglam"], w=["neglam"])
            op("act", act, out=Aneg[:], in_=hp[:, 32:64], func=AF.Exp, r=["hp"], w=["Aneg"])
            op("dve", ts, out=Aneg[:], in0=Aneg[:], scalar1=-1.0, scalar2=None, op0=MUL, r=["Aneg"], w=["Aneg"])

        PARAMS = ["nw", "qkw", "dl", "sw", "cw", "cb", "hp", "snw", "neglam", "Aneg"]

        def phase_A(l, hT, xsrc):
            with Scope() as sc:
                nw = sc.sb("nw", [128, D], F32)
                op("sp", nc.sync.dma_start, out=nw[:], in_=nw_d[l], w=["nw"], dma="nw")
                xt = sc.pool("xt", 2, [128, D], F32)
                hb = sc.pool("hb", 2, [128, D], BF16)
                junk = sc.sb("junk", [128, D], BF16)
                ssp = sc.pool("ss", 2, [128, 1], F32)
                rsp = sc.pool("rs", 2, [128, 1], F32)
                for t in range(32):
                    x_t, kx = xt.next()
                    op("sp", nc.sync.dma_start, out=x_t[:], in_=xsrc[t * 128:(t + 1) * 128, :],
                       r=[("xres", t // 4)], w=[kx], dma=kx)
                    s_t, ks = ssp.next()
                    op("act", act, out=junk[:], in_=x_t[:], func=AF.Square, accum_out=s_t[:], r=[kx], w=["junkA", ks])
                    r_t, kr = rsp.next()
                    op("act", act, out=r_t[:], in_=s_t[:], func=AF.Sqrt, bias=epst[:], scale=1.0 / D,
                       r=[ks, "eps"], w=[kr])
                    op("dve", nc.vector.reciprocal, out=r_t[:], in_=r_t[:], r=[kr], w=[kr])
                    h_t, kh = hb.next()
                    op("dve", stt, out=h_t[:], in0=x_t[:], scalar=r_t[:], in1=nw[:], op0=MUL, op1=MUL,
                       r=[kx, kr, "nw"], w=[kh])
                    b = t % 2
                    ptv = bf(b).rearrange("p (a b) -> p a b", b=128)
                    for kc in range(8):
                        op("pe", tr, out=ptv[:, kc, :], in_=h_t[:, kc * 128:(kc + 1) * 128], identity=ident_b,
                           r=[kh] + CST, w=[bk(b)])
                    if t % 2 == 0:
                        op("act", nc.scalar.copy, out=hT[:, :, t * 128:(t + 1) * 128], in_=ptv, r=[bk(b)], w=[("hT", t)])
                    else:
                        op("dve", nc.vector.tensor_copy, out=hT[:, :, t * 128:(t + 1) * 128], in_=ptv, r=[bk(b)],
                           w=[("hT", t)])

        def hTk(tb):
            return [("hT", 4 * tb + i) for i in range(4)]

        def phase_S1(l, hT):
            w_l = w_in_d[l].rearrange("(kc p) n -> p kc n", p=128)
            xtm_v = xtm_d.rearrange("(t p) c -> p t c", p=128)
            with Scope() as sc:
                wp = sc.pool("wcc", 2, [128, 8, 128], BF16)
                xcp = sc.pool("xc", 2, [128, S + 4], BF16)
                dwp = sc.pool("dw", 2, [128, 4, 128], BF16)
                xop = sc.pool("xo", 2, [128, S], BF16)
                tmp_ = sc.pool("tmt", 3, [128, 8, 128], BF16)
                for t_, k_ in ((xcp.tiles[0], (xcp.name, 0)), (xcp.tiles[1], (xcp.name, 1))):
                    op("dve", nc.vector.memset, t_[:, 0:3], 0.0, w=[k_])
                nb = 0
                nc_ = 0
                for cc in range(24):
                    w_t, kw_ = wp.next()
                    c0 = OFF_XBC + cc * 128
                    op("pool", nc.gpsimd.dma_start, out=w_t[:], in_=w_l[:, :, c0:c0 + 128], w=[kw_], dma=kw_)
                    dw, kdw = dwp.next()
                    for k in range(4):
                        op("pool", nc.gpsimd.tensor_scalar, out=dw[:, k, :], in0=ident_f, scalar1=cw[:, cc * 4 + k:cc * 4 + k + 1],
                           scalar2=None, op0=MUL, r=["cw"] + CST, w=[kdw])
                    xc, kxc = xcp.next()
                    for tb in range(8):
                        b = nb % 4
                        nb += 1
                        for kc in range(8):
                            op("pe", mm, banks[b][:], lhsT=w_t[:, kc, :], rhs=hT[:, kc, tb * 512:(tb + 1) * 512],
                               start=(kc == 0), stop=(kc == 7), r=[kw_] + hTk(tb), w=[bk(b)])
                        op("act", nc.scalar.copy, out=xc[:, 3 + tb * 512: 3 + (tb + 1) * 512], in_=banks[b][:],
                           r=[bk(b)], w=[kxc + (tb,)])
                    xo, kxo = xop.next()
                    for tb in range(8):
                        b = 6 + (nc_ % 2)
                        nc_ += 1
                        rk = [kxc + (tb,)] + ([kxc + (tb - 1,)] if tb > 0 else [kxc])
                        for k in range(4):
                            op("pe", mm, banks[b][:], lhsT=dw[:, k, :], rhs=xc[:, tb * 512 + k: tb * 512 + k + 512],
                               start=(k == 0), stop=(k == 3), r=[kdw] + rk, w=[bk(b)])
                        op("act", act, out=xo[:, tb * 512:(tb + 1) * 512], in_=banks[b][:], func=AF.Silu, bias=cb[:, cc:cc + 1],
                           r=[bk(b), "cb"], w=[kxo])
                    if cc >= 16:
                        op("sp", nc.sync.dma_start, out=bct_d[cc - 16], in_=xo[:], r=[kxo], w=[("bct", cc - 16)], dma=kxo)
                    if cc < 20:
                        for q4 in range(4):
                            b = 4 + (q4 % 2)
                            ptv = bf(b).rearrange("p (a b) -> p a b", b=128)
                            for i in range(8):
                                t0 = (q4 * 8 + i) * 128
                                op("pe", tr, out=ptv[:, i, :], in_=xo[:, t0:t0 + 128], identity=ident_b,
                                   r=[kxo] + CST, w=[bk(b)])
                            tm, ktm = tmp_.next()
                            if q4 % 2 == 0:
                                op("act", nc.scalar.copy, out=tm[:], in_=ptv, r=[bk(b)], w=[ktm])
                            else:
                                op("dve", nc.vector.tensor_copy, out=tm[:], in_=ptv, r=[bk(b)], w=[ktm])
                            op("sp", nc.sync.dma_start, out=xtm_v[:, q4 * 8:(q4 + 1) * 8, cc * 128:(cc + 1) * 128],
                               in_=tm[:], r=[ktm], w=[("xtm", cc, q4)], dma=ktm)

        def phase_S2(l, hT):
            w_l = w_in_d[l].rearrange("(kc p) n -> p kc n", p=128)
            bct_v = bct_d.rearrange("g p t -> p g t")
            ysT_v = ysT_d.rearrange("k p t -> p k t")
            with Scope() as sc:
                wzs = sc.sb("wzs", [128, 8, 2048], BF16)
                wdt = sc.sb("wdt", [128, 8, 32], BF16)
                op("pool", nc.gpsimd.dma_start, out=wzs[:], in_=w_l[:, :, OFF_ZS:OFF_ZS + 2048], w=["wzs"], dma="wzs")
                op("pool", nc.gpsimd.dma_start, out=wdt[:], in_=w_l[:, :, OFF_DT:OFF_DT + 32], w=["wdt"], dma="wdt")
                st = sc.sb("st", [128, 2048], F32)
                stb = sc.sb("stb", [128, 2048], BF16)
                op("dve", nc.vector.memset, st[:], 0.0, w=[("st", i) for i in range(4)])
                op("dve", nc.vector.memset, stb[:], 0.0, w=[("stb", i) for i in range(4)])
                xtp = sc.pool("xtc", 2, [128, 2560], BF16)
                bcp = sc.pool("bcc", 2, [128, 8, 128], BF16)
                smp = sc.pool("sm", 2, [128, 8, 32], F32)
                xdtp = sc.pool("xdt", 2, [128, 32, 64], BF16)
                xdsp = sc.pool("xds", 2, [128, 32, 64], BF16)
                xDp = sc.pool("xD", 2, [128, 32, 64], BF16)
                szap = sc.pool("szall", 2, [128, 2048], BF16)
                cbmp = sc.pool("cbm", 2, [128, 4, 128], BF16)
                ltp = sc.pool("lt", 2, [128, 4, 128], F32)
                dcp = sc.pool("dcT", 2, [128, 4, 128], BF16)
                mtap = sc.pool("MTall", 2, [128, 32, 128], BF16)
                sztp = sc.pool("szt", 1, [128, 512], F32)
                t1p = sc.pool("t1", 2, [128, 512], F32)
                gnp = sc.pool("gn", 2, [128, 512], BF16)
                junk = sc.sb("junkS", [128, 512], BF16)
                sqp = sc.pool("ssq", 2, [128, 1], F32)
                rqp = sc.pool("rsq", 2, [128, 1], F32)
                ysp = sc.pool("ysg", 3, [128, 4, 128], BF16)
                ctx = {}

                def stageA(c):
                    tok = slice(c * 128, (c + 1) * 128)
                    hk = [("hT", c)]
                    xt_c, kxt = xtp.next()
                    op("sp", nc.sync.dma_start, out=xt_c[:], in_=xtm_d[tok, :],
                       r=[("xtm", cc, c // 8) for cc in range(20)], w=[kxt], dma=kxt)
                    bc_c, kbc = bcp.next()
                    op("sp", nc.sync.dma_start, out=bc_c[:], in_=bct_v[:, :, tok],
                       r=[("bct", i) for i in range(8)], w=[kbc], dma=kbc)
                    sm, ksm = smp.next()
                    for kc in range(8):
                        op("pe", mm, banks[0][:, 0:32], lhsT=hT[:, kc, tok], rhs=wdt[:, kc, :], start=(kc == 0),
                           stop=(kc == 7), r=hk + ["wdt"], w=[bk(0)])
                    op("dve", tt, out=sm[:, 0, :], in0=banks[0][:, 0:32], in1=hp[:, 0:32], op=ADD, r=[bk(0), "hp"], w=[ksm])
                    op("act", act, out=sm[:, 1, :], in_=sm[:, 0, :], func=AF.Exp, r=[ksm], w=[ksm])
                    op("act", act, out=sm[:, 2, :], in_=sm[:, 1, :], func=AF.Ln, bias=1.0, r=[ksm], w=[ksm])
                    op("dve", tt, out=sm[:, 3, :], in0=sm[:, 2, :], in1=Aneg[:], op=MUL, r=[ksm, "Aneg"], w=[ksm])
                    op("pe", mm, banks[0][:, 32:64], lhsT=U_f, rhs=sm[:, 3, :], start=True, stop=True,
                       r=[ksm] + CST, w=[bk(0)])
                    op("pe", mm, banks[0][:, 64:96], lhsT=ones_f[:], rhs=sm[:, 3, :], start=True, stop=True,
                       r=[ksm, "ones_f"], w=[bk(0)])
                    op("dve", nc.vector.tensor_copy, out=sm[:, 4, :], in_=banks[0][:, 32:64], r=[bk(0)], w=[ksm])
                    op("act", act, out=sm[:, 5, :], in_=banks[0][:, 32:64], func=AF.Exp, r=[bk(0)], w=[ksm])
                    op("dve", tt, out=sm[:, 6, :], in0=banks[0][:, 64:96], in1=sm[:, 4, :], op=SUB, r=[bk(0), ksm], w=[ksm])
                    op("act", act, out=sm[:, 6, :], in_=sm[:, 6, :], func=AF.Exp, r=[ksm], w=[ksm])
                    op("act", act, out=sm[:, 7, :], in_=banks[0][:, 64:96], func=AF.Exp, r=[bk(0)], w=[ksm])
                    xv = xt_c[:, 0:2048].rearrange("p (a b) -> p a b", b=64)
                    xdt, kxdt = xdtp.next()
                    op("dve", tt, out=xdt[:], in0=xv, in1=sm[:, 2, :].unsqueeze(2).broadcast_to([128, 32, 64]), op=MUL,
                       r=[kxt, ksm], w=[kxdt])
                    xds, kxds = xdsp.next()
                    op("pool", nc.gpsimd.tensor_tensor, out=xds[:], in0=xdt[:],
                       in1=sm[:, 6, :].unsqueeze(2).broadcast_to([128, 32, 64]), op=MUL, r=[kxdt, ksm], w=[kxds])
                    xD, kxD = xDp.next()
                    op("pool", nc.gpsimd.tensor_tensor, out=xD[:], in0=xv,
                       in1=hp[:, 64:96].unsqueeze(2).broadcast_to([128, 32, 64]), op=MUL, r=[kxt, "hp"], w=[kxD])
                    cbv = banks[1][:].rearrange("p (a b) -> p a b", b=128)
                    for gi in range(4):
                        op("pe", mm, cbv[:, gi, :], lhsT=bc_c[:, gi, :], rhs=bc_c[:, 4 + gi, :], start=True, stop=True,
                           r=[kbc], w=[bk(1)])
                    cbm, kcbm = cbmp.next()
                    op("dve", tt, out=cbm[:], in0=cbv, in1=maskU_b.unsqueeze(1).broadcast_to([128, 4, 128]), op=MUL,
                       r=[bk(1)] + CST, w=[kcbm])
                    mta, kmta = mtap.next()
                    sza, ksza = szap.next()
                    for q8 in range(8):
                        gi = q8 // 2
                        h0 = q8 * 4
                        lt, klt = ltp.next()
                        op("pool", nc.gpsimd.tensor_tensor, out=lt[:],
                           in0=Ls_f.unsqueeze(1).broadcast_to([128, 4, 128]),
                           in1=sm[:, 3, h0:h0 + 4].unsqueeze(2).broadcast_to([128, 4, 128]), op=MUL,
                           r=[ksm] + CST, w=[klt])
                        rbv = banks[2][:].rearrange("p (a b) -> p a b", b=128)
                        for i in range(4):
                            op("pe", mm, rbv[:, i, :], lhsT=lt[:, i, :], rhs=U_f, start=True, stop=True,
                               r=[klt] + CST, w=[bk(2)])
                        dc, kdc = dcp.next()
                        op("act", act, out=dc[:], in_=rbv, func=AF.Exp, r=[bk(2)], w=[kdc])
                        op("dve", tt, out=mta[:, h0:h0 + 4, :], in0=dc[:], in1=cbm[:, gi:gi + 1, :].broadcast_to([128, 4, 128]),
                           op=MUL, r=[kdc, kcbm], w=[kmta + (q8,)])
                        if q8 % 2 == 1:
                            for kc in range(8):
                                op("pe", mm, banks[3][:], lhsT=hT[:, kc, tok], rhs=wzs[:, kc, gi * 512:(gi + 1) * 512],
                                   start=(kc == 0), stop=(kc == 7), r=hk + ["wzs"], w=[bk(3)])
                            szt, kszt = sztp.next()
                            op("act", act, out=szt[:], in_=banks[3][:], func=AF.Exp, scale=-1.0, r=[bk(3)], w=[kszt])
                            op("act", act, out=szt[:], in_=szt[:], func=AF.Ln, bias=1.0, r=[kszt], w=[kszt])
                            op("act", act, out=szt[:], in_=szt[:], func=AF.Exp, scale=-1.0, r=[kszt], w=[kszt])
                            op("dve", tt, out=sza[:, gi * 512:(gi + 1) * 512], in0=banks[3][:], in1=szt[:], op=MUL,
                               r=[bk(3), kszt], w=[ksza + (gi,)])
                    ctx[c] = dict(tok=tok, xt_c=xt_c, kxt=kxt, bc_c=bc_c, kbc=kbc, sm=sm, ksm=ksm, xdt=xdt, kxdt=kxdt,
                                  xds=xds, kxds=kxds, xD=xD, kxD=kxD, mta=mta, kmta=kmta, sza=sza, ksza=ksza)

                def stageB(c):
                    d_ = ctx.pop(c)
                    tok, xt_c, kxt, bc_c, kbc = d_["tok"], d_["xt_c"], d_["kxt"], d_["bc_c"], d_["kbc"]
                    sm, ksm, xdt, kxdt, xds, kxds = d_["sm"], d_["ksm"], d_["xdt"], d_["kxdt"], d_["xds"], d_["kxds"]
                    xD, kxD, mta, kmta, sza, ksza = d_["xD"], d_["kxD"], d_["mta"], d_["kmta"], d_["sza"], d_["ksza"]
                    for pair in ((0, 1), (2, 3)):
                        YB = {pair[0]: 4, pair[1]: 6}
                        XB = {pair[0]: 5, pair[1]: 7}
                        T = {}
                        for gi in pair:
                            yb_, xb_ = YB[gi], XB[gi]
                            op("pe", mm, banks[yb_][:], lhsT=ident_b, rhs=xD[:, gi * 8:(gi + 1) * 8, :], start=True, stop=False,
                               r=[kxD] + CST, w=[bk(yb_)])
                            for j in range(8):
                                hh = gi * 8 + j
                                op("pe", mm, banks[yb_][:, j * 64:(j + 1) * 64], lhsT=mta[:, hh, :], rhs=xdt[:, hh, :],
                                   start=False, stop=(j == 7), skip_group_check=True,
                                   r=[kmta + (hh // 4,), kxdt], w=[bk(yb_)])
                            op("pe", mm, banks[xb_][:], lhsT=bc_c[:, 4 + gi, :], rhs=stb[:, gi * 512:(gi + 1) * 512],
                               start=True, stop=True, r=[kbc, ("stb", gi)], w=[bk(xb_)])
                        for gi in pair:
                            t1, kt1 = t1p.next()
                            T[gi] = (t1, kt1)
                            op("dve", tt, out=t1[:].rearrange("p (a b) -> p a b", b=64),
                               in0=banks[XB[gi]][:].rearrange("p (a b) -> p a b", b=64),
                               in1=sm[:, 5, gi * 8:(gi + 1) * 8].unsqueeze(2).broadcast_to([128, 8, 64]), op=MUL,
                               r=[bk(XB[gi]), ksm], w=[kt1])
                        for gi in pair:
                            t1, kt1 = T[gi]
                            op("dve", tt, out=t1[:], in0=banks[YB[gi]][:], in1=t1[:], op=ADD, r=[bk(YB[gi]), kt1], w=[kt1])
                        for gi in pair:
                            op("pe", mm, banks[XB[gi]][:], lhsT=xt_c[:, 2048 + gi * 128:2048 + (gi + 1) * 128],
                               rhs=xds[:, gi * 8:(gi + 1) * 8, :], start=True, stop=True, r=[kxt, kxds], w=[bk(XB[gi])])
                        for gi in pair:
                            t1, kt1 = T[gi]
                            op("pool", nc.gpsimd.tensor_tensor, out=t1[:], in0=t1[:], in1=sza[:, gi * 512:(gi + 1) * 512],
                               op=MUL, r=[kt1, ksza + (gi,)], w=[kt1])
                        R = {}
                        for gi in pair:
                            t1, kt1 = T[gi]
                            sq, ksq = sqp.next()
                            op("act", act, out=junk[:], in_=t1[:], func=AF.Square, accum_out=sq[:], r=[kt1], w=["junkS", ksq])
                            rq, krq = rqp.next()
                            R[gi] = (sq, ksq, rq, krq)
                        for gi in pair:
                            sq, ksq, rq, krq = R[gi]
                            op("act", act, out=rq[:], in_=sq[:], func=AF.Ln, bias=epst[:], scale=1.0 / 512.0,
                               r=[ksq, "eps"], w=[krq])
                        for gi in pair:
                            sq, ksq, rq, krq = R[gi]
                            op("act", act, out=rq[:], in_=rq[:], func=AF.Exp, scale=-0.5, r=[krq], w=[krq])
                        for gi in pair:
                            stv = st[:, gi * 512:(gi + 1) * 512]
                            op("pool", nc.gpsimd.tensor_tensor, out=stv.rearrange("p (a b) -> p a b", b=64),
                               in0=stv.rearrange("p (a b) -> p a b", b=64),
                               in1=sm[:, 7, gi * 8:(gi + 1) * 8].unsqueeze(2).broadcast_to([128, 8, 64]), op=MUL,
                               r=[("st", gi), ksm], w=[("st", gi)])
                        for gi in pair:
                            stv = st[:, gi * 512:(gi + 1) * 512]
                            op("dve", tt, out=stv, in0=banks[XB[gi]][:], in1=stv, op=ADD, r=[bk(XB[gi]), ("st", gi)],
                               w=[("st", gi)])
                        G = {}
                        for gi in pair:
                            t1, kt1 = T[gi]
                            sq, ksq, rq, krq = R[gi]
                            gn, kgn = gnp.next()
                            G[gi] = (gn, kgn)
                            op("dve", stt, out=gn[:], in0=t1[:], scalar=rq[:], in1=snw[:, gi * 512:(gi + 1) * 512],
                               op0=MUL, op1=MUL, r=[kt1, krq, "snw"], w=[kgn])
                        for gi in pair:
                            stv = st[:, gi * 512:(gi + 1) * 512]
                            op("act", nc.scalar.copy, out=stb[:, gi * 512:(gi + 1) * 512], in_=stv, r=[("st", gi)],
                               w=[("stb", gi)])
                        for gi in pair:
                            gn, kgn = G[gi]
                            ptv = bf(XB[gi]).rearrange("p (a b) -> p a b", b=128)
                            for i in range(4):
                                op("pe", tr, out=ptv[:, i, :], in_=gn[:, i * 128:(i + 1) * 128], identity=ident_b,
                                   r=[kgn] + CST, w=[bk(XB[gi])])
                        for gi in pair:
                            ptv = bf(XB[gi]).rearrange("p (a b) -> p a b", b=128)
                            ys_g, kys = ysp.next()
                            op("act", nc.scalar.copy, out=ys_g[:], in_=ptv[:, 0:4, :], r=[bk(XB[gi])], w=[kys])
                            op("sp", nc.sync.dma_start, out=ysT_v[:, gi * 4:(gi + 1) * 4, tok], in_=ys_g[:], r=[kys],
                               w=[("ysT", c // 4, gi)], dma=kys)

                def record(fn, c):
                    saved = P.ops
                    P.ops = []
                    fn(c)
                    lst = P.ops
                    P.ops = saved
                    return lst

                stageA(0)
                for c in range(32):
                    la = record(stageA, c + 1) if c + 1 < 32 else []
                    lb = record(stageB, c)
                    ia = ib = 0
                    while ia < len(la) or ib < len(lb):
                        if ib >= len(lb) or (ia < len(la) and ia * len(lb) <= ib * len(la)):
                            P.ops.append(la[ia])
                            ia += 1
                        else:
                            P.ops.append(lb[ib])
                            ib += 1

        def phase_G(l, hT):
            w_l = w_in_d[l].rearrange("(kc p) n -> p kc n", p=128)
            with Scope() as sc:
                wp = sc.pool("wg", 2, [128, 8, 128], BF16)
                sgp = sc.pool("sg", 2, [128, S], BF16)
                nb = 0
                for gc in range(16):
                    w_t, kw_ = wp.next()
                    c0 = OFF_G + gc * 128
                    op("pool", nc.gpsimd.dma_start, out=w_t[:], in_=w_l[:, :, c0:c0 + 128], w=[kw_], dma=kw_)
                    sg, ksg = sgp.next()
                    for tb in range(8):
                        b = nb % 4
                        nb += 1
                        for kc in range(8):
                            op("pe", mm, banks[b][:], lhsT=w_t[:, kc, :], rhs=hT[:, kc, tb * 512:(tb + 1) * 512],
                               start=(kc == 0), stop=(kc == 7), r=[kw_] + hTk(tb), w=[bk(b)])
                        op("act", act, out=sg[:, tb * 512:(tb + 1) * 512], in_=banks[b][:], func=AF.Sigmoid,
                           r=[bk(b)], w=[ksg])
                    op("sp", nc.sync.dma_start, out=sgT_d[gc], in_=sg[:], r=[ksg], w=[("sgT", gc)], dma=ksg)

        def phase_T(l, hT):
            w_l = w_in_d[l].rearrange("(kc p) n -> p kc n", p=128)
            LOOK = 2
            LSC = 2.0 ** -10
            CSC = EPS / (LSC * LSC)
            with Scope() as sc:
                whp = sc.pool("wh", 2, [128, 8, 4, 128], BF16)
                qz = [sc.sb("qz%d" % m, [128, S], BF16) for m in range(2)]
                kT = sc.sb("kT", [128, S], BF16)
                vt = sc.sb("vt", [128, 32, 128], BF16)
                szT = sc.sb("szT", [128, S], BF16)
                sqp = sc.pool("sq", 2, [128, 512], BF16)
                lnp = sc.pool("lnq", 2, [128, 512], F32)
                rsp = sc.pool("rst", 2, [128, 512], F32)
                ezp = sc.pool("ez", 2, [128, 512], F32)
                ptp = sc.pool("PT", 4, [128, 2, 512], BF16)
                s1pp = sc.pool("s1p", 2, [128, 512], F32)
                s0cp = sc.pool("s0c", 2, [128, 512], BF16)
                s1cp = sc.pool("s1c", 2, [128, 512], BF16)
                s1bp = sc.pool("s1b", 2, [128, 512], BF16)
                lb0p = sc.pool("lb0", 1, [128, 512], F32)
                lb1p = sc.pool("lb1", 1, [128, 512], F32)
                u0p = sc.pool("u0", 1, [128, 512], F32)
                u1p = sc.pool("u1", 1, [128, 512], F32)
                tqp = sc.pool("tq", 1, [128, 512], F32)
                sqo = sc.pool("sqo", 1, [128, 512], BF16)
                agp = sc.pool("arg", 1, [128, 512], F32)
                ybp = sc.pool("yb", 2, [128, 512], BF16)
                op("dve", nc.vector.memset, qz[0][64:128, :], 0.0, w=["qz0"])
                op("dve", nc.vector.memset, qz[1][0:64, :], 0.0, w=["qz1"])
                nS = [0]
                nI = [0]
                pairs = ((2, 3), (4, 5))
                for h in range(8):
                    wh, kwh = whp.next()
                    kws = []
                    for i, off in enumerate((OFF_Q, OFF_K, OFF_V, OFF_ZA)):
                        c0 = off + h * 128
                        kwi = kwh + (i,)
                        kws.append(kwi)
                        op("pool", nc.gpsimd.dma_start, out=wh[:, :, i, :], in_=w_l[:, :, c0:c0 + 128], w=[kwi], dma=kwi)
                    for which in (0, 1):
                        for tb in range(8):
                            ba = 2 + 2 * (nI[0] % 2)
                            bs = ba + 1
                            nI[0] += 1
                            cs = slice(tb * 512, (tb + 1) * 512)
                            for kc in range(8):
                                op("pe", mm, banks[ba][:], lhsT=wh[:, kc, which, :], rhs=hT[:, kc, cs],
                                   start=(kc == 0), stop=(kc == 7), r=[kws[which]] + hTk(tb), w=[bk(ba)])
                            sq, ksq = sqp.next()
                            op("act", act, out=sq[:], in_=banks[ba][:], func=AF.Square, r=[bk(ba)], w=[ksq])
                            op("pe", mm, banks[bs][:], lhsT=BD_b, rhs=sq[:], start=True, stop=True, r=[ksq] + CST, w=[bk(bs)])
                            ln, kln = lnp.next()
                            op("act", act, out=ln[:], in_=banks[bs][:], func=AF.Ln, bias=epst[:], scale=1.0 / 64.0,
                               r=[bk(bs), "eps"], w=[kln])
                            rs, krs = rsp.next()
                            op("act", act, out=rs[:], in_=ln[:], func=AF.Exp, scale=-0.5, r=[kln], w=[krs])
                            if which == 0:
                                for m in range(2):
                                    pr = slice(m * 64, (m + 1) * 64)
                                    op("dve", stt, out=qz[m][pr, cs], in0=banks[ba][pr, :], scalar=qkw[pr, 0:1], in1=rs[pr, :],
                                       op0=MUL, op1=MUL, r=[bk(ba), krs, "qkw"], w=["qz%d" % m])
                            else:
                                op("dve", stt, out=kT[:, cs], in0=banks[ba][:], scalar=qkw[:, 1:2], in1=rs[:],
                                   op0=MUL, op1=MUL, r=[bk(ba), krs, "qkw"], w=["kT"])
                    for tb in range(8):
                        ba = 2 + (nI[0] % 4)
                        nI[0] += 1
                        cs = slice(tb * 512, (tb + 1) * 512)
                        for kc in range(8):
                            op("pe", mm, banks[ba][:], lhsT=wh[:, kc, 3, :], rhs=hT[:, kc, cs],
                               start=(kc == 0), stop=(kc == 7), r=[kws[3]] + hTk(tb), w=[bk(ba)])
                        ez, kez = ezp.next()
                        op("act", act, out=ez[:], in_=banks[ba][:], func=AF.Exp, scale=-1.0, r=[bk(ba)], w=[kez])
                        op("act", act, out=ez[:], in_=ez[:], func=AF.Ln, bias=1.0, r=[kez], w=[kez])
                        op("act", act, out=ez[:], in_=ez[:], func=AF.Exp, scale=-1.0, r=[kez], w=[kez])
                        op("dve", stt, out=szT[:, cs], in0=banks[ba][:], scalar=swc[:, 0:1], in1=ez[:], op0=MUL, op1=MUL,
                           r=[bk(ba), kez, "swc"], w=["szT"])
                    for t4 in range(8):
                        b = 2 + (nI[0] % 4)
                        nI[0] += 1
                        pv = banks[b][:].rearrange("p (a b) -> p a b", b=128)
                        for i in range(4):
                            t = t4 * 4 + i
                            for kc in range(8):
                                op("pe", mm, pv[:, i, :], lhsT=hT[:, kc, t * 128:(t + 1) * 128], rhs=wh[:, kc, 2, :],
                                   start=(kc == 0), stop=(kc == 7), r=[kws[2], ("hT", t)], w=[bk(b)])
                        op("dve", nc.vector.tensor_copy, out=vt[:, t4 * 4:(t4 + 1) * 4, :], in_=pv, r=[bk(b)], w=["vt"])
                    steps = [(qb, t) for qb in range(8) for t in range(4 * qb + 4)]
                    pts = {}
                    s1ps = {}

                    def emit_qk(j):
                        qb, t = steps[j]
                        off = max(0, t - 4 * qb) * 128
                        pi = nS[0] % 2
                        nS[0] += 1
                        pb = pairs[pi]
                        diag = t >= 4 * qb
                        for m in range(2):
                            op("pe", mm, banks[pb[m]][:, 0:512 - off], lhsT=kT[:, t * 128:(t + 1) * 128],
                               rhs=qz[m][:, qb * 512 + off:(qb + 1) * 512], start=True, stop=not diag,
                               r=["kT", "qz%d" % m], w=[bk(pb[m])])
                            if diag:
                                op("pe", mm, banks[pb[m]][:, 0:128], lhsT=ident_b, rhs=Mneg_b, start=False, stop=True,
                                   r=CST, w=[bk(pb[m])])
                        pt, kpt = ptp.next()
                        pview = pbig[1 + pi][:].rearrange("p (a b) -> p a b", b=512)
                        op("act", act, out=pt[:, :, 0:512 - off], in_=pview[:, :, 0:512 - off], func=AF.Exp,
                           r=[bk(pb[0]), bk(pb[1])], w=[kpt])
                        pts[j] = (pt, kpt)

                    def emit_pv(i):
                        qb, t = steps[i]
                        off = max(0, t - 4 * qb) * 128
                        pt, kpt = pts.pop(i)
                        for m in range(2):
                            op("pe", mm, banks[m][:, off:512], lhsT=vt[:, t, :], rhs=pt[:, m, 0:512 - off],
                               start=(t == 0), stop=(t == 4 * qb + 3), r=[kpt, "vt"], w=[bk(m)])
                        if t == 0:
                            op("dve", nc.vector.tensor_copy, out=banks[6][:], in_=pt[:, 0, :], r=[kpt], w=[bk(6)])
                            op("dve", nc.vector.tensor_copy, out=banks[7][:], in_=pt[:, 1, :], r=[kpt], w=[bk(7)])
                            s1ps[qb] = s1pp.next()
                            op("pool", nc.gpsimd.memset, s1ps[qb][0][:], 0.0, w=[s1ps[qb][1]])
                        else:
                            op("dve", tt, out=banks[6][:, off:512], in0=banks[6][:, off:512], in1=pt[:, 0, 0:512 - off],
                               op=ADD, r=[kpt, bk(6)], w=[bk(6)])
                            if t % 3 == 0:
                                op("dve", tt, out=banks[7][:, off:512], in0=banks[7][:, off:512], in1=pt[:, 1, 0:512 - off],
                                   op=ADD, r=[kpt, bk(7)], w=[bk(7)])
                            else:
                                sp_, ksp = s1ps[qb]
                                op("pool", nc.gpsimd.tensor_tensor, out=sp_[:, off:512], in0=sp_[:, off:512],
                                   in1=pt[:, 1, 0:512 - off], op=ADD, r=[kpt, ksp], w=[ksp])

                    def emit_fin(qb):
                        cs = slice(qb * 512, (qb + 1) * 512)
                        s1p_, ks1p = s1ps.pop(qb)
                        s0c, ks0c = s0cp.next()
                        s1c, ks1c = s1cp.next()
                        op("dve", nc.vector.tensor_copy, out=s0c[:], in_=banks[6][:], r=[bk(6)], w=[ks0c])
                        op("dve", nc.vector.tensor_copy, out=s1c[:], in_=banks[7][:], r=[bk(7)], w=[ks1c])
                        pi = nS[0] % 2
                        nS[0] += 1
                        bx, by = pairs[pi]
                        s1b, ks1b = s1bp.next()
                        op("pool", nc.gpsimd.tensor_copy, out=s1b[:], in_=s1p_[:], r=[ks1p], w=[ks1b])
                        op("pe", mm, banks[bx][:], lhsT=ones_b[:], rhs=s0c[:], start=True, stop=True, r=[ks0c, "ones_b"], w=[bk(bx)])
                        op("pe", mm, banks[by][:], lhsT=ones_b[:], rhs=s1c[:], start=True, stop=False, r=[ks1c, "ones_b"], w=[bk(by)])
                        op("pe", mm, banks[by][:], lhsT=ones_b[:], rhs=s1b[:], start=False, stop=True, r=[ks1b, "ones_b"], w=[bk(by)])
                        lb0, kl0 = lb0p.next()
                        lb1, kl1 = lb1p.next()
                        op("act", act, out=lb0[:], in_=banks[bx][:], func=AF.Copy, scale=LSC, r=[bk(bx)], w=[kl0])
                        op("act", act, out=lb1[:], in_=banks[by][:], func=AF.Copy, scale=LSC, r=[bk(by)], w=[kl1])
                        u0, ku0 = u0p.next()
                        u1, ku1 = u1p.next()
                        op("dve", tt, out=u1[:], in0=banks[1][:], in1=lb0[:], op=MUL, r=[bk(1), kl0], w=[ku1])
                        op("dve", tt, out=u0[:], in0=banks[0][:], in1=lb1[:], op=MUL, r=[bk(0), kl1], w=[ku0])
                        op("dve", stt, out=u0[:], in0=u1[:], scalar=neglam[:, 0:1], in1=u0[:], op0=MUL, op1=ADD,
                           r=[ku0, ku1, "neglam"], w=[ku0])
                        tq, ktq = tqp.next()
                        op("pool", nc.gpsimd.tensor_tensor, out=tq[:], in0=lb0[:], in1=lb1[:], op=MUL, r=[kl0, kl1], w=[ktq])
                        op("pool", nc.gpsimd.tensor_tensor, out=tq[:], in0=tq[:], in1=tq[:], op=MUL, r=[ktq], w=[ktq])
                        sq, ksq = sqo.next()
                        op("pool", nc.gpsimd.tensor_tensor, out=sq[:], in0=u0[:], in1=u0[:], op=MUL, r=[ku0], w=[ksq])
                        op("pe", mm, banks[bx][:], lhsT=ones_b[:], rhs=sq[:], start=True, stop=True, r=[ksq, "ones_b"], w=[bk(bx)])
                        ag, kag = agp.next()
                        op("dve", stt, out=ag[:], in0=banks[bx][:], scalar=float(1.0 / (128.0 * CSC)), in1=tq[:],
                           op0=MUL, op1=ADD, r=[bk(bx), ktq], w=[kag])
                        op("act", act, out=ag[:], in_=ag[:], func=AF.Ln, r=[kag], w=[kag])
                        op("act", act, out=ag[:], in_=ag[:], func=AF.Exp, scale=-0.5, r=[kag], w=[kag])
                        op("dve", stt, out=u0[:], in0=u0[:], scalar=float(CSC ** -0.5), in1=ag[:], op0=MUL, op1=MUL,
                           r=[ku0, kag], w=[ku0])
                        yb, kyb = ybp.next()
                        op("pool", nc.gpsimd.tensor_tensor, out=yb[:], in0=u0[:], in1=szT[:, cs], op=MUL,
                           r=[ku0, "szT"], w=[kyb])
                        op("sp", nc.sync.dma_start, out=yaT_d[h][:, cs], in_=yb[:], r=[kyb], w=[("yaT", h, qb)], dma=kyb)

                    n = len(steps)
                    for i in range(-LOOK, n):
                        j = i + LOOK
                        if j < n:
                            emit_qk(j)
                        if i >= 0:
                            emit_pv(i)
                            qb, t = steps[i]
                            if t == 4 * qb + 3:
                                emit_fin(qb)

        def phase_D(l, xsrc):
            wpa_v = w_pa_d[l].rearrange("(kc p) n -> p kc n", p=128)
            wps_v = w_ps_d[l].rearrange("(kc p) n -> p kc n", p=128)
            wo_v = w_out_d[l].rearrange("(kc p) n -> p kc n", p=128)
            yaT_v = yaT_d.rearrange("h p t -> p h t")
            ysT_v = ysT_d.rearrange("k p t -> p k t")
            sgT_v = sgT_d.rearrange("k p t -> p k t")
            with Scope() as sc:
                wpa = sc.sb("wpa", [128, 8, D], BF16)
                wps = sc.sb("wps", [128, 16, D], BF16)
                wo = sc.sb("wo", [128, 8, D], BF16)
                op("pool", nc.gpsimd.dma_start, out=wpa[:], in_=wpa_v, w=["wpa"], dma="wpa")
                op("pool", nc.gpsimd.dma_start, out=wps[:, 0:8], in_=wps_v[:, 0:8], w=["wps"], dma="wps")
                op("pool", nc.gpsimd.dma_start, out=wps[:, 8:16], in_=wps_v[:, 8:16], w=["wps"], dma="wps")
                op("pool", nc.gpsimd.dma_start, out=wo[:], in_=wo_v, w=["wo"], dma="wo")
                yap = sc.pool("yab", 2, [128, 8, 512], BF16)
                ysp = sc.pool("ysb", 2, [128, 16, 512], BF16)
                sgp = sc.pool("sgb", 1, [128, 16, 512], BF16)
                xrp = sc.pool("xr", 1, [128, 4, D], F32)
                m1p = sc.pool("m1", 2, [128, 512], F32)
                m2p = sc.pool("m2", 2, [128, 512], F32)
                mTp = sc.pool("mT", 2, [128, 8, 512], BF16)
                xop = sc.pool("xo", 1, [128, 4, D], F32)
                nb = 0
                for tb in range(8):
                    tok = slice(tb * 512, (tb + 1) * 512)
                    ya, kya = yap.next()
                    ys, kys = ysp.next()
                    sg, ksg = sgp.next()
                    xr, kxr = xrp.next()
                    op("sp", nc.sync.dma_start, out=ya[:], in_=yaT_v[:, :, tok], r=[("yaT", h, tb) for h in range(8)],
                       w=[kya], dma=kya)
                    op("sp", nc.sync.dma_start, out=ys[:], in_=ysT_v[:, :, tok], r=[("ysT", tb, gi) for gi in range(4)], w=[kys], dma=kys)
                    op("sp", nc.sync.dma_start, out=sg[:], in_=sgT_v[:, :, tok], r=[("sgT", i) for i in range(16)],
                       w=[ksg], dma=ksg)
                    op("sp", nc.sync.dma_start, out=xr[:], in_=xsrc[tok, :].rearrange("(t p) c -> p t c", p=128),
                       r=[("xres", tb)], w=[kxr], dma=kxr)
                    mT, kmT = mTp.next()
                    for cc in range(8):
                        b1 = nb % 6
                        b2 = (nb + 1) % 6
                        nb += 2
                        for kc in range(8):
                            op("pe", mm, banks[b1][:], lhsT=wpa[:, kc, cc * 128:(cc + 1) * 128], rhs=ya[:, kc, :],
                               start=(kc == 0), stop=(kc == 7), r=["wpa", kya], w=[bk(b1)])
                        for kc in range(16):
                            op("pe", mm, banks[b2][:], lhsT=wps[:, kc, cc * 128:(cc + 1) * 128], rhs=ys[:, kc, :],
                               start=(kc == 0), stop=(kc == 15), r=["wps", kys], w=[bk(b2)])
                        m1, km1 = m1p.next()
                        op("dve", tt, out=m1[:], in0=banks[b1][:], in1=sg[:, cc, :], op=MUL, r=[bk(b1), ksg], w=[km1])
                        m2, km2 = m2p.next()
                        op("dve", tt, out=m2[:], in0=banks[b2][:], in1=sg[:, 8 + cc, :], op=MUL, r=[bk(b2), ksg], w=[km2])
                        op("pool", nc.gpsimd.tensor_tensor, out=mT[:, cc, :], in0=m1[:], in1=m2[:], op=ADD,
                           r=[km1, km2], w=[kmT])
                    xo, kxo = xop.next()
                    for t4 in range(4):
                        for half in range(2):
                            b = 6 + (t4 * 2 + half) % 2
                            for kc in range(8):
                                op("pe", mm, banks[b][:], lhsT=mT[:, kc, t4 * 128:(t4 + 1) * 128],
                                   rhs=wo[:, kc, half * 512:(half + 1) * 512], start=(kc == 0), stop=(kc == 7),
                                   r=[kmT, "wo"], w=[bk(b)])
                            op("dve", tt, out=xo[:, t4, half * 512:(half + 1) * 512], in0=banks[b][:],
                               in1=xr[:, t4, half * 512:(half + 1) * 512], op=ADD, r=[bk(b), kxr], w=[kxo])
                    op("sp", nc.sync.dma_start, out=out_d[tok, :].rearrange("(t p) c -> p t c", p=128), in_=xo[:],
                       r=[kxo], w=[("xres", tb)], dma=kxo)

        for l in range(depth):
            xsrc = x_d if l == 0 else out_d
            barrier()
            load_params(l)
            with Scope() as lsc:
                hT = lsc.sb("hT", [128, 8, S], BF16)
                if "A" in phases:
                    phase_A(l, hT, xsrc)
                    barrier()
                if "S" in phases:
                    if _os.environ.get("SKIPS1") != "1":
                        phase_S1(l, hT)
                        barrier()
                    if _os.environ.get("SKIPS2") != "1":
                        phase_S2(l, hT)
                        barrier()
                if "G" in phases:
                    phase_G(l, hT)
                    barrier()
                if "T" in phases:
                    phase_T(l, hT)
                    barrier()
            if "D" in phases:
                phase_D(l, xsrc)
        P.finish(sems)
    return nc, P.stats


def make_consts():
    i = np.arange(128)
    ident = np.eye(128, dtype=np.float32)
    maskU = (i[None, :] >= i[:, None]).astype(np.float32)
    Lstrict = (i[:, None] > i[None, :]).astype(np.float32)
    bd = ((i[:, None] // 64) == (i[None, :] // 64)).astype(np.float32)
    mneg = np.where(i[None, :] >= i[:, None], 0.0, -30000.0).astype(np.float32)
    return np.concatenate([ident, maskU, maskU, Lstrict, bd, mneg], axis=1).astype(np.float32)


def host_layout(inputs):
    inputs = {k: (np.asarray(v)[:DEPTH] if k != "x" else v) for k, v in inputs.items()}
    f = lambda a: np.ascontiguousarray(np.asarray(a, dtype=np.float32))
    rep = lambda a: np.ascontiguousarray(np.broadcast_to(np.asarray(a, np.float32)[:, None, :], (a.shape[0], 128, a.shape[1])))
    qn, kn = np.asarray(inputs["q_norm_w"], np.float32), np.asarray(inputs["k_norm_w"], np.float32)
    qkw = np.stack([np.tile(qn, (1, 2)), np.tile(kn, (1, 2))], axis=-1)
    cw = np.asarray(inputs["conv_w"], np.float32)
    convw = cw.transpose(0, 2, 1).reshape(DEPTH, 24, 128, 4).transpose(0, 2, 1, 3).reshape(DEPTH, 128, 96)
    convb = np.asarray(inputs["conv_b"], np.float32).reshape(DEPTH, 24, 128).transpose(0, 2, 1)
    hp = np.concatenate([inputs["dt_bias"], inputs["a_log"], inputs["d_skip"]], axis=-1).astype(np.float32)
    common = {
        "w_in": f(inputs["w_in"]), "w_pa": f(inputs["w_proj_attn"]), "w_ps": f(inputs["w_proj_ssd"]),
        "w_out": f(inputs["w_out"]),
        "nw_rep": rep(inputs["norm_w"]), "qkw": f(qkw),
        "dl_rep": rep(np.asarray(inputs["diff_lambda"], np.float32).reshape(DEPTH, 256)),
        "sw_rep": rep(inputs["subln_w"]), "swc": f(np.asarray(inputs["subln_w"], np.float32)[:, :, None]), "convw": f(convw), "convb": f(convb),
        "hp_rep": rep(hp), "snw_rep": rep(inputs["ssd_norm_w"]), "consts": make_consts(),
    }
    return common


_NC_CACHE = {}


def kernel(**inputs):
    common = host_layout(inputs)
    x = np.asarray(inputs["x"], np.float32)
    n = x.shape[0]
    if "nc" not in _NC_CACHE:
        _NC_CACHE["nc"] = build()[0]
    nc = _NC_CACHE["nc"]
    in_maps = [dict(common, x=np.ascontiguousarray(x[b])) for b in range(n)]
    res = run_bass_kernel_spmd(nc, in_maps, core_ids=list(range(n)))
    return np.stack([np.asarray(r["out"], np.float32) for r in res.results], axis=0)
```
